# Optimizing a Trainium2 kernel written in Bass

```python
import jax
import jax.numpy as jnp
from jax import lax
import numpy as np

D_MODEL = 1024
BATCH = 1
SEQ = 16384
DEPTH = 1
DEC_BATCH = 128
DEC_SEQ = 1
PAST_LEN = 8192
PAGE_SIZE = 128

A_WINDOWS = (128, 512, 2048)
A_DILATIONS = (1, 4, 16)
A_GROUPS = 3
A_HEADS = 8
A_HEAD_DIM = 64
A_NKEY = A_WINDOWS[0] // A_DILATIONS[0]
ROPE_THETA = 10000.0
HG_HEADS = 4
HG_K = 128
HG_V = 128
HG_CHUNK = 64
MEM_LEN = 256
XA_HEADS = 4
XA_HEAD_DIM = 128
EPS = 1e-6

A_QKV = A_GROUPS * A_HEADS * A_HEAD_DIM
A_OUT = A_HEADS * A_HEAD_DIM
HG_QK = HG_HEADS * HG_K
HG_OUT = HG_HEADS * HG_V
XA_W = XA_HEADS * XA_HEAD_DIM
IN_SPLITS = (A_QKV, A_QKV, A_QKV, A_OUT, HG_QK, HG_QK, HG_OUT, HG_OUT, XA_W, XA_W, D_MODEL, D_MODEL, D_MODEL)
IN_OFFSETS = tuple(sum(IN_SPLITS[:i + 1]) for i in range(len(IN_SPLITS) - 1))
N_IN = sum(IN_SPLITS)

kernel_name = 'hybrid_dilated_hgrn2_memory_step'


def _rmsnorm(x, g):
    x32 = x.astype(jnp.float32)
    y = x32 * lax.rsqrt(jnp.mean(x32 * x32, axis=-1, keepdims=True) + EPS)
    return (y * g.astype(jnp.float32)).astype(x.dtype)


def _rope(x, pos):
    dh = x.shape[-1]
    inv = ROPE_THETA ** (-jnp.arange(0, dh, 2, dtype=jnp.float32) / dh)
    ang = pos.astype(jnp.float32)[:, None] * inv[None, :]
    cos = jnp.cos(ang)[None, :, None, :]
    sin = jnp.sin(ang)[None, :, None, :]
    x32 = x.astype(jnp.float32)
    x1, x2 = x32[..., :dh // 2], x32[..., dh // 2:]
    return jnp.concatenate([x1 * cos - x2 * sin, x2 * cos + x1 * sin], axis=-1)


def _dilated_prompt(q, k, v, dil):
    B, T, H, Dh = q.shape
    L = T // dil
    blk = A_NKEY
    nb = -(-L // blk)
    Lp = nb * blk

    def residues(a):
        return a.reshape(B, L, dil, H, Dh).transpose(0, 2, 1, 3, 4)

    qb = jnp.pad(residues(q), ((0, 0), (0, 0), (0, Lp - L), (0, 0), (0, 0))).reshape(B, dil, nb, blk, H, Dh)

    def key_blocks(a):
        a = jnp.pad(residues(a), ((0, 0), (0, 0), (blk, Lp - L), (0, 0), (0, 0)))
        prev = a[:, :, :Lp].reshape(B, dil, nb, blk, H, Dh)
        cur = a[:, :, blk:].reshape(B, dil, nb, blk, H, Dh)
        return jnp.concatenate([prev, cur], axis=3)

    kb = key_blocks(k)
    vb = key_blocks(v)
    s = jnp.einsum('brnqhd,brnkhd->brnhqk', qb, kb) * (Dh ** -0.5)
    qi = jnp.arange(blk)[:, None]
    ki = jnp.arange(2 * blk)[None, :]
    dist = qi + blk - ki
    band = (dist >= 0) & (dist <= A_NKEY)
    real = (jnp.arange(nb)[:, None, None] > 0) | (ki[None] >= blk)
    mask = band[None] & real
    s = jnp.where(mask[None, None, :, None], s, -jnp.inf)
    m = jnp.max(s, axis=-1, keepdims=True)
    p = jnp.exp(s - m)
    l = jnp.sum(p, axis=-1, keepdims=True)
    o = jnp.einsum('brnhqk,brnkhd->brnqhd', p / l, vb)
    lse = (m + jnp.log(l))[..., 0].transpose(0, 1, 2, 4, 3)
    o = o.reshape(B, dil, Lp, H, Dh)[:, :, :L].transpose(0, 2, 1, 3, 4).reshape(B, T, H, Dh)
    lse = lse.reshape(B, dil, Lp, H)[:, :, :L].transpose(0, 2, 1, 3).reshape(B, T, H)
    return o, lse


def _dilated_sample(q, buf, k, v, dil):
    Wb = buf.shape[1]
    S = q.shape[1]
    Dh = q.shape[-1]
    kc = jnp.concatenate([buf[:, :, 0].astype(jnp.float32), k], axis=1)
    vc = jnp.concatenate([buf[:, :, 1].astype(jnp.float32), v], axis=1)
    idx = Wb + jnp.arange(S)[:, None] - dil * jnp.arange(A_NKEY + 1)[None, :]
    valid = idx >= 0
    idx = jnp.maximum(idx, 0)
    kg = kc[:, idx]
    vg = vc[:, idx]
    s = jnp.einsum('bshd,bsjhd->bshj', q, kg) * (Dh ** -0.5)
    s = jnp.where(valid[None, :, None, :], s, -jnp.inf)
    m = jnp.max(s, axis=-1, keepdims=True)
    p = jnp.exp(s - m)
    l = jnp.sum(p, axis=-1, keepdims=True)
    o = jnp.einsum('bshj,bsjhd->bshd', p / l, vg)
    lse = (m + jnp.log(l))[..., 0]
    return o, lse


def _hgrn2(q, k, v, logf, S0):
    B, T, H, K = q.shape
    C = min(HG_CHUNK, T)
    n = -(-T // C)
    Tp = n * C

    def chunks(a):
        a = jnp.pad(a, ((0, 0), (0, Tp - T), (0, 0), (0, 0)))
        return a.reshape(B, n, C, H, a.shape[-1]).transpose(1, 0, 2, 3, 4)

    tri = jnp.tril(jnp.ones((C, C), dtype=bool))

    def step(S, inp):
        qc, kc, vc, gc = inp
        b = jnp.cumsum(gc, axis=1)
        o_inter = jnp.einsum('bthk,bhkv->bthv', qc * jnp.exp(b), S)
        diff = b[:, :, None] - b[:, None, :]
        dec = jnp.exp(jnp.where(tri[None, :, :, None, None], diff, -jnp.inf))
        att = jnp.einsum('bthk,btshk,bshk->bhts', qc, dec, kc)
        o_intra = jnp.einsum('bhts,bshv->bthv', att, vc)
        b_last = b[:, -1]
        S_new = jnp.exp(b_last)[..., None] * S + jnp.einsum('bshk,bshv->bhkv', kc * jnp.exp(b_last[:, None] - b), vc)
        return S_new, o_inter + o_intra

    S_fin, o = lax.scan(step, S0, (chunks(q), chunks(k), chunks(v), chunks(logf)))
    o = o.transpose(1, 0, 2, 3, 4).reshape(B, Tp, H, v.shape[-1])[:, :T]
    return o, S_fin


def _layer(h, pos, attend, S0, mem_kv, lb, pre_g, post_g, w_in, hg_g, w_pa, w_pb, w_pc, w_out):
    B, T, _ = h.shape
    u = _rmsnorm(h, pre_g) @ w_in
    aq, ak, av, az, bq, bf, bi, bz, cq, cz, ga, gb, gc = jnp.split(u, IN_OFFSETS, axis=-1)

    ga_shape = (B, T, A_GROUPS, A_HEADS, A_HEAD_DIM)
    aq = _rope(aq.reshape(B, T, A_GROUPS * A_HEADS, A_HEAD_DIM), pos).reshape(ga_shape)
    ak = _rope(ak.reshape(B, T, A_GROUPS * A_HEADS, A_HEAD_DIM), pos).reshape(ga_shape)
    av = av.astype(jnp.float32).reshape(ga_shape)
    outs = []
    lses = []
    for g in range(A_GROUPS):
        o_g, lse_g = attend(g, aq[:, :, g], ak[:, :, g], av[:, :, g])
        outs.append(o_g)
        lses.append(lse_g)
    alpha = jax.nn.softmax(jnp.stack(lses, axis=0), axis=0)
    oa = jnp.sum(alpha[..., None] * jnp.stack(outs, axis=0), axis=0).reshape(B, T, A_OUT)
    ya = oa.astype(h.dtype) * jax.nn.silu(az)

    bf32 = bf.astype(jnp.float32)
    logf = jnp.log(lb + (1.0 - lb) * jax.nn.sigmoid(bf32))
    kin = (1.0 - lb) * jax.nn.sigmoid(-bf32)
    hs4 = (B, T, HG_HEADS, HG_K)
    ob, S_fin = _hgrn2(bq.astype(jnp.float32).reshape(hs4), kin.reshape(hs4),
                       bi.astype(jnp.float32).reshape(B, T, HG_HEADS, HG_V), logf.reshape(hs4), S0)
    ob = ob * lax.rsqrt(jnp.mean(ob * ob, axis=-1, keepdims=True) + EPS) * hg_g.astype(jnp.float32).reshape(HG_HEADS, HG_V)
    yb = ob.reshape(B, T, HG_OUT).astype(h.dtype) * jax.nn.silu(bz)

    mk = mem_kv[:, :, 0].astype(jnp.float32)
    mv = mem_kv[:, :, 1].astype(jnp.float32)
    cq32 = cq.astype(jnp.float32).reshape(B, T, XA_HEADS, XA_HEAD_DIM)
    sc = jnp.einsum('bthd,bmhd->bhtm', cq32, mk) * (XA_HEAD_DIM ** -0.5)
    pc = jax.nn.softmax(sc, axis=-1)
    oc = jnp.einsum('bhtm,bmhd->bthd', pc, mv).reshape(B, T, XA_W)
    yc = oc.astype(h.dtype) * jax.nn.silu(cz)

    merged = (jax.nn.sigmoid(ga) * (ya @ w_pa) + jax.nn.sigmoid(gb) * (yb @ w_pb)
              + jax.nn.sigmoid(gc) * (yc @ w_pc))
    out = h + _rmsnorm(merged @ w_out, post_g)
    return out, ak, av, S_fin


def setup_inputs(seed: int = 0) -> dict:
    key = jax.random.key(seed)
    ks = jax.random.split(key, 20)

    def nrm(k, shape, scale):
        return jax.random.normal(k, shape, jnp.float32) * scale

    wb = [min(w, PAST_LEN) for w in A_WINDOWS]
    return {
        'x_prompt': nrm(ks[0], (BATCH, SEQ, D_MODEL), 1.0),
        'x_sample': nrm(ks[1], (DEC_BATCH, DEC_SEQ, D_MODEL), 1.0),
        'mem_prompt': nrm(ks[2], (BATCH, MEM_LEN, D_MODEL), 1.0),
        'cache_win128_kv': nrm(ks[3], (DEPTH, DEC_BATCH, wb[0], 2, A_HEADS, A_HEAD_DIM), 1.0),
        'cache_win512_kv': nrm(ks[4], (DEPTH, DEC_BATCH, wb[1], 2, A_HEADS, A_HEAD_DIM), 1.0),
        'cache_win2048_kv': nrm(ks[5], (DEPTH, DEC_BATCH, wb[2], 2, A_HEADS, A_HEAD_DIM), 1.0),
        'state_hgrn': nrm(ks[6], (DEPTH, DEC_BATCH, HG_HEADS, HG_K, HG_V), 0.5),
        'cache_mem_kv': nrm(ks[7], (DEPTH, DEC_BATCH, MEM_LEN, 2, XA_HEADS, XA_HEAD_DIM), 1.0),
        'norm_pre': 1.0 + nrm(ks[8], (DEPTH, D_MODEL), 0.05),
        'norm_post': 1.0 + nrm(ks[9], (DEPTH, D_MODEL), 0.05),
        'w_in': nrm(ks[10], (DEPTH, D_MODEL, N_IN), D_MODEL ** -0.5),
        'hgrn_lb_logits': nrm(ks[11], (DEPTH + 1, HG_QK), 0.1),
        'hgrn_out_norm': 1.0 + nrm(ks[12], (DEPTH, HG_OUT), 0.05),
        'mem_norm': 1.0 + nrm(ks[13], (DEPTH, D_MODEL), 0.05),
        'w_mem_kv': nrm(ks[14], (DEPTH, D_MODEL, 2 * XA_W), D_MODEL ** -0.5),
        'w_branch_a': nrm(ks[15], (DEPTH, A_OUT, D_MODEL), A_OUT ** -0.5),
        'w_branch_b': nrm(ks[16], (DEPTH, HG_OUT, D_MODEL), HG_OUT ** -0.5),
        'w_branch_c': nrm(ks[17], (DEPTH, XA_W, D_MODEL), XA_W ** -0.5),
        'w_out': nrm(ks[18], (DEPTH, D_MODEL, D_MODEL), D_MODEL ** -0.5),
    }


def reference(x_prompt, x_sample, mem_prompt, cache_win128_kv, cache_win512_kv, cache_win2048_kv,
              state_hgrn, cache_mem_kv, norm_pre, norm_post, w_in, hgrn_lb_logits, hgrn_out_norm,
              mem_norm, w_mem_kv, w_branch_a, w_branch_b, w_branch_c, w_out):
    win_caches = (cache_win128_kv, cache_win512_kv, cache_win2048_kv)
    lb_all = jnp.cumsum(jax.nn.softmax(hgrn_lb_logits.astype(jnp.float32), axis=0), axis=0)
    Bp, T, _ = x_prompt.shape
    Bd, S, _ = x_sample.shape
    M = mem_prompt.shape[1]
    pos_p = jnp.arange(T, dtype=jnp.int32)
    pos_s = PAST_LEN + jnp.arange(S, dtype=jnp.int32)
    hp = x_prompt
    hs = x_sample
    p_win = [[] for _ in range(A_GROUPS)]
    s_win = [[] for _ in range(A_GROUPS)]
    p_hg, s_hg, p_mem = [], [], []

    def attend_p(g, q, k, v):
        return _dilated_prompt(q, k, v, A_DILATIONS[g])

    for l in range(DEPTH):
        shared = (lb_all[l], norm_pre[l], norm_post[l], w_in[l], hgrn_out_norm[l],
                  w_branch_a[l], w_branch_b[l], w_branch_c[l], w_out[l])
        mem_kv = (_rmsnorm(mem_prompt, mem_norm[l]) @ w_mem_kv[l]).reshape(Bp, M, 2, XA_HEADS, XA_HEAD_DIM)
        S0 = jnp.zeros((Bp, HG_HEADS, HG_K, HG_V), jnp.float32)
        hp, akp, avp, Sp = _layer(hp, pos_p, attend_p, S0, mem_kv, *shared)
        for g in range(A_GROUPS):
            wb = min(A_WINDOWS[g], T)
            p_win[g].append(jnp.stack([akp[:, T - wb:, g], avp[:, T - wb:, g]], axis=2).astype(x_prompt.dtype))
        p_hg.append(Sp.astype(x_prompt.dtype))
        p_mem.append(mem_kv)
        bufs = tuple(c[l] for c in win_caches)

        def attend_s(g, q, k, v, bufs=bufs):
            return _dilated_sample(q, bufs[g], k, v, A_DILATIONS[g])

        hs, aks, avs, Ss = _layer(hs, pos_s, attend_s, state_hgrn[l].astype(jnp.float32), cache_mem_kv[l], *shared)
        for g in range(A_GROUPS):
            s_win[g].append(jnp.stack([aks[:, :, g], avs[:, :, g]], axis=2).astype(x_sample.dtype))
        s_hg.append(Ss.astype(x_sample.dtype))

    new_win128_p = jnp.stack(p_win[0], axis=0)
    new_win512_p = jnp.stack(p_win[1], axis=0)
    new_win2048_p = jnp.stack(p_win[2], axis=0)
    new_hgrn_p = jnp.stack(p_hg, axis=0)
    new_mem_kv_p = jnp.stack(p_mem, axis=0)
    new_win128_s = jnp.stack(s_win[0], axis=0)
    new_win512_s = jnp.stack(s_win[1], axis=0)
    new_win2048_s = jnp.stack(s_win[2], axis=0)
    new_hgrn_s = jnp.stack(s_hg, axis=0)
    return (hp, hs, new_win128_p, new_win512_p, new_win2048_p, new_hgrn_p, new_mem_kv_p,
            new_win128_s, new_win512_s, new_win2048_s, new_hgrn_s)
```

```python
import numpy as np
from contextlib import ExitStack
import concourse.bass as bass
import concourse.mybir as mybir
from concourse.bass_utils import run_bass_kernel_spmd

F32 = mybir.dt.float32
BF16 = mybir.dt.bfloat16
AF = mybir.ActivationFunctionType
ALU = mybir.AluOpType
AX = mybir.AxisListType

NCORE = 8
T = 16384
TPC = 2048
NT = 16
D = 1024
NIN = 11264
EPS = 1e-6
PAST = 8192
DILS = (1, 4, 16)
NPREV = (NCORE - 1) * NT
NEG = -30000.0
C_AQ, C_AK, C_AV, C_AZ = 0, 1536, 3072, 4608
C_BQ, C_BF, C_BI, C_BZ = 5120, 5632, 6144, 6656
C_CQ, C_CZ = 7168, 7680
C_GA, C_GB, C_GC = 8192, 9216, 10240

_CF = {}
_off = 0
for _n, _w in (("identf", 128), ("Um", 128), ("M2", 128), ("attm", 128), ("ind", 2), ("pregT", 8), ("memgT", 8),
               ("postg", 1024), ("hg", 512), ("l0", 512), ("l1", 512), ("cmask", 8), ("eye16", 256),
               ("ropes", 64)):
    _CF[_n] = (_off, _off + _w)
    _off += _w
NCF = _off
NCB = 772

def _pa_blocks():
    out = []
    for g, d in enumerate(DILS):
        nb = NT // d
        for r in range(d):
            for n in range(-1, nb):
                out.append((g, r, n))
    return out
PA_BLOCKS = _pa_blocks()
PA_INDEX = {b: i for i, b in enumerate(PA_BLOCKS)}


class Trk:
    ENGS = ("pe", "act", "dve", "pool", "sp")

    def __init__(self):
        self.streams = {e: [] for e in self.ENGS}
        self.cnt = {e: 0 for e in self.ENGS}
        self.known = {e: {} for e in self.ENGS}
        self.lastw = {}
        self.readers = {}
        self.lanegroups = {}
        self.lanecnt = {}
        self.lane_rr = {}

    def add_lanes(self, group, n):
        names = [f"{group}{i}" for i in range(n)]
        self.lanegroups[group] = names
        self.lane_rr[group] = 0
        for x in names:
            self.lanecnt[x] = 0

    def _deps(self, eng, r, w):
        deps = {}

        def add(k, c):
            if deps.get(k, 0) < c:
                deps[k] = c
        for key in r:
            lw = self.lastw.get(key)
            if lw is not None:
                add(lw[0], lw[1])
        for key in w:
            lw = self.lastw.get(key)
            if lw is not None:
                add(lw[0], lw[1])
            for k, c in self.readers.get(key, {}).items():
                add(k, c)
        waits = []
        for k, c in deps.items():
            if k == ("E", "pe") and eng == "pe":
                continue
            if self.known[eng].get(k, 0) >= c:
                continue
            self.known[eng][k] = c
            waits.append((k, c))
        return waits

    def _commit(self, k, c, r, w):
        for key in w:
            self.lastw[key] = (k, c)
            self.readers[key] = {}
        for key in r:
            d = self.readers.setdefault(key, {})
            if d.get(k, 0) < c:
                d[k] = c

    def op(self, eng, fn, r=(), w=()):
        waits = self._deps(eng, r, w)
        self.cnt[eng] += 1
        k = ("E", eng)
        self.streams[eng].append((waits, fn, k))
        self._commit(k, self.cnt[eng], r, w)

    def dma(self, q, group, fn, r=(), w=()):
        lanes = self.lanegroups[group]
        i = self.lane_rr[group]
        self.lane_rr[group] = (i + 1) % len(lanes)
        lane = lanes[i]
        waits = self._deps(q, r, w)
        k = ("L", lane)
        c = self.lanecnt[lane]
        if c > 0 and self.known[q].get(k, 0) < c:
            self.known[q][k] = c
            waits.append((k, c))
        self.lanecnt[lane] = c + 1
        self.streams[q].append((waits, fn, k))
        self._commit(k, c + 1, r, w)

    def barrier(self):
        for e in self.ENGS:
            waits = []
            for o in self.ENGS:
                if o == "sp" or self.cnt[o] == 0:
                    continue
                k = ("E", o)
                if o == e and e == "pe":
                    continue
                if self.known[e].get(k, 0) < self.cnt[o]:
                    self.known[e][k] = self.cnt[o]
                    waits.append((k, self.cnt[o]))
            for lane, c in self.lanecnt.items():
                k = ("L", lane)
                if c > 0 and self.known[e].get(k, 0) < c:
                    self.known[e][k] = c
                    waits.append((k, c))
            if waits:
                self.streams[e].append((waits, None, None))

    def emit(self, nc):
        with ExitStack() as st:
            sems = {}
            for e in self.ENGS:
                if e != "sp":
                    sems[("E", e)] = st.enter_context(nc.semaphore("e_" + e))
            for lane in self.lanecnt:
                sems[("L", lane)] = st.enter_context(nc.semaphore("l_" + lane))
            block = st.enter_context(nc.Block())

            def mult(k):
                return 16 if (k[0] == "L" and not k[1].startswith("cc")) else 1

            def run(eng, h):
                for waits, fn, mine in self.streams[eng]:
                    for k, c in waits:
                        h.wait_ge(sems[k], c * mult(k))
                    if fn is None:
                        continue
                    ins = fn(h)
                    if mine[0] == "L" and mine[1].startswith("cc"):
                        ins.then_inc(sems[mine])
                    else:
                        ins.then_inc(sems[mine], mult(mine))

            block.tensor(lambda h: run("pe", h))
            block.scalar(lambda h: run("act", h))
            block.vector(lambda h: run("dve", h))
            block.gpsimd(lambda h: run("pool", h))
            block.sync(lambda h: run("sp", h))


import os
class _Stop(Exception):
    pass


def build_program(do_sample=True):
    nc, trk, body = _build_program(do_sample)
    try:
        body()
    except _Stop:
        pass
    trk.barrier()
    trk.emit(nc)
    return nc


def _build_program(do_sample=True):
    do_sample = do_sample and not os.environ.get("K_NOSAMPLE")
    no_cc = bool(os.environ.get("K_NOCC"))
    stop = float(os.environ.get("K_STOP", "99"))
    nc = bass.Bass("TRN2", target_bir_lowering=False)
    trk = Trk()
    trk.add_lanes("ld", 6)
    trk.add_lanes("st", 8)
    trk.add_lanes("wl", 4)
    trk.add_lanes("cc", 1)

    def din(name, shape):
        return nc.dram_tensor(name, list(shape), F32, kind="ExternalInput").ap()

    def dout(name, shape):
        return nc.dram_tensor(name, list(shape), F32, kind="ExternalOutput").ap()

    xe = din("xe", (2 * TPC, D))
    SM = 16 if do_sample else 1
    xh = din("xh", (NPREV * 128, D))
    xsamp = din("xs", (16, D))
    mem = din("mem", (256, D))
    cw = [din("cw0", (SM, 128, 1024)), din("cw1", (SM, 512, 1024)), din("cw2", (SM, 2048, 1024))]
    sh = din("sh", (SM, 4, 128, 128))
    cm = din("cm", (SM, 256, 1024))
    w_in = din("w_in", (D, NIN))
    w_mem = din("w_mem", (D, 1024))
    w_pa = din("w_pa", (512, D))
    w_pb = din("w_pb", (512, D))
    w_pc = din("w_pc", (512, D))
    w_out = din("w_out", (D, D))
    cf_d = din("cf", (128, NCF))
    cb_d = din("cb", (128, NCB))
    rope_d = din("rope", (len(PA_BLOCKS), 128, 64))
    selb_d = din("selb", (16, 16, 128))

    y_p = dout("y_p", (TPC, D))
    y_s = dout("y_s", (16, D))
    nw = [dout("nw0", (128, 2, 512)), dout("nw1", (512, 2, 512)), dout("nw2", (2048, 2, 512))]
    hg_p = dout("hg_p", (4, 128, 128))
    mkv = dout("mkv", (256, 1024))
    nws = [dout("nws0", (16, 2, 512)), dout("nws1", (16, 2, 512)), dout("nws2", (16, 2, 512))]
    hg_s = dout("hg_s", (16, 4, 128, 128))

    ga_d = nc.dram_tensor("ga_scr", [TPC, D], F32).ap()

    w_in_r = w_in.rearrange("(c p) n -> p c n", p=128)

    st = ExitStack()

    def chk(v):
        if stop <= v:
            raise _Stop()

    def body():
        def sb(name, shape, dt=F32):
            return st.enter_context(nc.sbuf_tensor(name, list(shape), dt))

        cf = sb("cf_sb", (128, NCF))
        cb = sb("cb_sb", (128, NCB), BF16)
        big = sb("big", (128, 30720))
        xt = [sb("xt0", (128, D)), sb("xt1", (128, D))]
        xsb = sb("xsb", (128, D), BF16)
        xnT = [sb("xnT0", (128, 8, 128), BF16), sb("xnT1", (128, 8, 128), BF16)]
        ft = [sb(f"ft{i}", (128, 512)) for i in range(10)]
        bt = [sb(f"bt{i}", (128, 512), BF16) for i in range(8)]
        sm = sb("sm", (128, 128))
        Sst = sb("Sst", (128, 4, 128))
        Sbf = sb("Sbf", (128, 4, 128), BF16)
        Dacc = sb("Dacc", (128, 4))
        ropet = [sb("ropet0", (128, 64)), sb("ropet1", (128, 64))]
        ptb = [sb("ptb0", (128, 256), BF16), sb("ptb1", (128, 256), BF16)]
        _kb = 28000
        kTs = [big[:, _kb + i * 256:_kb + (i + 1) * 256].bitcast(BF16).rearrange("p (c n) -> p c n", c=4) for i in range(3)]
        _vb = _kb + 768
        vss = [big[:, _vb + i * 260:_vb + (i + 1) * 260].bitcast(BF16).rearrange("p (h d) -> p h d", h=8) for i in range(3)]
        _qb = _vb + 780
        qTs = [big[:, _qb + i * 256:_qb + (i + 1) * 256].bitcast(BF16).rearrange("p (c n) -> p c n", c=4) for i in range(2)]
        mkT = sb("mkT", (128, 4, 256), BF16)
        mvp = sb("mvp", (128, 2, 4, 129), BF16)
        gall = big[:, 8192:8192 + 8 * 516].rearrange("p (c n) -> p c n", c=8)
        psum = [st.enter_context(nc.psum_tensor(f"ps{i}", [128, 1024], F32)) for i in range(4)]

        def C(name):
            a, b = _CF[name]
            return cf[:, a:b]

        identf = C("identf")
        identb = cb[:, 0:128]
        mprev = cb[:, 128:256]
        mcur = cb[:, 256:384]
        mfirst = cb[:, 384:512]
        M2b = cb[:, 512:640]
        Umb = cb[:, 640:768]
        indb = cb[:, 768:770]
        lh = [sb(f"lh{i}", (128, 512), BF16) for i in range(4)]

        def bank(i):
            return psum[i // 2][:, (i % 2) * 512:(i % 2) * 512 + 512]

        def bankb(i):
            return bank(i).bitcast(BF16)

        _rr = {}

        def rr(name, items):
            i = _rr.get(name, 0)
            _rr[name] = i + 1
            return items[i % len(items)]

        def rri(name, n):
            i = _rr.get(name, 0)
            _rr[name] = i + 1
            return i % n

        def act(out, in_, func, r, w, **kw):
            trk.op("act", lambda e: e.activation(out=out, in_=in_, func=func, **kw), r, w)

        def tt(eng, out, in0, in1, op, r, w):
            trk.op(eng, lambda e: e.tensor_tensor(out=out, in0=in0, in1=in1, op=op), r, w)

        def ts(eng, out, in0, s1, s2, op0, op1, r, w):
            if s2 is None:
                trk.op(eng, lambda e: e.tensor_scalar(out=out, in0=in0, scalar1=s1, scalar2=None, op0=op0), r, w)
            else:
                trk.op(eng, lambda e: e.tensor_scalar(out=out, in0=in0, scalar1=s1, scalar2=s2, op0=op0, op1=op1), r, w)

        def stt(eng, out, in0, scalar, in1, op0, op1, r, w):
            trk.op(eng, lambda e: e.scalar_tensor_tensor(out=out, in0=in0, scalar=scalar, in1=in1, op0=op0, op1=op1), r, w)

        def cp(eng, out, in_, r, w):
            if eng == "act":
                trk.op("act", lambda e: e.copy(out=out, in_=in_), r, w)
            else:
                trk.op(eng, lambda e: e.tensor_copy(out=out, in_=in_), r, w)

        def ld(out, in_, w, q="sp", r=()):
            trk.dma(q, "ld", lambda e: e.dma_start(out=out, in_=in_), r, w)

        def stdma(out, in_, r, q="sp"):
            trk.dma(q, "st", lambda e: e.dma_start(out=out, in_=in_), r, ())

        def wload(out, in_, w):
            trk.dma("pool", "wl", lambda e: e.dma_start(out=out, in_=in_), (), w)

        _stg = [0]

        def wload_nocast(out, in_, w):
            i = _stg[0] % 2
            _stg[0] += 1
            n = 1
            for d_ in out.shape[1:]:
                n *= d_
            stage = big[:, 12288 + i * 4096:12288 + i * 4096 + n]
            if len(out.shape) == 3:
                stage = stage.rearrange("p (c n) -> p c n", c=out.shape[1])
            ld(stage, in_, [f"stage{i}"])
            cp("pool" if i else "dve", out, stage, [f"stage{i}"], w)

        def mm(groups, r, w):
            def f(e):
                ins = None
                for (o, l, rh, s0, s1) in groups:
                    ins = e.matmul(o, lhsT=l, rhs=rh, start=s0, stop=s1)
                return ins
            trk.op("pe", f, r, w)

        def transposes(items, r, w):
            def f(e):
                ins = None
                for (o, i, idn) in items:
                    ins = e.transpose(out=o, in_=i, identity=idn)
                return ins
            trk.op("pe", f, r, w)

        def rstd_from_ss(ss, rs, n, scale, key):
            ts("dve", rs, ss, scale, EPS, ALU.mult, ALU.add, [key], [key + "r"])
            act(rs, rs, AF.Sqrt, [key + "r"], [key + "r"])
            trk.op("dve", lambda e: e.reciprocal(out=rs, in_=rs), [key + "r"], [key + "r"])

        nt_i = [0]

        def norm_T(src, n, gT, keep_x=False):
            i = nt_i[0]
            nt_i[0] += 1
            x_t = xt[i % 2]
            xk = f"xt{i % 2}"
            xn = xnT[i % 2]
            xnk = f"xnT{i % 2}"
            ld(x_t[:n, :], src, [xk])
            ss = sm[:n, 0:1]
            rs = sm[:n, 1:2]
            act(xsb[:n, :], x_t[:n, :], AF.Square, [xk], ["xsb", "nss"], accum_out=ss)
            rstd_from_ss(ss, rs, n, 1.0 / D, "nss")
            act(xsb[:n, :], x_t[:n, :], AF.Copy, [xk, "nssr"], ["xsb"], scale=rs)
            b = rr("tb", [6, 7])
            pb = bankb(b).rearrange("p (c n) -> p c n", c=8)
            transposes([(pb[:, c, :n], xsb[:n, c * 128:(c + 1) * 128], identb[:n, :n]) for c in range(8)],
                       ["xsb"], [f"bank{b}"])
            tt("dve", xn[:, :, :n], pb[:, :, :n], gT.unsqueeze(2).to_broadcast([128, 8, n]), ALU.mult,
               [f"bank{b}"], [xnk])
            return xn, xnk, x_t, xk

        def proj(xn, xnk, n, wblk, wkey, b):
            mm([(bank(b)[:n, :], xn[:, c, :n], wblk[:, c, :], c == 0, c == 7) for c in range(8)],
               [xnk, wkey], [f"bank{b}"])

        def carve_w(idx):
            return big[:, idx * 2048:(idx + 1) * 2048].bitcast(BF16).rearrange("p (c n) -> p c n", c=8)

        def bk(i):
            return f"bank{i}"

        def rope4(ap):
            return ap.rearrange("p (h t f) -> p h t f", h=8, t=2)

        def rope(src, srckeys, out_ap, okey, n, cosb, sinb, tabkeys, A=None, B=None):
            A = ft[0] if A is None else A
            B = ft[1] if B is None else B
            ka, kb_ = "ft%d" % ft.index(A), "ft%d" % ft.index(B)
            tt("dve", rope4(A[:n, :]), rope4(src), cosb, ALU.mult, list(srckeys) + list(tabkeys), [ka])
            tt("dve", rope4(B[:n, :]), rope4(src), sinb, ALU.mult, list(srckeys) + list(tabkeys), [kb_])
            A4 = rope4(A[:n, :]); B4 = rope4(B[:n, :]); O4 = rope4(out_ap)
            tt("pool", O4[:, :, 0, :], A4[:, :, 0, :], B4[:, :, 1, :], ALU.subtract, [ka, kb_], [okey])
            tt("pool", O4[:, :, 1, :], A4[:, :, 1, :], B4[:, :, 0, :], ALU.add, [ka, kb_], [okey])

        ld(cf[:, :], cf_d[:, :], ["cf"])
        wload(cb[:, :], cb_d[:, :], ["cb"])
        trk.barrier()
        l0 = C("l0")
        l1 = C("l1")
        lbr = l0
        omlr = l1
        tt("dve", l0, l0, l1, ALU.subtract, ["cf"], ["cf"])
        act(l0, l0, AF.Sigmoid, ["cf"], ["cf"])
        ts("dve", l1, l0, -1.0, 1.0, ALU.mult, ALU.add, ["cf"], ["cf"])
        trk.op("pool", lambda e: e.memset(Sst[:], 0.0), (), ["S"])
        trk.op("pool", lambda e: e.memset(Dacc[:], 1.0), (), ["Dacc"])
        trk.op("pool", lambda e: e.memset(mvp[:, :, :, 128:129], 1.0), (), ["mvp"])
        trk.barrier()

        wm = [carve_w(0), carve_w(1)]
        w_mem_r = w_mem.rearrange("(c p) n -> p c n", p=128)
        for j in range(2):
            wload(wm[j], w_mem_r[:, :, j * 512:(j + 1) * 512], [f"wm{j}"])
        for mt in range(2):
            xn, xnk, _, _ = norm_T(mem[mt * 128:(mt + 1) * 128, :], 128, C("memgT"))
            for j in range(2):
                b = rri("pj", 4)
                proj(xn, xnk, 128, wm[j], f"wm{j}", b)
                fi = rri("ftA", 2)
                f = ft[fi]; fk = f"ft{fi}"
                cp("act", f[:, :], bank(b), [bk(b)], [fk])
                stdma(mkv[mt * 128:(mt + 1) * 128, j * 512:(j + 1) * 512], f[:, :], [fk])
                if j == 0:
                    cp("dve", bt[0][:, :], f[:, :], [fk], ["bt0"])
                    tb = 6 + rri("tb", 2)
                    pbk = bankb(tb).rearrange("p (c n) -> p c n", c=8)
                    transposes([(pbk[:, h, :], bt[0][:, h * 128:(h + 1) * 128], identb) for h in range(4)],
                               ["bt0"], [bk(tb)])
                    cp("dve", mkT[:, :, mt * 128:(mt + 1) * 128], pbk[:, 0:4, :], [bk(tb)], ["mkT"])
                else:
                    cp("dve", mvp[:, mt, :, 0:128], f[:, :].rearrange("p (h d) -> p h d", h=4), [fk], ["mvp"])
        trk.barrier()

        chk(0)

        def hgrn_tile(xn, xnk, full, wkeys, par=0):
            wq, wf, wi, wz = wkeys
            Um = C("Um"); M2 = C("M2"); attm = C("attm"); ind = C("ind")
            fo = 4 * par; bo = 2 * par
            b_f, b_i = (0, 1) if par == 0 else (4, 5)
            proj(xn, xnk, 128, wf[0], wf[1], b_f)
            proj(xn, xnk, 128, wi[0], wi[1], b_i)
            sig = ft[fo + 0]; logf = ft[fo + 1]; kin = ft[fo + 2]
            k0, k1, k2, k3 = (f"ft{fo + i}" for i in range(4))
            kb0, kb1 = f"bt{bo}", f"bt{bo + 1}"
            eblk = f"ebl{par}"
            act(sig[:, :], bank(b_f), AF.Sigmoid, [bk(b_f)], [k0])
            tt("dve", sig[:, :], sig[:, :], omlr, ALU.mult, [k0], [k0])
            tt("pool", sig[:, :], sig[:, :], lbr, ALU.add, [k0], [k0])
            act(logf[:, :], sig[:, :], AF.Ln, [k0], [k1])
            ts("pool", kin[:, :], sig[:, :], -1.0, 1.0, ALU.mult, ALU.add, [k0], [k2])
            vb = bt[bo]
            cp("act", vb[:, :], bank(b_i), [bk(b_i)], [kb0])
            lhi = lh[2 * par]; llo = lh[2 * par + 1]
            khi, klo = f"lh{2 * par}", f"lh{2 * par + 1}"
            cp("dve", lhi[:, :], logf[:, :], [k1], [khi])
            tt("pool", llo[:, :], logf[:, :], lhi[:, :], ALU.subtract, [k1, khi], [klo])
            mm([(bank(2), M2b, lhi[:, :], True, False), (bank(2), M2b, llo[:, :], False, True)], [khi, klo], [bk(2)])
            ek = ft[fo + 3]
            act(ek[:, :], bank(2), AF.Exp, [bk(2)], [k3])
            khat = bt[bo + 1]
            tt("dve", khat[:, :], kin[:, :], ek[:, :], ALU.mult, [k2, k3], [kb1])
            pbl = bank(3)[:, 0:8].rearrange("p (h t) -> p h t", t=2)
            grp_ = []
            for h in range(4):
                grp_.append((pbl[:, h, :], lhi[:, h * 128:(h + 1) * 128], indb, True, False))
                grp_.append((pbl[:, h, :], llo[:, h * 128:(h + 1) * 128], indb, False, True))
            mm(grp_, [khi, klo], [bk(3)])
            ebl = (sm[:, 8:16] if par == 0 else sm[:, 100:108]).rearrange("p (h t) -> p h t", t=2)
            act(ebl, pbl, AF.Exp, [bk(3)], [eblk])
            if full:
                chk(3.52)
                proj(xn, xnk, 128, wq[0], wq[1], 4)
                proj(xn, xnk, 128, wz[0], wz[1], 5)
                mm([(bank(2), Umb, lhi[:, :], True, False), (bank(2), Umb, llo[:, :], False, True)], [khi, klo], [bk(2)])
                eq = ft[3]; ekm = ft[4]
                act(eq[:, :], bank(2), AF.Exp, [bk(2)], ["ft3"])
                act(ekm[:, :], bank(2), AF.Exp, [bk(2)], ["ft4"], scale=-1.0)
                chk(3.55)
                qt = bt[2]; kt = bt[3]
                tt("dve", qt[:, :], bank(4), eq[:, :], ALU.mult, [bk(4), "ft3"], ["bt2"])
                tt("pool", kt[:, :], kin[:, :], ekm[:, :], ALU.mult, ["ft2", "ft4"], ["bt3"])
                G = ft[5]
                act(G[:, :], bank(5), AF.Silu, [bk(5)], ["ft5"])
                tt("pool", G[:, :], G[:, :], C("hg"), ALU.mult, ["ft5"], ["ft5"])
                chk(3.57)
                pq = bankb(6).rearrange("p (c n) -> p c n", c=8)
                pk2 = bankb(7).rearrange("p (c n) -> p c n", c=8)
                transposes([(pq[:, h, :], qt[:, h * 128:(h + 1) * 128], identb) for h in range(4)], ["bt2"], [bk(6)])
                transposes([(pk2[:, h, :], kt[:, h * 128:(h + 1) * 128], identb) for h in range(4)], ["bt3"], [bk(7)])
                qkT = bt[4].rearrange("p (c n) -> p c n", c=4)
                kkT = bt[5].rearrange("p (c n) -> p c n", c=4)
                cp("act", qkT, pq[:, 0:4, :], [bk(6)], ["bt4"])
                cp("dve", kkT, pk2[:, 0:4, :], [bk(7)], ["bt5"])
                chk(3.6)
                tt("dve", Sbf[:], Sst[:], ebl[:, :, 0:1].to_broadcast([128, 4, 128]), ALU.mult, ["S", eblk], ["Sbf"])
                pa = bank(2)
                mm([(pa[:, h * 128:(h + 1) * 128], kkT[:, h, :], qkT[:, h, :], True, True) for h in range(4)],
                   ["bt4", "bt5"], [bk(2)])
                attb = bt[6]
                tt("dve", attb[:, :].rearrange("p (h t) -> p h t", h=4), pa.rearrange("p (h t) -> p h t", h=4),
                   attm.unsqueeze(1).to_broadcast([128, 4, 128]), ALU.mult, [bk(2)], ["bt6"])
                chk(3.7)
                po = bank(4)
                grp = []
                for h in range(4):
                    grp.append((po[:, h * 128:(h + 1) * 128], qkT[:, h, :], Sbf[:, h, :], True, False))
                    grp.append((po[:, h * 128:(h + 1) * 128], attb[:, h * 128:(h + 1) * 128], vb[:, h * 128:(h + 1) * 128], False, True))
                mm(grp, ["bt4", "Sbf", "bt6", "bt0"], [bk(4)])
                chk(3.8)
            pP = bank(b_f)
            mm([(pP[:, h * 128:(h + 1) * 128], khat[:, h * 128:(h + 1) * 128], vb[:, h * 128:(h + 1) * 128], True, True)
                for h in range(4)], [kb1, kb0], [bk(b_f)])
            for h in range(4):
                stt("dve", Sst[:, h, :], Sst[:, h, :], ebl[:, h, 1:2], pP[:, h * 128:(h + 1) * 128], ALU.mult, ALU.add,
                    ["S", eblk, bk(b_f)], ["S"])
            if not full:
                return None
            return head_norm_gate(bank(4), bk(4), G, "ft5", 128)

        def head_norm_gate(po, pok, G, gk, n):
            ss4 = sm[:n, 16:20]
            rs4 = sm[:n, 20:24]
            for h in range(4):
                act(ft[6][:n, h * 128:(h + 1) * 128], po[:n, h * 128:(h + 1) * 128], AF.Square, [pok], ["ft6", "ss4"],
                    accum_out=ss4[:, h:h + 1])
            rstd_from_ss(ss4, rs4, n, 1.0 / 128, "ss4")
            yb = bt[2]
            for h in range(4):
                stt("dve", yb[:n, h * 128:(h + 1) * 128], po[:n, h * 128:(h + 1) * 128], rs4[:, h:h + 1],
                    G[:n, h * 128:(h + 1) * 128], ALU.mult, ALU.mult, [pok, "ss4r", gk], ["bt2"])
            pq = bankb(6).rearrange("p (c n) -> p c n", c=8)
            transposes([(pq[:, h, :n], yb[:n, h * 128:(h + 1) * 128], identb[:n, :n]) for h in range(4)], ["bt2"], [bk(6)])
            ybT = bt[7].rearrange("p (c n) -> p c n", c=4)
            cp("act", ybT[:, :, :n], pq[:, 0:4, :n], [bk(6)], ["bt7"])
            return ybT

        wf_b = carve_w(0); wi_b = carve_w(1)
        wload(wf_b, w_in_r[:, :, C_BF:C_BF + 512], ["w0"])
        wload(wi_b, w_in_r[:, :, C_BI:C_BI + 512], ["w1"])
        for t in range(NPREV):
            xn, xnk, _, _ = norm_T(xh[t * 128:(t + 1) * 128, :], 128, C("pregT"))
            hgrn_tile(xn, xnk, False, (None, (wf_b, "w0"), (wi_b, "w1"), None), par=t % 2)
        trk.barrier()
        chk(1)

        acc = big[:, 0:16384].rearrange("p (h t) -> p h t", h=8)
        trk.op("pool", lambda e: e.memset(big[0:65, 0:16384], 0.0), (), ["acc"])
        for i in range(3):
            trk.op("pool", lambda e, i=i: e.memset(vss[i][:, :, 64:65], 1.0), (), [f"vss{i}"])
        WB = 8
        for g, d in enumerate(DILS):
            wq = carve_w(WB + 0); wk = carve_w(WB + 1); wv = carve_w(WB + 2)
            wload(wq, w_in_r[:, :, C_AQ + g * 512:C_AQ + (g + 1) * 512], ["wA0"])
            wload(wk, w_in_r[:, :, C_AK + g * 512:C_AK + (g + 1) * 512], ["wA1"])
            wload(wv, w_in_r[:, :, C_AV + g * 512:C_AV + (g + 1) * 512], ["wA2"])
            nb = NT // d
            wb_tokens = 128 * d
            kv_i = 0
            for r in range(d):
                prev = None
                for n in range(-1, nb):
                    own = n >= 0
                    start = TPC + d * 128 * n + r
                    src = xe[start:start + 127 * d + 1:d, :] if d > 1 else xe[start:start + 128, :]
                    xn, xnk, _, _ = norm_T(src, 128, C("pregT"))
                    ri = rri("ropet", 2); rt = ropet[ri]; rk = f"ropet{ri}"
                    ld(rt[:, :], rope_d[PA_INDEX[(g, r, n)], :, :], [rk])
                    cosb = rt[:, 0:32].unsqueeze(1).unsqueeze(1).to_broadcast([128, 8, 2, 32])
                    sinb = rt[:, 32:64].unsqueeze(1).unsqueeze(1).to_broadcast([128, 8, 2, 32])
                    ks = kv_i % 3
                    kv_i += 1
                    proj(xn, xnk, 128, wk, "wA1", 1)
                    rope(bank(1), [bk(1)], ft[2][:, :], "ft2", 128, cosb, sinb, [rk])
                    need_out = own and (d * (128 * n + 127) + r >= TPC - wb_tokens)
                    if need_out:
                        t0 = d * 128 * n + r - (TPC - wb_tokens)
                        dst = nw[g][t0:t0 + 127 * d + 1:d, 0, :] if d > 1 else nw[g][t0:t0 + 128, 0, :]
                        stdma(dst, ft[2][:, :], ["ft2"])
                    cp("pool", bt[0][:, :], ft[2][:, :], ["ft2"], ["bt0"])
                    pk = bankb(6).rearrange("p (c n) -> p c n", c=8)
                    transposes([(pk[:, c, :], bt[0][:, c * 128:(c + 1) * 128], identb) for c in range(4)], ["bt0"], [bk(6)])
                    cp("act", kTs[ks], pk[:, 0:4, :], [bk(6)], [f"kTs{ks}"])
                    proj(xn, xnk, 128, wv, "wA2", 2)
                    if need_out:
                        cp("act", ft[3][:, :], bank(2), [bk(2)], ["ft3"])
                        dst = nw[g][t0:t0 + 127 * d + 1:d, 1, :] if d > 1 else nw[g][t0:t0 + 128, 1, :]
                        stdma(dst, ft[3][:, :], ["ft3"])
                    cp("act", vss[ks][:, :, 0:64], bank(2).rearrange("p (h d) -> p h d", h=8), [bk(2)], [f"vss{ks}"])
                    if own:
                        proj(xn, xnk, 128, wq, "wA0", 0)
                        rope(bank(0), [bk(0)], ft[6][:, :], "ft6", 128, cosb, sinb, [rk], A=ft[4], B=ft[5])
                        cp("pool", bt[1][:, :], ft[6][:, :], ["ft6"], ["bt1"])
                        pq = bankb(7).rearrange("p (c n) -> p c n", c=8)
                        transposes([(pq[:, c, :], bt[1][:, c * 128:(c + 1) * 128], identb) for c in range(4)], ["bt1"], [bk(7)])
                        qi = rri("qTs", 2)
                        cp("dve", qTs[qi], pq[:, 0:4, :], [bk(7)], [f"qTs{qi}"])
                        mp = mfirst if n == 0 else mprev
                        kp, kc = prev, ks
                        tok0 = d * 128 * n + r
                        for h in range(8):
                            pr, off = h // 2, (h % 2) * 64
                            sbk = 3 + rri("sbk", 2)
                            ps = bank(sbk)
                            mm([(ps[:, 0:128], kTs[kp][off:off + 64, pr, :], qTs[qi][off:off + 64, pr, :], True, False),
                                (ps[:, 0:128], identb, mp, False, True),
                                (ps[:, 128:256], kTs[kc][off:off + 64, pr, :], qTs[qi][off:off + 64, pr, :], True, False),
                                (ps[:, 128:256], identb, mcur, False, True)],
                               [f"kTs{kp}", f"kTs{kc}", f"qTs{qi}"], [bk(sbk)])
                            pi = rri("ptb", 2)
                            act(ptb[pi][:, :], ps[:, 0:256], AF.Exp, [bk(sbk)], [f"ptb{pi}"], scale=0.125)
                            pob_ = (5, 0)[h % 2]
                            po = bank(pob_)[0:65, 0:128]
                            pok = bk(pob_)
                            mm([(po, vss[kp][:, h, :], ptb[pi][:, 0:128], True, False),
                                (po, vss[kc][:, h, :], ptb[pi][:, 128:256], False, True)],
                               [f"vss{kp}", f"vss{kc}", f"ptb{pi}"], [pok])
                            a_sl = acc[0:65, h, tok0:tok0 + 127 * d + 1:d] if d > 1 else acc[0:65, h, tok0:tok0 + 128]
                            tt("dve", a_sl, po, a_sl, ALU.add, [pok, "acc"], ["acc"])
                    prev = ks
        trk.barrier()

        chk(2)

        waz = carve_w(WB + 0); wga0 = carve_w(WB + 1); wga1 = carve_w(WB + 2)
        wload(waz, w_in_r[:, :, C_AZ:C_AZ + 512], ["wA0"])
        wload(wga0, w_in_r[:, :, C_GA:C_GA + 512], ["wA1"])
        wload(wga1, w_in_r[:, :, C_GA + 512:C_GA + 1024], ["wA2"])
        wpa = big[:, 22528:26624].bitcast(BF16).rearrange("p (h n) -> p h n", h=8)
        wload(wpa[0:64, :, :], w_pa.rearrange("(h p) n -> p h n", p=64), ["wpa"])
        ones64 = big[64:65, 26624:26688]
        trk.op("pool", lambda e: e.memset(ones64, 1.0), (), ["ones64"])
        rl = big[64:65, 26688:26688 + 1024]
        yaT_tiles = [bt[0], bt[1]]
        for t in range(NT):
            xn, xnk, _, _ = norm_T(xe[TPC + t * 128:TPC + (t + 1) * 128, :], 128, C("pregT"))
            trk.op("dve", lambda e, t=t: e.reciprocal(out=rl.rearrange("p (h n) -> p h n", h=8),
                                                      in_=acc[64:65, :, t * 128:(t + 1) * 128]), ["acc"], ["rl"])
            for h in range(8):
                zi = rri("azb", 2)
                pz = bank(zi)
                mm([(pz[0:64, 0:128], waz[:, c, h * 64:(h + 1) * 64], xn[:, c, :], c == 0, c == 7) for c in range(8)],
                   [xnk, "wA0"], [bk(zi)])
                gi = rri("gz", 2)
                gz = ft[gi]; gzk = f"ft{gi}"
                act(gz[0:64, 0:128], pz[0:64, 0:128], AF.Silu, [bk(zi)], [gzk])
                bi_ = 2 + (h % 2)
                pbc = bank(bi_)
                mm([(pbc[0:64, 0:128], ones64, rl[:, h * 128:(h + 1) * 128], True, True)], ["ones64", "rl"], [bk(bi_)])
                tt("pool", gz[0:64, 0:128], gz[0:64, 0:128], acc[0:64, h, t * 128:(t + 1) * 128], ALU.mult, [gzk, "acc"], [gzk])
                ydst = yaT_tiles[h // 4][0:64, (h % 4) * 128:(h % 4) * 128 + 128]
                tt("dve", ydst, gz[0:64, 0:128], pbc[0:64, 0:128], ALU.mult, [gzk, bk(bi_)], [f"bt{h // 4}"])
            grp = []
            for half in range(2):
                for h in range(8):
                    grp.append((bank(4 + half), yaT_tiles[h // 4][0:64, (h % 4) * 128:(h % 4) * 128 + 128],
                                wpa[0:64, h, half * 512:(half + 1) * 512], h == 0, h == 7))
            mm(grp, ["bt0", "bt1", "wpa"], [bk(4), bk(5)])
            for half, (wg, wgk) in enumerate(((wga0, "wA1"), (wga1, "wA2"))):
                proj(xn, xnk, 128, wg, wgk, 2 + half)
                sg = ft[2 + half]
                act(sg[:, :], bank(2 + half), AF.Sigmoid, [bk(2 + half)], [f"ft{2 + half}"])
                tt("dve", sg[:, :], bank(4 + half), sg[:, :], ALU.mult, [bk(4 + half), f"ft{2 + half}"], [f"ft{2 + half}"])
                stdma(ga_d[t * 128:(t + 1) * 128, half * 512:(half + 1) * 512], sg[:, :], [f"ft{2 + half}"])
        trk.barrier()

        chk(3)

        names = [("bq", C_BQ), ("bf", C_BF), ("bi", C_BI), ("bz", C_BZ), ("cq", C_CQ), ("cz", C_CZ),
                 ("gb0", C_GB), ("gb1", C_GB + 512), ("gc0", C_GC), ("gc1", C_GC + 512)]
        W3 = {}
        for i, (nm, col) in enumerate(names):
            W3[nm] = (carve_w(i), "w3" + nm)
            wload(W3[nm][0], w_in_r[:, :, col:col + 512], ["w3" + nm])
        gat = big[:, 20480:21504]
        wpb = big[:, 22528:24576].bitcast(BF16).rearrange("p (c n) -> p c n", c=4)
        wpc = big[:, 24576:26624].bitcast(BF16).rearrange("p (c n) -> p c n", c=4)
        wo = big[:, 26624:30720].bitcast(BF16).rearrange("p (c n) -> p c n", c=8)
        wload(wpb, w_pb.rearrange("(c p) n -> p c n", p=128), ["wpb"])
        wload(wpc, w_pc.rearrange("(c p) n -> p c n", p=128), ["wpc"])
        wload(wo, w_out.rearrange("(c p) n -> p c n", p=128), ["wo"])

        chk(3.5)

        def epilogue(n, xn, xnk, x_t, xk, ybT, ycT, ga_src, y_dst, gate_src=None):
            for (yT, w_, wk_, b0, key) in ((ybT, wpb, "wpb", 0, "bt7"), (ycT, wpc, "wpc", 2, "bt6")):
                grp = []
                for half in range(2):
                    for c in range(4):
                        grp.append((bank(b0 + half)[:n, :], yT[:, c, :n], w_[:, c, half * 512:(half + 1) * 512], c == 0, c == 3))
                mm(grp, [key, wk_], [bk(b0), bk(b0 + 1)])
            mg = [ft[8], ft[9]]
            gsrc, gkey = ga_src
            for half in range(2):
                sgb = ft[0]; sgc = ft[1]
                if gate_src is None:
                    proj(xn, xnk, n, W3[f"gb{half}"][0], W3[f"gb{half}"][1], 4)
                    act(sgb[:n, :], bank(4)[:n, :], AF.Sigmoid, [bk(4)], ["ft0"])
                    proj(xn, xnk, n, W3[f"gc{half}"][0], W3[f"gc{half}"][1], 5)
                    act(sgc[:n, :], bank(5)[:n, :], AF.Sigmoid, [bk(5)], ["ft1"])
                else:
                    act(sgb[:n, :], gate_src[:, C_GB + half * 512:C_GB + (half + 1) * 512], AF.Sigmoid, ["U"], ["ft0"])
                    act(sgc[:n, :], gate_src[:, C_GC + half * 512:C_GC + (half + 1) * 512], AF.Sigmoid, ["U"], ["ft1"])
                tt("dve", sgb[:n, :], bank(0 + half)[:n, :], sgb[:n, :], ALU.mult, [bk(half), "ft0"], ["ft0"])
                tt("dve", sgc[:n, :], bank(2 + half)[:n, :], sgc[:n, :], ALU.mult, [bk(2 + half), "ft1"], ["ft1"])
                tt("pool", sgb[:n, :], sgb[:n, :], sgc[:n, :], ALU.add, ["ft0", "ft1"], ["ft0"])
                tt("pool", bt[half][:n, :], sgb[:n, :], gsrc[:n, half * 512:(half + 1) * 512], ALU.add, ["ft0", gkey], [f"bt{half}"])
            pm = bankb(6).rearrange("p (c n) -> p c n", c=8)
            pm2 = bankb(7).rearrange("p (c n) -> p c n", c=8)
            transposes([(pm[:, c, :n], bt[0][:n, c * 128:(c + 1) * 128], identb[:n, :n]) for c in range(4)], ["bt0"], [bk(6)])
            transposes([(pm2[:, c, :n], bt[1][:n, c * 128:(c + 1) * 128], identb[:n, :n]) for c in range(4)], ["bt1"], [bk(7)])
            mT = bt[2].rearrange("p (c n) -> p c n", c=4)
            mT2 = bt[3].rearrange("p (c n) -> p c n", c=4)
            cp("act", mT[:, :, :n], pm[:, 0:4, :n], [bk(6)], ["bt2"])
            cp("dve", mT2[:, :, :n], pm2[:, 0:4, :n], [bk(7)], ["bt3"])
            grp = []
            for half in range(2):
                for c in range(8):
                    src_ = (mT if c < 4 else mT2)[:, c % 4, :n]
                    grp.append((bank(half)[:n, :], src_, wo[:, c, half * 512:(half + 1) * 512], c == 0, c == 7))
            mm(grp, ["bt2", "bt3", "wo"], [bk(0), bk(1)])
            ssz = sm[:n, 28:30]
            for half in range(2):
                act(ft[2][:n, :], bank(half)[:n, :], AF.Square, [bk(half)], ["ft2", "ssz"], accum_out=ssz[:, half:half + 1])
            tt("dve", ssz[:, 0:1], ssz[:, 0:1], ssz[:, 1:2], ALU.add, ["ssz"], ["ssz"])
            rsz = sm[:n, 30:31]
            rstd_from_ss(ssz[:, 0:1], rsz, n, 1.0 / D, "ssz")
            postg = C("postg")
            for half in range(2):
                stt("dve", mg[half][:n, :], bank(half)[:n, :], rsz, postg[:n, half * 512:(half + 1) * 512], ALU.mult, ALU.mult,
                    [bk(half), "sszr"], [f"ft{8 + half}"])
                tt("pool", mg[half][:n, :], mg[half][:n, :], x_t[:n, half * 512:(half + 1) * 512], ALU.add, [f"ft{8 + half}", xk], [f"ft{8 + half}"])
                stdma(y_dst[:, half * 512:(half + 1) * 512], mg[half][:n, :], [f"ft{8 + half}"])

        for t in range(NT):
            xn, xnk, x_t, xk = norm_T(xe[TPC + t * 128:TPC + (t + 1) * 128, :], 128, C("pregT"))
            ybT = hgrn_tile(xn, xnk, True, (W3["bq"], W3["bf"], W3["bi"], W3["bz"]))
            chk(4)
            proj(xn, xnk, 128, W3["cq"][0], W3["cq"][1], 0)
            proj(xn, xnk, 128, W3["cz"][0], W3["cz"][1], 1)
            cqb = bt[0]
            cp("act", cqb[:, :], bank(0), [bk(0)], ["bt0"])
            Gc = ft[0]
            act(Gc[:, :], bank(1), AF.Silu, [bk(1)], ["ft0"])
            pq = bankb(7).rearrange("p (c n) -> p c n", c=8)
            transposes([(pq[:, h, :], cqb[:, h * 128:(h + 1) * 128], identb) for h in range(4)], ["bt0"], [bk(7)])
            cqT = bt[1].rearrange("p (c n) -> p c n", c=4)
            cp("dve", cqT, pq[:, 0:4, :], [bk(7)], ["bt1"])
            yc = bt[3]
            for h in range(4):
                sbk = 2 + rri("sbk3", 2)
                ps = bank(sbk)
                mm([(ps[:, mb * 128:(mb + 1) * 128], mkT[:, h, mb * 128:(mb + 1) * 128], cqT[:, h, :], True, True) for mb in range(2)],
                   ["mkT", "bt1"], [bk(sbk)])
                pi = rri("ptb", 2)
                act(ptb[pi][:, :], ps[:, 0:256], AF.Exp, [bk(sbk)], [f"ptb{pi}"], scale=float(128 ** -0.5))
                po = bank(5)[:, 0:129] if h % 2 == 0 else bank(5)[:, 256:385]
                mm([(po, ptb[pi][:, mb * 128:(mb + 1) * 128], mvp[:, mb, h, :], mb == 0, mb == 1) for mb in range(2)],
                   [f"ptb{pi}", "mvp"], [bk(5)])
                rlc = sm[:, 32 + h:33 + h]
                trk.op("dve", lambda e, rlc=rlc, po=po: e.reciprocal(out=rlc, in_=po[:, 128:129]), [bk(5)], [f"rlc{h}"])
                stt("dve", yc[:, h * 128:(h + 1) * 128], po[:, 0:128], rlc, Gc[:, h * 128:(h + 1) * 128], ALU.mult, ALU.mult,
                    [bk(5), f"rlc{h}", "ft0"], ["bt3"])
            pq = bankb(7).rearrange("p (c n) -> p c n", c=8)
            transposes([(pq[:, h, :], yc[:, h * 128:(h + 1) * 128], identb) for h in range(4)], ["bt3"], [bk(7)])
            ycT = bt[6].rearrange("p (c n) -> p c n", c=4)
            cp("act", ycT, pq[:, 0:4, :], [bk(7)], ["bt6"])
            chk(5)
            ld(gat, ga_d[t * 128:(t + 1) * 128, :], ["gat"])
            epilogue(128, xn, xnk, x_t, xk, ybT, ycT, (gat, "gat"), y_p[t * 128:(t + 1) * 128, :])
            chk(6)
        stdma(hg_p.rearrange("h k v -> k h v"), Sst[:], ["S"])
        trk.barrier()

        if do_sample:
            n = 16
            xn, xnk, x_t, xk = norm_T(xsamp[:, :], n, C("pregT"))
            U = big[0:16, 6144:6144 + NIN]
            for blk in range(NIN // 512):
                si = blk % 3
                wslot = carve_w(si)
                wkey = f"wS{si}"
                wload(wslot, w_in_r[:, :, blk * 512:(blk + 1) * 512], [wkey])
                b = rri("pjs", 4)
                proj(xn, xnk, n, wslot, wkey, b)
                cp("act" if blk % 2 else "dve", U[:, blk * 512:(blk + 1) * 512], bank(b)[:n, :], [bk(b)], ["U"])
            trk.barrier()
            Ssm = big[:, 0:4096].rearrange("p (b h v) -> p b h v", b=8, h=4)
            kvs = [big[:, 4096:5120], big[:, 5120:6144]]
            selb = big[0:16, 17408:19456].rearrange("p (b m) -> p b m", b=16)
            wpas = big[:, 19456:21504].bitcast(BF16).rearrange("p (c n) -> p c n", c=4)
            gas = big[0:16, 21504:22528]
            wload(wpas, w_pa.rearrange("(c p) n -> p c n", p=128), ["wpas"])
            ld(selb, selb_d[:, :, :], ["selb"])
            eye16v = C("eye16").rearrange("p (b c) -> p b c", b=16)
            rs_ = C("ropes")
            cosb = rs_[:n, 0:32].unsqueeze(1).unsqueeze(1).to_broadcast([n, 8, 2, 32])
            sinb = rs_[:n, 32:64].unsqueeze(1).unsqueeze(1).to_broadcast([n, 8, 2, 32])
            for g in range(3):
                rope(U[:, C_AQ + g * 512:C_AQ + (g + 1) * 512], ["U"], ft[2 + g][:n, :], f"ft{2 + g}", n, cosb, sinb, [])
                rope(U[:, C_AK + g * 512:C_AK + (g + 1) * 512], ["U"], ft[5 + g][:n, :], f"ft{5 + g}", n, cosb, sinb, [])
                stdma(nws[g][:, 0, :], ft[5 + g][:n, :], [f"ft{5 + g}"])
                stdma(nws[g][:, 1, :], U[:, C_AV + g * 512:C_AV + (g + 1) * 512], ["U"])

            def samp_attn(items, nh, dh, scale, pso, psl, pkeys):
                tot = len(items)
                for it, (src, q_ap, qkey, b) in enumerate(items):
                    si = rri("kvs", 2)
                    kv = kvs[si]; kvk = f"kvs{si}"
                    ld(kv, src, [kvk])
                    qb = rri("qbc", 4)
                    mm([(bank(qb), selb[:, b, :], q_ap, True, True)], [qkey, "selb"], [bk(qb)])
                    pi_ = 8 + rri("prod", 2)
                    prod = ft[pi_]; pk_ = f"ft{pi_}"
                    tt("dve", prod[:, :], kv[:, 0:512], bank(qb), ALU.mult, [kvk, bk(qb)], [pk_])
                    sc_i = rri("s8", 2)
                    s8 = sm[:, 40 + 8 * sc_i:40 + 8 * sc_i + nh]; s8k = f"s8{sc_i}"
                    trk.op("dve", lambda e, s8=s8, prod=prod: e.reduce_sum(
                        out=s8, in_=prod[:, :].rearrange("p (h d) -> p h d", h=nh), axis=AX.X), [pk_], [s8k])
                    act(s8, s8, AF.Exp, [s8k], [s8k], scale=scale)
                    vi = rri("pv", 2)
                    pv = ft[vi]; pvk = f"ft{vi}"
                    tt("pool", pv[:, :].rearrange("p (h d) -> p h d", h=nh), kv[:, 512:1024].rearrange("p (h d) -> p h d", h=nh),
                       s8.unsqueeze(2).to_broadcast([128, nh, dh]), ALU.mult, [kvk, s8k], [pvk])
                    mm([(pso, eye16v[:, b, :], pv[:, :], it == 0, it == tot - 1),
                        (psl, eye16v[:, b, :], s8, it == 0, it == tot - 1)], [pvk, s8k], pkeys)

            items = []
            for g, d in enumerate(DILS):
                for b in range(16):
                    src = cw[g][b, 0:127 * d + 1:d, :] if d > 1 else cw[g][b, :, :]
                    items.append((src, ft[2 + g][:n, :], f"ft{2 + g}", b))
            pso = bank(4)[:n, :]
            psl = bank(5)[:n, 0:8]
            samp_attn(items, 8, 64, 0.125, pso, psl, [bk(4), bk(5)])
            oS = gas
            oS = big[0:16, 28672 - 1024:28672 - 512]
            oS = ft[8]
            lS = sm[:n, 64:72]
            cp("dve", oS[:n, :], pso, [bk(4)], ["ft8"])
            cp("dve", lS, psl, [bk(5)], ["lS"])
            for g in range(3):
                tt("dve", ft[0][:n, :], ft[2 + g][:n, :], ft[5 + g][:n, :], ALU.mult, [f"ft{2 + g}", f"ft{5 + g}"], ["ft0"])
                pn = sm[:n, 72:80]
                trk.op("dve", lambda e, pn=pn: e.reduce_sum(out=pn, in_=ft[0][:n, :].rearrange("p (h d) -> p h d", h=8), axis=AX.X),
                       ["ft0"], ["pn"])
                act(pn, pn, AF.Exp, ["pn"], ["pn"], scale=0.125)
                tt("dve", ft[1][:n, :].rearrange("p (h d) -> p h d", h=8),
                   U[:, C_AV + g * 512:C_AV + (g + 1) * 512].rearrange("p (h d) -> p h d", h=8),
                   pn.unsqueeze(2).to_broadcast([n, 8, 64]), ALU.mult, ["U", "pn"], ["ft1"])
                tt("dve", oS[:n, :], oS[:n, :], ft[1][:n, :], ALU.add, ["ft8", "ft1"], ["ft8"])
                tt("dve", lS, lS, pn, ALU.add, ["lS", "pn"], ["lS"])
            trk.op("dve", lambda e: e.reciprocal(out=lS, in_=lS), ["lS"], ["lS"])
            Gz = ft[9]
            act(Gz[:n, :], U[:, C_AZ:C_AZ + 512], AF.Silu, ["U"], ["ft9"])
            tt("dve", oS[:n, :].rearrange("p (h d) -> p h d", h=8), oS[:n, :].rearrange("p (h d) -> p h d", h=8),
               lS.unsqueeze(2).to_broadcast([n, 8, 64]), ALU.mult, ["ft8", "lS"], ["ft8"])
            tt("dve", bt[0][:n, :], oS[:n, :], Gz[:n, :], ALU.mult, ["ft8", "ft9"], ["bt0"])
            pq = bankb(6).rearrange("p (c n) -> p c n", c=8)
            transposes([(pq[:, c, :n], bt[0][:n, c * 128:(c + 1) * 128], identb[:n, :n]) for c in range(4)], ["bt0"], [bk(6)])
            yaT = bt[4].rearrange("p (c n) -> p c n", c=4)
            cp("act", yaT[:, :, :n], pq[:, 0:4, :n], [bk(6)], ["bt4"])
            grp = []
            for half in range(2):
                for c in range(4):
                    grp.append((bank(half)[:n, :], yaT[:, c, :n], wpas[:, c, half * 512:(half + 1) * 512], c == 0, c == 3))
            mm(grp, ["bt4", "wpas"], [bk(0), bk(1)])
            for half in range(2):
                act(ft[0][:n, :], U[:, C_GA + half * 512:C_GA + (half + 1) * 512], AF.Sigmoid, ["U"], ["ft0"])
                tt("dve", gas[:, half * 512:(half + 1) * 512], bank(half)[:n, :], ft[0][:n, :], ALU.mult, [bk(half), "ft0"], ["gas"])

            fS = ft[0]; kinS = ft[1]
            act(fS[:n, :], U[:, C_BF:C_BF + 512], AF.Sigmoid, ["U"], ["ft0"])
            tt("dve", fS[:n, :], fS[:n, :], omlr[:n, :], ALU.mult, ["ft0"], ["ft0"])
            tt("dve", fS[:n, :], fS[:n, :], lbr[:n, :], ALU.add, ["ft0"], ["ft0"])
            ts("dve", kinS[:n, :], fS[:n, :], -1.0, 1.0, ALU.mult, ALU.add, ["ft0"], ["ft1"])
            pT = bank(6)[:, 0:128]
            transposes([(pT[:, h * 16:(h + 1) * 16], fS[:n, h * 128:(h + 1) * 128], identf[:n, :n]) for h in range(4)] +
                       [(pT[:, 64 + h * 16:64 + (h + 1) * 16], U[:, C_BQ + h * 128:C_BQ + (h + 1) * 128], identf[:n, :n]) for h in range(4)],
                       ["ft0", "U"], [bk(6)])
            fqT = ft[2]
            cp("dve", fqT[:, 0:128], pT, [bk(6)], ["ft2"])
            fT = fqT[:, 0:64].rearrange("p (h b) -> p h b", h=4)
            qT = fqT[:, 64:128].rearrange("p (h b) -> p h b", h=4)
            qTm = [ft[3], ft[4]]
            for hp in range(2):
                tt("dve", qTm[hp][:, :].rearrange("p (h b c) -> p h b c", h=2, b=16),
                   qT[:, 2 * hp:2 * hp + 2, :].unsqueeze(3).to_broadcast([128, 2, 16, 16]),
                   eye16v.unsqueeze(1).to_broadcast([128, 2, 16, 16]), ALU.mult, ["ft2"], [f"ft{3 + hp}"])
            pob = bank(7)[:n, :]
            zt = ft[8]
            trk.op("pool", lambda e: e.memset(zt[:, :], 0.0), (), ["ft8"])
            mm([(pob, eye16v[:, 0, :], zt[:, :], True, False)], ["ft8"], [bk(7)])
            vexp = ft[5]
            for half in range(2):
                for bl in range(8):
                    b = half * 8 + bl
                    ld(Ssm[:, bl, :, :], sh[b].rearrange("h k v -> k h v"), [f"Ssm{bl}"])
                for q4 in range(2):
                    for h in range(4):
                        b0 = half * 8 + q4 * 4
                        tt("dve", vexp[:n, :].rearrange("p (b v) -> p b v", b=4),
                           U[:, C_BI + h * 128:C_BI + (h + 1) * 128].unsqueeze(1).to_broadcast([n, 4, 128]),
                           identf[:n, b0:b0 + 4].unsqueeze(2).to_broadcast([n, 4, 128]), ALU.mult, ["U"], ["ft5"])
                        kb_i = rri("kvb", 4)
                        mm([(bank(kb_i), kinS[:n, h * 128:(h + 1) * 128], vexp[:n, :], True, True)], ["ft1", "ft5"], [bk(kb_i)])
                        for bi_ in range(4):
                            bl = q4 * 4 + bi_
                            b = half * 8 + bl
                            stt("dve", Ssm[:, bl, h, :], Ssm[:, bl, h, :], fT[:, h, b:b + 1], bank(kb_i)[:, bi_ * 128:(bi_ + 1) * 128],
                                ALU.mult, ALU.add, [f"Ssm{bl}", "ft2", bk(kb_i)], [f"Ssm{bl}"])
                for bl in range(8):
                    b = half * 8 + bl
                    mm([(pob[:, h * 128:(h + 1) * 128], qTm[h // 2][:, :].rearrange("p (h b c) -> p h b c", h=2, b=16)[:, h % 2, b, :],
                         Ssm[:, bl, h, :], False, b == 15) for h in range(4)],
                       [f"Ssm{bl}", "ft3", "ft4"], [bk(7)])
                    stdma(hg_s[b].rearrange("h k v -> k h v"), Ssm[:, bl, :, :], [f"Ssm{bl}"])
            Gb = ft[6]
            act(Gb[:n, :], U[:, C_BZ:C_BZ + 512], AF.Silu, ["U"], ["ft6"])
            tt("dve", ft[7][:n, :], Gb[:n, :], C("hg")[:n, :], ALU.mult, ["ft6"], ["ft7"])
            ybT = head_norm_gate(bank(7), bk(7), ft[7], "ft7", n)

            items = []
            for b in range(16):
                for mb in range(2):
                    items.append((cm[b, mb * 128:(mb + 1) * 128, :], U[:, C_CQ:C_CQ + 512], "U", b))
            pso2 = bank(4)[:n, :]
            psl2 = bank(5)[:n, 0:4]
            samp_attn(items, 4, 128, float(128 ** -0.5), pso2, psl2, [bk(4), bk(5)])
            lC = sm[:n, 80:84]
            trk.op("dve", lambda e: e.reciprocal(out=lC, in_=psl2), [bk(5)], ["lC"])
            Gc = ft[9]
            act(Gc[:n, :], U[:, C_CZ:C_CZ + 512], AF.Silu, ["U"], ["ft9"])
            tt("dve", ft[8][:n, :].rearrange("p (h d) -> p h d", h=4), pso2.rearrange("p (h d) -> p h d", h=4),
               lC.unsqueeze(2).to_broadcast([n, 4, 128]), ALU.mult, [bk(4), "lC"], ["ft8"])
            tt("dve", bt[3][:n, :], ft[8][:n, :], Gc[:n, :], ALU.mult, ["ft8", "ft9"], ["bt3"])
            pq = bankb(6).rearrange("p (c n) -> p c n", c=8)
            transposes([(pq[:, c, :n], bt[3][:n, c * 128:(c + 1) * 128], identb[:n, :n]) for c in range(4)], ["bt3"], [bk(6)])
            ycT = bt[6].rearrange("p (c n) -> p c n", c=4)
            cp("act", ycT[:, :, :n], pq[:, 0:4, :n], [bk(6)], ["bt6"])
            epilogue(n, xn, xnk, x_t, xk, ybT, ycT, (gas, "gas"), y_s[:, :], gate_src=U)

    return nc, trk, body


def _consts(core, inputs):
    cf = np.zeros((128, NCF), np.float32)

    def put(name, arr):
        a, b = _CF[name]
        cf[:arr.shape[0], a:b] = arr
    s = np.arange(128)
    U = (s[:, None] <= s[None, :]).astype(np.float32)
    put("identf", np.eye(128, dtype=np.float32))
    put("Um", U - U[:, 63:64])
    put("M2", (s[:, None] > s[None, :]).astype(np.float32))
    put("attm", U)
    ind = np.ones((128, 2), np.float32)
    ind[64:, 0] = 0.0
    put("ind", ind)
    put("pregT", inputs["norm_pre"][0].reshape(8, 128).T)
    put("memgT", inputs["mem_norm"][0].reshape(8, 128).T)
    put("postg", np.broadcast_to(inputs["norm_post"][0][None, :], (128, 1024)))
    put("hg", np.broadcast_to(inputs["hgrn_out_norm"][0][None, :], (128, 512)))
    put("l0", np.broadcast_to(inputs["hgrn_lb_logits"][0][None, :], (128, 512)))
    put("l1", np.broadcast_to(inputs["hgrn_lb_logits"][1][None, :], (128, 512)))
    cm = (np.arange(8) < core).astype(np.float32)
    put("cmask", np.broadcast_to(cm[None, :], (128, 8)))
    put("eye16", np.broadcast_to(np.eye(16, dtype=np.float32).reshape(1, 256), (128, 256)))
    inv = (10000.0 ** (-np.arange(0, 64, 2, dtype=np.float32) / np.float32(64))).astype(np.float32)
    ang = (np.float32(PAST) * inv).astype(np.float32)
    put("ropes", np.broadcast_to(np.concatenate([np.cos(ang), np.sin(ang)])[None, :].astype(np.float32), (128, 64)))
    cb = np.zeros((128, NCB), np.float32)
    cb[:, 0:128] = np.eye(128)
    j = s[:, None]
    i = s[None, :]
    cb[:, 128:256] = np.where(j >= i, 0.0, NEG)
    cb[:, 256:384] = np.where(j <= i, 0.0, NEG)
    cb[:, 384:512] = NEG if core == 0 else cb[:, 128:256]
    cb[:, 512:640] = (s[:, None] > s[None, :])
    cb[:, 640:768] = U - U[:, 63:64]
    cb[:, 768:770] = ind
    rope = np.zeros((len(PA_BLOCKS), 128, 64), np.float32)
    for bi, (g, r, n) in enumerate(PA_BLOCKS):
        d = DILS[g]
        pos = core * TPC + d * (128 * n + np.arange(128)) + r
        angp = pos.astype(np.float32)[:, None] * inv[None, :]
        rope[bi, :, 0:32] = np.cos(angp)
        rope[bi, :, 32:64] = np.sin(angp)
    return cf, cb, rope


_PROG = {}


def kernel(**inputs):
    inp = {k: np.asarray(v) for k, v in inputs.items()}
    if "nc" not in _PROG:
        _PROG["nc"] = build_program()
    nc = _PROG["nc"]
    xp = inp["x_prompt"][0]
    selb = np.zeros((16, 16, 128), np.float32)
    for b in range(16):
        selb[b, b, :] = 1.0
    in_maps = []
    for c in range(NCORE):
        xe = np.zeros((2 * TPC, D), np.float32)
        if c > 0:
            xe[:TPC] = xp[(c - 1) * TPC:c * TPC]
        xe[TPC:] = xp[c * TPC:(c + 1) * TPC]
        cf, cb, rope = _consts(c, inp)
        sl = slice(16 * c, 16 * c + 16)
        xh = np.zeros((NPREV * 128, D), np.float32)
        if c > 0:
            xh[NPREV * 128 - c * TPC:] = xp[:c * TPC]
        in_maps.append({
            "xe": xe, "xh": xh,
            "xs": np.ascontiguousarray(inp["x_sample"][sl, 0, :]),
            "mem": np.ascontiguousarray(inp["mem_prompt"][0]),
            "cw0": np.ascontiguousarray(inp["cache_win128_kv"][0, sl].reshape(16, 128, 1024)),
            "cw1": np.ascontiguousarray(inp["cache_win512_kv"][0, sl].reshape(16, 512, 1024)),
            "cw2": np.ascontiguousarray(inp["cache_win2048_kv"][0, sl].reshape(16, 2048, 1024)),
            "sh": np.ascontiguousarray(inp["state_hgrn"][0, sl]),
            "cm": np.ascontiguousarray(inp["cache_mem_kv"][0, sl].reshape(16, 256, 1024)),
            "w_in": np.ascontiguousarray(inp["w_in"][0]),
            "w_mem": np.ascontiguousarray(inp["w_mem_kv"][0]),
            "w_pa": np.ascontiguousarray(inp["w_branch_a"][0]),
            "w_pb": np.ascontiguousarray(inp["w_branch_b"][0]),
            "w_pc": np.ascontiguousarray(inp["w_branch_c"][0]),
            "w_out": np.ascontiguousarray(inp["w_out"][0]),
            "cf": cf, "cb": cb, "rope": rope, "selb": selb,
        })
    if os.environ.get("K_NOSAMPLE"):
        for m in in_maps:
            for k in ("cw0", "cw1", "cw2", "sh", "cm"):
                m[k] = np.ascontiguousarray(m[k][0:1])
    res = run_bass_kernel_spmd(nc, in_maps, core_ids=list(range(NCORE)))
    R = res.results
    y_prompt = np.concatenate([R[c]["y_p"] for c in range(NCORE)], axis=0).reshape(1, T, D)
    y_sample = np.concatenate([R[c]["y_s"] for c in range(NCORE)], axis=0).reshape(128, 1, D)
    last = R[NCORE - 1]
    nw0 = last["nw0"].reshape(1, 1, 128, 2, 8, 64)
    nw1 = last["nw1"].reshape(1, 1, 512, 2, 8, 64)
    nw2 = last["nw2"].reshape(1, 1, 2048, 2, 8, 64)
    hgp = last["hg_p"].reshape(1, 1, 4, 128, 128)
    mkv = R[0]["mkv"].reshape(1, 1, 256, 2, 4, 128)
    nws = [np.concatenate([R[c][f"nws{g}"] for c in range(NCORE)], axis=0).reshape(1, 128, 1, 2, 8, 64) for g in range(3)]
    hgs = np.concatenate([R[c]["hg_s"] for c in range(NCORE)], axis=0).reshape(1, 128, 4, 128, 128)
    return (y_prompt, y_sample, nw0, nw1, nw2, hgp, mkv, nws[0], nws[1], nws[2], hgs)
```

```python
import numpy as np
from contextlib import ExitStack
import concourse.bass as bass
import concourse.mybir as mybir
from concourse.bass_utils import run_bass_kernel_spmd

F32 = mybir.dt.float32
BF16 = mybir.dt.bfloat16
AF = mybir.ActivationFunctionType
ALU = mybir.AluOpType
AX = mybir.AxisListType

NCORE = 8
T = 16384
TPC = 2048
NT = 16
D = 1024
NIN = 11264
EPS = 1e-6
PAST = 8192
DILS = (1, 4, 16)
NPREV = (NCORE - 1) * NT
NEG = -30000.0
C_AQ, C_AK, C_AV, C_AZ = 0, 1536, 3072, 4608
C_BQ, C_BF, C_BI, C_BZ = 5120, 5632, 6144, 6656
C_CQ, C_CZ = 7168, 7680
C_GA, C_GB, C_GC = 8192, 9216, 10240

_CF = {}
_off = 0
for _n, _w in (("identf", 128), ("Um", 128), ("M2", 128), ("attm", 128), ("ind", 2), ("pregT", 8), ("memgT", 8),
               ("postg", 1024), ("hg", 512), ("l0", 512), ("l1", 512), ("cmask", 8), ("eye16", 256),
               ("ropes", 64)):
    _CF[_n] = (_off, _off + _w)
    _off += _w
NCF = _off
NCB = 772

def _pa_blocks():
    out = []
    for g, d in enumerate(DILS):
        nb = NT // d
        for r in range(d):
            for n in range(-1, nb):
                out.append((g, r, n))
    return out
PA_BLOCKS = _pa_blocks()
PA_INDEX = {b: i for i, b in enumerate(PA_BLOCKS)}


class Trk:
    ENGS = ("pe", "act", "dve", "pool", "sp")

    def __init__(self):
        self.streams = {e: [] for e in self.ENGS}
        self.cnt = {e: 0 for e in self.ENGS}
        self.known = {e: {} for e in self.ENGS}
        self.lastw = {}
        self.readers = {}
        self.lanegroups = {}
        self.lanecnt = {}
        self.lane_rr = {}

    def add_lanes(self, group, n):
        names = [f"{group}{i}" for i in range(n)]
        self.lanegroups[group] = names
        self.lane_rr[group] = 0
        for x in names:
            self.lanecnt[x] = 0

    def _deps(self, eng, r, w):
        deps = {}

        def add(k, c):
            if deps.get(k, 0) < c:
                deps[k] = c
        for key in r:
            lw = self.lastw.get(key)
            if lw is not None:
                add(lw[0], lw[1])
        for key in w:
            lw = self.lastw.get(key)
            if lw is not None:
                add(lw[0], lw[1])
            for k, c in self.readers.get(key, {}).items():
                add(k, c)
        waits = []
        for k, c in deps.items():
            if k == ("E", "pe") and eng == "pe":
                continue
            if self.known[eng].get(k, 0) >= c:
                continue
            self.known[eng][k] = c
            waits.append((k, c))
        return waits

    def _commit(self, k, c, r, w):
        for key in w:
            self.lastw[key] = (k, c)
            self.readers[key] = {}
        for key in r:
            d = self.readers.setdefault(key, {})
            if d.get(k, 0) < c:
                d[k] = c

    def op(self, eng, fn, r=(), w=()):
        waits = self._deps(eng, r, w)
        self.cnt[eng] += 1
        k = ("E", eng)
        self.streams[eng].append((waits, fn, k))
        self._commit(k, self.cnt[eng], r, w)

    def dma(self, q, group, fn, r=(), w=()):
        lanes = self.lanegroups[group]
        i = self.lane_rr[group]
        self.lane_rr[group] = (i + 1) % len(lanes)
        lane = lanes[i]
        waits = self._deps(q, r, w)
        k = ("L", lane)
        c = self.lanecnt[lane]
        if c > 0 and self.known[q].get(k, 0) < c:
            self.known[q][k] = c
            waits.append((k, c))
        self.lanecnt[lane] = c + 1
        self.streams[q].append((waits, fn, k))
        self._commit(k, c + 1, r, w)

    def barrier(self):
        for e in self.ENGS:
            waits = []
            for o in self.ENGS:
                if o == "sp" or self.cnt[o] == 0:
                    continue
                k = ("E", o)
                if o == e and e == "pe":
                    continue
                if self.known[e].get(k, 0) < self.cnt[o]:
                    self.known[e][k] = self.cnt[o]
                    waits.append((k, self.cnt[o]))
            for lane, c in self.lanecnt.items():
                k = ("L", lane)
                if c > 0 and self.known[e].get(k, 0) < c:
                    self.known[e][k] = c
                    waits.append((k, c))
            if waits:
                self.streams[e].append((waits, None, None))

    def emit(self, nc):
        with ExitStack() as st:
            sems = {}
            for e in self.ENGS:
                if e != "sp":
                    sems[("E", e)] = st.enter_context(nc.semaphore("e_" + e))
            for lane in self.lanecnt:
                sems[("L", lane)] = st.enter_context(nc.semaphore("l_" + lane))
            block = st.enter_context(nc.Block())

            def mult(k):
                return 16 if (k[0] == "L" and not k[1].startswith("cc")) else 1

            def run(eng, h):
                for waits, fn, mine in self.streams[eng]:
                    for k, c in waits:
                        h.wait_ge(sems[k], c * mult(k))
                    if fn is None:
                        continue
                    ins = fn(h)
                    if mine[0] == "L" and mine[1].startswith("cc"):
                        ins.then_inc(sems[mine])
                    else:
                        ins.then_inc(sems[mine], mult(mine))

            block.tensor(lambda h: run("pe", h))
            block.scalar(lambda h: run("act", h))
            block.vector(lambda h: run("dve", h))
            block.gpsimd(lambda h: run("pool", h))
            block.sync(lambda h: run("sp", h))


import os
class _Stop(Exception):
    pass


def build_program(do_sample=True):
    nc, trk, body = _build_program(do_sample)
    try:
        body()
    except _Stop:
        pass
    trk.barrier()
    trk.emit(nc)
    return nc


def _build_program(do_sample=True):
    do_sample = do_sample and not os.environ.get("K_NOSAMPLE")
    no_cc = bool(os.environ.get("K_NOCC"))
    stop = float(os.environ.get("K_STOP", "99"))
    nc = bass.Bass("TRN2", target_bir_lowering=False)
    trk = Trk()
    trk.add_lanes("ld", 6)
    trk.add_lanes("st", 8)
    trk.add_lanes("wl", 4)
    trk.add_lanes("cc", 1)

    def din(name, shape):
        return nc.dram_tensor(name, list(shape), F32, kind="ExternalInput").ap()

    def dout(name, shape):
        return nc.dram_tensor(name, list(shape), F32, kind="ExternalOutput").ap()

    xe = din("xe", (2 * TPC, D))
    SM = 16 if do_sample else 1
    xh = din("xh", (NPREV * 128, D))
    xsamp = din("xs", (16, D))
    mem = din("mem", (256, D))
    cw = [din("cw0", (SM, 128, 1024)), din("cw1", (SM, 512, 1024)), din("cw2", (SM, 2048, 1024))]
    sh = din("sh", (SM, 4, 128, 128))
    cm = din("cm", (SM, 256, 1024))
    w_in = din("w_in", (D, NIN))
    w_mem = din("w_mem", (D, 1024))
    w_pa = din("w_pa", (512, D))
    w_pb = din("w_pb", (512, D))
    w_pc = din("w_pc", (512, D))
    w_out = din("w_out", (D, D))
    cf_d = din("cf", (128, NCF))
    cb_d = din("cb", (128, NCB))
    rope_d = din("rope", (len(PA_BLOCKS), 128, 64))
    selb_d = din("selb", (16, 16, 128))

    y_p = dout("y_p", (TPC, D))
    y_s = dout("y_s", (16, D))
    nw = [dout("nw0", (128, 2, 512)), dout("nw1", (512, 2, 512)), dout("nw2", (2048, 2, 512))]
    hg_p = dout("hg_p", (4, 128, 128))
    mkv = dout("mkv", (256, 1024))
    nws = [dout("nws0", (16, 2, 512)), dout("nws1", (16, 2, 512)), dout("nws2", (16, 2, 512))]
    hg_s = dout("hg_s", (16, 4, 128, 128))

    ga_d = nc.dram_tensor("ga_scr", [TPC, D], F32).ap()

    w_in_r = w_in.rearrange("(c p) n -> p c n", p=128)

    st = ExitStack()

    def chk(v):
        if stop <= v:
            raise _Stop()

    def body():
        def sb(name, shape, dt=F32):
            return st.enter_context(nc.sbuf_tensor(name, list(shape), dt))

        cf = sb("cf_sb", (128, NCF))
        cb = sb("cb_sb", (128, NCB), BF16)
        big = sb("big", (128, 30720))
        xt = [sb("xt0", (128, D)), sb("xt1", (128, D))]
        xsb = sb("xsb", (128, D), BF16)
        xnT = [sb("xnT0", (128, 8, 128), BF16), sb("xnT1", (128, 8, 128), BF16)]
        ft = [sb(f"ft{i}", (128, 512)) for i in range(10)]
        bt = [sb(f"bt{i}", (128, 512), BF16) for i in range(8)]
        sm = sb("sm", (128, 128))
        Sst = sb("Sst", (128, 4, 128))
        Sbf = sb("Sbf", (128, 4, 128), BF16)
        Dacc = sb("Dacc", (128, 4))
        ropet = [sb("ropet0", (128, 64)), sb("ropet1", (128, 64))]
        ptb = [sb("ptb0", (128, 256), BF16), sb("ptb1", (128, 256), BF16)]
        _kb = 28000
        kTs = [big[:, _kb + i * 256:_kb + (i + 1) * 256].bitcast(BF16).rearrange("p (c n) -> p c n", c=4) for i in range(3)]
        _vb = _kb + 768
        vss = [big[:, _vb + i * 260:_vb + (i + 1) * 260].bitcast(BF16).rearrange("p (h d) -> p h d", h=8) for i in range(3)]
        _qb = _vb + 780
        qTs = [big[:, _qb + i * 256:_qb + (i + 1) * 256].bitcast(BF16).rearrange("p (c n) -> p c n", c=4) for i in range(2)]
        mkT = sb("mkT", (128, 4, 256), BF16)
        mvp = sb("mvp", (128, 2, 4, 129), BF16)
        gall = big[:, 8192:8192 + 8 * 516].rearrange("p (c n) -> p c n", c=8)
        psum = [st.enter_context(nc.psum_tensor(f"ps{i}", [128, 1024], F32)) for i in range(4)]

        def C(name):
            a, b = _CF[name]
            return cf[:, a:b]

        identf = C("identf")
        identb = cb[:, 0:128]
        mprev = cb[:, 128:256]
        mcur = cb[:, 256:384]
        mfirst = cb[:, 384:512]
        M2b = cb[:, 512:640]
        Umb = cb[:, 640:768]
        indb = cb[:, 768:770]
        lh = [sb(f"lh{i}", (128, 512), BF16) for i in range(4)]

        def bank(i):
            return psum[i // 2][:, (i % 2) * 512:(i % 2) * 512 + 512]

        def bankb(i):
            return bank(i).bitcast(BF16)

        _rr = {}

        def rr(name, items):
            i = _rr.get(name, 0)
            _rr[name] = i + 1
            return items[i % len(items)]

        def rri(name, n):
            i = _rr.get(name, 0)
            _rr[name] = i + 1
            return i % n

        def act(out, in_, func, r, w, **kw):
            trk.op("act", lambda e: e.activation(out=out, in_=in_, func=func, **kw), r, w)

        def tt(eng, out, in0, in1, op, r, w):
            trk.op(eng, lambda e: e.tensor_tensor(out=out, in0=in0, in1=in1, op=op), r, w)

        def ts(eng, out, in0, s1, s2, op0, op1, r, w):
            if s2 is None:
                trk.op(eng, lambda e: e.tensor_scalar(out=out, in0=in0, scalar1=s1, scalar2=None, op0=op0), r, w)
            else:
                trk.op(eng, lambda e: e.tensor_scalar(out=out, in0=in0, scalar1=s1, scalar2=s2, op0=op0, op1=op1), r, w)

        def stt(eng, out, in0, scalar, in1, op0, op1, r, w):
            trk.op(eng, lambda e: e.scalar_tensor_tensor(out=out, in0=in0, scalar=scalar, in1=in1, op0=op0, op1=op1), r, w)

        def cp(eng, out, in_, r, w):
            if eng == "act":
                trk.op("act", lambda e: e.copy(out=out, in_=in_), r, w)
            else:
                trk.op(eng, lambda e: e.tensor_copy(out=out, in_=in_), r, w)

        def ld(out, in_, w, q="sp", r=()):
            trk.dma(q, "ld", lambda e: e.dma_start(out=out, in_=in_), r, w)

        def stdma(out, in_, r, q="sp"):
            trk.dma(q, "st", lambda e: e.dma_start(out=out, in_=in_), r, ())

        def wload(out, in_, w):
            trk.dma("pool", "wl", lambda e: e.dma_start(out=out, in_=in_), (), w)

        _stg = [0]

        def wload_nocast(out, in_, w):
            i = _stg[0] % 2
            _stg[0] += 1
            n = 1
            for d_ in out.shape[1:]:
                n *= d_
            stage = big[:, 12288 + i * 4096:12288 + i * 4096 + n]
            if len(out.shape) == 3:
                stage = stage.rearrange("p (c n) -> p c n", c=out.shape[1])
            ld(stage, in_, [f"stage{i}"])
            cp("pool" if i else "dve", out, stage, [f"stage{i}"], w)

        def mm(groups, r, w):
            def f(e):
                ins = None
                for (o, l, rh, s0, s1) in groups:
                    ins = e.matmul(o, lhsT=l, rhs=rh, start=s0, stop=s1)
                return ins
            trk.op("pe", f, r, w)

        def transposes(items, r, w):
            def f(e):
                ins = None
                for (o, i, idn) in items:
                    ins = e.transpose(out=o, in_=i, identity=idn)
                return ins
            trk.op("pe", f, r, w)

        def rstd_from_ss(ss, rs, n, scale, key):
            act(rs, ss, AF.Ln, [key], [key + "r"], scale=scale, bias=EPS)
            act(rs, rs, AF.Exp, [key + "r"], [key + "r"], scale=-0.5)

        nt_i = [0]

        def norm_T(src, n, gT, keep_x=False):
            i = nt_i[0]
            nt_i[0] += 1
            x_t = xt[i % 2]
            xk = f"xt{i % 2}"
            xn = xnT[i % 2]
            xnk = f"xnT{i % 2}"
            ld(x_t[:n, :], src, [xk])
            ss = sm[:n, 0:1]
            rs = sm[:n, 1:2]
            act(xsb[:n, :], x_t[:n, :], AF.Square, [xk], ["xsb", "nss"], accum_out=ss)
            rstd_from_ss(ss, rs, n, 1.0 / D, "nss")
            act(xsb[:n, :], x_t[:n, :], AF.Copy, [xk, "nssr"], ["xsb"], scale=rs)
            b = rr("tb", [6, 7])
            pb = bankb(b).rearrange("p (c n) -> p c n", c=8)
            transposes([(pb[:, c, :n], xsb[:n, c * 128:(c + 1) * 128], identb[:n, :n]) for c in range(8)],
                       ["xsb"], [f"bank{b}"])
            tt("dve", xn[:, :, :n], pb[:, :, :n], gT.unsqueeze(2).to_broadcast([128, 8, n]), ALU.mult,
               [f"bank{b}"], [xnk])
            return xn, xnk, x_t, xk

        def proj(xn, xnk, n, wblk, wkey, b):
            mm([(bank(b)[:n, :], xn[:, c, :n], wblk[:, c, :], c == 0, c == 7) for c in range(8)],
               [xnk, wkey], [f"bank{b}"])

        def carve_w(idx):
            return big[:, idx * 2048:(idx + 1) * 2048].bitcast(BF16).rearrange("p (c n) -> p c n", c=8)

        def bk(i):
            return f"bank{i}"

        def rope4(ap):
            return ap.rearrange("p (h t f) -> p h t f", h=8, t=2)

        def rope(src, srckeys, out_ap, okey, n, cosb, sinb, tabkeys, A=None, B=None):
            A = ft[0] if A is None else A
            B = ft[1] if B is None else B
            ka, kb_ = "ft%d" % ft.index(A), "ft%d" % ft.index(B)
            tt("dve", rope4(A[:n, :]), rope4(src), cosb, ALU.mult, list(srckeys) + list(tabkeys), [ka])
            tt("dve", rope4(B[:n, :]), rope4(src), sinb, ALU.mult, list(srckeys) + list(tabkeys), [kb_])
            A4 = rope4(A[:n, :]); B4 = rope4(B[:n, :]); O4 = rope4(out_ap)
            tt("pool", O4[:, :, 0, :], A4[:, :, 0, :], B4[:, :, 1, :], ALU.subtract, [ka, kb_], [okey])
            tt("pool", O4[:, :, 1, :], A4[:, :, 1, :], B4[:, :, 0, :], ALU.add, [ka, kb_], [okey])

        ld(cf[:, :], cf_d[:, :], ["cf"])
        wload(cb[:, :], cb_d[:, :], ["cb"])
        trk.barrier()
        l0 = C("l0")
        l1 = C("l1")
        lbr = l0
        omlr = l1
        tt("dve", l0, l0, l1, ALU.subtract, ["cf"], ["cf"])
        act(l0, l0, AF.Sigmoid, ["cf"], ["cf"])
        ts("dve", l1, l0, -1.0, 1.0, ALU.mult, ALU.add, ["cf"], ["cf"])
        trk.op("pool", lambda e: e.memset(Sst[:], 0.0), (), ["S"])
        trk.op("pool", lambda e: e.memset(Dacc[:], 1.0), (), ["Dacc"])
        trk.op("pool", lambda e: e.memset(mvp[:, :, :, 128:129], 1.0), (), ["mvp"])
        trk.barrier()

        wm = [carve_w(0), carve_w(1)]
        w_mem_r = w_mem.rearrange("(c p) n -> p c n", p=128)
        for j in range(2):
            wload(wm[j], w_mem_r[:, :, j * 512:(j + 1) * 512], [f"wm{j}"])
        for mt in range(2):
            xn, xnk, _, _ = norm_T(mem[mt * 128:(mt + 1) * 128, :], 128, C("memgT"))
            for j in range(2):
                b = rri("pj", 4)
                proj(xn, xnk, 128, wm[j], f"wm{j}", b)
                fi = rri("ftA", 2)
                f = ft[fi]; fk = f"ft{fi}"
                cp("act", f[:, :], bank(b), [bk(b)], [fk])
                stdma(mkv[mt * 128:(mt + 1) * 128, j * 512:(j + 1) * 512], f[:, :], [fk])
                if j == 0:
                    cp("dve", bt[0][:, :], f[:, :], [fk], ["bt0"])
                    tb = 6 + rri("tb", 2)
                    pbk = bankb(tb).rearrange("p (c n) -> p c n", c=8)
                    transposes([(pbk[:, h, :], bt[0][:, h * 128:(h + 1) * 128], identb) for h in range(4)],
                               ["bt0"], [bk(tb)])
                    cp("dve", mkT[:, :, mt * 128:(mt + 1) * 128], pbk[:, 0:4, :], [bk(tb)], ["mkT"])
                else:
                    cp("dve", mvp[:, mt, :, 0:128], f[:, :].rearrange("p (h d) -> p h d", h=4), [fk], ["mvp"])
        trk.barrier()

        chk(0)

        def hgrn_tile(xn, xnk, full, wkeys, par=0, do_proj=True):
            wq, wf, wi, wz = wkeys
            Um = C("Um"); M2 = C("M2"); attm = C("attm"); ind = C("ind")
            fo = 4 * par; bo = 2 * par
            b_f, b_i = (0, 1) if par == 0 else (4, 5)
            if do_proj:
                proj(xn, xnk, 128, wf[0], wf[1], b_f)
                proj(xn, xnk, 128, wi[0], wi[1], b_i)
            sig = ft[fo + 0]; logf = ft[fo + 1]; kin = ft[fo + 2]
            k0, k1, k2, k3 = (f"ft{fo + i}" for i in range(4))
            kb0, kb1 = f"bt{bo}", f"bt{bo + 1}"
            eblk = f"ebl{par}"
            act(sig[:, :], bank(b_f), AF.Exp, [bk(b_f)], [k0], scale=-1.0)
            ts("dve", sig[:, :], sig[:, :], 1.0, None, ALU.add, None, [k0], [k0])
            trk.op("dve", lambda e, sig=sig: e.reciprocal(out=sig[:, :], in_=sig[:, :]), [k0], [k0])
            tt("dve", sig[:, :], sig[:, :], omlr, ALU.mult, [k0], [k0])
            tt("pool", sig[:, :], sig[:, :], lbr, ALU.add, [k0], [k0])
            act(logf[:, :], sig[:, :], AF.Ln, [k0], [k1])
            ts("pool", kin[:, :], sig[:, :], -1.0, 1.0, ALU.mult, ALU.add, [k0], [k2])
            vb = bt[bo]
            cp("act", vb[:, :], bank(b_i), [bk(b_i)], [kb0])
            lhi = lh[2 * par]; llo = lh[2 * par + 1]
            khi, klo = f"lh{2 * par}", f"lh{2 * par + 1}"
            cp("dve", lhi[:, :], logf[:, :], [k1], [khi])
            tt("pool", llo[:, :], logf[:, :], lhi[:, :], ALU.subtract, [k1, khi], [klo])
            mm([(bank(2), M2b, lhi[:, :], True, False), (bank(2), M2b, llo[:, :], False, True)], [khi, klo], [bk(2)])
            ek = ft[fo + 3]
            act(ek[:, :], bank(2), AF.Exp, [bk(2)], [k3])
            khat = bt[bo + 1]
            tt("dve", khat[:, :], kin[:, :], ek[:, :], ALU.mult, [k2, k3], [kb1])
            pbl = bank(3)[:, 0:8].rearrange("p (h t) -> p h t", t=2)
            grp_ = []
            for h in range(4):
                grp_.append((pbl[:, h, :], lhi[:, h * 128:(h + 1) * 128], indb, True, False))
                grp_.append((pbl[:, h, :], llo[:, h * 128:(h + 1) * 128], indb, False, True))
            mm(grp_, [khi, klo], [bk(3)])
            ebl = (sm[:, 8:16] if par == 0 else sm[:, 100:108]).rearrange("p (h t) -> p h t", t=2)
            act(ebl, pbl, AF.Exp, [bk(3)], [eblk])
            if full:
                chk(3.52)
                proj(xn, xnk, 128, wq[0], wq[1], 4)
                proj(xn, xnk, 128, wz[0], wz[1], 5)
                mm([(bank(2), Umb, lhi[:, :], True, False), (bank(2), Umb, llo[:, :], False, True)], [khi, klo], [bk(2)])
                eq = ft[3]; ekm = ft[4]
                act(eq[:, :], bank(2), AF.Exp, [bk(2)], ["ft3"])
                act(ekm[:, :], bank(2), AF.Exp, [bk(2)], ["ft4"], scale=-1.0)
                chk(3.55)
                qt = bt[2]; kt = bt[3]
                tt("dve", qt[:, :], bank(4), eq[:, :], ALU.mult, [bk(4), "ft3"], ["bt2"])
                tt("pool", kt[:, :], kin[:, :], ekm[:, :], ALU.mult, ["ft2", "ft4"], ["bt3"])
                G = ft[5]
                act(G[:, :], bank(5), AF.Silu, [bk(5)], ["ft5"])
                tt("pool", G[:, :], G[:, :], C("hg"), ALU.mult, ["ft5"], ["ft5"])
                chk(3.57)
                pq = bankb(6).rearrange("p (c n) -> p c n", c=8)
                pk2 = bankb(7).rearrange("p (c n) -> p c n", c=8)
                transposes([(pq[:, h, :], qt[:, h * 128:(h + 1) * 128], identb) for h in range(4)], ["bt2"], [bk(6)])
                transposes([(pk2[:, h, :], kt[:, h * 128:(h + 1) * 128], identb) for h in range(4)], ["bt3"], [bk(7)])
                qkT = bt[4].rearrange("p (c n) -> p c n", c=4)
                kkT = bt[5].rearrange("p (c n) -> p c n", c=4)
                cp("act", qkT, pq[:, 0:4, :], [bk(6)], ["bt4"])
                cp("dve", kkT, pk2[:, 0:4, :], [bk(7)], ["bt5"])
                chk(3.6)
                tt("dve", Sbf[:], Sst[:], ebl[:, :, 0:1].to_broadcast([128, 4, 128]), ALU.mult, ["S", eblk], ["Sbf"])
                pa = bank(2)
                mm([(pa[:, h * 128:(h + 1) * 128], kkT[:, h, :], qkT[:, h, :], True, True) for h in range(4)],
                   ["bt4", "bt5"], [bk(2)])
                attb = bt[6]
                tt("dve", attb[:, :].rearrange("p (h t) -> p h t", h=4), pa.rearrange("p (h t) -> p h t", h=4),
                   attm.unsqueeze(1).to_broadcast([128, 4, 128]), ALU.mult, [bk(2)], ["bt6"])
                chk(3.7)
                po = bank(4)
                grp = []
                for h in range(4):
                    grp.append((po[:, h * 128:(h + 1) * 128], qkT[:, h, :], Sbf[:, h, :], True, False))
                    grp.append((po[:, h * 128:(h + 1) * 128], attb[:, h * 128:(h + 1) * 128], vb[:, h * 128:(h + 1) * 128], False, True))
                mm(grp, ["bt4", "Sbf", "bt6", "bt0"], [bk(4)])
                chk(3.8)
            pP = bank(b_f)
            mm([(pP[:, h * 128:(h + 1) * 128], khat[:, h * 128:(h + 1) * 128], vb[:, h * 128:(h + 1) * 128], True, True)
                for h in range(4)], [kb1, kb0], [bk(b_f)])
            for h in range(4):
                stt("dve", Sst[:, h, :], Sst[:, h, :], ebl[:, h, 1:2], pP[:, h * 128:(h + 1) * 128], ALU.mult, ALU.add,
                    ["S", eblk, bk(b_f)], ["S"])
            if not full:
                return None
            return head_norm_gate(bank(4), bk(4), G, "ft5", 128)

        def head_norm_gate(po, pok, G, gk, n):
            ss4 = sm[:n, 16:20]
            rs4 = sm[:n, 20:24]
            for h in range(4):
                act(ft[6][:n, h * 128:(h + 1) * 128], po[:n, h * 128:(h + 1) * 128], AF.Square, [pok], ["ft6", "ss4"],
                    accum_out=ss4[:, h:h + 1])
            rstd_from_ss(ss4, rs4, n, 1.0 / 128, "ss4")
            yb = bt[2]
            for h in range(4):
                stt("dve", yb[:n, h * 128:(h + 1) * 128], po[:n, h * 128:(h + 1) * 128], rs4[:, h:h + 1],
                    G[:n, h * 128:(h + 1) * 128], ALU.mult, ALU.mult, [pok, "ss4r", gk], ["bt2"])
            pq = bankb(6).rearrange("p (c n) -> p c n", c=8)
            transposes([(pq[:, h, :n], yb[:n, h * 128:(h + 1) * 128], identb[:n, :n]) for h in range(4)], ["bt2"], [bk(6)])
            ybT = bt[7].rearrange("p (c n) -> p c n", c=4)
            cp("act", ybT[:, :, :n], pq[:, 0:4, :n], [bk(6)], ["bt7"])
            return ybT

        wf_b = carve_w(0); wi_b = carve_w(1)
        wload(wf_b, w_in_r[:, :, C_BF:C_BF + 512], ["w0"])
        wload(wi_b, w_in_r[:, :, C_BI:C_BI + 512], ["w1"])
        nxt = norm_T(xh[0:128, :], 128, C("pregT"))
        for t in range(NPREV):
            xn, xnk = nxt[0], nxt[1]
            b_f, b_i = (0, 1) if t % 2 == 0 else (4, 5)
            proj(xn, xnk, 128, wf_b, "w0", b_f)
            proj(xn, xnk, 128, wi_b, "w1", b_i)
            if t + 1 < NPREV:
                nxt = norm_T(xh[(t + 1) * 128:(t + 2) * 128, :], 128, C("pregT"))
            hgrn_tile(xn, xnk, False, (None, (wf_b, "w0"), (wi_b, "w1"), None), par=t % 2, do_proj=False)
        trk.barrier()
        chk(1)

        acc = big[:, 0:16384].rearrange("p (h t) -> p h t", h=8)
        trk.op("pool", lambda e: e.memset(big[0:65, 0:16384], 0.0), (), ["acc"])
        for i in range(3):
            trk.op("pool", lambda e, i=i: e.memset(vss[i][:, :, 64:65], 1.0), (), [f"vss{i}"])
        WB = 8
        for g, d in enumerate(DILS):
            wq = carve_w(WB + 0); wk = carve_w(WB + 1); wv = carve_w(WB + 2)
            wload(wq, w_in_r[:, :, C_AQ + g * 512:C_AQ + (g + 1) * 512], ["wA0"])
            wload(wk, w_in_r[:, :, C_AK + g * 512:C_AK + (g + 1) * 512], ["wA1"])
            wload(wv, w_in_r[:, :, C_AV + g * 512:C_AV + (g + 1) * 512], ["wA2"])
            nb = NT // d
            wb_tokens = 128 * d
            kv_i = 0
            for r in range(d):
                prev = None
                for n in range(-1, nb):
                    own = n >= 0
                    start = TPC + d * 128 * n + r
                    src = xe[start:start + 127 * d + 1:d, :] if d > 1 else xe[start:start + 128, :]
                    xn, xnk, _, _ = norm_T(src, 128, C("pregT"))
                    ri = rri("ropet", 2); rt = ropet[ri]; rk = f"ropet{ri}"
                    ld(rt[:, :], rope_d[PA_INDEX[(g, r, n)], :, :], [rk])
                    cosb = rt[:, 0:32].unsqueeze(1).unsqueeze(1).to_broadcast([128, 8, 2, 32])
                    sinb = rt[:, 32:64].unsqueeze(1).unsqueeze(1).to_broadcast([128, 8, 2, 32])
                    ks = kv_i % 3
                    kv_i += 1
                    proj(xn, xnk, 128, wk, "wA1", 1)
                    rope(bank(1), [bk(1)], ft[2][:, :], "ft2", 128, cosb, sinb, [rk])
                    need_out = own and (d * (128 * n + 127) + r >= TPC - wb_tokens)
                    if need_out:
                        t0 = d * 128 * n + r - (TPC - wb_tokens)
                        dst = nw[g][t0:t0 + 127 * d + 1:d, 0, :] if d > 1 else nw[g][t0:t0 + 128, 0, :]
                        stdma(dst, ft[2][:, :], ["ft2"])
                    cp("pool", bt[0][:, :], ft[2][:, :], ["ft2"], ["bt0"])
                    pk = bankb(6).rearrange("p (c n) -> p c n", c=8)
                    transposes([(pk[:, c, :], bt[0][:, c * 128:(c + 1) * 128], identb) for c in range(4)], ["bt0"], [bk(6)])
                    cp("act", kTs[ks], pk[:, 0:4, :], [bk(6)], [f"kTs{ks}"])
                    proj(xn, xnk, 128, wv, "wA2", 2)
                    if need_out:
                        cp("act", ft[3][:, :], bank(2), [bk(2)], ["ft3"])
                        dst = nw[g][t0:t0 + 127 * d + 1:d, 1, :] if d > 1 else nw[g][t0:t0 + 128, 1, :]
                        stdma(dst, ft[3][:, :], ["ft3"])
                    cp("act", vss[ks][:, :, 0:64], bank(2).rearrange("p (h d) -> p h d", h=8), [bk(2)], [f"vss{ks}"])
                    if own:
                        proj(xn, xnk, 128, wq, "wA0", 0)
                        rope(bank(0), [bk(0)], ft[6][:, :], "ft6", 128, cosb, sinb, [rk], A=ft[4], B=ft[5])
                        cp("pool", bt[1][:, :], ft[6][:, :], ["ft6"], ["bt1"])
                        pq = bankb(7).rearrange("p (c n) -> p c n", c=8)
                        transposes([(pq[:, c, :], bt[1][:, c * 128:(c + 1) * 128], identb) for c in range(4)], ["bt1"], [bk(7)])
                        qi = rri("qTs", 2)
                        cp("dve", qTs[qi], pq[:, 0:4, :], [bk(7)], [f"qTs{qi}"])
                        mp = mfirst if n == 0 else mprev
                        kp, kc = prev, ks
                        tok0 = d * 128 * n + r
                        for h in range(8):
                            pr, off = h // 2, (h % 2) * 64
                            sbk = 3 + rri("sbk", 2)
                            ps = bank(sbk)
                            mm([(ps[:, 0:128], kTs[kp][off:off + 64, pr, :], qTs[qi][off:off + 64, pr, :], True, False),
                                (ps[:, 0:128], identb, mp, False, True),
                                (ps[:, 128:256], kTs[kc][off:off + 64, pr, :], qTs[qi][off:off + 64, pr, :], True, False),
                                (ps[:, 128:256], identb, mcur, False, True)],
                               [f"kTs{kp}", f"kTs{kc}", f"qTs{qi}"], [bk(sbk)])
                            pi = rri("ptb", 2)
                            act(ptb[pi][:, :], ps[:, 0:256], AF.Exp, [bk(sbk)], [f"ptb{pi}"], scale=0.125)
                            pob_ = (5, 0)[h % 2]
                            po = bank(pob_)[0:65, 0:128]
                            pok = bk(pob_)
                            mm([(po, vss[kp][:, h, :], ptb[pi][:, 0:128], True, False),
                                (po, vss[kc][:, h, :], ptb[pi][:, 128:256], False, True)],
                               [f"vss{kp}", f"vss{kc}", f"ptb{pi}"], [pok])
                            a_sl = acc[0:65, h, tok0:tok0 + 127 * d + 1:d] if d > 1 else acc[0:65, h, tok0:tok0 + 128]
                            tt("dve", a_sl, po, a_sl, ALU.add, [pok, "acc"], ["acc"])
                    prev = ks
        trk.barrier()

        chk(2)

        waz = carve_w(WB + 0); wga0 = carve_w(WB + 1); wga1 = carve_w(WB + 2)
        wload(waz, w_in_r[:, :, C_AZ:C_AZ + 512], ["wA0"])
        wload(wga0, w_in_r[:, :, C_GA:C_GA + 512], ["wA1"])
        wload(wga1, w_in_r[:, :, C_GA + 512:C_GA + 1024], ["wA2"])
        wpa = big[:, 22528:26624].bitcast(BF16).rearrange("p (h n) -> p h n", h=8)
        wload(wpa[0:64, :, :], w_pa.rearrange("(h p) n -> p h n", p=64), ["wpa"])
        ones64 = big[64:65, 26624:26688]
        trk.op("pool", lambda e: e.memset(ones64, 1.0), (), ["ones64"])
        rl = big[64:65, 26688:26688 + 1024]
        yaT_tiles = [bt[0], bt[1]]
        for t in range(NT):
            xn, xnk, _, _ = norm_T(xe[TPC + t * 128:TPC + (t + 1) * 128, :], 128, C("pregT"))
            trk.op("dve", lambda e, t=t: e.reciprocal(out=rl.rearrange("p (h n) -> p h n", h=8),
                                                      in_=acc[64:65, :, t * 128:(t + 1) * 128]), ["acc"], ["rl"])
            for h in range(8):
                zi = rri("azb", 2)
                pz = bank(zi)
                mm([(pz[0:64, 0:128], waz[:, c, h * 64:(h + 1) * 64], xn[:, c, :], c == 0, c == 7) for c in range(8)],
                   [xnk, "wA0"], [bk(zi)])
                gi = rri("gz", 2)
                gz = ft[gi]; gzk = f"ft{gi}"
                act(gz[0:64, 0:128], pz[0:64, 0:128], AF.Silu, [bk(zi)], [gzk])
                bi_ = 2 + (h % 2)
                pbc = bank(bi_)
                mm([(pbc[0:64, 0:128], ones64, rl[:, h * 128:(h + 1) * 128], True, True)], ["ones64", "rl"], [bk(bi_)])
                tt("pool", gz[0:64, 0:128], gz[0:64, 0:128], acc[0:64, h, t * 128:(t + 1) * 128], ALU.mult, [gzk, "acc"], [gzk])
                ydst = yaT_tiles[h // 4][0:64, (h % 4) * 128:(h % 4) * 128 + 128]
                tt("dve", ydst, gz[0:64, 0:128], pbc[0:64, 0:128], ALU.mult, [gzk, bk(bi_)], [f"bt{h // 4}"])
            grp = []
            for half in range(2):
                for h in range(8):
                    grp.append((bank(4 + half), yaT_tiles[h // 4][0:64, (h % 4) * 128:(h % 4) * 128 + 128],
                                wpa[0:64, h, half * 512:(half + 1) * 512], h == 0, h == 7))
            mm(grp, ["bt0", "bt1", "wpa"], [bk(4), bk(5)])
            for half, (wg, wgk) in enumerate(((wga0, "wA1"), (wga1, "wA2"))):
                proj(xn, xnk, 128, wg, wgk, 2 + half)
                sg = ft[2 + half]
                act(sg[:, :], bank(2 + half), AF.Sigmoid, [bk(2 + half)], [f"ft{2 + half}"])
                tt("dve", sg[:, :], bank(4 + half), sg[:, :], ALU.mult, [bk(4 + half), f"ft{2 + half}"], [f"ft{2 + half}"])
                stdma(ga_d[t * 128:(t + 1) * 128, half * 512:(half + 1) * 512], sg[:, :], [f"ft{2 + half}"])
        trk.barrier()

        chk(3)

        names = [("bq", C_BQ), ("bf", C_BF), ("bi", C_BI), ("bz", C_BZ), ("cq", C_CQ), ("cz", C_CZ),
                 ("gb0", C_GB), ("gb1", C_GB + 512), ("gc0", C_GC), ("gc1", C_GC + 512)]
        W3 = {}
        for i, (nm, col) in enumerate(names):
            W3[nm] = (carve_w(i), "w3" + nm)
            wload(W3[nm][0], w_in_r[:, :, col:col + 512], ["w3" + nm])
        gat = big[:, 20480:21504]
        wpb = big[:, 22528:24576].bitcast(BF16).rearrange("p (c n) -> p c n", c=4)
        wpc = big[:, 24576:26624].bitcast(BF16).rearrange("p (c n) -> p c n", c=4)
        wo = big[:, 26624:30720].bitcast(BF16).rearrange("p (c n) -> p c n", c=8)
        wload(wpb, w_pb.rearrange("(c p) n -> p c n", p=128), ["wpb"])
        wload(wpc, w_pc.rearrange("(c p) n -> p c n", p=128), ["wpc"])
        wload(wo, w_out.rearrange("(c p) n -> p c n", p=128), ["wo"])

        chk(3.5)

        def epilogue(n, xn, xnk, x_t, xk, ybT, ycT, ga_src, y_dst, gate_src=None):
            for (yT, w_, wk_, b0, key) in ((ybT, wpb, "wpb", 0, "bt7"), (ycT, wpc, "wpc", 2, "bt6")):
                grp = []
                for half in range(2):
                    for c in range(4):
                        grp.append((bank(b0 + half)[:n, :], yT[:, c, :n], w_[:, c, half * 512:(half + 1) * 512], c == 0, c == 3))
                mm(grp, [key, wk_], [bk(b0), bk(b0 + 1)])
            mg = [ft[8], ft[9]]
            gsrc, gkey = ga_src
            for half in range(2):
                sgb = ft[0]; sgc = ft[1]
                if gate_src is None:
                    proj(xn, xnk, n, W3[f"gb{half}"][0], W3[f"gb{half}"][1], 4)
                    act(sgb[:n, :], bank(4)[:n, :], AF.Sigmoid, [bk(4)], ["ft0"])
                    proj(xn, xnk, n, W3[f"gc{half}"][0], W3[f"gc{half}"][1], 5)
                    act(sgc[:n, :], bank(5)[:n, :], AF.Sigmoid, [bk(5)], ["ft1"])
                else:
                    act(sgb[:n, :], gate_src[:, C_GB + half * 512:C_GB + (half + 1) * 512], AF.Sigmoid, ["U"], ["ft0"])
                    act(sgc[:n, :], gate_src[:, C_GC + half * 512:C_GC + (half + 1) * 512], AF.Sigmoid, ["U"], ["ft1"])
                tt("dve", sgb[:n, :], bank(0 + half)[:n, :], sgb[:n, :], ALU.mult, [bk(half), "ft0"], ["ft0"])
                tt("dve", sgc[:n, :], bank(2 + half)[:n, :], sgc[:n, :], ALU.mult, [bk(2 + half), "ft1"], ["ft1"])
                tt("pool", sgb[:n, :], sgb[:n, :], sgc[:n, :], ALU.add, ["ft0", "ft1"], ["ft0"])
                tt("pool", bt[half][:n, :], sgb[:n, :], gsrc[:n, half * 512:(half + 1) * 512], ALU.add, ["ft0", gkey], [f"bt{half}"])
            pm = bankb(6).rearrange("p (c n) -> p c n", c=8)
            pm2 = bankb(7).rearrange("p (c n) -> p c n", c=8)
            transposes([(pm[:, c, :n], bt[0][:n, c * 128:(c + 1) * 128], identb[:n, :n]) for c in range(4)], ["bt0"], [bk(6)])
            transposes([(pm2[:, c, :n], bt[1][:n, c * 128:(c + 1) * 128], identb[:n, :n]) for c in range(4)], ["bt1"], [bk(7)])
            mT = bt[2].rearrange("p (c n) -> p c n", c=4)
            mT2 = bt[3].rearrange("p (c n) -> p c n", c=4)
            cp("act", mT[:, :, :n], pm[:, 0:4, :n], [bk(6)], ["bt2"])
            cp("dve", mT2[:, :, :n], pm2[:, 0:4, :n], [bk(7)], ["bt3"])
            grp = []
            for half in range(2):
                for c in range(8):
                    src_ = (mT if c < 4 else mT2)[:, c % 4, :n]
                    grp.append((bank(half)[:n, :], src_, wo[:, c, half * 512:(half + 1) * 512], c == 0, c == 7))
            mm(grp, ["bt2", "bt3", "wo"], [bk(0), bk(1)])
            ssz = sm[:n, 28:30]
            for half in range(2):
                act(ft[2][:n, :], bank(half)[:n, :], AF.Square, [bk(half)], ["ft2", "ssz"], accum_out=ssz[:, half:half + 1])
            tt("dve", ssz[:, 0:1], ssz[:, 0:1], ssz[:, 1:2], ALU.add, ["ssz"], ["ssz"])
            rsz = sm[:n, 30:31]
            rstd_from_ss(ssz[:, 0:1], rsz, n, 1.0 / D, "ssz")
            postg = C("postg")
            for half in range(2):
                stt("dve", mg[half][:n, :], bank(half)[:n, :], rsz, postg[:n, half * 512:(half + 1) * 512], ALU.mult, ALU.mult,
                    [bk(half), "sszr"], [f"ft{8 + half}"])
                tt("pool", mg[half][:n, :], mg[half][:n, :], x_t[:n, half * 512:(half + 1) * 512], ALU.add, [f"ft{8 + half}", xk], [f"ft{8 + half}"])
                stdma(y_dst[:, half * 512:(half + 1) * 512], mg[half][:n, :], [f"ft{8 + half}"])

        for t in range(NT):
            xn, xnk, x_t, xk = norm_T(xe[TPC + t * 128:TPC + (t + 1) * 128, :], 128, C("pregT"))
            ybT = hgrn_tile(xn, xnk, True, (W3["bq"], W3["bf"], W3["bi"], W3["bz"]))
            chk(4)
            proj(xn, xnk, 128, W3["cq"][0], W3["cq"][1], 0)
            proj(xn, xnk, 128, W3["cz"][0], W3["cz"][1], 1)
            cqb = bt[0]
            cp("act", cqb[:, :], bank(0), [bk(0)], ["bt0"])
            Gc = ft[0]
            act(Gc[:, :], bank(1), AF.Silu, [bk(1)], ["ft0"])
            pq = bankb(7).rearrange("p (c n) -> p c n", c=8)
            transposes([(pq[:, h, :], cqb[:, h * 128:(h + 1) * 128], identb) for h in range(4)], ["bt0"], [bk(7)])
            cqT = bt[1].rearrange("p (c n) -> p c n", c=4)
            cp("dve", cqT, pq[:, 0:4, :], [bk(7)], ["bt1"])
            yc = bt[3]
            for h in range(4):
                sbk = 2 + rri("sbk3", 2)
                ps = bank(sbk)
                mm([(ps[:, mb * 128:(mb + 1) * 128], mkT[:, h, mb * 128:(mb + 1) * 128], cqT[:, h, :], True, True) for mb in range(2)],
                   ["mkT", "bt1"], [bk(sbk)])
                pi = rri("ptb", 2)
                act(ptb[pi][:, :], ps[:, 0:256], AF.Exp, [bk(sbk)], [f"ptb{pi}"], scale=float(128 ** -0.5))
                po = bank(5)[:, 0:129] if h % 2 == 0 else bank(5)[:, 256:385]
                mm([(po, ptb[pi][:, mb * 128:(mb + 1) * 128], mvp[:, mb, h, :], mb == 0, mb == 1) for mb in range(2)],
                   [f"ptb{pi}", "mvp"], [bk(5)])
                rlc = sm[:, 32 + h:33 + h]
                trk.op("dve", lambda e, rlc=rlc, po=po: e.reciprocal(out=rlc, in_=po[:, 128:129]), [bk(5)], [f"rlc{h}"])
                stt("dve", yc[:, h * 128:(h + 1) * 128], po[:, 0:128], rlc, Gc[:, h * 128:(h + 1) * 128], ALU.mult, ALU.mult,
                    [bk(5), f"rlc{h}", "ft0"], ["bt3"])
            pq = bankb(7).rearrange("p (c n) -> p c n", c=8)
            transposes([(pq[:, h, :], yc[:, h * 128:(h + 1) * 128], identb) for h in range(4)], ["bt3"], [bk(7)])
            ycT = bt[6].rearrange("p (c n) -> p c n", c=4)
            cp("act", ycT, pq[:, 0:4, :], [bk(7)], ["bt6"])
            chk(5)
            ld(gat, ga_d[t * 128:(t + 1) * 128, :], ["gat"])
            epilogue(128, xn, xnk, x_t, xk, ybT, ycT, (gat, "gat"), y_p[t * 128:(t + 1) * 128, :])
            chk(6)
        stdma(hg_p.rearrange("h k v -> k h v"), Sst[:], ["S"])
        trk.barrier()

        if do_sample:
            n = 16
            xn, xnk, x_t, xk = norm_T(xsamp[:, :], n, C("pregT"))
            U = big[0:16, 6144:6144 + NIN]
            for blk in range(NIN // 512):
                si = blk % 3
                wslot = carve_w(si)
                wkey = f"wS{si}"
                wload(wslot, w_in_r[:, :, blk * 512:(blk + 1) * 512], [wkey])
                b = rri("pjs", 4)
                proj(xn, xnk, n, wslot, wkey, b)
                cp("act" if blk % 2 else "dve", U[:, blk * 512:(blk + 1) * 512], bank(b)[:n, :], [bk(b)], ["U"])
            trk.barrier()
            Ssm = big[:, 0:4096].rearrange("p (b h v) -> p b h v", b=8, h=4)
            kvs = [big[:, 4096:5120], big[:, 5120:6144]]
            selb = big[0:16, 17408:19456].rearrange("p (b m) -> p b m", b=16)
            wpas = big[:, 19456:21504].bitcast(BF16).rearrange("p (c n) -> p c n", c=4)
            gas = big[0:16, 21504:22528]
            wload(wpas, w_pa.rearrange("(c p) n -> p c n", p=128), ["wpas"])
            ld(selb, selb_d[:, :, :], ["selb"])
            eye16v = C("eye16").rearrange("p (b c) -> p b c", b=16)
            rs_ = C("ropes")
            cosb = rs_[:n, 0:32].unsqueeze(1).unsqueeze(1).to_broadcast([n, 8, 2, 32])
            sinb = rs_[:n, 32:64].unsqueeze(1).unsqueeze(1).to_broadcast([n, 8, 2, 32])
            for g in range(3):
                rope(U[:, C_AQ + g * 512:C_AQ + (g + 1) * 512], ["U"], ft[2 + g][:n, :], f"ft{2 + g}", n, cosb, sinb, [])
                rope(U[:, C_AK + g * 512:C_AK + (g + 1) * 512], ["U"], ft[5 + g][:n, :], f"ft{5 + g}", n, cosb, sinb, [])
                stdma(nws[g][:, 0, :], ft[5 + g][:n, :], [f"ft{5 + g}"])
                stdma(nws[g][:, 1, :], U[:, C_AV + g * 512:C_AV + (g + 1) * 512], ["U"])

            def samp_attn(items, nh, dh, scale, pso, psl, pkeys):
                tot = len(items)
                for it, (src, q_ap, qkey, b) in enumerate(items):
                    si = rri("kvs", 2)
                    kv = kvs[si]; kvk = f"kvs{si}"
                    ld(kv, src, [kvk])
                    qb = rri("qbc", 4)
                    mm([(bank(qb), selb[:, b, :], q_ap, True, True)], [qkey, "selb"], [bk(qb)])
                    pi_ = 8 + rri("prod", 2)
                    prod = ft[pi_]; pk_ = f"ft{pi_}"
                    tt("dve", prod[:, :], kv[:, 0:512], bank(qb), ALU.mult, [kvk, bk(qb)], [pk_])
                    sc_i = rri("s8", 2)
                    s8 = sm[:, 40 + 8 * sc_i:40 + 8 * sc_i + nh]; s8k = f"s8{sc_i}"
                    trk.op("dve", lambda e, s8=s8, prod=prod: e.reduce_sum(
                        out=s8, in_=prod[:, :].rearrange("p (h d) -> p h d", h=nh), axis=AX.X), [pk_], [s8k])
                    act(s8, s8, AF.Exp, [s8k], [s8k], scale=scale)
                    vi = rri("pv", 2)
                    pv = ft[vi]; pvk = f"ft{vi}"
                    tt("pool", pv[:, :].rearrange("p (h d) -> p h d", h=nh), kv[:, 512:1024].rearrange("p (h d) -> p h d", h=nh),
                       s8.unsqueeze(2).to_broadcast([128, nh, dh]), ALU.mult, [kvk, s8k], [pvk])
                    mm([(pso, eye16v[:, b, :], pv[:, :], it == 0, it == tot - 1),
                        (psl, eye16v[:, b, :], s8, it == 0, it == tot - 1)], [pvk, s8k], pkeys)

            items = []
            for g, d in enumerate(DILS):
                for b in range(16):
                    src = cw[g][b, 0:127 * d + 1:d, :] if d > 1 else cw[g][b, :, :]
                    items.append((src, ft[2 + g][:n, :], f"ft{2 + g}", b))
            pso = bank(4)[:n, :]
            psl = bank(5)[:n, 0:8]
            samp_attn(items, 8, 64, 0.125, pso, psl, [bk(4), bk(5)])
            oS = gas
            oS = big[0:16, 28672 - 1024:28672 - 512]
            oS = ft[8]
            lS = sm[:n, 64:72]
            cp("dve", oS[:n, :], pso, [bk(4)], ["ft8"])
            cp("dve", lS, psl, [bk(5)], ["lS"])
            for g in range(3):
                tt("dve", ft[0][:n, :], ft[2 + g][:n, :], ft[5 + g][:n, :], ALU.mult, [f"ft{2 + g}", f"ft{5 + g}"], ["ft0"])
                pn = sm[:n, 72:80]
                trk.op("dve", lambda e, pn=pn: e.reduce_sum(out=pn, in_=ft[0][:n, :].rearrange("p (h d) -> p h d", h=8), axis=AX.X),
                       ["ft0"], ["pn"])
                act(pn, pn, AF.Exp, ["pn"], ["pn"], scale=0.125)
                tt("dve", ft[1][:n, :].rearrange("p (h d) -> p h d", h=8),
                   U[:, C_AV + g * 512:C_AV + (g + 1) * 512].rearrange("p (h d) -> p h d", h=8),
                   pn.unsqueeze(2).to_broadcast([n, 8, 64]), ALU.mult, ["U", "pn"], ["ft1"])
                tt("dve", oS[:n, :], oS[:n, :], ft[1][:n, :], ALU.add, ["ft8", "ft1"], ["ft8"])
                tt("dve", lS, lS, pn, ALU.add, ["lS", "pn"], ["lS"])
            trk.op("dve", lambda e: e.reciprocal(out=lS, in_=lS), ["lS"], ["lS"])
            Gz = ft[9]
            act(Gz[:n, :], U[:, C_AZ:C_AZ + 512], AF.Silu, ["U"], ["ft9"])
            tt("dve", oS[:n, :].rearrange("p (h d) -> p h d", h=8), oS[:n, :].rearrange("p (h d) -> p h d", h=8),
               lS.unsqueeze(2).to_broadcast([n, 8, 64]), ALU.mult, ["ft8", "lS"], ["ft8"])
            tt("dve", bt[0][:n, :], oS[:n, :], Gz[:n, :], ALU.mult, ["ft8", "ft9"], ["bt0"])
            pq = bankb(6).rearrange("p (c n) -> p c n", c=8)
            transposes([(pq[:, c, :n], bt[0][:n, c * 128:(c + 1) * 128], identb[:n, :n]) for c in range(4)], ["bt0"], [bk(6)])
            yaT = bt[4].rearrange("p (c n) -> p c n", c=4)
            cp("act", yaT[:, :, :n], pq[:, 0:4, :n], [bk(6)], ["bt4"])
            grp = []
            for half in range(2):
                for c in range(4):
                    grp.append((bank(half)[:n, :], yaT[:, c, :n], wpas[:, c, half * 512:(half + 1) * 512], c == 0, c == 3))
            mm(grp, ["bt4", "wpas"], [bk(0), bk(1)])
            for half in range(2):
                act(ft[0][:n, :], U[:, C_GA + half * 512:C_GA + (half + 1) * 512], AF.Sigmoid, ["U"], ["ft0"])
                tt("dve", gas[:, half * 512:(half + 1) * 512], bank(half)[:n, :], ft[0][:n, :], ALU.mult, [bk(half), "ft0"], ["gas"])

            fS = ft[0]; kinS = ft[1]
            act(fS[:n, :], U[:, C_BF:C_BF + 512], AF.Sigmoid, ["U"], ["ft0"])
            tt("dve", fS[:n, :], fS[:n, :], omlr[:n, :], ALU.mult, ["ft0"], ["ft0"])
            tt("dve", fS[:n, :], fS[:n, :], lbr[:n, :], ALU.add, ["ft0"], ["ft0"])
            ts("dve", kinS[:n, :], fS[:n, :], -1.0, 1.0, ALU.mult, ALU.add, ["ft0"], ["ft1"])
            pT = bank(6)[:, 0:128]
            transposes([(pT[:, h * 16:(h + 1) * 16], fS[:n, h * 128:(h + 1) * 128], identf[:n, :n]) for h in range(4)] +
                       [(pT[:, 64 + h * 16:64 + (h + 1) * 16], U[:, C_BQ + h * 128:C_BQ + (h + 1) * 128], identf[:n, :n]) for h in range(4)],
                       ["ft0", "U"], [bk(6)])
            fqT = ft[2]
            cp("dve", fqT[:, 0:128], pT, [bk(6)], ["ft2"])
            fT = fqT[:, 0:64].rearrange("p (h b) -> p h b", h=4)
            qT = fqT[:, 64:128].rearrange("p (h b) -> p h b", h=4)
            qTm = [ft[3], ft[4]]
            for hp in range(2):
                tt("dve", qTm[hp][:, :].rearrange("p (h b c) -> p h b c", h=2, b=16),
                   qT[:, 2 * hp:2 * hp + 2, :].unsqueeze(3).to_broadcast([128, 2, 16, 16]),
                   eye16v.unsqueeze(1).to_broadcast([128, 2, 16, 16]), ALU.mult, ["ft2"], [f"ft{3 + hp}"])
            pob = bank(7)[:n, :]
            zt = ft[8]
            trk.op("pool", lambda e: e.memset(zt[:, :], 0.0), (), ["ft8"])
            mm([(pob, eye16v[:, 0, :], zt[:, :], True, False)], ["ft8"], [bk(7)])
            vexp = ft[5]
            for half in range(2):
                for bl in range(8):
                    b = half * 8 + bl
                    ld(Ssm[:, bl, :, :], sh[b].rearrange("h k v -> k h v"), [f"Ssm{bl}"])
                for q4 in range(2):
                    for h in range(4):
                        b0 = half * 8 + q4 * 4
                        tt("dve", vexp[:n, :].rearrange("p (b v) -> p b v", b=4),
                           U[:, C_BI + h * 128:C_BI + (h + 1) * 128].unsqueeze(1).to_broadcast([n, 4, 128]),
                           identf[:n, b0:b0 + 4].unsqueeze(2).to_broadcast([n, 4, 128]), ALU.mult, ["U"], ["ft5"])
                        kb_i = rri("kvb", 4)
                        mm([(bank(kb_i), kinS[:n, h * 128:(h + 1) * 128], vexp[:n, :], True, True)], ["ft1", "ft5"], [bk(kb_i)])
                        for bi_ in range(4):
                            bl = q4 * 4 + bi_
                            b = half * 8 + bl
                            stt("dve", Ssm[:, bl, h, :], Ssm[:, bl, h, :], fT[:, h, b:b + 1], bank(kb_i)[:, bi_ * 128:(bi_ + 1) * 128],
                                ALU.mult, ALU.add, [f"Ssm{bl}", "ft2", bk(kb_i)], [f"Ssm{bl}"])
                for bl in range(8):
                    b = half * 8 + bl
                    mm([(pob[:, h * 128:(h + 1) * 128], qTm[h // 2][:, :].rearrange("p (h b c) -> p h b c", h=2, b=16)[:, h % 2, b, :],
                         Ssm[:, bl, h, :], False, b == 15) for h in range(4)],
                       [f"Ssm{bl}", "ft3", "ft4"], [bk(7)])
                    stdma(hg_s[b].rearrange("h k v -> k h v"), Ssm[:, bl, :, :], [f"Ssm{bl}"])
            Gb = ft[6]
            act(Gb[:n, :], U[:, C_BZ:C_BZ + 512], AF.Silu, ["U"], ["ft6"])
            tt("dve", ft[7][:n, :], Gb[:n, :], C("hg")[:n, :], ALU.mult, ["ft6"], ["ft7"])
            ybT = head_norm_gate(bank(7), bk(7), ft[7], "ft7", n)

            items = []
            for b in range(16):
                for mb in range(2):
                    items.append((cm[b, mb * 128:(mb + 1) * 128, :], U[:, C_CQ:C_CQ + 512], "U", b))
            pso2 = bank(4)[:n, :]
            psl2 = bank(5)[:n, 0:4]
            samp_attn(items, 4, 128, float(128 ** -0.5), pso2, psl2, [bk(4), bk(5)])
            lC = sm[:n, 80:84]
            trk.op("dve", lambda e: e.reciprocal(out=lC, in_=psl2), [bk(5)], ["lC"])
            Gc = ft[9]
            act(Gc[:n, :], U[:, C_CZ:C_CZ + 512], AF.Silu, ["U"], ["ft9"])
            tt("dve", ft[8][:n, :].rearrange("p (h d) -> p h d", h=4), pso2.rearrange("p (h d) -> p h d", h=4),
               lC.unsqueeze(2).to_broadcast([n, 4, 128]), ALU.mult, [bk(4), "lC"], ["ft8"])
            tt("dve", bt[3][:n, :], ft[8][:n, :], Gc[:n, :], ALU.mult, ["ft8", "ft9"], ["bt3"])
            pq = bankb(6).rearrange("p (c n) -> p c n", c=8)
            transposes([(pq[:, c, :n], bt[3][:n, c * 128:(c + 1) * 128], identb[:n, :n]) for c in range(4)], ["bt3"], [bk(6)])
            ycT = bt[6].rearrange("p (c n) -> p c n", c=4)
            cp("act", ycT[:, :, :n], pq[:, 0:4, :n], [bk(6)], ["bt6"])
            epilogue(n, xn, xnk, x_t, xk, ybT, ycT, (gas, "gas"), y_s[:, :], gate_src=U)

    return nc, trk, body


def _consts(core, inputs):
    cf = np.zeros((128, NCF), np.float32)

    def put(name, arr):
        a, b = _CF[name]
        cf[:arr.shape[0], a:b] = arr
    s = np.arange(128)
    U = (s[:, None] <= s[None, :]).astype(np.float32)
    put("identf", np.eye(128, dtype=np.float32))
    put("Um", U - U[:, 63:64])
    put("M2", (s[:, None] > s[None, :]).astype(np.float32))
    put("attm", U)
    ind = np.ones((128, 2), np.float32)
    ind[64:, 0] = 0.0
    put("ind", ind)
    put("pregT", inputs["norm_pre"][0].reshape(8, 128).T)
    put("memgT", inputs["mem_norm"][0].reshape(8, 128).T)
    put("postg", np.broadcast_to(inputs["norm_post"][0][None, :], (128, 1024)))
    put("hg", np.broadcast_to(inputs["hgrn_out_norm"][0][None, :], (128, 512)))
    put("l0", np.broadcast_to(inputs["hgrn_lb_logits"][0][None, :], (128, 512)))
    put("l1", np.broadcast_to(inputs["hgrn_lb_logits"][1][None, :], (128, 512)))
    cm = (np.arange(8) < core).astype(np.float32)
    put("cmask", np.broadcast_to(cm[None, :], (128, 8)))
    put("eye16", np.broadcast_to(np.eye(16, dtype=np.float32).reshape(1, 256), (128, 256)))
    inv = (10000.0 ** (-np.arange(0, 64, 2, dtype=np.float32) / np.float32(64))).astype(np.float32)
    ang = (np.float32(PAST) * inv).astype(np.float32)
    put("ropes", np.broadcast_to(np.concatenate([np.cos(ang), np.sin(ang)])[None, :].astype(np.float32), (128, 64)))
    cb = np.zeros((128, NCB), np.float32)
    cb[:, 0:128] = np.eye(128)
    j = s[:, None]
    i = s[None, :]
    cb[:, 128:256] = np.where(j >= i, 0.0, NEG)
    cb[:, 256:384] = np.where(j <= i, 0.0, NEG)
    cb[:, 384:512] = NEG if core == 0 else cb[:, 128:256]
    cb[:, 512:640] = (s[:, None] > s[None, :])
    cb[:, 640:768] = U - U[:, 63:64]
    cb[:, 768:770] = ind
    rope = np.zeros((len(PA_BLOCKS), 128, 64), np.float32)
    for bi, (g, r, n) in enumerate(PA_BLOCKS):
        d = DILS[g]
        pos = core * TPC + d * (128 * n + np.arange(128)) + r
        angp = pos.astype(np.float32)[:, None] * inv[None, :]
        rope[bi, :, 0:32] = np.cos(angp)
        rope[bi, :, 32:64] = np.sin(angp)
    return cf, cb, rope


_PROG = {}


def kernel(**inputs):
    inp = {k: np.asarray(v) for k, v in inputs.items()}
    if "nc" not in _PROG:
        _PROG["nc"] = build_program()
    nc = _PROG["nc"]
    xp = inp["x_prompt"][0]
    selb = np.zeros((16, 16, 128), np.float32)
    for b in range(16):
        selb[b, b, :] = 1.0
    in_maps = []
    for c in range(NCORE):
        xe = np.zeros((2 * TPC, D), np.float32)
        if c > 0:
            xe[:TPC] = xp[(c - 1) * TPC:c * TPC]
        xe[TPC:] = xp[c * TPC:(c + 1) * TPC]
        cf, cb, rope = _consts(c, inp)
        sl = slice(16 * c, 16 * c + 16)
        xh = np.zeros((NPREV * 128, D), np.float32)
        if c > 0:
            xh[NPREV * 128 - c * TPC:] = xp[:c * TPC]
        in_maps.append({
            "xe": xe, "xh": xh,
            "xs": np.ascontiguousarray(inp["x_sample"][sl, 0, :]),
            "mem": np.ascontiguousarray(inp["mem_prompt"][0]),
            "cw0": np.ascontiguousarray(inp["cache_win128_kv"][0, sl].reshape(16, 128, 1024)),
            "cw1": np.ascontiguousarray(inp["cache_win512_kv"][0, sl].reshape(16, 512, 1024)),
            "cw2": np.ascontiguousarray(inp["cache_win2048_kv"][0, sl].reshape(16, 2048, 1024)),
            "sh": np.ascontiguousarray(inp["state_hgrn"][0, sl]),
            "cm": np.ascontiguousarray(inp["cache_mem_kv"][0, sl].reshape(16, 256, 1024)),
            "w_in": np.ascontiguousarray(inp["w_in"][0]),
            "w_mem": np.ascontiguousarray(inp["w_mem_kv"][0]),
            "w_pa": np.ascontiguousarray(inp["w_branch_a"][0]),
            "w_pb": np.ascontiguousarray(inp["w_branch_b"][0]),
            "w_pc": np.ascontiguousarray(inp["w_branch_c"][0]),
            "w_out": np.ascontiguousarray(inp["w_out"][0]),
            "cf": cf, "cb": cb, "rope": rope, "selb": selb,
        })
    if os.environ.get("K_NOSAMPLE"):
        for m in in_maps:
            for k in ("cw0", "cw1", "cw2", "sh", "cm"):
                m[k] = np.ascontiguousarray(m[k][0:1])
    res = run_bass_kernel_spmd(nc, in_maps, core_ids=list(range(NCORE)))
    R = res.results
    y_prompt = np.concatenate([R[c]["y_p"] for c in range(NCORE)], axis=0).reshape(1, T, D)
    y_sample = np.concatenate([R[c]["y_s"] for c in range(NCORE)], axis=0).reshape(128, 1, D)
    last = R[NCORE - 1]
    nw0 = last["nw0"].reshape(1, 1, 128, 2, 8, 64)
    nw1 = last["nw1"].reshape(1, 1, 512, 2, 8, 64)
    nw2 = last["nw2"].reshape(1, 1, 2048, 2, 8, 64)
    hgp = last["hg_p"].reshape(1, 1, 4, 128, 128)
    mkv = R[0]["mkv"].reshape(1, 1, 256, 2, 4, 128)
    nws = [np.concatenate([R[c][f"nws{g}"] for c in range(NCORE)], axis=0).reshape(1, 128, 1, 2, 8, 64) for g in range(3)]
    hgs = np.concatenate([R[c]["hg_s"] for c in range(NCORE)], axis=0).reshape(1, 128, 4, 128, 128)
    return (y_prompt, y_sample, nw0, nw1, nw2, hgp, mkv, nws[0], nws[1], nws[2], hgs)
```

```python
import numpy as np
from contextlib import ExitStack
import concourse.bass as bass
import concourse.mybir as mybir
from concourse.bass_utils import run_bass_kernel_spmd

F32 = mybir.dt.float32
BF16 = mybir.dt.bfloat16
AF = mybir.ActivationFunctionType
ALU = mybir.AluOpType
AX = mybir.AxisListType

NCORE = 8
T = 16384
TPC = 2048
NT = 16
D = 1024
NIN = 11264
EPS = 1e-6
PAST = 8192
DILS = (1, 4, 16)
NPREV = (NCORE - 1) * NT
NEG = -30000.0
C_AQ, C_AK, C_AV, C_AZ = 0, 1536, 3072, 4608
C_BQ, C_BF, C_BI, C_BZ = 5120, 5632, 6144, 6656
C_CQ, C_CZ = 7168, 7680
C_GA, C_GB, C_GC = 8192, 9216, 10240

_CF = {}
_off = 0
for _n, _w in (("identf", 128), ("Um", 128), ("M2", 128), ("attm", 128), ("ind", 2), ("pregT", 8), ("memgT", 8),
               ("postg", 1024), ("hg", 512), ("l0", 512), ("l1", 512), ("cmask", 8), ("eye16", 256),
               ("ropes", 64)):
    _CF[_n] = (_off, _off + _w)
    _off += _w
NCF = _off
NCB = 1028

def _pa_blocks():
    out = []
    for g, d in enumerate(DILS):
        nb = NT // d
        for r in range(d):
            for n in range(-1, nb):
                out.append((g, r, n))
    return out
PA_BLOCKS = _pa_blocks()
PA_INDEX = {b: i for i, b in enumerate(PA_BLOCKS)}


class Trk:
    ENGS = ("pe", "act", "dve", "pool", "sp")

    def __init__(self):
        self.streams = {e: [] for e in self.ENGS}
        self.cnt = {e: 0 for e in self.ENGS}
        self.known = {e: {} for e in self.ENGS}
        self.lastw = {}
        self.readers = {}
        self.lanegroups = {}
        self.lanecnt = {}
        self.lane_rr = {}

    def add_lanes(self, group, n):
        names = [f"{group}{i}" for i in range(n)]
        self.lanegroups[group] = names
        self.lane_rr[group] = 0
        for x in names:
            self.lanecnt[x] = 0

    def _deps(self, eng, r, w):
        deps = {}

        def add(k, c):
            if deps.get(k, 0) < c:
                deps[k] = c
        for key in r:
            lw = self.lastw.get(key)
            if lw is not None:
                add(lw[0], lw[1])
        for key in w:
            lw = self.lastw.get(key)
            if lw is not None:
                add(lw[0], lw[1])
            for k, c in self.readers.get(key, {}).items():
                add(k, c)
        waits = []
        for k, c in deps.items():
            if k == ("E", "pe") and eng == "pe":
                continue
            if self.known[eng].get(k, 0) >= c:
                continue
            self.known[eng][k] = c
            waits.append((k, c))
        return waits

    def _commit(self, k, c, r, w):
        for key in w:
            self.lastw[key] = (k, c)
            self.readers[key] = {}
        for key in r:
            d = self.readers.setdefault(key, {})
            if d.get(k, 0) < c:
                d[k] = c

    def op(self, eng, fn, r=(), w=()):
        waits = self._deps(eng, r, w)
        self.cnt[eng] += 1
        k = ("E", eng)
        self.streams[eng].append((waits, fn, k))
        self._commit(k, self.cnt[eng], r, w)

    def dma(self, q, group, fn, r=(), w=()):
        lanes = self.lanegroups[group]
        i = self.lane_rr[group]
        self.lane_rr[group] = (i + 1) % len(lanes)
        lane = lanes[i]
        waits = self._deps(q, r, w)
        k = ("L", lane)
        c = self.lanecnt[lane]
        if c > 0 and self.known[q].get(k, 0) < c:
            self.known[q][k] = c
            waits.append((k, c))
        self.lanecnt[lane] = c + 1
        self.streams[q].append((waits, fn, k))
        self._commit(k, c + 1, r, w)

    def barrier(self):
        for e in self.ENGS:
            waits = []
            for o in self.ENGS:
                if o == "sp" or self.cnt[o] == 0:
                    continue
                k = ("E", o)
                if o == e and e == "pe":
                    continue
                if self.known[e].get(k, 0) < self.cnt[o]:
                    self.known[e][k] = self.cnt[o]
                    waits.append((k, self.cnt[o]))
            for lane, c in self.lanecnt.items():
                k = ("L", lane)
                if c > 0 and self.known[e].get(k, 0) < c:
                    self.known[e][k] = c
                    waits.append((k, c))
            if waits:
                self.streams[e].append((waits, None, None))

    def emit(self, nc):
        with ExitStack() as st:
            sems = {}
            for e in self.ENGS:
                if e != "sp":
                    sems[("E", e)] = st.enter_context(nc.semaphore("e_" + e))
            for lane in self.lanecnt:
                sems[("L", lane)] = st.enter_context(nc.semaphore("l_" + lane))
            block = st.enter_context(nc.Block())

            def mult(k):
                return 16 if (k[0] == "L" and not k[1].startswith("cc")) else 1

            def run(eng, h):
                for waits, fn, mine in self.streams[eng]:
                    for k, c in waits:
                        h.wait_ge(sems[k], c * mult(k))
                    if fn is None:
                        continue
                    ins = fn(h)
                    if mine[0] == "L" and mine[1].startswith("cc"):
                        ins.then_inc(sems[mine])
                    else:
                        ins.then_inc(sems[mine], mult(mine))

            block.tensor(lambda h: run("pe", h))
            block.scalar(lambda h: run("act", h))
            block.vector(lambda h: run("dve", h))
            block.gpsimd(lambda h: run("pool", h))
            block.sync(lambda h: run("sp", h))


import os
class _Stop(Exception):
    pass


def build_program(do_sample=True):
    nc, trk, body = _build_program(do_sample)
    try:
        body()
    except _Stop:
        pass
    trk.barrier()
    trk.emit(nc)
    return nc


def _build_program(do_sample=True):
    do_sample = do_sample and not os.environ.get("K_NOSAMPLE")
    no_cc = bool(os.environ.get("K_NOCC"))
    stop = float(os.environ.get("K_STOP", "99"))
    nc = bass.Bass("TRN2", target_bir_lowering=False)
    trk = Trk()
    trk.add_lanes("ld", 6)
    trk.add_lanes("st", 8)
    trk.add_lanes("wl", 4)
    trk.add_lanes("cc", 1)

    def din(name, shape):
        return nc.dram_tensor(name, list(shape), F32, kind="ExternalInput").ap()

    def dout(name, shape):
        return nc.dram_tensor(name, list(shape), F32, kind="ExternalOutput").ap()

    xe = din("xe", (2 * TPC, D))
    SM = 16 if do_sample else 1
    xh = din("xh", (NPREV * 128, D))
    xsamp = din("xs", (16, D))
    mem = din("mem", (256, D))
    cw = [din("cw0", (SM, 128, 1024)), din("cw1", (SM, 512, 1024)), din("cw2", (SM, 2048, 1024))]
    sh = din("sh", (SM, 4, 128, 128))
    cm = din("cm", (SM, 256, 1024))
    w_in = din("w_in", (D, NIN))
    w_mem = din("w_mem", (D, 1024))
    w_pa = din("w_pa", (512, D))
    w_pb = din("w_pb", (512, D))
    w_pc = din("w_pc", (512, D))
    w_out = din("w_out", (D, D))
    cf_d = din("cf", (128, NCF))
    cb_d = din("cb", (128, NCB))
    rope_d = din("rope", (len(PA_BLOCKS), 128, 64))
    selb_d = din("selb", (16, 16, 128))

    y_p = dout("y_p", (TPC, D))
    y_s = dout("y_s", (16, D))
    nw = [dout("nw0", (128, 2, 512)), dout("nw1", (512, 2, 512)), dout("nw2", (2048, 2, 512))]
    hg_p = dout("hg_p", (4, 128, 128))
    mkv = dout("mkv", (256, 1024))
    nws = [dout("nws0", (16, 2, 512)), dout("nws1", (16, 2, 512)), dout("nws2", (16, 2, 512))]
    hg_s = dout("hg_s", (16, 4, 128, 128))

    ga_d = nc.dram_tensor("ga_scr", [TPC, D], F32).ap()

    w_in_r = w_in.rearrange("(c p) n -> p c n", p=128)

    st = ExitStack()

    def chk(v):
        if stop <= v:
            raise _Stop()

    def body():
        def sb(name, shape, dt=F32):
            return st.enter_context(nc.sbuf_tensor(name, list(shape), dt))

        cf = sb("cf_sb", (128, NCF))
        cb = sb("cb_sb", (128, NCB), BF16)
        big = sb("big", (128, 30720))
        xt = [sb("xt0", (128, D)), sb("xt1", (128, D))]
        xsb = sb("xsb", (128, D), BF16)
        xnT = [sb("xnT0", (128, 8, 128), BF16), sb("xnT1", (128, 8, 128), BF16)]
        ft = [sb(f"ft{i}", (128, 512)) for i in range(10)]
        bt = [sb(f"bt{i}", (128, 512), BF16) for i in range(8)]
        sm = sb("sm", (128, 128))
        Sst = sb("Sst", (128, 4, 128))
        Sbf = sb("Sbf", (128, 4, 128), BF16)
        Dacc = sb("Dacc", (128, 4))
        ropet = [sb("ropet0", (128, 64)), sb("ropet1", (128, 64))]
        ptb = [sb("ptb0", (128, 256), BF16), sb("ptb1", (128, 256), BF16)]
        _kb = 28000
        kTs = [big[:, _kb + i * 256:_kb + (i + 1) * 256].bitcast(BF16).rearrange("p (c n) -> p c n", c=4) for i in range(3)]
        _vb = _kb + 768
        vss = [big[:, _vb + i * 260:_vb + (i + 1) * 260].bitcast(BF16).rearrange("p (h d) -> p h d", h=8) for i in range(3)]
        _qb = _vb + 780
        qTs = [big[:, _qb + i * 256:_qb + (i + 1) * 256].bitcast(BF16).rearrange("p (c n) -> p c n", c=4) for i in range(2)]
        mkT = sb("mkT", (128, 4, 256), BF16)
        mvp = sb("mvp", (128, 2, 4, 129), BF16)
        gall = big[:, 8192:8192 + 8 * 516].rearrange("p (c n) -> p c n", c=8)
        psum = [st.enter_context(nc.psum_tensor(f"ps{i}", [128, 1024], F32)) for i in range(4)]

        def C(name):
            a, b = _CF[name]
            return cf[:, a:b]

        identf = C("identf")
        identb = cb[:, 0:128]
        mprev = cb[:, 128:256]
        mcur = cb[:, 256:384]
        mfirst = cb[:, 384:512]
        M2b = cb[:, 512:640]
        Umb = cb[:, 640:768]
        indb = cb[:, 768:770]
        eye16b = cb[:, 772:1028].rearrange("p (b c) -> p b c", b=16)
        lh = [sb(f"lh{i}", (128, 512), BF16) for i in range(4)]

        def bank(i):
            return psum[i // 2][:, (i % 2) * 512:(i % 2) * 512 + 512]

        def bankb(i):
            return bank(i).bitcast(BF16)

        _rr = {}

        def rr(name, items):
            i = _rr.get(name, 0)
            _rr[name] = i + 1
            return items[i % len(items)]

        def rri(name, n):
            i = _rr.get(name, 0)
            _rr[name] = i + 1
            return i % n

        def act(out, in_, func, r, w, **kw):
            trk.op("act", lambda e: e.activation(out=out, in_=in_, func=func, **kw), r, w)

        def tt(eng, out, in0, in1, op, r, w):
            trk.op(eng, lambda e: e.tensor_tensor(out=out, in0=in0, in1=in1, op=op), r, w)

        def ts(eng, out, in0, s1, s2, op0, op1, r, w):
            if s2 is None:
                trk.op(eng, lambda e: e.tensor_scalar(out=out, in0=in0, scalar1=s1, scalar2=None, op0=op0), r, w)
            else:
                trk.op(eng, lambda e: e.tensor_scalar(out=out, in0=in0, scalar1=s1, scalar2=s2, op0=op0, op1=op1), r, w)

        def stt(eng, out, in0, scalar, in1, op0, op1, r, w):
            trk.op(eng, lambda e: e.scalar_tensor_tensor(out=out, in0=in0, scalar=scalar, in1=in1, op0=op0, op1=op1), r, w)

        def cp(eng, out, in_, r, w):
            if eng == "act":
                trk.op("act", lambda e: e.copy(out=out, in_=in_), r, w)
            else:
                trk.op(eng, lambda e: e.tensor_copy(out=out, in_=in_), r, w)

        def ld(out, in_, w, q="sp", r=()):
            trk.dma(q, "ld", lambda e: e.dma_start(out=out, in_=in_), r, w)

        def stdma(out, in_, r, q="sp"):
            trk.dma(q, "st", lambda e: e.dma_start(out=out, in_=in_), r, ())

        def wload(out, in_, w):
            trk.dma("pool", "wl", lambda e: e.dma_start(out=out, in_=in_), (), w)

        _stg = [0]

        def wload_nocast(out, in_, w):
            i = _stg[0] % 2
            _stg[0] += 1
            n = 1
            for d_ in out.shape[1:]:
                n *= d_
            stage = big[:, 12288 + i * 4096:12288 + i * 4096 + n]
            if len(out.shape) == 3:
                stage = stage.rearrange("p (c n) -> p c n", c=out.shape[1])
            ld(stage, in_, [f"stage{i}"])
            cp("pool" if i else "dve", out, stage, [f"stage{i}"], w)

        def mm(groups, r, w):
            def f(e):
                ins = None
                for (o, l, rh, s0, s1) in groups:
                    ins = e.matmul(o, lhsT=l, rhs=rh, start=s0, stop=s1)
                return ins
            trk.op("pe", f, r, w)

        def transposes(items, r, w):
            def f(e):
                ins = None
                for (o, i, idn) in items:
                    ins = e.transpose(out=o, in_=i, identity=idn)
                return ins
            trk.op("pe", f, r, w)

        def rstd_from_ss(ss, rs, n, scale, key):
            act(rs, ss, AF.Ln, [key], [key + "r"], scale=scale, bias=EPS)
            act(rs, rs, AF.Exp, [key + "r"], [key + "r"], scale=-0.5)

        nt_i = [0]

        def norm_T(src, n, gT, keep_x=False):
            i = nt_i[0]
            nt_i[0] += 1
            x_t = xt[i % 2]
            xk = f"xt{i % 2}"
            xn = xnT[i % 2]
            xnk = f"xnT{i % 2}"
            ld(x_t[:n, :], src, [xk])
            ss = sm[:n, 0:1]
            rs = sm[:n, 1:2]
            act(xsb[:n, :], x_t[:n, :], AF.Square, [xk], ["xsb", "nss"], accum_out=ss)
            rstd_from_ss(ss, rs, n, 1.0 / D, "nss")
            act(xsb[:n, :], x_t[:n, :], AF.Copy, [xk, "nssr"], ["xsb"], scale=rs)
            b = rr("tb", [6, 7])
            pb = bankb(b).rearrange("p (c n) -> p c n", c=8)
            transposes([(pb[:, c, :n], xsb[:n, c * 128:(c + 1) * 128], identb[:n, :n]) for c in range(8)],
                       ["xsb"], [f"bank{b}"])
            tt("dve", xn[:, :, :n], pb[:, :, :n], gT.unsqueeze(2).to_broadcast([128, 8, n]), ALU.mult,
               [f"bank{b}"], [xnk])
            return xn, xnk, x_t, xk

        def proj(xn, xnk, n, wblk, wkey, b):
            mm([(bank(b)[:n, :], xn[:, c, :n], wblk[:, c, :], c == 0, c == 7) for c in range(8)],
               [xnk, wkey], [f"bank{b}"])

        def carve_w(idx):
            return big[:, idx * 2048:(idx + 1) * 2048].bitcast(BF16).rearrange("p (c n) -> p c n", c=8)

        def bk(i):
            return f"bank{i}"

        def rope4(ap):
            return ap.rearrange("p (h t f) -> p h t f", h=8, t=2)

        def rope(src, srckeys, out_ap, okey, n, cosb, sinb, tabkeys, A=None, B=None):
            A = ft[0] if A is None else A
            B = ft[1] if B is None else B
            ka, kb_ = "ft%d" % ft.index(A), "ft%d" % ft.index(B)
            tt("dve", rope4(A[:n, :]), rope4(src), cosb, ALU.mult, list(srckeys) + list(tabkeys), [ka])
            tt("dve", rope4(B[:n, :]), rope4(src), sinb, ALU.mult, list(srckeys) + list(tabkeys), [kb_])
            A4 = rope4(A[:n, :]); B4 = rope4(B[:n, :]); O4 = rope4(out_ap)
            tt("pool", O4[:, :, 0, :], A4[:, :, 0, :], B4[:, :, 1, :], ALU.subtract, [ka, kb_], [okey])
            tt("pool", O4[:, :, 1, :], A4[:, :, 1, :], B4[:, :, 0, :], ALU.add, [ka, kb_], [okey])

        ld(cf[:, :], cf_d[:, :], ["cf"])
        wload(cb[:, :], cb_d[:, :], ["cb"])
        trk.barrier()
        l0 = C("l0")
        l1 = C("l1")
        lbr = l0
        omlr = l1
        tt("dve", l0, l0, l1, ALU.subtract, ["cf"], ["cf"])
        act(l0, l0, AF.Sigmoid, ["cf"], ["cf"])
        ts("dve", l1, l0, -1.0, 1.0, ALU.mult, ALU.add, ["cf"], ["cf"])
        trk.op("pool", lambda e: e.memset(Sst[:], 0.0), (), ["S"])
        trk.op("pool", lambda e: e.memset(Dacc[:], 1.0), (), ["Dacc"])
        trk.op("pool", lambda e: e.memset(mvp[:, :, :, 128:129], 1.0), (), ["mvp"])
        trk.barrier()

        wm = [carve_w(0), carve_w(1)]
        w_mem_r = w_mem.rearrange("(c p) n -> p c n", p=128)
        for j in range(2):
            wload(wm[j], w_mem_r[:, :, j * 512:(j + 1) * 512], [f"wm{j}"])
        for mt in range(2):
            xn, xnk, _, _ = norm_T(mem[mt * 128:(mt + 1) * 128, :], 128, C("memgT"))
            for j in range(2):
                b = rri("pj", 4)
                proj(xn, xnk, 128, wm[j], f"wm{j}", b)
                fi = rri("ftA", 2)
                f = ft[fi]; fk = f"ft{fi}"
                cp("act", f[:, :], bank(b), [bk(b)], [fk])
                stdma(mkv[mt * 128:(mt + 1) * 128, j * 512:(j + 1) * 512], f[:, :], [fk])
                if j == 0:
                    cp("dve", bt[0][:, :], f[:, :], [fk], ["bt0"])
                    tb = 6 + rri("tb", 2)
                    pbk = bankb(tb).rearrange("p (c n) -> p c n", c=8)
                    transposes([(pbk[:, h, :], bt[0][:, h * 128:(h + 1) * 128], identb) for h in range(4)],
                               ["bt0"], [bk(tb)])
                    cp("dve", mkT[:, :, mt * 128:(mt + 1) * 128], pbk[:, 0:4, :], [bk(tb)], ["mkT"])
                else:
                    cp("dve", mvp[:, mt, :, 0:128], f[:, :].rearrange("p (h d) -> p h d", h=4), [fk], ["mvp"])
        trk.barrier()

        chk(0)

        def hgrn_tile(xn, xnk, full, wkeys, par=0, do_proj=True):
            wq, wf, wi, wz = wkeys
            Um = C("Um"); M2 = C("M2"); attm = C("attm"); ind = C("ind")
            fo = 4 * par; bo = 2 * par
            b_f, b_i = (0, 1) if par == 0 else (4, 5)
            if do_proj:
                proj(xn, xnk, 128, wf[0], wf[1], b_f)
                proj(xn, xnk, 128, wi[0], wi[1], b_i)
            sig = ft[fo + 0]; logf = ft[fo + 1]; kin = ft[fo + 2]
            k0, k1, k2, k3 = (f"ft{fo + i}" for i in range(4))
            kb0, kb1 = f"bt{bo}", f"bt{bo + 1}"
            eblk = f"ebl{par}"
            act(sig[:, :], bank(b_f), AF.Exp, [bk(b_f)], [k0], scale=-1.0)
            ts("dve", sig[:, :], sig[:, :], 1.0, None, ALU.add, None, [k0], [k0])
            trk.op("dve", lambda e, sig=sig: e.reciprocal(out=sig[:, :], in_=sig[:, :]), [k0], [k0])
            tt("dve", sig[:, :], sig[:, :], omlr, ALU.mult, [k0], [k0])
            tt("pool", sig[:, :], sig[:, :], lbr, ALU.add, [k0], [k0])
            act(logf[:, :], sig[:, :], AF.Ln, [k0], [k1])
            ts("pool", kin[:, :], sig[:, :], -1.0, 1.0, ALU.mult, ALU.add, [k0], [k2])
            vb = bt[bo]
            cp("act", vb[:, :], bank(b_i), [bk(b_i)], [kb0])
            lhi = lh[2 * par]; llo = lh[2 * par + 1]
            khi, klo = f"lh{2 * par}", f"lh{2 * par + 1}"
            cp("dve", lhi[:, :], logf[:, :], [k1], [khi])
            tt("pool", llo[:, :], logf[:, :], lhi[:, :], ALU.subtract, [k1, khi], [klo])
            mm([(bank(2), M2b, lhi[:, :], True, False), (bank(2), M2b, llo[:, :], False, True)], [khi, klo], [bk(2)])
            ek = ft[fo + 3]
            act(ek[:, :], bank(2), AF.Exp, [bk(2)], [k3])
            khat = bt[bo + 1]
            tt("dve", khat[:, :], kin[:, :], ek[:, :], ALU.mult, [k2, k3], [kb1])
            pbl = bank(3)[:, 0:8].rearrange("p (h t) -> p h t", t=2)
            grp_ = []
            for h in range(4):
                grp_.append((pbl[:, h, :], lhi[:, h * 128:(h + 1) * 128], indb, True, False))
                grp_.append((pbl[:, h, :], llo[:, h * 128:(h + 1) * 128], indb, False, True))
            mm(grp_, [khi, klo], [bk(3)])
            ebl = (sm[:, 8:16] if par == 0 else sm[:, 100:108]).rearrange("p (h t) -> p h t", t=2)
            act(ebl, pbl, AF.Exp, [bk(3)], [eblk])
            if full:
                chk(3.52)
                proj(xn, xnk, 128, wq[0], wq[1], 4)
                proj(xn, xnk, 128, wz[0], wz[1], 5)
                mm([(bank(2), Umb, lhi[:, :], True, False), (bank(2), Umb, llo[:, :], False, True)], [khi, klo], [bk(2)])
                eq = ft[3]; ekm = ft[4]
                act(eq[:, :], bank(2), AF.Exp, [bk(2)], ["ft3"])
                act(ekm[:, :], bank(2), AF.Exp, [bk(2)], ["ft4"], scale=-1.0)
                chk(3.55)
                qt = bt[2]; kt = bt[3]
                tt("dve", qt[:, :], bank(4), eq[:, :], ALU.mult, [bk(4), "ft3"], ["bt2"])
                tt("pool", kt[:, :], kin[:, :], ekm[:, :], ALU.mult, ["ft2", "ft4"], ["bt3"])
                G = ft[5]
                act(G[:, :], bank(5), AF.Silu, [bk(5)], ["ft5"])
                tt("pool", G[:, :], G[:, :], C("hg"), ALU.mult, ["ft5"], ["ft5"])
                chk(3.57)
                pq = bankb(6).rearrange("p (c n) -> p c n", c=8)
                pk2 = bankb(7).rearrange("p (c n) -> p c n", c=8)
                transposes([(pq[:, h, :], qt[:, h * 128:(h + 1) * 128], identb) for h in range(4)], ["bt2"], [bk(6)])
                transposes([(pk2[:, h, :], kt[:, h * 128:(h + 1) * 128], identb) for h in range(4)], ["bt3"], [bk(7)])
                qkT = bt[4].rearrange("p (c n) -> p c n", c=4)
                kkT = bt[5].rearrange("p (c n) -> p c n", c=4)
                cp("act", qkT, pq[:, 0:4, :], [bk(6)], ["bt4"])
                cp("dve", kkT, pk2[:, 0:4, :], [bk(7)], ["bt5"])
                chk(3.6)
                tt("dve", Sbf[:], Sst[:], ebl[:, :, 0:1].to_broadcast([128, 4, 128]), ALU.mult, ["S", eblk], ["Sbf"])
                pa = bank(2)
                mm([(pa[:, h * 128:(h + 1) * 128], kkT[:, h, :], qkT[:, h, :], True, True) for h in range(4)],
                   ["bt4", "bt5"], [bk(2)])
                attb = bt[6]
                tt("dve", attb[:, :].rearrange("p (h t) -> p h t", h=4), pa.rearrange("p (h t) -> p h t", h=4),
                   attm.unsqueeze(1).to_broadcast([128, 4, 128]), ALU.mult, [bk(2)], ["bt6"])
                chk(3.7)
                po = bank(4)
                grp = []
                for h in range(4):
                    grp.append((po[:, h * 128:(h + 1) * 128], qkT[:, h, :], Sbf[:, h, :], True, False))
                    grp.append((po[:, h * 128:(h + 1) * 128], attb[:, h * 128:(h + 1) * 128], vb[:, h * 128:(h + 1) * 128], False, True))
                mm(grp, ["bt4", "Sbf", "bt6", "bt0"], [bk(4)])
                chk(3.8)
            pP = bank(b_f)
            mm([(pP[:, h * 128:(h + 1) * 128], khat[:, h * 128:(h + 1) * 128], vb[:, h * 128:(h + 1) * 128], True, True)
                for h in range(4)], [kb1, kb0], [bk(b_f)])
            for h in range(4):
                stt("dve", Sst[:, h, :], Sst[:, h, :], ebl[:, h, 1:2], pP[:, h * 128:(h + 1) * 128], ALU.mult, ALU.add,
                    ["S", eblk, bk(b_f)], ["S"])
            if not full:
                return None
            return head_norm_gate(bank(4), bk(4), G, "ft5", 128)

        def head_norm_gate(po, pok, G, gk, n):
            ss4 = sm[:n, 16:20]
            rs4 = sm[:n, 20:24]
            for h in range(4):
                act(ft[6][:n, h * 128:(h + 1) * 128], po[:n, h * 128:(h + 1) * 128], AF.Square, [pok], ["ft6", "ss4"],
                    accum_out=ss4[:, h:h + 1])
            rstd_from_ss(ss4, rs4, n, 1.0 / 128, "ss4")
            yb = bt[2]
            for h in range(4):
                stt("dve", yb[:n, h * 128:(h + 1) * 128], po[:n, h * 128:(h + 1) * 128], rs4[:, h:h + 1],
                    G[:n, h * 128:(h + 1) * 128], ALU.mult, ALU.mult, [pok, "ss4r", gk], ["bt2"])
            pq = bankb(6).rearrange("p (c n) -> p c n", c=8)
            transposes([(pq[:, h, :n], yb[:n, h * 128:(h + 1) * 128], identb[:n, :n]) for h in range(4)], ["bt2"], [bk(6)])
            ybT = bt[7].rearrange("p (c n) -> p c n", c=4)
            cp("act", ybT[:, :, :n], pq[:, 0:4, :n], [bk(6)], ["bt7"])
            return ybT

        wf_b = carve_w(0); wi_b = carve_w(1)
        wload(wf_b, w_in_r[:, :, C_BF:C_BF + 512], ["w0"])
        wload(wi_b, w_in_r[:, :, C_BI:C_BI + 512], ["w1"])
        nxt = norm_T(xh[0:128, :], 128, C("pregT"))
        for t in range(NPREV):
            xn, xnk = nxt[0], nxt[1]
            b_f, b_i = (0, 1) if t % 2 == 0 else (4, 5)
            proj(xn, xnk, 128, wf_b, "w0", b_f)
            proj(xn, xnk, 128, wi_b, "w1", b_i)
            if t + 1 < NPREV:
                nxt = norm_T(xh[(t + 1) * 128:(t + 2) * 128, :], 128, C("pregT"))
            hgrn_tile(xn, xnk, False, (None, (wf_b, "w0"), (wi_b, "w1"), None), par=t % 2, do_proj=False)
        trk.barrier()
        chk(1)

        acc = big[:, 0:16384].rearrange("p (h t) -> p h t", h=8)
        trk.op("pool", lambda e: e.memset(big[0:65, 0:16384], 0.0), (), ["acc"])
        for i in range(3):
            trk.op("pool", lambda e, i=i: e.memset(vss[i][:, :, 64:65], 1.0), (), [f"vss{i}"])
        WB = 8
        for g, d in enumerate(DILS):
            wq = carve_w(WB + 0); wk = carve_w(WB + 1); wv = carve_w(WB + 2)
            wload(wq, w_in_r[:, :, C_AQ + g * 512:C_AQ + (g + 1) * 512], ["wA0"])
            wload(wk, w_in_r[:, :, C_AK + g * 512:C_AK + (g + 1) * 512], ["wA1"])
            wload(wv, w_in_r[:, :, C_AV + g * 512:C_AV + (g + 1) * 512], ["wA2"])
            nb = NT // d
            wb_tokens = 128 * d
            kv_i = 0
            for r in range(d):
                prev = None
                for n in range(-1, nb):
                    own = n >= 0
                    start = TPC + d * 128 * n + r
                    src = xe[start:start + 127 * d + 1:d, :] if d > 1 else xe[start:start + 128, :]
                    xn, xnk, _, _ = norm_T(src, 128, C("pregT"))
                    ri = rri("ropet", 2); rt = ropet[ri]; rk = f"ropet{ri}"
                    ld(rt[:, :], rope_d[PA_INDEX[(g, r, n)], :, :], [rk])
                    cosb = rt[:, 0:32].unsqueeze(1).unsqueeze(1).to_broadcast([128, 8, 2, 32])
                    sinb = rt[:, 32:64].unsqueeze(1).unsqueeze(1).to_broadcast([128, 8, 2, 32])
                    ks = kv_i % 3
                    kv_i += 1
                    proj(xn, xnk, 128, wk, "wA1", 1)
                    rope(bank(1), [bk(1)], ft[2][:, :], "ft2", 128, cosb, sinb, [rk])
                    need_out = own and (d * (128 * n + 127) + r >= TPC - wb_tokens)
                    if need_out:
                        t0 = d * 128 * n + r - (TPC - wb_tokens)
                        dst = nw[g][t0:t0 + 127 * d + 1:d, 0, :] if d > 1 else nw[g][t0:t0 + 128, 0, :]
                        stdma(dst, ft[2][:, :], ["ft2"])
                    cp("pool", bt[0][:, :], ft[2][:, :], ["ft2"], ["bt0"])
                    pk = bankb(6).rearrange("p (c n) -> p c n", c=8)
                    transposes([(pk[:, c, :], bt[0][:, c * 128:(c + 1) * 128], identb) for c in range(4)], ["bt0"], [bk(6)])
                    cp("act", kTs[ks], pk[:, 0:4, :], [bk(6)], [f"kTs{ks}"])
                    proj(xn, xnk, 128, wv, "wA2", 2)
                    if need_out:
                        cp("act", ft[3][:, :], bank(2), [bk(2)], ["ft3"])
                        dst = nw[g][t0:t0 + 127 * d + 1:d, 1, :] if d > 1 else nw[g][t0:t0 + 128, 1, :]
                        stdma(dst, ft[3][:, :], ["ft3"])
                    cp("act", vss[ks][:, :, 0:64], bank(2).rearrange("p (h d) -> p h d", h=8), [bk(2)], [f"vss{ks}"])
                    if own:
                        proj(xn, xnk, 128, wq, "wA0", 0)
                        rope(bank(0), [bk(0)], ft[6][:, :], "ft6", 128, cosb, sinb, [rk], A=ft[4], B=ft[5])
                        cp("pool", bt[1][:, :], ft[6][:, :], ["ft6"], ["bt1"])
                        pq = bankb(7).rearrange("p (c n) -> p c n", c=8)
                        transposes([(pq[:, c, :], bt[1][:, c * 128:(c + 1) * 128], identb) for c in range(4)], ["bt1"], [bk(7)])
                        qi = rri("qTs", 2)
                        cp("dve", qTs[qi], pq[:, 0:4, :], [bk(7)], [f"qTs{qi}"])
                        mp = mfirst if n == 0 else mprev
                        kp, kc = prev, ks
                        tok0 = d * 128 * n + r
                        for h in range(8):
                            pr, off = h // 2, (h % 2) * 64
                            sbk = 3 + rri("sbk", 2)
                            ps = bank(sbk)
                            mm([(ps[:, 0:128], kTs[kp][off:off + 64, pr, :], qTs[qi][off:off + 64, pr, :], True, False),
                                (ps[:, 0:128], identb, mp, False, True),
                                (ps[:, 128:256], kTs[kc][off:off + 64, pr, :], qTs[qi][off:off + 64, pr, :], True, False),
                                (ps[:, 128:256], identb, mcur, False, True)],
                               [f"kTs{kp}", f"kTs{kc}", f"qTs{qi}"], [bk(sbk)])
                            pi = rri("ptb", 2)
                            act(ptb[pi][:, :], ps[:, 0:256], AF.Exp, [bk(sbk)], [f"ptb{pi}"], scale=0.125)
                            pob_ = (5, 0)[h % 2]
                            po = bank(pob_)[0:65, 0:128]
                            pok = bk(pob_)
                            mm([(po, vss[kp][:, h, :], ptb[pi][:, 0:128], True, False),
                                (po, vss[kc][:, h, :], ptb[pi][:, 128:256], False, True)],
                               [f"vss{kp}", f"vss{kc}", f"ptb{pi}"], [pok])
                            a_sl = acc[0:65, h, tok0:tok0 + 127 * d + 1:d] if d > 1 else acc[0:65, h, tok0:tok0 + 128]
                            tt("dve", a_sl, po, a_sl, ALU.add, [pok, "acc"], ["acc"])
                    prev = ks
        trk.barrier()

        chk(2)

        waz = carve_w(WB + 0); wga0 = carve_w(WB + 1); wga1 = carve_w(WB + 2)
        wload(waz, w_in_r[:, :, C_AZ:C_AZ + 512], ["wA0"])
        wload(wga0, w_in_r[:, :, C_GA:C_GA + 512], ["wA1"])
        wload(wga1, w_in_r[:, :, C_GA + 512:C_GA + 1024], ["wA2"])
        wpa = big[:, 22528:26624].bitcast(BF16).rearrange("p (h n) -> p h n", h=8)
        wload(wpa[0:64, :, :], w_pa.rearrange("(h p) n -> p h n", p=64), ["wpa"])
        ones64 = big[64:65, 26624:26688]
        trk.op("pool", lambda e: e.memset(ones64, 1.0), (), ["ones64"])
        rl = big[64:65, 26688:26688 + 1024]
        yaT_tiles = [bt[0], bt[1]]
        for t in range(NT):
            xn, xnk, _, _ = norm_T(xe[TPC + t * 128:TPC + (t + 1) * 128, :], 128, C("pregT"))
            trk.op("dve", lambda e, t=t: e.reciprocal(out=rl.rearrange("p (h n) -> p h n", h=8),
                                                      in_=acc[64:65, :, t * 128:(t + 1) * 128]), ["acc"], ["rl"])
            for h in range(8):
                zi = rri("azb", 2)
                pz = bank(zi)
                mm([(pz[0:64, 0:128], waz[:, c, h * 64:(h + 1) * 64], xn[:, c, :], c == 0, c == 7) for c in range(8)],
                   [xnk, "wA0"], [bk(zi)])
                gi = rri("gz", 2)
                gz = ft[gi]; gzk = f"ft{gi}"
                act(gz[0:64, 0:128], pz[0:64, 0:128], AF.Silu, [bk(zi)], [gzk])
                bi_ = 2 + (h % 2)
                pbc = bank(bi_)
                mm([(pbc[0:64, 0:128], ones64, rl[:, h * 128:(h + 1) * 128], True, True)], ["ones64", "rl"], [bk(bi_)])
                tt("pool", gz[0:64, 0:128], gz[0:64, 0:128], acc[0:64, h, t * 128:(t + 1) * 128], ALU.mult, [gzk, "acc"], [gzk])
                ydst = yaT_tiles[h // 4][0:64, (h % 4) * 128:(h % 4) * 128 + 128]
                tt("dve", ydst, gz[0:64, 0:128], pbc[0:64, 0:128], ALU.mult, [gzk, bk(bi_)], [f"bt{h // 4}"])
            grp = []
            for half in range(2):
                for h in range(8):
                    grp.append((bank(4 + half), yaT_tiles[h // 4][0:64, (h % 4) * 128:(h % 4) * 128 + 128],
                                wpa[0:64, h, half * 512:(half + 1) * 512], h == 0, h == 7))
            mm(grp, ["bt0", "bt1", "wpa"], [bk(4), bk(5)])
            for half, (wg, wgk) in enumerate(((wga0, "wA1"), (wga1, "wA2"))):
                proj(xn, xnk, 128, wg, wgk, 2 + half)
                sg = ft[2 + half]
                act(sg[:, :], bank(2 + half), AF.Sigmoid, [bk(2 + half)], [f"ft{2 + half}"])
                tt("dve", sg[:, :], bank(4 + half), sg[:, :], ALU.mult, [bk(4 + half), f"ft{2 + half}"], [f"ft{2 + half}"])
                stdma(ga_d[t * 128:(t + 1) * 128, half * 512:(half + 1) * 512], sg[:, :], [f"ft{2 + half}"])
        trk.barrier()

        chk(3)

        names = [("bq", C_BQ), ("bf", C_BF), ("bi", C_BI), ("bz", C_BZ), ("cq", C_CQ), ("cz", C_CZ),
                 ("gb0", C_GB), ("gb1", C_GB + 512), ("gc0", C_GC), ("gc1", C_GC + 512)]
        W3 = {}
        for i, (nm, col) in enumerate(names):
            W3[nm] = (carve_w(i), "w3" + nm)
            wload(W3[nm][0], w_in_r[:, :, col:col + 512], ["w3" + nm])
        gat = big[:, 20480:21504]
        wpb = big[:, 22528:24576].bitcast(BF16).rearrange("p (c n) -> p c n", c=4)
        wpc = big[:, 24576:26624].bitcast(BF16).rearrange("p (c n) -> p c n", c=4)
        wo = big[:, 26624:30720].bitcast(BF16).rearrange("p (c n) -> p c n", c=8)
        wload(wpb, w_pb.rearrange("(c p) n -> p c n", p=128), ["wpb"])
        wload(wpc, w_pc.rearrange("(c p) n -> p c n", p=128), ["wpc"])
        wload(wo, w_out.rearrange("(c p) n -> p c n", p=128), ["wo"])

        chk(3.5)

        def epilogue(n, xn, xnk, x_t, xk, ybT, ycT, ga_src, y_dst, gate_src=None):
            for (yT, w_, wk_, b0, key) in ((ybT, wpb, "wpb", 0, "bt7"), (ycT, wpc, "wpc", 2, "bt6")):
                grp = []
                for half in range(2):
                    for c in range(4):
                        grp.append((bank(b0 + half)[:n, :], yT[:, c, :n], w_[:, c, half * 512:(half + 1) * 512], c == 0, c == 3))
                mm(grp, [key, wk_], [bk(b0), bk(b0 + 1)])
            mg = [ft[8], ft[9]]
            gsrc, gkey = ga_src
            for half in range(2):
                sgb = ft[0]; sgc = ft[1]
                if gate_src is None:
                    proj(xn, xnk, n, W3[f"gb{half}"][0], W3[f"gb{half}"][1], 4)
                    act(sgb[:n, :], bank(4)[:n, :], AF.Sigmoid, [bk(4)], ["ft0"])
                    proj(xn, xnk, n, W3[f"gc{half}"][0], W3[f"gc{half}"][1], 5)
                    act(sgc[:n, :], bank(5)[:n, :], AF.Sigmoid, [bk(5)], ["ft1"])
                else:
                    act(sgb[:n, :], gate_src[:, C_GB + half * 512:C_GB + (half + 1) * 512], AF.Sigmoid, ["U"], ["ft0"])
                    act(sgc[:n, :], gate_src[:, C_GC + half * 512:C_GC + (half + 1) * 512], AF.Sigmoid, ["U"], ["ft1"])
                tt("dve", sgb[:n, :], bank(0 + half)[:n, :], sgb[:n, :], ALU.mult, [bk(half), "ft0"], ["ft0"])
                tt("dve", sgc[:n, :], bank(2 + half)[:n, :], sgc[:n, :], ALU.mult, [bk(2 + half), "ft1"], ["ft1"])
                tt("pool", sgb[:n, :], sgb[:n, :], sgc[:n, :], ALU.add, ["ft0", "ft1"], ["ft0"])
                tt("pool", bt[half][:n, :], sgb[:n, :], gsrc[:n, half * 512:(half + 1) * 512], ALU.add, ["ft0", gkey], [f"bt{half}"])
            pm = bankb(6).rearrange("p (c n) -> p c n", c=8)
            pm2 = bankb(7).rearrange("p (c n) -> p c n", c=8)
            transposes([(pm[:, c, :n], bt[0][:n, c * 128:(c + 1) * 128], identb[:n, :n]) for c in range(4)], ["bt0"], [bk(6)])
            transposes([(pm2[:, c, :n], bt[1][:n, c * 128:(c + 1) * 128], identb[:n, :n]) for c in range(4)], ["bt1"], [bk(7)])
            mT = bt[2].rearrange("p (c n) -> p c n", c=4)
            mT2 = bt[3].rearrange("p (c n) -> p c n", c=4)
            cp("act", mT[:, :, :n], pm[:, 0:4, :n], [bk(6)], ["bt2"])
            cp("dve", mT2[:, :, :n], pm2[:, 0:4, :n], [bk(7)], ["bt3"])
            grp = []
            for half in range(2):
                for c in range(8):
                    src_ = (mT if c < 4 else mT2)[:, c % 4, :n]
                    grp.append((bank(half)[:n, :], src_, wo[:, c, half * 512:(half + 1) * 512], c == 0, c == 7))
            mm(grp, ["bt2", "bt3", "wo"], [bk(0), bk(1)])
            ssz = sm[:n, 28:30]
            for half in range(2):
                act(ft[2][:n, :], bank(half)[:n, :], AF.Square, [bk(half)], ["ft2", "ssz"], accum_out=ssz[:, half:half + 1])
            tt("dve", ssz[:, 0:1], ssz[:, 0:1], ssz[:, 1:2], ALU.add, ["ssz"], ["ssz"])
            rsz = sm[:n, 30:31]
            rstd_from_ss(ssz[:, 0:1], rsz, n, 1.0 / D, "ssz")
            postg = C("postg")
            for half in range(2):
                stt("dve", mg[half][:n, :], bank(half)[:n, :], rsz, postg[:n, half * 512:(half + 1) * 512], ALU.mult, ALU.mult,
                    [bk(half), "sszr"], [f"ft{8 + half}"])
                tt("pool", mg[half][:n, :], mg[half][:n, :], x_t[:n, half * 512:(half + 1) * 512], ALU.add, [f"ft{8 + half}", xk], [f"ft{8 + half}"])
                stdma(y_dst[:, half * 512:(half + 1) * 512], mg[half][:n, :], [f"ft{8 + half}"])

        for t in range(NT):
            xn, xnk, x_t, xk = norm_T(xe[TPC + t * 128:TPC + (t + 1) * 128, :], 128, C("pregT"))
            ybT = hgrn_tile(xn, xnk, True, (W3["bq"], W3["bf"], W3["bi"], W3["bz"]))
            chk(4)
            proj(xn, xnk, 128, W3["cq"][0], W3["cq"][1], 0)
            proj(xn, xnk, 128, W3["cz"][0], W3["cz"][1], 1)
            cqb = bt[0]
            cp("act", cqb[:, :], bank(0), [bk(0)], ["bt0"])
            Gc = ft[0]
            act(Gc[:, :], bank(1), AF.Silu, [bk(1)], ["ft0"])
            pq = bankb(7).rearrange("p (c n) -> p c n", c=8)
            transposes([(pq[:, h, :], cqb[:, h * 128:(h + 1) * 128], identb) for h in range(4)], ["bt0"], [bk(7)])
            cqT = bt[1].rearrange("p (c n) -> p c n", c=4)
            cp("dve", cqT, pq[:, 0:4, :], [bk(7)], ["bt1"])
            yc = bt[3]
            for h in range(4):
                sbk = 2 + rri("sbk3", 2)
                ps = bank(sbk)
                mm([(ps[:, mb * 128:(mb + 1) * 128], mkT[:, h, mb * 128:(mb + 1) * 128], cqT[:, h, :], True, True) for mb in range(2)],
                   ["mkT", "bt1"], [bk(sbk)])
                pi = rri("ptb", 2)
                act(ptb[pi][:, :], ps[:, 0:256], AF.Exp, [bk(sbk)], [f"ptb{pi}"], scale=float(128 ** -0.5))
                po = bank(5)[:, 0:129] if h % 2 == 0 else bank(5)[:, 256:385]
                mm([(po, ptb[pi][:, mb * 128:(mb + 1) * 128], mvp[:, mb, h, :], mb == 0, mb == 1) for mb in range(2)],
                   [f"ptb{pi}", "mvp"], [bk(5)])
                rlc = sm[:, 32 + h:33 + h]
                trk.op("dve", lambda e, rlc=rlc, po=po: e.reciprocal(out=rlc, in_=po[:, 128:129]), [bk(5)], [f"rlc{h}"])
                stt("dve", yc[:, h * 128:(h + 1) * 128], po[:, 0:128], rlc, Gc[:, h * 128:(h + 1) * 128], ALU.mult, ALU.mult,
                    [bk(5), f"rlc{h}", "ft0"], ["bt3"])
            pq = bankb(7).rearrange("p (c n) -> p c n", c=8)
            transposes([(pq[:, h, :], yc[:, h * 128:(h + 1) * 128], identb) for h in range(4)], ["bt3"], [bk(7)])
            ycT = bt[6].rearrange("p (c n) -> p c n", c=4)
            cp("act", ycT, pq[:, 0:4, :], [bk(7)], ["bt6"])
            chk(5)
            ld(gat, ga_d[t * 128:(t + 1) * 128, :], ["gat"])
            epilogue(128, xn, xnk, x_t, xk, ybT, ycT, (gat, "gat"), y_p[t * 128:(t + 1) * 128, :])
            chk(6)
        stdma(hg_p.rearrange("h k v -> k h v"), Sst[:], ["S"])
        trk.barrier()

        if do_sample:
            n = 16
            xn, xnk, x_t, xk = norm_T(xsamp[:, :], n, C("pregT"))
            U = big[0:16, 6144:6144 + NIN]
            for blk in range(NIN // 512):
                si = blk % 3
                wslot = carve_w(si)
                wkey = f"wS{si}"
                wload(wslot, w_in_r[:, :, blk * 512:(blk + 1) * 512], [wkey])
                b = rri("pjs", 4)
                proj(xn, xnk, n, wslot, wkey, b)
                cp("act" if blk % 2 else "dve", U[:, blk * 512:(blk + 1) * 512], bank(b)[:n, :], [bk(b)], ["U"])
            trk.barrier()
            Ssm = big[:, 0:4096].rearrange("p (b h v) -> p b h v", b=8, h=4)
            kvs = [big[:, 4096:5120], big[:, 5120:6144]]
            selb = big[0:16, 17408:19456].rearrange("p (b m) -> p b m", b=16)
            wpas = big[:, 19456:21504].bitcast(BF16).rearrange("p (c n) -> p c n", c=4)
            gas = big[0:16, 21504:22528]
            wload(wpas, w_pa.rearrange("(c p) n -> p c n", p=128), ["wpas"])
            ld(selb, selb_d[:, :, :], ["selb"])
            eye16v = C("eye16").rearrange("p (b c) -> p b c", b=16)
            rs_ = C("ropes")
            cosb = rs_[:n, 0:32].unsqueeze(1).unsqueeze(1).to_broadcast([n, 8, 2, 32])
            sinb = rs_[:n, 32:64].unsqueeze(1).unsqueeze(1).to_broadcast([n, 8, 2, 32])
            for g in range(3):
                rope(U[:, C_AQ + g * 512:C_AQ + (g + 1) * 512], ["U"], ft[2 + g][:n, :], f"ft{2 + g}", n, cosb, sinb, [])
                rope(U[:, C_AK + g * 512:C_AK + (g + 1) * 512], ["U"], ft[5 + g][:n, :], f"ft{5 + g}", n, cosb, sinb, [])
                stdma(nws[g][:, 0, :], ft[5 + g][:n, :], [f"ft{5 + g}"])
                stdma(nws[g][:, 1, :], U[:, C_AV + g * 512:C_AV + (g + 1) * 512], ["U"])

            def samp_attn(items, nh, dh, scale, pso, psl, pkeys):
                tot = len(items)

                def qbc(j):
                    _, q_j, qkey_j, b_j = items[j]
                    qb_ = rri("qbc", 4)
                    mm([(bank(qb_), selb[:, b_j, :], q_j, True, True)], [qkey_j, "selb"], [bk(qb_)])
                    return qb_
                qb_next = qbc(0)
                for it, (src, q_ap, qkey, b) in enumerate(items):
                    si = rri("kvs", 2)
                    kv = kvs[si]; kvk = f"kvs{si}"
                    ld(kv, src, [kvk])
                    qb = qb_next
                    if it + 1 < tot:
                        qb_next = qbc(it + 1)
                    pi_ = 8 + rri("prod", 2)
                    prod = ft[pi_]; pk_ = f"ft{pi_}"
                    tt("dve", prod[:, :], kv[:, 0:512], bank(qb), ALU.mult, [kvk, bk(qb)], [pk_])
                    sc_i = rri("s8", 2)
                    s8 = sm[:, 40 + 8 * sc_i:40 + 8 * sc_i + nh]; s8k = f"s8{sc_i}"
                    trk.op("dve", lambda e, s8=s8, prod=prod: e.reduce_sum(
                        out=s8, in_=prod[:, :].rearrange("p (h d) -> p h d", h=nh), axis=AX.X), [pk_], [s8k])
                    act(s8, s8, AF.Exp, [s8k], [s8k], scale=scale)
                    vi = rri("pv", 2)
                    pv = (bt[1], bt[5])[vi]; pvk = ("bt1", "bt5")[vi]
                    tt("pool", pv[:, :].rearrange("p (h d) -> p h d", h=nh), kv[:, 512:1024].rearrange("p (h d) -> p h d", h=nh),
                       s8.unsqueeze(2).to_broadcast([128, nh, dh]), ALU.mult, [kvk, s8k], [pvk])
                    mm([(pso, eye16b[:, b, :], pv[:, :], it == 0, it == tot - 1),
                        (psl, eye16v[:, b, :], s8, it == 0, it == tot - 1)], [pvk, s8k], pkeys)

            items = []
            for g, d in enumerate(DILS):
                for b in range(16):
                    src = cw[g][b, 0:127 * d + 1:d, :] if d > 1 else cw[g][b, :, :]
                    items.append((src, ft[2 + g][:n, :], f"ft{2 + g}", b))
            pso = bank(4)[:n, :]
            psl = bank(5)[:n, 0:8]
            samp_attn(items, 8, 64, 0.125, pso, psl, [bk(4), bk(5)])
            oS = gas
            oS = big[0:16, 28672 - 1024:28672 - 512]
            oS = ft[8]
            lS = sm[:n, 64:72]
            cp("dve", oS[:n, :], pso, [bk(4)], ["ft8"])
            cp("dve", lS, psl, [bk(5)], ["lS"])
            for g in range(3):
                tt("dve", ft[0][:n, :], ft[2 + g][:n, :], ft[5 + g][:n, :], ALU.mult, [f"ft{2 + g}", f"ft{5 + g}"], ["ft0"])
                pn = sm[:n, 72:80]
                trk.op("dve", lambda e, pn=pn: e.reduce_sum(out=pn, in_=ft[0][:n, :].rearrange("p (h d) -> p h d", h=8), axis=AX.X),
                       ["ft0"], ["pn"])
                act(pn, pn, AF.Exp, ["pn"], ["pn"], scale=0.125)
                tt("dve", ft[1][:n, :].rearrange("p (h d) -> p h d", h=8),
                   U[:, C_AV + g * 512:C_AV + (g + 1) * 512].rearrange("p (h d) -> p h d", h=8),
                   pn.unsqueeze(2).to_broadcast([n, 8, 64]), ALU.mult, ["U", "pn"], ["ft1"])
                tt("dve", oS[:n, :], oS[:n, :], ft[1][:n, :], ALU.add, ["ft8", "ft1"], ["ft8"])
                tt("dve", lS, lS, pn, ALU.add, ["lS", "pn"], ["lS"])
            trk.op("dve", lambda e: e.reciprocal(out=lS, in_=lS), ["lS"], ["lS"])
            Gz = ft[9]
            act(Gz[:n, :], U[:, C_AZ:C_AZ + 512], AF.Silu, ["U"], ["ft9"])
            tt("dve", oS[:n, :].rearrange("p (h d) -> p h d", h=8), oS[:n, :].rearrange("p (h d) -> p h d", h=8),
               lS.unsqueeze(2).to_broadcast([n, 8, 64]), ALU.mult, ["ft8", "lS"], ["ft8"])
            tt("dve", bt[0][:n, :], oS[:n, :], Gz[:n, :], ALU.mult, ["ft8", "ft9"], ["bt0"])
            pq = bankb(6).rearrange("p (c n) -> p c n", c=8)
            transposes([(pq[:, c, :n], bt[0][:n, c * 128:(c + 1) * 128], identb[:n, :n]) for c in range(4)], ["bt0"], [bk(6)])
            yaT = bt[4].rearrange("p (c n) -> p c n", c=4)
            cp("act", yaT[:, :, :n], pq[:, 0:4, :n], [bk(6)], ["bt4"])
            grp = []
            for half in range(2):
                for c in range(4):
                    grp.append((bank(half)[:n, :], yaT[:, c, :n], wpas[:, c, half * 512:(half + 1) * 512], c == 0, c == 3))
            mm(grp, ["bt4", "wpas"], [bk(0), bk(1)])
            for half in range(2):
                act(ft[0][:n, :], U[:, C_GA + half * 512:C_GA + (half + 1) * 512], AF.Sigmoid, ["U"], ["ft0"])
                tt("dve", gas[:, half * 512:(half + 1) * 512], bank(half)[:n, :], ft[0][:n, :], ALU.mult, [bk(half), "ft0"], ["gas"])

            fS = ft[0]; kinS = ft[1]
            act(fS[:n, :], U[:, C_BF:C_BF + 512], AF.Sigmoid, ["U"], ["ft0"])
            tt("dve", fS[:n, :], fS[:n, :], omlr[:n, :], ALU.mult, ["ft0"], ["ft0"])
            tt("dve", fS[:n, :], fS[:n, :], lbr[:n, :], ALU.add, ["ft0"], ["ft0"])
            ts("dve", kinS[:n, :], fS[:n, :], -1.0, 1.0, ALU.mult, ALU.add, ["ft0"], ["ft1"])
            pT = bank(6)[:, 0:128]
            transposes([(pT[:, h * 16:(h + 1) * 16], fS[:n, h * 128:(h + 1) * 128], identf[:n, :n]) for h in range(4)] +
                       [(pT[:, 64 + h * 16:64 + (h + 1) * 16], U[:, C_BQ + h * 128:C_BQ + (h + 1) * 128], identf[:n, :n]) for h in range(4)],
                       ["ft0", "U"], [bk(6)])
            fqT = ft[2]
            cp("dve", fqT[:, 0:128], pT, [bk(6)], ["ft2"])
            fT = fqT[:, 0:64].rearrange("p (h b) -> p h b", h=4)
            qT = fqT[:, 64:128].rearrange("p (h b) -> p h b", h=4)
            qTm = [ft[3], ft[4]]
            for hp in range(2):
                tt("dve", qTm[hp][:, :].rearrange("p (h b c) -> p h b c", h=2, b=16),
                   qT[:, 2 * hp:2 * hp + 2, :].unsqueeze(3).to_broadcast([128, 2, 16, 16]),
                   eye16v.unsqueeze(1).to_broadcast([128, 2, 16, 16]), ALU.mult, ["ft2"], [f"ft{3 + hp}"])
            pob = bank(7)[:n, :]
            zt = ft[8]
            trk.op("pool", lambda e: e.memset(zt[:, :], 0.0), (), ["ft8"])
            mm([(pob, eye16v[:, 0, :], zt[:, :], True, False)], ["ft8"], [bk(7)])
            vexp = ft[5]
            for half in range(2):
                for bl in range(8):
                    b = half * 8 + bl
                    ld(Ssm[:, bl, :, :], sh[b].rearrange("h k v -> k h v"), [f"Ssm{bl}"])
                for q4 in range(2):
                    for h in range(4):
                        b0 = half * 8 + q4 * 4
                        tt("dve", vexp[:n, :].rearrange("p (b v) -> p b v", b=4),
                           U[:, C_BI + h * 128:C_BI + (h + 1) * 128].unsqueeze(1).to_broadcast([n, 4, 128]),
                           identf[:n, b0:b0 + 4].unsqueeze(2).to_broadcast([n, 4, 128]), ALU.mult, ["U"], ["ft5"])
                        kb_i = rri("kvb", 4)
                        mm([(bank(kb_i), kinS[:n, h * 128:(h + 1) * 128], vexp[:n, :], True, True)], ["ft1", "ft5"], [bk(kb_i)])
                        for bi_ in range(4):
                            bl = q4 * 4 + bi_
                            b = half * 8 + bl
                            stt("dve", Ssm[:, bl, h, :], Ssm[:, bl, h, :], fT[:, h, b:b + 1], bank(kb_i)[:, bi_ * 128:(bi_ + 1) * 128],
                                ALU.mult, ALU.add, [f"Ssm{bl}", "ft2", bk(kb_i)], [f"Ssm{bl}"])
                for bl in range(8):
                    b = half * 8 + bl
                    mm([(pob[:, h * 128:(h + 1) * 128], qTm[h // 2][:, :].rearrange("p (h b c) -> p h b c", h=2, b=16)[:, h % 2, b, :],
                         Ssm[:, bl, h, :], False, b == 15) for h in range(4)],
                       [f"Ssm{bl}", "ft3", "ft4"], [bk(7)])
                    stdma(hg_s[b].rearrange("h k v -> k h v"), Ssm[:, bl, :, :], [f"Ssm{bl}"])
            Gb = ft[6]
            act(Gb[:n, :], U[:, C_BZ:C_BZ + 512], AF.Silu, ["U"], ["ft6"])
            tt("dve", ft[7][:n, :], Gb[:n, :], C("hg")[:n, :], ALU.mult, ["ft6"], ["ft7"])
            ybT = head_norm_gate(bank(7), bk(7), ft[7], "ft7", n)

            items = []
            for b in range(16):
                for mb in range(2):
                    items.append((cm[b, mb * 128:(mb + 1) * 128, :], U[:, C_CQ:C_CQ + 512], "U", b))
            pso2 = bank(4)[:n, :]
            psl2 = bank(5)[:n, 0:4]
            samp_attn(items, 4, 128, float(128 ** -0.5), pso2, psl2, [bk(4), bk(5)])
            lC = sm[:n, 80:84]
            trk.op("dve", lambda e: e.reciprocal(out=lC, in_=psl2), [bk(5)], ["lC"])
            Gc = ft[9]
            act(Gc[:n, :], U[:, C_CZ:C_CZ + 512], AF.Silu, ["U"], ["ft9"])
            tt("dve", ft[8][:n, :].rearrange("p (h d) -> p h d", h=4), pso2.rearrange("p (h d) -> p h d", h=4),
               lC.unsqueeze(2).to_broadcast([n, 4, 128]), ALU.mult, [bk(4), "lC"], ["ft8"])
            tt("dve", bt[3][:n, :], ft[8][:n, :], Gc[:n, :], ALU.mult, ["ft8", "ft9"], ["bt3"])
            pq = bankb(6).rearrange("p (c n) -> p c n", c=8)
            transposes([(pq[:, c, :n], bt[3][:n, c * 128:(c + 1) * 128], identb[:n, :n]) for c in range(4)], ["bt3"], [bk(6)])
            ycT = bt[6].rearrange("p (c n) -> p c n", c=4)
            cp("act", ycT[:, :, :n], pq[:, 0:4, :n], [bk(6)], ["bt6"])
            epilogue(n, xn, xnk, x_t, xk, ybT, ycT, (gas, "gas"), y_s[:, :], gate_src=U)

    return nc, trk, body


def _consts(core, inputs):
    cf = np.zeros((128, NCF), np.float32)

    def put(name, arr):
        a, b = _CF[name]
        cf[:arr.shape[0], a:b] = arr
    s = np.arange(128)
    U = (s[:, None] <= s[None, :]).astype(np.float32)
    put("identf", np.eye(128, dtype=np.float32))
    put("Um", U - U[:, 63:64])
    put("M2", (s[:, None] > s[None, :]).astype(np.float32))
    put("attm", U)
    ind = np.ones((128, 2), np.float32)
    ind[64:, 0] = 0.0
    put("ind", ind)
    put("pregT", inputs["norm_pre"][0].reshape(8, 128).T)
    put("memgT", inputs["mem_norm"][0].reshape(8, 128).T)
    put("postg", np.broadcast_to(inputs["norm_post"][0][None, :], (128, 1024)))
    put("hg", np.broadcast_to(inputs["hgrn_out_norm"][0][None, :], (128, 512)))
    put("l0", np.broadcast_to(inputs["hgrn_lb_logits"][0][None, :], (128, 512)))
    put("l1", np.broadcast_to(inputs["hgrn_lb_logits"][1][None, :], (128, 512)))
    cm = (np.arange(8) < core).astype(np.float32)
    put("cmask", np.broadcast_to(cm[None, :], (128, 8)))
    put("eye16", np.broadcast_to(np.eye(16, dtype=np.float32).reshape(1, 256), (128, 256)))
    inv = (10000.0 ** (-np.arange(0, 64, 2, dtype=np.float32) / np.float32(64))).astype(np.float32)
    ang = (np.float32(PAST) * inv).astype(np.float32)
    put("ropes", np.broadcast_to(np.concatenate([np.cos(ang), np.sin(ang)])[None, :].astype(np.float32), (128, 64)))
    cb = np.zeros((128, NCB), np.float32)
    cb[:, 0:128] = np.eye(128)
    j = s[:, None]
    i = s[None, :]
    cb[:, 128:256] = np.where(j >= i, 0.0, NEG)
    cb[:, 256:384] = np.where(j <= i, 0.0, NEG)
    cb[:, 384:512] = NEG if core == 0 else cb[:, 128:256]
    cb[:, 512:640] = (s[:, None] > s[None, :])
    cb[:, 640:768] = U - U[:, 63:64]
    cb[:, 768:770] = ind
    cb[:, 772:1028] = np.eye(16, dtype=np.float32).reshape(1, 256)
    rope = np.zeros((len(PA_BLOCKS), 128, 64), np.float32)
    for bi, (g, r, n) in enumerate(PA_BLOCKS):
        d = DILS[g]
        pos = core * TPC + d * (128 * n + np.arange(128)) + r
        angp = pos.astype(np.float32)[:, None] * inv[None, :]
        rope[bi, :, 0:32] = np.cos(angp)
        rope[bi, :, 32:64] = np.sin(angp)
    return cf, cb, rope


_PROG = {}


def kernel(**inputs):
    inp = {k: np.asarray(v) for k, v in inputs.items()}
    if "nc" not in _PROG:
        _PROG["nc"] = build_program()
    nc = _PROG["nc"]
    xp = inp["x_prompt"][0]
    selb = np.zeros((16, 16, 128), np.float32)
    for b in range(16):
        selb[b, b, :] = 1.0
    in_maps = []
    for c in range(NCORE):
        xe = np.zeros((2 * TPC, D), np.float32)
        if c > 0:
            xe[:TPC] = xp[(c - 1) * TPC:c * TPC]
        xe[TPC:] = xp[c * TPC:(c + 1) * TPC]
        cf, cb, rope = _consts(c, inp)
        sl = slice(16 * c, 16 * c + 16)
        xh = np.zeros((NPREV * 128, D), np.float32)
        if c > 0:
            xh[NPREV * 128 - c * TPC:] = xp[:c * TPC]
        in_maps.append({
            "xe": xe, "xh": xh,
            "xs": np.ascontiguousarray(inp["x_sample"][sl, 0, :]),
            "mem": np.ascontiguousarray(inp["mem_prompt"][0]),
            "cw0": np.ascontiguousarray(inp["cache_win128_kv"][0, sl].reshape(16, 128, 1024)),
            "cw1": np.ascontiguousarray(inp["cache_win512_kv"][0, sl].reshape(16, 512, 1024)),
            "cw2": np.ascontiguousarray(inp["cache_win2048_kv"][0, sl].reshape(16, 2048, 1024)),
            "sh": np.ascontiguousarray(inp["state_hgrn"][0, sl]),
            "cm": np.ascontiguousarray(inp["cache_mem_kv"][0, sl].reshape(16, 256, 1024)),
            "w_in": np.ascontiguousarray(inp["w_in"][0]),
            "w_mem": np.ascontiguousarray(inp["w_mem_kv"][0]),
            "w_pa": np.ascontiguousarray(inp["w_branch_a"][0]),
            "w_pb": np.ascontiguousarray(inp["w_branch_b"][0]),
            "w_pc": np.ascontiguousarray(inp["w_branch_c"][0]),
            "w_out": np.ascontiguousarray(inp["w_out"][0]),
            "cf": cf, "cb": cb, "rope": rope, "selb": selb,
        })
    if os.environ.get("K_NOSAMPLE"):
        for m in in_maps:
            for k in ("cw0", "cw1", "cw2", "sh", "cm"):
                m[k] = np.ascontiguousarray(m[k][0:1])
    res = run_bass_kernel_spmd(nc, in_maps, core_ids=list(range(NCORE)))
    R = res.results
    y_prompt = np.concatenate([R[c]["y_p"] for c in range(NCORE)], axis=0).reshape(1, T, D)
    y_sample = np.concatenate([R[c]["y_s"] for c in range(NCORE)], axis=0).reshape(128, 1, D)
    last = R[NCORE - 1]
    nw0 = last["nw0"].reshape(1, 1, 128, 2, 8, 64)
    nw1 = last["nw1"].reshape(1, 1, 512, 2, 8, 64)
    nw2 = last["nw2"].reshape(1, 1, 2048, 2, 8, 64)
    hgp = last["hg_p"].reshape(1, 1, 4, 128, 128)
    mkv = R[0]["mkv"].reshape(1, 1, 256, 2, 4, 128)
    nws = [np.concatenate([R[c][f"nws{g}"] for c in range(NCORE)], axis=0).reshape(1, 128, 1, 2, 8, 64) for g in range(3)]
    hgs = np.concatenate([R[c]["hg_s"] for c in range(NCORE)], axis=0).reshape(1, 128, 4, 128, 128)
    return (y_prompt, y_sample, nw0, nw1, nw2, hgp, mkv, nws[0], nws[1], nws[2], hgs)
```

```python
import numpy as np
from contextlib import ExitStack
import concourse.bass as bass
import concourse.mybir as mybir
from concourse.bass_utils import run_bass_kernel_spmd

F32 = mybir.dt.float32
BF16 = mybir.dt.bfloat16
AF = mybir.ActivationFunctionType
ALU = mybir.AluOpType
AX = mybir.AxisListType

NCORE = 8
T = 16384
TPC = 2048
NT = 16
D = 1024
NIN = 11264
EPS = 1e-6
PAST = 8192
DILS = (1, 4, 16)
NPREV = (NCORE - 1) * NT
NEG = -30000.0
C_AQ, C_AK, C_AV, C_AZ = 0, 1536, 3072, 4608
C_BQ, C_BF, C_BI, C_BZ = 5120, 5632, 6144, 6656
C_CQ, C_CZ = 7168, 7680
C_GA, C_GB, C_GC = 8192, 9216, 10240

_CF = {}
_off = 0
for _n, _w in (("identf", 128), ("Um", 128), ("M2", 128), ("attm", 128), ("ind", 2), ("pregT", 8), ("memgT", 8),
               ("postg", 1024), ("hg", 512), ("l0", 512), ("l1", 512), ("cmask", 8), ("eye16", 256),
               ("ropes", 64)):
    _CF[_n] = (_off, _off + _w)
    _off += _w
NCF = _off
NCB = 1028

def _pa_blocks():
    out = []
    for g, d in enumerate(DILS):
        nb = NT // d
        for r in range(d):
            for n in range(-1, nb):
                out.append((g, r, n))
    return out
PA_BLOCKS = _pa_blocks()
PA_INDEX = {b: i for i, b in enumerate(PA_BLOCKS)}


class Trk:
    ENGS = ("pe", "act", "dve", "pool", "sp")

    def __init__(self):
        self.streams = {e: [] for e in self.ENGS}
        self.cnt = {e: 0 for e in self.ENGS}
        self.known = {e: {} for e in self.ENGS}
        self.lastw = {}
        self.readers = {}
        self.lanegroups = {}
        self.lanecnt = {}
        self.lane_rr = {}

    def add_lanes(self, group, n):
        names = [f"{group}{i}" for i in range(n)]
        self.lanegroups[group] = names
        self.lane_rr[group] = 0
        for x in names:
            self.lanecnt[x] = 0

    def _deps(self, eng, r, w):
        deps = {}

        def add(k, c):
            if deps.get(k, 0) < c:
                deps[k] = c
        for key in r:
            lw = self.lastw.get(key)
            if lw is not None:
                add(lw[0], lw[1])
        for key in w:
            lw = self.lastw.get(key)
            if lw is not None:
                add(lw[0], lw[1])
            for k, c in self.readers.get(key, {}).items():
                add(k, c)
        waits = []
        for k, c in deps.items():
            if k == ("E", "pe") and eng == "pe":
                continue
            if self.known[eng].get(k, 0) >= c:
                continue
            self.known[eng][k] = c
            waits.append((k, c))
        return waits

    def _commit(self, k, c, r, w):
        for key in w:
            self.lastw[key] = (k, c)
            self.readers[key] = {}
        for key in r:
            d = self.readers.setdefault(key, {})
            if d.get(k, 0) < c:
                d[k] = c

    def op(self, eng, fn, r=(), w=()):
        waits = self._deps(eng, r, w)
        self.cnt[eng] += 1
        k = ("E", eng)
        self.streams[eng].append((waits, fn, k))
        self._commit(k, self.cnt[eng], r, w)

    def dma(self, q, group, fn, r=(), w=()):
        lanes = self.lanegroups[group]
        i = self.lane_rr[group]
        self.lane_rr[group] = (i + 1) % len(lanes)
        lane = lanes[i]
        waits = self._deps(q, r, w)
        k = ("L", lane)
        c = self.lanecnt[lane]
        if c > 0 and self.known[q].get(k, 0) < c:
            self.known[q][k] = c
            waits.append((k, c))
        self.lanecnt[lane] = c + 1
        self.streams[q].append((waits, fn, k))
        self._commit(k, c + 1, r, w)

    def barrier(self):
        for e in self.ENGS:
            waits = []
            for o in self.ENGS:
                if o == "sp" or self.cnt[o] == 0:
                    continue
                k = ("E", o)
                if o == e and e == "pe":
                    continue
                if self.known[e].get(k, 0) < self.cnt[o]:
                    self.known[e][k] = self.cnt[o]
                    waits.append((k, self.cnt[o]))
            for lane, c in self.lanecnt.items():
                k = ("L", lane)
                if c > 0 and self.known[e].get(k, 0) < c:
                    self.known[e][k] = c
                    waits.append((k, c))
            if waits:
                self.streams[e].append((waits, None, None))

    def emit(self, nc):
        with ExitStack() as st:
            sems = {}
            for e in self.ENGS:
                if e != "sp":
                    sems[("E", e)] = st.enter_context(nc.semaphore("e_" + e))
            for lane in self.lanecnt:
                sems[("L", lane)] = st.enter_context(nc.semaphore("l_" + lane))
            block = st.enter_context(nc.Block())

            def mult(k):
                return 16 if (k[0] == "L" and not k[1].startswith("cc")) else 1

            def run(eng, h):
                for waits, fn, mine in self.streams[eng]:
                    for k, c in waits:
                        h.wait_ge(sems[k], c * mult(k))
                    if fn is None:
                        continue
                    ins = fn(h)
                    if mine[0] == "L" and mine[1].startswith("cc"):
                        ins.then_inc(sems[mine])
                    else:
                        ins.then_inc(sems[mine], mult(mine))

            block.tensor(lambda h: run("pe", h))
            block.scalar(lambda h: run("act", h))
            block.vector(lambda h: run("dve", h))
            block.gpsimd(lambda h: run("pool", h))
            block.sync(lambda h: run("sp", h))


import os
class _Stop(Exception):
    pass


def build_program(do_sample=True):
    nc, trk, body = _build_program(do_sample)
    try:
        body()
    except _Stop:
        pass
    trk.barrier()
    trk.emit(nc)
    return nc


def _build_program(do_sample=True):
    do_sample = do_sample and not os.environ.get("K_NOSAMPLE")
    no_cc = bool(os.environ.get("K_NOCC"))
    stop = float(os.environ.get("K_STOP", "99"))
    nc = bass.Bass("TRN2", target_bir_lowering=False)
    trk = Trk()
    trk.add_lanes("ld", 6)
    trk.add_lanes("st", 8)
    trk.add_lanes("wl", 4)
    trk.add_lanes("cc", 1)

    def din(name, shape):
        return nc.dram_tensor(name, list(shape), F32, kind="ExternalInput").ap()

    def dout(name, shape):
        return nc.dram_tensor(name, list(shape), F32, kind="ExternalOutput").ap()

    xe = din("xe", (2 * TPC, D))
    SM = 16 if do_sample else 1
    xh = din("xh", (NPREV * 128, D))
    xsamp = din("xs", (16, D))
    mem = din("mem", (256, D))
    cw = [din("cw0", (SM, 128, 1024)), din("cw1", (SM, 512, 1024)), din("cw2", (SM, 2048, 1024))]
    sh = din("sh", (SM, 4, 128, 128))
    cm = din("cm", (SM, 256, 1024))
    w_in = din("w_in", (D, NIN))
    w_mem = din("w_mem", (D, 1024))
    w_pa = din("w_pa", (512, D))
    w_pb = din("w_pb", (512, D))
    w_pc = din("w_pc", (512, D))
    w_out = din("w_out", (D, D))
    cf_d = din("cf", (128, NCF))
    cb_d = din("cb", (128, NCB))
    rope_d = din("rope", (len(PA_BLOCKS), 128, 64))
    selb_d = din("selb", (16, 16, 128))

    y_p = dout("y_p", (TPC, D))
    y_s = dout("y_s", (16, D))
    nw = [dout("nw0", (128, 2, 512)), dout("nw1", (512, 2, 512)), dout("nw2", (2048, 2, 512))]
    hg_p = dout("hg_p", (4, 128, 128))
    mkv = dout("mkv", (256, 1024))
    nws = [dout("nws0", (16, 2, 512)), dout("nws1", (16, 2, 512)), dout("nws2", (16, 2, 512))]
    hg_s = dout("hg_s", (16, 4, 128, 128))

    ga_d = nc.dram_tensor("ga_scr", [TPC, D], F32).ap()

    w_in_r = w_in.rearrange("(c p) n -> p c n", p=128)

    st = ExitStack()

    def chk(v):
        if stop <= v:
            raise _Stop()

    def body():
        def sb(name, shape, dt=F32):
            return st.enter_context(nc.sbuf_tensor(name, list(shape), dt))

        cf = sb("cf_sb", (128, NCF))
        cb = sb("cb_sb", (128, NCB), BF16)
        big = sb("big", (128, 30720))
        xt = [sb("xt0", (128, D)), sb("xt1", (128, D))]
        xsb = sb("xsb", (128, D), BF16)
        xnT = [sb("xnT0", (128, 8, 128), BF16), sb("xnT1", (128, 8, 128), BF16)]
        ft = [sb(f"ft{i}", (128, 512)) for i in range(10)]
        bt = [sb(f"bt{i}", (128, 512), BF16) for i in range(8)]
        sm = sb("sm", (128, 128))
        Sst = sb("Sst", (128, 4, 128))
        Sbf = sb("Sbf", (128, 4, 128), BF16)
        Dacc = sb("Dacc", (128, 4))
        ropet = [sb("ropet0", (128, 64)), sb("ropet1", (128, 64))]
        ptb = [sb("ptb0", (128, 256), BF16), sb("ptb1", (128, 256), BF16)]
        _kb = 28000
        kTs = [big[:, _kb + i * 256:_kb + (i + 1) * 256].bitcast(BF16).rearrange("p (c n) -> p c n", c=4) for i in range(3)]
        _vb = _kb + 768
        vss = [big[:, _vb + i * 260:_vb + (i + 1) * 260].bitcast(BF16).rearrange("p (h d) -> p h d", h=8) for i in range(3)]
        _qb = _vb + 780
        qTs = [big[:, _qb + i * 256:_qb + (i + 1) * 256].bitcast(BF16).rearrange("p (c n) -> p c n", c=4) for i in range(2)]
        mkT = sb("mkT", (128, 4, 256), BF16)
        mvp = sb("mvp", (128, 2, 4, 129), BF16)
        gall = big[:, 8192:8192 + 8 * 516].rearrange("p (c n) -> p c n", c=8)
        psum = [st.enter_context(nc.psum_tensor(f"ps{i}", [128, 1024], F32)) for i in range(4)]

        def C(name):
            a, b = _CF[name]
            return cf[:, a:b]

        identf = C("identf")
        identb = cb[:, 0:128]
        mprev = cb[:, 128:256]
        mcur = cb[:, 256:384]
        mfirst = cb[:, 384:512]
        M2b = cb[:, 512:640]
        Umb = cb[:, 640:768]
        indb = cb[:, 768:770]
        eye16b = cb[:, 772:1028].rearrange("p (b c) -> p b c", b=16)
        lh = [sb(f"lh{i}", (128, 512), BF16) for i in range(4)]

        def bank(i):
            return psum[i // 2][:, (i % 2) * 512:(i % 2) * 512 + 512]

        def bankb(i):
            return bank(i).bitcast(BF16)

        _rr = {}

        def rr(name, items):
            i = _rr.get(name, 0)
            _rr[name] = i + 1
            return items[i % len(items)]

        def rri(name, n):
            i = _rr.get(name, 0)
            _rr[name] = i + 1
            return i % n

        def act(out, in_, func, r, w, **kw):
            trk.op("act", lambda e: e.activation(out=out, in_=in_, func=func, **kw), r, w)

        def tt(eng, out, in0, in1, op, r, w):
            trk.op(eng, lambda e: e.tensor_tensor(out=out, in0=in0, in1=in1, op=op), r, w)

        def ts(eng, out, in0, s1, s2, op0, op1, r, w):
            if s2 is None:
                trk.op(eng, lambda e: e.tensor_scalar(out=out, in0=in0, scalar1=s1, scalar2=None, op0=op0), r, w)
            else:
                trk.op(eng, lambda e: e.tensor_scalar(out=out, in0=in0, scalar1=s1, scalar2=s2, op0=op0, op1=op1), r, w)

        def stt(eng, out, in0, scalar, in1, op0, op1, r, w):
            trk.op(eng, lambda e: e.scalar_tensor_tensor(out=out, in0=in0, scalar=scalar, in1=in1, op0=op0, op1=op1), r, w)

        def cp(eng, out, in_, r, w):
            if eng == "act":
                trk.op("act", lambda e: e.copy(out=out, in_=in_), r, w)
            else:
                trk.op(eng, lambda e: e.tensor_copy(out=out, in_=in_), r, w)

        def ld(out, in_, w, q="sp", r=()):
            trk.dma(q, "ld", lambda e: e.dma_start(out=out, in_=in_), r, w)

        def stdma(out, in_, r, q="sp"):
            trk.dma(q, "st", lambda e: e.dma_start(out=out, in_=in_), r, ())

        def wload(out, in_, w):
            trk.dma("pool", "wl", lambda e: e.dma_start(out=out, in_=in_), (), w)

        _stg = [0]

        def wload_nocast(out, in_, w):
            i = _stg[0] % 2
            _stg[0] += 1
            n = 1
            for d_ in out.shape[1:]:
                n *= d_
            stage = big[:, 12288 + i * 4096:12288 + i * 4096 + n]
            if len(out.shape) == 3:
                stage = stage.rearrange("p (c n) -> p c n", c=out.shape[1])
            ld(stage, in_, [f"stage{i}"])
            cp("pool" if i else "dve", out, stage, [f"stage{i}"], w)

        def mm(groups, r, w):
            def f(e):
                ins = None
                for (o, l, rh, s0, s1) in groups:
                    ins = e.matmul(o, lhsT=l, rhs=rh, start=s0, stop=s1)
                return ins
            trk.op("pe", f, r, w)

        def transposes(items, r, w):
            def f(e):
                ins = None
                for (o, i, idn) in items:
                    ins = e.transpose(out=o, in_=i, identity=idn)
                return ins
            trk.op("pe", f, r, w)

        def rstd_from_ss(ss, rs, n, scale, key):
            act(rs, ss, AF.Ln, [key], [key + "r"], scale=scale, bias=EPS)
            act(rs, rs, AF.Exp, [key + "r"], [key + "r"], scale=-0.5)

        nt_i = [0]

        def norm_T(src, n, gT, keep_x=False):
            i = nt_i[0]
            nt_i[0] += 1
            x_t = xt[i % 2]
            xk = f"xt{i % 2}"
            xn = xnT[i % 2]
            xnk = f"xnT{i % 2}"
            ld(x_t[:n, :], src, [xk])
            ss = sm[:n, 0:1]
            rs = sm[:n, 1:2]
            act(xsb[:n, :], x_t[:n, :], AF.Square, [xk], ["xsb", "nss"], accum_out=ss)
            rstd_from_ss(ss, rs, n, 1.0 / D, "nss")
            act(xsb[:n, :], x_t[:n, :], AF.Copy, [xk, "nssr"], ["xsb"], scale=rs)
            b = rr("tb", [6, 7])
            pb = bankb(b).rearrange("p (c n) -> p c n", c=8)
            transposes([(pb[:, c, :n], xsb[:n, c * 128:(c + 1) * 128], identb[:n, :n]) for c in range(8)],
                       ["xsb"], [f"bank{b}"])
            tt("dve", xn[:, :, :n], pb[:, :, :n], gT.unsqueeze(2).to_broadcast([128, 8, n]), ALU.mult,
               [f"bank{b}"], [xnk])
            return xn, xnk, x_t, xk

        def proj(xn, xnk, n, wblk, wkey, b):
            mm([(bank(b)[:n, :], xn[:, c, :n], wblk[:, c, :], c == 0, c == 7) for c in range(8)],
               [xnk, wkey], [f"bank{b}"])

        def carve_w(idx):
            return big[:, idx * 2048:(idx + 1) * 2048].bitcast(BF16).rearrange("p (c n) -> p c n", c=8)

        def bk(i):
            return f"bank{i}"

        def rope4(ap):
            return ap.rearrange("p (h t f) -> p h t f", h=8, t=2)

        def rope(src, srckeys, out_ap, okey, n, cosb, sinb, tabkeys, A=None, B=None):
            A = ft[0] if A is None else A
            B = ft[1] if B is None else B
            ka, kb_ = "ft%d" % ft.index(A), "ft%d" % ft.index(B)
            tt("dve", rope4(A[:n, :]), rope4(src), cosb, ALU.mult, list(srckeys) + list(tabkeys), [ka])
            tt("dve", rope4(B[:n, :]), rope4(src), sinb, ALU.mult, list(srckeys) + list(tabkeys), [kb_])
            A4 = rope4(A[:n, :]); B4 = rope4(B[:n, :]); O4 = rope4(out_ap)
            tt("pool", O4[:, :, 0, :], A4[:, :, 0, :], B4[:, :, 1, :], ALU.subtract, [ka, kb_], [okey])
            tt("pool", O4[:, :, 1, :], A4[:, :, 1, :], B4[:, :, 0, :], ALU.add, [ka, kb_], [okey])

        ld(cf[:, :], cf_d[:, :], ["cf"])
        wload(cb[:, :], cb_d[:, :], ["cb"])
        trk.barrier()
        l0 = C("l0")
        l1 = C("l1")
        lbr = l0
        omlr = l1
        tt("dve", l0, l0, l1, ALU.subtract, ["cf"], ["cf"])
        act(l0, l0, AF.Sigmoid, ["cf"], ["cf"])
        ts("dve", l1, l0, -1.0, 1.0, ALU.mult, ALU.add, ["cf"], ["cf"])
        trk.op("pool", lambda e: e.memset(Sst[:], 0.0), (), ["S"])
        trk.op("pool", lambda e: e.memset(Dacc[:], 1.0), (), ["Dacc"])
        trk.op("pool", lambda e: e.memset(mvp[:, :, :, 128:129], 1.0), (), ["mvp"])
        trk.barrier()

        wm = [carve_w(0), carve_w(1)]
        w_mem_r = w_mem.rearrange("(c p) n -> p c n", p=128)
        for j in range(2):
            wload(wm[j], w_mem_r[:, :, j * 512:(j + 1) * 512], [f"wm{j}"])
        for mt in range(2):
            xn, xnk, _, _ = norm_T(mem[mt * 128:(mt + 1) * 128, :], 128, C("memgT"))
            for j in range(2):
                b = rri("pj", 4)
                proj(xn, xnk, 128, wm[j], f"wm{j}", b)
                fi = rri("ftA", 2)
                f = ft[fi]; fk = f"ft{fi}"
                cp("act", f[:, :], bank(b), [bk(b)], [fk])
                stdma(mkv[mt * 128:(mt + 1) * 128, j * 512:(j + 1) * 512], f[:, :], [fk])
                if j == 0:
                    cp("dve", bt[0][:, :], f[:, :], [fk], ["bt0"])
                    tb = 6 + rri("tb", 2)
                    pbk = bankb(tb).rearrange("p (c n) -> p c n", c=8)
                    transposes([(pbk[:, h, :], bt[0][:, h * 128:(h + 1) * 128], identb) for h in range(4)],
                               ["bt0"], [bk(tb)])
                    cp("dve", mkT[:, :, mt * 128:(mt + 1) * 128], pbk[:, 0:4, :], [bk(tb)], ["mkT"])
                else:
                    cp("dve", mvp[:, mt, :, 0:128], f[:, :].rearrange("p (h d) -> p h d", h=4), [fk], ["mvp"])
        trk.barrier()

        chk(0)

        def hgrn_tile(xn, xnk, full, wkeys, par=0, do_proj=True):
            wq, wf, wi, wz = wkeys
            Um = C("Um"); M2 = C("M2"); attm = C("attm"); ind = C("ind")
            fo = 4 * par; bo = 2 * par
            b_f, b_i = (0, 1) if par == 0 else (4, 5)
            if do_proj:
                proj(xn, xnk, 128, wf[0], wf[1], b_f)
                proj(xn, xnk, 128, wi[0], wi[1], b_i)
            sig = ft[fo + 0]; logf = ft[fo + 1]; kin = ft[fo + 2]
            k0, k1, k2, k3 = (f"ft{fo + i}" for i in range(4))
            kb0, kb1 = f"bt{bo}", f"bt{bo + 1}"
            eblk = f"ebl{par}"
            act(sig[:, :], bank(b_f), AF.Exp, [bk(b_f)], [k0], scale=-1.0)
            ts("dve", sig[:, :], sig[:, :], 1.0, None, ALU.add, None, [k0], [k0])
            trk.op("dve", lambda e, sig=sig: e.reciprocal(out=sig[:, :], in_=sig[:, :]), [k0], [k0])
            tt("dve", sig[:, :], sig[:, :], omlr, ALU.mult, [k0], [k0])
            tt("pool", sig[:, :], sig[:, :], lbr, ALU.add, [k0], [k0])
            act(logf[:, :], sig[:, :], AF.Ln, [k0], [k1])
            ts("pool", kin[:, :], sig[:, :], -1.0, 1.0, ALU.mult, ALU.add, [k0], [k2])
            vb = bt[bo]
            cp("act", vb[:, :], bank(b_i), [bk(b_i)], [kb0])
            lhi = lh[2 * par]; llo = lh[2 * par + 1]
            khi, klo = f"lh{2 * par}", f"lh{2 * par + 1}"
            cp("dve", lhi[:, :], logf[:, :], [k1], [khi])
            tt("pool", llo[:, :], logf[:, :], lhi[:, :], ALU.subtract, [k1, khi], [klo])
            mm([(bank(2), M2b, lhi[:, :], True, False), (bank(2), M2b, llo[:, :], False, True)], [khi, klo], [bk(2)])
            ek = ft[fo + 3]
            act(ek[:, :], bank(2), AF.Exp, [bk(2)], [k3])
            khat = bt[bo + 1]
            tt("dve", khat[:, :], kin[:, :], ek[:, :], ALU.mult, [k2, k3], [kb1])
            pbl = bank(3)[:, 0:8].rearrange("p (h t) -> p h t", t=2)
            grp_ = []
            for h in range(4):
                grp_.append((pbl[:, h, :], lhi[:, h * 128:(h + 1) * 128], indb, True, False))
                grp_.append((pbl[:, h, :], llo[:, h * 128:(h + 1) * 128], indb, False, True))
            mm(grp_, [khi, klo], [bk(3)])
            ebl = (sm[:, 8:16] if par == 0 else sm[:, 100:108]).rearrange("p (h t) -> p h t", t=2)
            act(ebl, pbl, AF.Exp, [bk(3)], [eblk])
            if full:
                chk(3.52)
                proj(xn, xnk, 128, wq[0], wq[1], 4)
                proj(xn, xnk, 128, wz[0], wz[1], 5)
                mm([(bank(2), Umb, lhi[:, :], True, False), (bank(2), Umb, llo[:, :], False, True)], [khi, klo], [bk(2)])
                eq = ft[3]; ekm = ft[4]
                act(eq[:, :], bank(2), AF.Exp, [bk(2)], ["ft3"])
                act(ekm[:, :], bank(2), AF.Exp, [bk(2)], ["ft4"], scale=-1.0)
                chk(3.55)
                qt = bt[2]; kt = bt[3]
                tt("dve", qt[:, :], bank(4), eq[:, :], ALU.mult, [bk(4), "ft3"], ["bt2"])
                tt("pool", kt[:, :], kin[:, :], ekm[:, :], ALU.mult, ["ft2", "ft4"], ["bt3"])
                G = ft[5]
                act(G[:, :], bank(5), AF.Silu, [bk(5)], ["ft5"])
                tt("pool", G[:, :], G[:, :], C("hg"), ALU.mult, ["ft5"], ["ft5"])
                chk(3.57)
                pq = bankb(6).rearrange("p (c n) -> p c n", c=8)
                pk2 = bankb(7).rearrange("p (c n) -> p c n", c=8)
                transposes([(pq[:, h, :], qt[:, h * 128:(h + 1) * 128], identb) for h in range(4)], ["bt2"], [bk(6)])
                transposes([(pk2[:, h, :], kt[:, h * 128:(h + 1) * 128], identb) for h in range(4)], ["bt3"], [bk(7)])
                qkT = bt[4].rearrange("p (c n) -> p c n", c=4)
                kkT = bt[5].rearrange("p (c n) -> p c n", c=4)
                cp("act", qkT, pq[:, 0:4, :], [bk(6)], ["bt4"])
                cp("dve", kkT, pk2[:, 0:4, :], [bk(7)], ["bt5"])
                chk(3.6)
                tt("dve", Sbf[:], Sst[:], ebl[:, :, 0:1].to_broadcast([128, 4, 128]), ALU.mult, ["S", eblk], ["Sbf"])
                pa = bank(2)
                mm([(pa[:, h * 128:(h + 1) * 128], kkT[:, h, :], qkT[:, h, :], True, True) for h in range(4)],
                   ["bt4", "bt5"], [bk(2)])
                attb = bt[6]
                tt("dve", attb[:, :].rearrange("p (h t) -> p h t", h=4), pa.rearrange("p (h t) -> p h t", h=4),
                   attm.unsqueeze(1).to_broadcast([128, 4, 128]), ALU.mult, [bk(2)], ["bt6"])
                chk(3.7)
                po = bank(4)
                grp = []
                for h in range(4):
                    grp.append((po[:, h * 128:(h + 1) * 128], qkT[:, h, :], Sbf[:, h, :], True, False))
                    grp.append((po[:, h * 128:(h + 1) * 128], attb[:, h * 128:(h + 1) * 128], vb[:, h * 128:(h + 1) * 128], False, True))
                mm(grp, ["bt4", "Sbf", "bt6", "bt0"], [bk(4)])
                chk(3.8)
            pP = bank(b_f)
            mm([(pP[:, h * 128:(h + 1) * 128], khat[:, h * 128:(h + 1) * 128], vb[:, h * 128:(h + 1) * 128], True, True)
                for h in range(4)], [kb1, kb0], [bk(b_f)])
            for h in range(4):
                stt("dve", Sst[:, h, :], Sst[:, h, :], ebl[:, h, 1:2], pP[:, h * 128:(h + 1) * 128], ALU.mult, ALU.add,
                    ["S", eblk, bk(b_f)], ["S"])
            if not full:
                return None
            return head_norm_gate(bank(4), bk(4), G, "ft5", 128)

        def head_norm_gate(po, pok, G, gk, n):
            ss4 = sm[:n, 16:20]
            rs4 = sm[:n, 20:24]
            for h in range(4):
                act(ft[6][:n, h * 128:(h + 1) * 128], po[:n, h * 128:(h + 1) * 128], AF.Square, [pok], ["ft6", "ss4"],
                    accum_out=ss4[:, h:h + 1])
            rstd_from_ss(ss4, rs4, n, 1.0 / 128, "ss4")
            yb = bt[2]
            for h in range(4):
                stt("dve", yb[:n, h * 128:(h + 1) * 128], po[:n, h * 128:(h + 1) * 128], rs4[:, h:h + 1],
                    G[:n, h * 128:(h + 1) * 128], ALU.mult, ALU.mult, [pok, "ss4r", gk], ["bt2"])
            pq = bankb(6).rearrange("p (c n) -> p c n", c=8)
            transposes([(pq[:, h, :n], yb[:n, h * 128:(h + 1) * 128], identb[:n, :n]) for h in range(4)], ["bt2"], [bk(6)])
            ybT = bt[7].rearrange("p (c n) -> p c n", c=4)
            cp("act", ybT[:, :, :n], pq[:, 0:4, :n], [bk(6)], ["bt7"])
            return ybT

        wf_b = carve_w(0); wi_b = carve_w(1)
        wload(wf_b, w_in_r[:, :, C_BF:C_BF + 512], ["w0"])
        wload(wi_b, w_in_r[:, :, C_BI:C_BI + 512], ["w1"])
        def p1_stage(par, stage):
            fo = 4 * par; bo = 2 * par
            b_f, b_i = (0, 1) if par == 0 else (4, 5)
            sig = ft[fo + 0]; logf = ft[fo + 1]; kin = ft[fo + 2]; ek = ft[fo + 3]
            k0, k1, k2, k3 = (f"ft{fo + i}" for i in range(4))
            vb = bt[bo]; khat = bt[bo + 1]
            kb0, kb1 = f"bt{bo}", f"bt{bo + 1}"
            lhi = lh[2 * par]; llo = lh[2 * par + 1]
            khi, klo = f"lh{2 * par}", f"lh{2 * par + 1}"
            eblk = f"ebl{par}"
            pbl = bank(3)[:, 0:8].rearrange("p (h t) -> p h t", t=2)
            ebl = (sm[:, 8:16] if par == 0 else sm[:, 100:108]).rearrange("p (h t) -> p h t", t=2)
            if stage == "g1":
                act(sig[:, :], bank(b_f), AF.Exp, [bk(b_f)], [k0], scale=-1.0)
                cp("act", vb[:, :], bank(b_i), [bk(b_i)], [kb0])
                ts("dve", sig[:, :], sig[:, :], 1.0, None, ALU.add, None, [k0], [k0])
                trk.op("dve", lambda e, sig=sig: e.reciprocal(out=sig[:, :], in_=sig[:, :]), [k0], [k0])
                tt("dve", sig[:, :], sig[:, :], omlr, ALU.mult, [k0], [k0])
                tt("pool", sig[:, :], sig[:, :], lbr, ALU.add, [k0], [k0])
            elif stage == "g2":
                act(logf[:, :], sig[:, :], AF.Ln, [k0], [k1])
                ts("pool", kin[:, :], sig[:, :], -1.0, 1.0, ALU.mult, ALU.add, [k0], [k2])
                cp("dve", lhi[:, :], logf[:, :], [k1], [khi])
                tt("pool", llo[:, :], logf[:, :], lhi[:, :], ALU.subtract, [k1, khi], [klo])
            elif stage == "mid":
                mm([(bank(2), M2b, lhi[:, :], True, False), (bank(2), M2b, llo[:, :], False, True)], [khi, klo], [bk(2)])
                grp_ = []
                for h in range(4):
                    grp_.append((pbl[:, h, :], lhi[:, h * 128:(h + 1) * 128], indb, True, False))
                    grp_.append((pbl[:, h, :], llo[:, h * 128:(h + 1) * 128], indb, False, True))
                mm(grp_, [khi, klo], [bk(3)])
            else:
                act(ek[:, :], bank(2), AF.Exp, [bk(2)], [k3])
                tt("dve", khat[:, :], kin[:, :], ek[:, :], ALU.mult, [k2, k3], [kb1])
                act(ebl, pbl, AF.Exp, [bk(3)], [eblk])
                pP = bank(2)
                mm([(pP[:, h * 128:(h + 1) * 128], khat[:, h * 128:(h + 1) * 128], vb[:, h * 128:(h + 1) * 128], True, True)
                    for h in range(4)], [kb1, kb0], [bk(2)])
                for h in range(4):
                    stt("dve", Sst[:, h, :], Sst[:, h, :], ebl[:, h, 1:2], pP[:, h * 128:(h + 1) * 128], ALU.mult, ALU.add,
                        ["S", eblk, bk(2)], ["S"])

        def p1_proj(xq, par):
            b_f, b_i = (0, 1) if par == 0 else (4, 5)
            proj(xq[0], xq[1], 128, wf_b, "w0", b_f)
            proj(xq[0], xq[1], 128, wi_b, "w1", b_i)

        xq = [norm_T(xh[0:128, :], 128, C("pregT")), norm_T(xh[128:256, :], 128, C("pregT"))]
        p1_proj(xq[0], 0)
        for t in range(NPREV):
            par = t % 2
            if t + 1 < NPREV:
                p1_proj(xq[(t + 1) % 2], (t + 1) % 2)
            p1_stage(par, "g1")
            if t + 2 < NPREV:
                xq[par] = norm_T(xh[(t + 2) * 128:(t + 3) * 128, :], 128, C("pregT"))
            p1_stage(par, "g2")
            p1_stage(par, "mid")
            p1_stage(par, "back")
        trk.barrier()
        chk(1)

        acc = big[:, 0:16384].rearrange("p (h t) -> p h t", h=8)
        trk.op("pool", lambda e: e.memset(big[0:65, 0:16384], 0.0), (), ["acc"])
        for i in range(3):
            trk.op("pool", lambda e, i=i: e.memset(vss[i][:, :, 64:65], 1.0), (), [f"vss{i}"])
        WB = 8
        for g, d in enumerate(DILS):
            wq = carve_w(WB + 0); wk = carve_w(WB + 1); wv = carve_w(WB + 2)
            wload(wq, w_in_r[:, :, C_AQ + g * 512:C_AQ + (g + 1) * 512], ["wA0"])
            wload(wk, w_in_r[:, :, C_AK + g * 512:C_AK + (g + 1) * 512], ["wA1"])
            wload(wv, w_in_r[:, :, C_AV + g * 512:C_AV + (g + 1) * 512], ["wA2"])
            nb = NT // d
            wb_tokens = 128 * d
            kv_i = 0
            for r in range(d):
                prev = None
                for n in range(-1, nb):
                    own = n >= 0
                    start = TPC + d * 128 * n + r
                    src = xe[start:start + 127 * d + 1:d, :] if d > 1 else xe[start:start + 128, :]
                    xn, xnk, _, _ = norm_T(src, 128, C("pregT"))
                    ri = rri("ropet", 2); rt = ropet[ri]; rk = f"ropet{ri}"
                    ld(rt[:, :], rope_d[PA_INDEX[(g, r, n)], :, :], [rk])
                    cosb = rt[:, 0:32].unsqueeze(1).unsqueeze(1).to_broadcast([128, 8, 2, 32])
                    sinb = rt[:, 32:64].unsqueeze(1).unsqueeze(1).to_broadcast([128, 8, 2, 32])
                    ks = kv_i % 3
                    kv_i += 1
                    proj(xn, xnk, 128, wk, "wA1", 1)
                    rope(bank(1), [bk(1)], ft[2][:, :], "ft2", 128, cosb, sinb, [rk])
                    need_out = own and (d * (128 * n + 127) + r >= TPC - wb_tokens)
                    if need_out:
                        t0 = d * 128 * n + r - (TPC - wb_tokens)
                        dst = nw[g][t0:t0 + 127 * d + 1:d, 0, :] if d > 1 else nw[g][t0:t0 + 128, 0, :]
                        stdma(dst, ft[2][:, :], ["ft2"])
                    cp("pool", bt[0][:, :], ft[2][:, :], ["ft2"], ["bt0"])
                    pk = bankb(6).rearrange("p (c n) -> p c n", c=8)
                    transposes([(pk[:, c, :], bt[0][:, c * 128:(c + 1) * 128], identb) for c in range(4)], ["bt0"], [bk(6)])
                    cp("act", kTs[ks], pk[:, 0:4, :], [bk(6)], [f"kTs{ks}"])
                    proj(xn, xnk, 128, wv, "wA2", 2)
                    if need_out:
                        cp("act", ft[3][:, :], bank(2), [bk(2)], ["ft3"])
                        dst = nw[g][t0:t0 + 127 * d + 1:d, 1, :] if d > 1 else nw[g][t0:t0 + 128, 1, :]
                        stdma(dst, ft[3][:, :], ["ft3"])
                    cp("act", vss[ks][:, :, 0:64], bank(2).rearrange("p (h d) -> p h d", h=8), [bk(2)], [f"vss{ks}"])
                    if own:
                        proj(xn, xnk, 128, wq, "wA0", 0)
                        rope(bank(0), [bk(0)], ft[6][:, :], "ft6", 128, cosb, sinb, [rk], A=ft[4], B=ft[5])
                        cp("pool", bt[1][:, :], ft[6][:, :], ["ft6"], ["bt1"])
                        pq = bankb(7).rearrange("p (c n) -> p c n", c=8)
                        transposes([(pq[:, c, :], bt[1][:, c * 128:(c + 1) * 128], identb) for c in range(4)], ["bt1"], [bk(7)])
                        qi = rri("qTs", 2)
                        cp("dve", qTs[qi], pq[:, 0:4, :], [bk(7)], [f"qTs{qi}"])
                        mp = mfirst if n == 0 else mprev
                        kp, kc = prev, ks
                        tok0 = d * 128 * n + r
                        for h in range(8):
                            pr, off = h // 2, (h % 2) * 64
                            sbk = 3 + rri("sbk", 2)
                            ps = bank(sbk)
                            mm([(ps[:, 0:128], kTs[kp][off:off + 64, pr, :], qTs[qi][off:off + 64, pr, :], True, False),
                                (ps[:, 0:128], identb, mp, False, True),
                                (ps[:, 128:256], kTs[kc][off:off + 64, pr, :], qTs[qi][off:off + 64, pr, :], True, False),
                                (ps[:, 128:256], identb, mcur, False, True)],
                               [f"kTs{kp}", f"kTs{kc}", f"qTs{qi}"], [bk(sbk)])
                            pi = rri("ptb", 2)
                            act(ptb[pi][:, :], ps[:, 0:256], AF.Exp, [bk(sbk)], [f"ptb{pi}"], scale=0.125)
                            pob_ = (5, 0)[h % 2]
                            po = bank(pob_)[0:65, 0:128]
                            pok = bk(pob_)
                            mm([(po, vss[kp][:, h, :], ptb[pi][:, 0:128], True, False),
                                (po, vss[kc][:, h, :], ptb[pi][:, 128:256], False, True)],
                               [f"vss{kp}", f"vss{kc}", f"ptb{pi}"], [pok])
                            a_sl = acc[0:65, h, tok0:tok0 + 127 * d + 1:d] if d > 1 else acc[0:65, h, tok0:tok0 + 128]
                            tt("dve", a_sl, po, a_sl, ALU.add, [pok, "acc"], ["acc"])
                    prev = ks
        trk.barrier()

        chk(2)

        waz = carve_w(WB + 0); wga0 = carve_w(WB + 1); wga1 = carve_w(WB + 2)
        wload(waz, w_in_r[:, :, C_AZ:C_AZ + 512], ["wA0"])
        wload(wga0, w_in_r[:, :, C_GA:C_GA + 512], ["wA1"])
        wload(wga1, w_in_r[:, :, C_GA + 512:C_GA + 1024], ["wA2"])
        wpa = big[:, 22528:26624].bitcast(BF16).rearrange("p (h n) -> p h n", h=8)
        wload(wpa[0:64, :, :], w_pa.rearrange("(h p) n -> p h n", p=64), ["wpa"])
        ones64 = big[64:65, 26624:26688]
        trk.op("pool", lambda e: e.memset(ones64, 1.0), (), ["ones64"])
        rl = big[64:65, 26688:26688 + 1024]
        yaT_tiles = [bt[0], bt[1]]
        for t in range(NT):
            xn, xnk, _, _ = norm_T(xe[TPC + t * 128:TPC + (t + 1) * 128, :], 128, C("pregT"))
            trk.op("dve", lambda e, t=t: e.reciprocal(out=rl.rearrange("p (h n) -> p h n", h=8),
                                                      in_=acc[64:65, :, t * 128:(t + 1) * 128]), ["acc"], ["rl"])
            for h in range(8):
                zi = rri("azb", 2)
                pz = bank(zi)
                mm([(pz[0:64, 0:128], waz[:, c, h * 64:(h + 1) * 64], xn[:, c, :], c == 0, c == 7) for c in range(8)],
                   [xnk, "wA0"], [bk(zi)])
                gi = rri("gz", 2)
                gz = ft[gi]; gzk = f"ft{gi}"
                act(gz[0:64, 0:128], pz[0:64, 0:128], AF.Silu, [bk(zi)], [gzk])
                bi_ = 2 + (h % 2)
                pbc = bank(bi_)
                mm([(pbc[0:64, 0:128], ones64, rl[:, h * 128:(h + 1) * 128], True, True)], ["ones64", "rl"], [bk(bi_)])
                tt("pool", gz[0:64, 0:128], gz[0:64, 0:128], acc[0:64, h, t * 128:(t + 1) * 128], ALU.mult, [gzk, "acc"], [gzk])
                ydst = yaT_tiles[h // 4][0:64, (h % 4) * 128:(h % 4) * 128 + 128]
                tt("dve", ydst, gz[0:64, 0:128], pbc[0:64, 0:128], ALU.mult, [gzk, bk(bi_)], [f"bt{h // 4}"])
            grp = []
            for half in range(2):
                for h in range(8):
                    grp.append((bank(4 + half), yaT_tiles[h // 4][0:64, (h % 4) * 128:(h % 4) * 128 + 128],
                                wpa[0:64, h, half * 512:(half + 1) * 512], h == 0, h == 7))
            mm(grp, ["bt0", "bt1", "wpa"], [bk(4), bk(5)])
            for half, (wg, wgk) in enumerate(((wga0, "wA1"), (wga1, "wA2"))):
                proj(xn, xnk, 128, wg, wgk, 2 + half)
                sg = ft[2 + half]
                act(sg[:, :], bank(2 + half), AF.Sigmoid, [bk(2 + half)], [f"ft{2 + half}"])
                tt("dve", sg[:, :], bank(4 + half), sg[:, :], ALU.mult, [bk(4 + half), f"ft{2 + half}"], [f"ft{2 + half}"])
                stdma(ga_d[t * 128:(t + 1) * 128, half * 512:(half + 1) * 512], sg[:, :], [f"ft{2 + half}"])
        trk.barrier()

        chk(3)

        names = [("bq", C_BQ), ("bf", C_BF), ("bi", C_BI), ("bz", C_BZ), ("cq", C_CQ), ("cz", C_CZ),
                 ("gb0", C_GB), ("gb1", C_GB + 512), ("gc0", C_GC), ("gc1", C_GC + 512)]
        W3 = {}
        for i, (nm, col) in enumerate(names):
            W3[nm] = (carve_w(i), "w3" + nm)
            wload(W3[nm][0], w_in_r[:, :, col:col + 512], ["w3" + nm])
        gat = big[:, 20480:21504]
        wpb = big[:, 22528:24576].bitcast(BF16).rearrange("p (c n) -> p c n", c=4)
        wpc = big[:, 24576:26624].bitcast(BF16).rearrange("p (c n) -> p c n", c=4)
        wo = big[:, 26624:30720].bitcast(BF16).rearrange("p (c n) -> p c n", c=8)
        wload(wpb, w_pb.rearrange("(c p) n -> p c n", p=128), ["wpb"])
        wload(wpc, w_pc.rearrange("(c p) n -> p c n", p=128), ["wpc"])
        wload(wo, w_out.rearrange("(c p) n -> p c n", p=128), ["wo"])

        chk(3.5)

        def epilogue(n, xn, xnk, x_t, xk, ybT, ycT, ga_src, y_dst, gate_src=None):
            for (yT, w_, wk_, b0, key) in ((ybT, wpb, "wpb", 0, "bt7"), (ycT, wpc, "wpc", 2, "bt6")):
                grp = []
                for half in range(2):
                    for c in range(4):
                        grp.append((bank(b0 + half)[:n, :], yT[:, c, :n], w_[:, c, half * 512:(half + 1) * 512], c == 0, c == 3))
                mm(grp, [key, wk_], [bk(b0), bk(b0 + 1)])
            mg = [ft[8], ft[9]]
            gsrc, gkey = ga_src
            for half in range(2):
                sgb = ft[0]; sgc = ft[1]
                if gate_src is None:
                    proj(xn, xnk, n, W3[f"gb{half}"][0], W3[f"gb{half}"][1], 4)
                    act(sgb[:n, :], bank(4)[:n, :], AF.Sigmoid, [bk(4)], ["ft0"])
                    proj(xn, xnk, n, W3[f"gc{half}"][0], W3[f"gc{half}"][1], 5)
                    act(sgc[:n, :], bank(5)[:n, :], AF.Sigmoid, [bk(5)], ["ft1"])
                else:
                    act(sgb[:n, :], gate_src[:, C_GB + half * 512:C_GB + (half + 1) * 512], AF.Sigmoid, ["U"], ["ft0"])
                    act(sgc[:n, :], gate_src[:, C_GC + half * 512:C_GC + (half + 1) * 512], AF.Sigmoid, ["U"], ["ft1"])
                tt("dve", sgb[:n, :], bank(0 + half)[:n, :], sgb[:n, :], ALU.mult, [bk(half), "ft0"], ["ft0"])
                tt("dve", sgc[:n, :], bank(2 + half)[:n, :], sgc[:n, :], ALU.mult, [bk(2 + half), "ft1"], ["ft1"])
                tt("pool", sgb[:n, :], sgb[:n, :], sgc[:n, :], ALU.add, ["ft0", "ft1"], ["ft0"])
                tt("pool", bt[half][:n, :], sgb[:n, :], gsrc[:n, half * 512:(half + 1) * 512], ALU.add, ["ft0", gkey], [f"bt{half}"])
            pm = bankb(6).rearrange("p (c n) -> p c n", c=8)
            pm2 = bankb(7).rearrange("p (c n) -> p c n", c=8)
            transposes([(pm[:, c, :n], bt[0][:n, c * 128:(c + 1) * 128], identb[:n, :n]) for c in range(4)], ["bt0"], [bk(6)])
            transposes([(pm2[:, c, :n], bt[1][:n, c * 128:(c + 1) * 128], identb[:n, :n]) for c in range(4)], ["bt1"], [bk(7)])
            mT = bt[2].rearrange("p (c n) -> p c n", c=4)
            mT2 = bt[3].rearrange("p (c n) -> p c n", c=4)
            cp("act", mT[:, :, :n], pm[:, 0:4, :n], [bk(6)], ["bt2"])
            cp("dve", mT2[:, :, :n], pm2[:, 0:4, :n], [bk(7)], ["bt3"])
            grp = []
            for half in range(2):
                for c in range(8):
                    src_ = (mT if c < 4 else mT2)[:, c % 4, :n]
                    grp.append((bank(half)[:n, :], src_, wo[:, c, half * 512:(half + 1) * 512], c == 0, c == 7))
            mm(grp, ["bt2", "bt3", "wo"], [bk(0), bk(1)])
            ssz = sm[:n, 28:30]
            for half in range(2):
                act(ft[2][:n, :], bank(half)[:n, :], AF.Square, [bk(half)], ["ft2", "ssz"], accum_out=ssz[:, half:half + 1])
            tt("dve", ssz[:, 0:1], ssz[:, 0:1], ssz[:, 1:2], ALU.add, ["ssz"], ["ssz"])
            rsz = sm[:n, 30:31]
            rstd_from_ss(ssz[:, 0:1], rsz, n, 1.0 / D, "ssz")
            postg = C("postg")
            for half in range(2):
                stt("dve", mg[half][:n, :], bank(half)[:n, :], rsz, postg[:n, half * 512:(half + 1) * 512], ALU.mult, ALU.mult,
                    [bk(half), "sszr"], [f"ft{8 + half}"])
                tt("pool", mg[half][:n, :], mg[half][:n, :], x_t[:n, half * 512:(half + 1) * 512], ALU.add, [f"ft{8 + half}", xk], [f"ft{8 + half}"])
                stdma(y_dst[:, half * 512:(half + 1) * 512], mg[half][:n, :], [f"ft{8 + half}"])

        for t in range(NT):
            xn, xnk, x_t, xk = norm_T(xe[TPC + t * 128:TPC + (t + 1) * 128, :], 128, C("pregT"))
            ybT = hgrn_tile(xn, xnk, True, (W3["bq"], W3["bf"], W3["bi"], W3["bz"]))
            chk(4)
            proj(xn, xnk, 128, W3["cq"][0], W3["cq"][1], 0)
            proj(xn, xnk, 128, W3["cz"][0], W3["cz"][1], 1)
            cqb = bt[0]
            cp("act", cqb[:, :], bank(0), [bk(0)], ["bt0"])
            Gc = ft[0]
            act(Gc[:, :], bank(1), AF.Silu, [bk(1)], ["ft0"])
            pq = bankb(7).rearrange("p (c n) -> p c n", c=8)
            transposes([(pq[:, h, :], cqb[:, h * 128:(h + 1) * 128], identb) for h in range(4)], ["bt0"], [bk(7)])
            cqT = bt[1].rearrange("p (c n) -> p c n", c=4)
            cp("dve", cqT, pq[:, 0:4, :], [bk(7)], ["bt1"])
            yc = bt[3]
            for h in range(4):
                sbk = 2 + rri("sbk3", 2)
                ps = bank(sbk)
                mm([(ps[:, mb * 128:(mb + 1) * 128], mkT[:, h, mb * 128:(mb + 1) * 128], cqT[:, h, :], True, True) for mb in range(2)],
                   ["mkT", "bt1"], [bk(sbk)])
                pi = rri("ptb", 2)
                act(ptb[pi][:, :], ps[:, 0:256], AF.Exp, [bk(sbk)], [f"ptb{pi}"], scale=float(128 ** -0.5))
                po = bank(5)[:, 0:129] if h % 2 == 0 else bank(5)[:, 256:385]
                mm([(po, ptb[pi][:, mb * 128:(mb + 1) * 128], mvp[:, mb, h, :], mb == 0, mb == 1) for mb in range(2)],
                   [f"ptb{pi}", "mvp"], [bk(5)])
                rlc = sm[:, 32 + h:33 + h]
                trk.op("dve", lambda e, rlc=rlc, po=po: e.reciprocal(out=rlc, in_=po[:, 128:129]), [bk(5)], [f"rlc{h}"])
                stt("dve", yc[:, h * 128:(h + 1) * 128], po[:, 0:128], rlc, Gc[:, h * 128:(h + 1) * 128], ALU.mult, ALU.mult,
                    [bk(5), f"rlc{h}", "ft0"], ["bt3"])
            pq = bankb(7).rearrange("p (c n) -> p c n", c=8)
            transposes([(pq[:, h, :], yc[:, h * 128:(h + 1) * 128], identb) for h in range(4)], ["bt3"], [bk(7)])
            ycT = bt[6].rearrange("p (c n) -> p c n", c=4)
            cp("act", ycT, pq[:, 0:4, :], [bk(7)], ["bt6"])
            chk(5)
            ld(gat, ga_d[t * 128:(t + 1) * 128, :], ["gat"])
            epilogue(128, xn, xnk, x_t, xk, ybT, ycT, (gat, "gat"), y_p[t * 128:(t + 1) * 128, :])
            chk(6)
        stdma(hg_p.rearrange("h k v -> k h v"), Sst[:], ["S"])
        trk.barrier()

        if do_sample:
            n = 16
            xn, xnk, x_t, xk = norm_T(xsamp[:, :], n, C("pregT"))
            U = big[0:16, 6144:6144 + NIN]
            for blk in range(NIN // 512):
                si = blk % 3
                wslot = carve_w(si)
                wkey = f"wS{si}"
                wload(wslot, w_in_r[:, :, blk * 512:(blk + 1) * 512], [wkey])
                b = rri("pjs", 4)
                proj(xn, xnk, n, wslot, wkey, b)
                cp("act" if blk % 2 else "dve", U[:, blk * 512:(blk + 1) * 512], bank(b)[:n, :], [bk(b)], ["U"])
            trk.barrier()
            Ssm = big[:, 0:4096].rearrange("p (b h v) -> p b h v", b=8, h=4)
            kvs = [big[:, 4096:5120], big[:, 5120:6144]]
            selb = big[0:16, 17408:19456].rearrange("p (b m) -> p b m", b=16)
            wpas = big[:, 19456:21504].bitcast(BF16).rearrange("p (c n) -> p c n", c=4)
            gas = big[0:16, 21504:22528]
            wload(wpas, w_pa.rearrange("(c p) n -> p c n", p=128), ["wpas"])
            ld(selb, selb_d[:, :, :], ["selb"])
            eye16v = C("eye16").rearrange("p (b c) -> p b c", b=16)
            rs_ = C("ropes")
            cosb = rs_[:n, 0:32].unsqueeze(1).unsqueeze(1).to_broadcast([n, 8, 2, 32])
            sinb = rs_[:n, 32:64].unsqueeze(1).unsqueeze(1).to_broadcast([n, 8, 2, 32])
            for g in range(3):
                rope(U[:, C_AQ + g * 512:C_AQ + (g + 1) * 512], ["U"], ft[2 + g][:n, :], f"ft{2 + g}", n, cosb, sinb, [])
                rope(U[:, C_AK + g * 512:C_AK + (g + 1) * 512], ["U"], ft[5 + g][:n, :], f"ft{5 + g}", n, cosb, sinb, [])
                stdma(nws[g][:, 0, :], ft[5 + g][:n, :], [f"ft{5 + g}"])
                stdma(nws[g][:, 1, :], U[:, C_AV + g * 512:C_AV + (g + 1) * 512], ["U"])

            def samp_attn(items, nh, dh, scale, pso, psl, pkeys):
                tot = len(items)

                def qbc(j):
                    _, q_j, qkey_j, b_j = items[j]
                    qb_ = rri("qbc", 4)
                    mm([(bank(qb_), selb[:, b_j, :], q_j, True, True)], [qkey_j, "selb"], [bk(qb_)])
                    return qb_
                qb_next = qbc(0)
                for it, (src, q_ap, qkey, b) in enumerate(items):
                    si = rri("kvs", 2)
                    kv = kvs[si]; kvk = f"kvs{si}"
                    ld(kv, src, [kvk])
                    qb = qb_next
                    if it + 1 < tot:
                        qb_next = qbc(it + 1)
                    pi_ = 8 + rri("prod", 2)
                    prod = ft[pi_]; pk_ = f"ft{pi_}"
                    tt("dve", prod[:, :], kv[:, 0:512], bank(qb), ALU.mult, [kvk, bk(qb)], [pk_])
                    sc_i = rri("s8", 2)
                    s8 = sm[:, 40 + 8 * sc_i:40 + 8 * sc_i + nh]; s8k = f"s8{sc_i}"
                    trk.op("dve", lambda e, s8=s8, prod=prod: e.reduce_sum(
                        out=s8, in_=prod[:, :].rearrange("p (h d) -> p h d", h=nh), axis=AX.X), [pk_], [s8k])
                    act(s8, s8, AF.Exp, [s8k], [s8k], scale=scale)
                    vi = rri("pv", 2)
                    pv = (bt[1], bt[5])[vi]; pvk = ("bt1", "bt5")[vi]
                    tt("pool", pv[:, :].rearrange("p (h d) -> p h d", h=nh), kv[:, 512:1024].rearrange("p (h d) -> p h d", h=nh),
                       s8.unsqueeze(2).to_broadcast([128, nh, dh]), ALU.mult, [kvk, s8k], [pvk])
                    mm([(pso, eye16b[:, b, :], pv[:, :], it == 0, it == tot - 1),
                        (psl, eye16v[:, b, :], s8, it == 0, it == tot - 1)], [pvk, s8k], pkeys)

            items = []
            for g, d in enumerate(DILS):
                for b in range(16):
                    src = cw[g][b, 0:127 * d + 1:d, :] if d > 1 else cw[g][b, :, :]
                    items.append((src, ft[2 + g][:n, :], f"ft{2 + g}", b))
            pso = bank(4)[:n, :]
            psl = bank(5)[:n, 0:8]
            samp_attn(items, 8, 64, 0.125, pso, psl, [bk(4), bk(5)])
            oS = gas
            oS = big[0:16, 28672 - 1024:28672 - 512]
            oS = ft[8]
            lS = sm[:n, 64:72]
            cp("dve", oS[:n, :], pso, [bk(4)], ["ft8"])
            cp("dve", lS, psl, [bk(5)], ["lS"])
            for g in range(3):
                tt("dve", ft[0][:n, :], ft[2 + g][:n, :], ft[5 + g][:n, :], ALU.mult, [f"ft{2 + g}", f"ft{5 + g}"], ["ft0"])
                pn = sm[:n, 72:80]
                trk.op("dve", lambda e, pn=pn: e.reduce_sum(out=pn, in_=ft[0][:n, :].rearrange("p (h d) -> p h d", h=8), axis=AX.X),
                       ["ft0"], ["pn"])
                act(pn, pn, AF.Exp, ["pn"], ["pn"], scale=0.125)
                tt("dve", ft[1][:n, :].rearrange("p (h d) -> p h d", h=8),
                   U[:, C_AV + g * 512:C_AV + (g + 1) * 512].rearrange("p (h d) -> p h d", h=8),
                   pn.unsqueeze(2).to_broadcast([n, 8, 64]), ALU.mult, ["U", "pn"], ["ft1"])
                tt("dve", oS[:n, :], oS[:n, :], ft[1][:n, :], ALU.add, ["ft8", "ft1"], ["ft8"])
                tt("dve", lS, lS, pn, ALU.add, ["lS", "pn"], ["lS"])
            trk.op("dve", lambda e: e.reciprocal(out=lS, in_=lS), ["lS"], ["lS"])
            Gz = ft[9]
            act(Gz[:n, :], U[:, C_AZ:C_AZ + 512], AF.Silu, ["U"], ["ft9"])
            tt("dve", oS[:n, :].rearrange("p (h d) -> p h d", h=8), oS[:n, :].rearrange("p (h d) -> p h d", h=8),
               lS.unsqueeze(2).to_broadcast([n, 8, 64]), ALU.mult, ["ft8", "lS"], ["ft8"])
            tt("dve", bt[0][:n, :], oS[:n, :], Gz[:n, :], ALU.mult, ["ft8", "ft9"], ["bt0"])
            pq = bankb(6).rearrange("p (c n) -> p c n", c=8)
            transposes([(pq[:, c, :n], bt[0][:n, c * 128:(c + 1) * 128], identb[:n, :n]) for c in range(4)], ["bt0"], [bk(6)])
            yaT = bt[4].rearrange("p (c n) -> p c n", c=4)
            cp("act", yaT[:, :, :n], pq[:, 0:4, :n], [bk(6)], ["bt4"])
            grp = []
            for half in range(2):
                for c in range(4):
                    grp.append((bank(half)[:n, :], yaT[:, c, :n], wpas[:, c, half * 512:(half + 1) * 512], c == 0, c == 3))
            mm(grp, ["bt4", "wpas"], [bk(0), bk(1)])
            for half in range(2):
                act(ft[0][:n, :], U[:, C_GA + half * 512:C_GA + (half + 1) * 512], AF.Sigmoid, ["U"], ["ft0"])
                tt("dve", gas[:, half * 512:(half + 1) * 512], bank(half)[:n, :], ft[0][:n, :], ALU.mult, [bk(half), "ft0"], ["gas"])

            fS = ft[0]; kinS = ft[1]
            act(fS[:n, :], U[:, C_BF:C_BF + 512], AF.Sigmoid, ["U"], ["ft0"])
            tt("dve", fS[:n, :], fS[:n, :], omlr[:n, :], ALU.mult, ["ft0"], ["ft0"])
            tt("dve", fS[:n, :], fS[:n, :], lbr[:n, :], ALU.add, ["ft0"], ["ft0"])
            ts("dve", kinS[:n, :], fS[:n, :], -1.0, 1.0, ALU.mult, ALU.add, ["ft0"], ["ft1"])
            pT = bank(6)[:, 0:128]
            transposes([(pT[:, h * 16:(h + 1) * 16], fS[:n, h * 128:(h + 1) * 128], identf[:n, :n]) for h in range(4)] +
                       [(pT[:, 64 + h * 16:64 + (h + 1) * 16], U[:, C_BQ + h * 128:C_BQ + (h + 1) * 128], identf[:n, :n]) for h in range(4)],
                       ["ft0", "U"], [bk(6)])
            fqT = ft[2]
            cp("dve", fqT[:, 0:128], pT, [bk(6)], ["ft2"])
            fT = fqT[:, 0:64].rearrange("p (h b) -> p h b", h=4)
            qT = fqT[:, 64:128].rearrange("p (h b) -> p h b", h=4)
            qTm = [ft[3], ft[4]]
            for hp in range(2):
                tt("dve", qTm[hp][:, :].rearrange("p (h b c) -> p h b c", h=2, b=16),
                   qT[:, 2 * hp:2 * hp + 2, :].unsqueeze(3).to_broadcast([128, 2, 16, 16]),
                   eye16v.unsqueeze(1).to_broadcast([128, 2, 16, 16]), ALU.mult, ["ft2"], [f"ft{3 + hp}"])
            pob = bank(7)[:n, :]
            zt = ft[8]
            trk.op("pool", lambda e: e.memset(zt[:, :], 0.0), (), ["ft8"])
            mm([(pob, eye16v[:, 0, :], zt[:, :], True, False)], ["ft8"], [bk(7)])
            vexp = ft[5]
            for half in range(2):
                for bl in range(8):
                    b = half * 8 + bl
                    ld(Ssm[:, bl, :, :], sh[b].rearrange("h k v -> k h v"), [f"Ssm{bl}"])
                for q4 in range(2):
                    for h in range(4):
                        b0 = half * 8 + q4 * 4
                        tt("dve", vexp[:n, :].rearrange("p (b v) -> p b v", b=4),
                           U[:, C_BI + h * 128:C_BI + (h + 1) * 128].unsqueeze(1).to_broadcast([n, 4, 128]),
                           identf[:n, b0:b0 + 4].unsqueeze(2).to_broadcast([n, 4, 128]), ALU.mult, ["U"], ["ft5"])
                        kb_i = rri("kvb", 4)
                        mm([(bank(kb_i), kinS[:n, h * 128:(h + 1) * 128], vexp[:n, :], True, True)], ["ft1", "ft5"], [bk(kb_i)])
                        for bi_ in range(4):
                            bl = q4 * 4 + bi_
                            b = half * 8 + bl
                            stt("dve", Ssm[:, bl, h, :], Ssm[:, bl, h, :], fT[:, h, b:b + 1], bank(kb_i)[:, bi_ * 128:(bi_ + 1) * 128],
                                ALU.mult, ALU.add, [f"Ssm{bl}", "ft2", bk(kb_i)], [f"Ssm{bl}"])
                for bl in range(8):
                    b = half * 8 + bl
                    mm([(pob[:, h * 128:(h + 1) * 128], qTm[h // 2][:, :].rearrange("p (h b c) -> p h b c", h=2, b=16)[:, h % 2, b, :],
                         Ssm[:, bl, h, :], False, b == 15) for h in range(4)],
                       [f"Ssm{bl}", "ft3", "ft4"], [bk(7)])
                    stdma(hg_s[b].rearrange("h k v -> k h v"), Ssm[:, bl, :, :], [f"Ssm{bl}"])
            Gb = ft[6]
            act(Gb[:n, :], U[:, C_BZ:C_BZ + 512], AF.Silu, ["U"], ["ft6"])
            tt("dve", ft[7][:n, :], Gb[:n, :], C("hg")[:n, :], ALU.mult, ["ft6"], ["ft7"])
            ybT = head_norm_gate(bank(7), bk(7), ft[7], "ft7", n)

            items = []
            for b in range(16):
                for mb in range(2):
                    items.append((cm[b, mb * 128:(mb + 1) * 128, :], U[:, C_CQ:C_CQ + 512], "U", b))
            pso2 = bank(4)[:n, :]
            psl2 = bank(5)[:n, 0:4]
            samp_attn(items, 4, 128, float(128 ** -0.5), pso2, psl2, [bk(4), bk(5)])
            lC = sm[:n, 80:84]
            trk.op("dve", lambda e: e.reciprocal(out=lC, in_=psl2), [bk(5)], ["lC"])
            Gc = ft[9]
            act(Gc[:n, :], U[:, C_CZ:C_CZ + 512], AF.Silu, ["U"], ["ft9"])
            tt("dve", ft[8][:n, :].rearrange("p (h d) -> p h d", h=4), pso2.rearrange("p (h d) -> p h d", h=4),
               lC.unsqueeze(2).to_broadcast([n, 4, 128]), ALU.mult, [bk(4), "lC"], ["ft8"])
            tt("dve", bt[3][:n, :], ft[8][:n, :], Gc[:n, :], ALU.mult, ["ft8", "ft9"], ["bt3"])
            pq = bankb(6).rearrange("p (c n) -> p c n", c=8)
            transposes([(pq[:, c, :n], bt[3][:n, c * 128:(c + 1) * 128], identb[:n, :n]) for c in range(4)], ["bt3"], [bk(6)])
            ycT = bt[6].rearrange("p (c n) -> p c n", c=4)
            cp("act", ycT[:, :, :n], pq[:, 0:4, :n], [bk(6)], ["bt6"])
            epilogue(n, xn, xnk, x_t, xk, ybT, ycT, (gas, "gas"), y_s[:, :], gate_src=U)

    return nc, trk, body


def _consts(core, inputs):
    cf = np.zeros((128, NCF), np.float32)

    def put(name, arr):
        a, b = _CF[name]
        cf[:arr.shape[0], a:b] = arr
    s = np.arange(128)
    U = (s[:, None] <= s[None, :]).astype(np.float32)
    put("identf", np.eye(128, dtype=np.float32))
    put("Um", U - U[:, 63:64])
    put("M2", (s[:, None] > s[None, :]).astype(np.float32))
    put("attm", U)
    ind = np.ones((128, 2), np.float32)
    ind[64:, 0] = 0.0
    put("ind", ind)
    put("pregT", inputs["norm_pre"][0].reshape(8, 128).T)
    put("memgT", inputs["mem_norm"][0].reshape(8, 128).T)
    put("postg", np.broadcast_to(inputs["norm_post"][0][None, :], (128, 1024)))
    put("hg", np.broadcast_to(inputs["hgrn_out_norm"][0][None, :], (128, 512)))
    put("l0", np.broadcast_to(inputs["hgrn_lb_logits"][0][None, :], (128, 512)))
    put("l1", np.broadcast_to(inputs["hgrn_lb_logits"][1][None, :], (128, 512)))
    cm = (np.arange(8) < core).astype(np.float32)
    put("cmask", np.broadcast_to(cm[None, :], (128, 8)))
    put("eye16", np.broadcast_to(np.eye(16, dtype=np.float32).reshape(1, 256), (128, 256)))
    inv = (10000.0 ** (-np.arange(0, 64, 2, dtype=np.float32) / np.float32(64))).astype(np.float32)
    ang = (np.float32(PAST) * inv).astype(np.float32)
    put("ropes", np.broadcast_to(np.concatenate([np.cos(ang), np.sin(ang)])[None, :].astype(np.float32), (128, 64)))
    cb = np.zeros((128, NCB), np.float32)
    cb[:, 0:128] = np.eye(128)
    j = s[:, None]
    i = s[None, :]
    cb[:, 128:256] = np.where(j >= i, 0.0, NEG)
    cb[:, 256:384] = np.where(j <= i, 0.0, NEG)
    cb[:, 384:512] = NEG if core == 0 else cb[:, 128:256]
    cb[:, 512:640] = (s[:, None] > s[None, :])
    cb[:, 640:768] = U - U[:, 63:64]
    cb[:, 768:770] = ind
    cb[:, 772:1028] = np.eye(16, dtype=np.float32).reshape(1, 256)
    rope = np.zeros((len(PA_BLOCKS), 128, 64), np.float32)
    for bi, (g, r, n) in enumerate(PA_BLOCKS):
        d = DILS[g]
        pos = core * TPC + d * (128 * n + np.arange(128)) + r
        angp = pos.astype(np.float32)[:, None] * inv[None, :]
        rope[bi, :, 0:32] = np.cos(angp)
        rope[bi, :, 32:64] = np.sin(angp)
    return cf, cb, rope


_PROG = {}


def kernel(**inputs):
    inp = {k: np.asarray(v) for k, v in inputs.items()}
    if "nc" not in _PROG:
        _PROG["nc"] = build_program()
    nc = _PROG["nc"]
    xp = inp["x_prompt"][0]
    selb = np.zeros((16, 16, 128), np.float32)
    for b in range(16):
        selb[b, b, :] = 1.0
    in_maps = []
    for c in range(NCORE):
        xe = np.zeros((2 * TPC, D), np.float32)
        if c > 0:
            xe[:TPC] = xp[(c - 1) * TPC:c * TPC]
        xe[TPC:] = xp[c * TPC:(c + 1) * TPC]
        cf, cb, rope = _consts(c, inp)
        sl = slice(16 * c, 16 * c + 16)
        xh = np.zeros((NPREV * 128, D), np.float32)
        if c > 0:
            xh[NPREV * 128 - c * TPC:] = xp[:c * TPC]
        in_maps.append({
            "xe": xe, "xh": xh,
            "xs": np.ascontiguousarray(inp["x_sample"][sl, 0, :]),
            "mem": np.ascontiguousarray(inp["mem_prompt"][0]),
            "cw0": np.ascontiguousarray(inp["cache_win128_kv"][0, sl].reshape(16, 128, 1024)),
            "cw1": np.ascontiguousarray(inp["cache_win512_kv"][0, sl].reshape(16, 512, 1024)),
            "cw2": np.ascontiguousarray(inp["cache_win2048_kv"][0, sl].reshape(16, 2048, 1024)),
            "sh": np.ascontiguousarray(inp["state_hgrn"][0, sl]),
            "cm": np.ascontiguousarray(inp["cache_mem_kv"][0, sl].reshape(16, 256, 1024)),
            "w_in": np.ascontiguousarray(inp["w_in"][0]),
            "w_mem": np.ascontiguousarray(inp["w_mem_kv"][0]),
            "w_pa": np.ascontiguousarray(inp["w_branch_a"][0]),
            "w_pb": np.ascontiguousarray(inp["w_branch_b"][0]),
            "w_pc": np.ascontiguousarray(inp["w_branch_c"][0]),
            "w_out": np.ascontiguousarray(inp["w_out"][0]),
            "cf": cf, "cb": cb, "rope": rope, "selb": selb,
        })
    if os.environ.get("K_NOSAMPLE"):
        for m in in_maps:
            for k in ("cw0", "cw1", "cw2", "sh", "cm"):
                m[k] = np.ascontiguousarray(m[k][0:1])
    res = run_bass_kernel_spmd(nc, in_maps, core_ids=list(range(NCORE)))
    R = res.results
    y_prompt = np.concatenate([R[c]["y_p"] for c in range(NCORE)], axis=0).reshape(1, T, D)
    y_sample = np.concatenate([R[c]["y_s"] for c in range(NCORE)], axis=0).reshape(128, 1, D)
    last = R[NCORE - 1]
    nw0 = last["nw0"].reshape(1, 1, 128, 2, 8, 64)
    nw1 = last["nw1"].reshape(1, 1, 512, 2, 8, 64)
    nw2 = last["nw2"].reshape(1, 1, 2048, 2, 8, 64)
    hgp = last["hg_p"].reshape(1, 1, 4, 128, 128)
    mkv = R[0]["mkv"].reshape(1, 1, 256, 2, 4, 128)
    nws = [np.concatenate([R[c][f"nws{g}"] for c in range(NCORE)], axis=0).reshape(1, 128, 1, 2, 8, 64) for g in range(3)]
    hgs = np.concatenate([R[c]["hg_s"] for c in range(NCORE)], axis=0).reshape(1, 128, 4, 128, 128)
    return (y_prompt, y_sample, nw0, nw1, nw2, hgp, mkv, nws[0], nws[1], nws[2], hgs)
```

```python
import numpy as np
from contextlib import ExitStack
import concourse.bass as bass
import concourse.mybir as mybir
from concourse.bass_utils import run_bass_kernel_spmd

F32 = mybir.dt.float32
BF16 = mybir.dt.bfloat16
AF = mybir.ActivationFunctionType
ALU = mybir.AluOpType
AX = mybir.AxisListType

NCORE = 8
T = 16384
TPC = 2048
NT = 16
D = 1024
NIN = 11264
EPS = 1e-6
PAST = 8192
DILS = (1, 4, 16)
NPREV = (NCORE - 1) * NT
NEG = -30000.0
C_AQ, C_AK, C_AV, C_AZ = 0, 1536, 3072, 4608
C_BQ, C_BF, C_BI, C_BZ = 5120, 5632, 6144, 6656
C_CQ, C_CZ = 7168, 7680
C_GA, C_GB, C_GC = 8192, 9216, 10240

_CF = {}
_off = 0
for _n, _w in (("identf", 128), ("Um", 128), ("M2", 128), ("attm", 128), ("ind", 2), ("pregT", 8), ("memgT", 8),
               ("postg", 1024), ("hg", 512), ("l0", 512), ("l1", 512), ("cmask", 8), ("eye16", 256),
               ("ropes", 64)):
    _CF[_n] = (_off, _off + _w)
    _off += _w
NCF = _off
NCB = 1028

def _pa_blocks():
    out = []
    for g, d in enumerate(DILS):
        nb = NT // d
        for r in range(d):
            for n in range(-1, nb):
                out.append((g, r, n))
    return out
PA_BLOCKS = _pa_blocks()
PA_INDEX = {b: i for i, b in enumerate(PA_BLOCKS)}


class Trk:
    ENGS = ("pe", "act", "dve", "pool", "sp")

    def __init__(self):
        self.streams = {e: [] for e in self.ENGS}
        self.cnt = {e: 0 for e in self.ENGS}
        self.known = {e: {} for e in self.ENGS}
        self.lastw = {}
        self.readers = {}
        self.lanegroups = {}
        self.lanecnt = {}
        self.lane_rr = {}

    def add_lanes(self, group, n):
        names = [f"{group}{i}" for i in range(n)]
        self.lanegroups[group] = names
        self.lane_rr[group] = 0
        for x in names:
            self.lanecnt[x] = 0

    def _deps(self, eng, r, w):
        deps = {}

        def add(k, c):
            if deps.get(k, 0) < c:
                deps[k] = c
        for key in r:
            lw = self.lastw.get(key)
            if lw is not None:
                add(lw[0], lw[1])
        for key in w:
            lw = self.lastw.get(key)
            if lw is not None:
                add(lw[0], lw[1])
            for k, c in self.readers.get(key, {}).items():
                add(k, c)
        waits = []
        for k, c in deps.items():
            if k == ("E", "pe") and eng == "pe":
                continue
            if self.known[eng].get(k, 0) >= c:
                continue
            self.known[eng][k] = c
            waits.append((k, c))
        return waits

    def _commit(self, k, c, r, w):
        for key in w:
            self.lastw[key] = (k, c)
            self.readers[key] = {}
        for key in r:
            d = self.readers.setdefault(key, {})
            if d.get(k, 0) < c:
                d[k] = c

    def op(self, eng, fn, r=(), w=()):
        waits = self._deps(eng, r, w)
        self.cnt[eng] += 1
        k = ("E", eng)
        self.streams[eng].append((waits, fn, k))
        self._commit(k, self.cnt[eng], r, w)

    def dma(self, q, group, fn, r=(), w=()):
        lanes = self.lanegroups[group]
        i = self.lane_rr[group]
        self.lane_rr[group] = (i + 1) % len(lanes)
        lane = lanes[i]
        waits = self._deps(q, r, w)
        k = ("L", lane)
        c = self.lanecnt[lane]
        if c > 0 and self.known[q].get(k, 0) < c:
            self.known[q][k] = c
            waits.append((k, c))
        self.lanecnt[lane] = c + 1
        self.streams[q].append((waits, fn, k))
        self._commit(k, c + 1, r, w)

    def barrier(self):
        for e in self.ENGS:
            waits = []
            for o in self.ENGS:
                if o == "sp" or self.cnt[o] == 0:
                    continue
                k = ("E", o)
                if o == e and e == "pe":
                    continue
                if self.known[e].get(k, 0) < self.cnt[o]:
                    self.known[e][k] = self.cnt[o]
                    waits.append((k, self.cnt[o]))
            for lane, c in self.lanecnt.items():
                k = ("L", lane)
                if c > 0 and self.known[e].get(k, 0) < c:
                    self.known[e][k] = c
                    waits.append((k, c))
            if waits:
                self.streams[e].append((waits, None, None))

    def emit(self, nc):
        with ExitStack() as st:
            sems = {}
            for e in self.ENGS:
                if e != "sp":
                    sems[("E", e)] = st.enter_context(nc.semaphore("e_" + e))
            for lane in self.lanecnt:
                sems[("L", lane)] = st.enter_context(nc.semaphore("l_" + lane))
            block = st.enter_context(nc.Block())

            def mult(k):
                return 16 if (k[0] == "L" and not k[1].startswith("cc")) else 1

            def run(eng, h):
                for waits, fn, mine in self.streams[eng]:
                    for k, c in waits:
                        h.wait_ge(sems[k], c * mult(k))
                    if fn is None:
                        continue
                    ins = fn(h)
                    if mine[0] == "L" and mine[1].startswith("cc"):
                        ins.then_inc(sems[mine])
                    else:
                        ins.then_inc(sems[mine], mult(mine))

            block.tensor(lambda h: run("pe", h))
            block.scalar(lambda h: run("act", h))
            block.vector(lambda h: run("dve", h))
            block.gpsimd(lambda h: run("pool", h))
            block.sync(lambda h: run("sp", h))


import os
class _Stop(Exception):
    pass


def build_program(do_sample=True):
    nc, trk, body = _build_program(do_sample)
    try:
        body()
    except _Stop:
        pass
    trk.barrier()
    trk.emit(nc)
    return nc


def _build_program(do_sample=True):
    do_sample = do_sample and not os.environ.get("K_NOSAMPLE")
    no_cc = bool(os.environ.get("K_NOCC"))
    stop = float(os.environ.get("K_STOP", "99"))
    nc = bass.Bass("TRN2", target_bir_lowering=False)
    trk = Trk()
    trk.add_lanes("ld", 6)
    trk.add_lanes("st", 8)
    trk.add_lanes("wl", 4)
    trk.add_lanes("cc", 1)

    def din(name, shape):
        return nc.dram_tensor(name, list(shape), F32, kind="ExternalInput").ap()

    def dout(name, shape):
        return nc.dram_tensor(name, list(shape), F32, kind="ExternalOutput").ap()

    xe = din("xe", (2 * TPC, D))
    SM = 16 if do_sample else 1
    xh = din("xh", (NPREV * 128, D))
    xsamp = din("xs", (16, D))
    mem = din("mem", (256, D))
    cw = [din("cw0", (SM, 128, 1024)), din("cw1", (SM, 512, 1024)), din("cw2", (SM, 2048, 1024))]
    sh = din("sh", (SM, 4, 128, 128))
    cm = din("cm", (SM, 256, 1024))
    w_in = din("w_in", (D, NIN))
    w_mem = din("w_mem", (D, 1024))
    w_pa = din("w_pa", (512, D))
    w_pb = din("w_pb", (512, D))
    w_pc = din("w_pc", (512, D))
    w_out = din("w_out", (D, D))
    cf_d = din("cf", (128, NCF))
    cb_d = din("cb", (128, NCB))
    rope_d = din("rope", (len(PA_BLOCKS), 128, 64))
    selb_d = din("selb", (16, 16, 128))

    y_p = dout("y_p", (TPC, D))
    y_s = dout("y_s", (16, D))
    nw = [dout("nw0", (128, 2, 512)), dout("nw1", (512, 2, 512)), dout("nw2", (2048, 2, 512))]
    hg_p = dout("hg_p", (4, 128, 128))
    mkv = dout("mkv", (256, 1024))
    nws = [dout("nws0", (16, 2, 512)), dout("nws1", (16, 2, 512)), dout("nws2", (16, 2, 512))]
    hg_s = dout("hg_s", (16, 4, 128, 128))

    ga_d = nc.dram_tensor("ga_scr", [TPC, D], F32).ap()

    w_in_r = w_in.rearrange("(c p) n -> p c n", p=128)

    st = ExitStack()

    def chk(v):
        if stop <= v:
            raise _Stop()

    def body():
        def sb(name, shape, dt=F32):
            return st.enter_context(nc.sbuf_tensor(name, list(shape), dt))

        cf = sb("cf_sb", (128, NCF))
        cb = sb("cb_sb", (128, NCB), BF16)
        big = sb("big", (128, 30720))
        xt = [sb("xt0", (128, D)), sb("xt1", (128, D))]
        xsb = sb("xsb", (128, D), BF16)
        xnT = [sb("xnT0", (128, 8, 128), BF16), sb("xnT1", (128, 8, 128), BF16)]
        ft = [sb(f"ft{i}", (128, 512)) for i in range(10)]
        bt = [sb(f"bt{i}", (128, 512), BF16) for i in range(8)]
        sm = sb("sm", (128, 128))
        Sst = sb("Sst", (128, 4, 128))
        Sbf = sb("Sbf", (128, 4, 128), BF16)
        Dacc = sb("Dacc", (128, 4))
        ropet = [sb("ropet0", (128, 64)), sb("ropet1", (128, 64))]
        ptb = [sb("ptb0", (128, 256), BF16), sb("ptb1", (128, 256), BF16)]
        _kb = 28000
        kTs = [big[:, _kb + i * 256:_kb + (i + 1) * 256].bitcast(BF16).rearrange("p (c n) -> p c n", c=4) for i in range(3)]
        _vb = _kb + 768
        vss = [big[:, _vb + i * 260:_vb + (i + 1) * 260].bitcast(BF16).rearrange("p (h d) -> p h d", h=8) for i in range(3)]
        _qb = _vb + 780
        qTs = [big[:, _qb + i * 256:_qb + (i + 1) * 256].bitcast(BF16).rearrange("p (c n) -> p c n", c=4) for i in range(2)]
        mkT = sb("mkT", (128, 4, 256), BF16)
        mvp = sb("mvp", (128, 2, 4, 129), BF16)
        gall = big[:, 8192:8192 + 8 * 516].rearrange("p (c n) -> p c n", c=8)
        psum = [st.enter_context(nc.psum_tensor(f"ps{i}", [128, 1024], F32)) for i in range(4)]

        def C(name):
            a, b = _CF[name]
            return cf[:, a:b]

        identf = C("identf")
        identb = cb[:, 0:128]
        mprev = cb[:, 128:256]
        mcur = cb[:, 256:384]
        mfirst = cb[:, 384:512]
        M2b = cb[:, 512:640]
        Umb = cb[:, 640:768]
        indb = cb[:, 768:770]
        eye16b = cb[:, 772:1028].rearrange("p (b c) -> p b c", b=16)
        lh = [sb(f"lh{i}", (128, 512), BF16) for i in range(4)]

        def bank(i):
            return psum[i // 2][:, (i % 2) * 512:(i % 2) * 512 + 512]

        def bankb(i):
            return bank(i).bitcast(BF16)

        _rr = {}

        def rr(name, items):
            i = _rr.get(name, 0)
            _rr[name] = i + 1
            return items[i % len(items)]

        def rri(name, n):
            i = _rr.get(name, 0)
            _rr[name] = i + 1
            return i % n

        def act(out, in_, func, r, w, **kw):
            trk.op("act", lambda e: e.activation(out=out, in_=in_, func=func, **kw), r, w)

        def tt(eng, out, in0, in1, op, r, w):
            trk.op(eng, lambda e: e.tensor_tensor(out=out, in0=in0, in1=in1, op=op), r, w)

        def ts(eng, out, in0, s1, s2, op0, op1, r, w):
            if s2 is None:
                trk.op(eng, lambda e: e.tensor_scalar(out=out, in0=in0, scalar1=s1, scalar2=None, op0=op0), r, w)
            else:
                trk.op(eng, lambda e: e.tensor_scalar(out=out, in0=in0, scalar1=s1, scalar2=s2, op0=op0, op1=op1), r, w)

        def stt(eng, out, in0, scalar, in1, op0, op1, r, w):
            trk.op(eng, lambda e: e.scalar_tensor_tensor(out=out, in0=in0, scalar=scalar, in1=in1, op0=op0, op1=op1), r, w)

        def cp(eng, out, in_, r, w):
            if eng == "act":
                trk.op("act", lambda e: e.copy(out=out, in_=in_), r, w)
            else:
                trk.op(eng, lambda e: e.tensor_copy(out=out, in_=in_), r, w)

        def ld(out, in_, w, q="sp", r=()):
            trk.dma(q, "ld", lambda e: e.dma_start(out=out, in_=in_), r, w)

        def stdma(out, in_, r, q="sp"):
            trk.dma(q, "st", lambda e: e.dma_start(out=out, in_=in_), r, ())

        def wload(out, in_, w):
            trk.dma("pool", "wl", lambda e: e.dma_start(out=out, in_=in_), (), w)

        _stg = [0]

        def wload_nocast(out, in_, w):
            i = _stg[0] % 2
            _stg[0] += 1
            n = 1
            for d_ in out.shape[1:]:
                n *= d_
            stage = big[:, 12288 + i * 4096:12288 + i * 4096 + n]
            if len(out.shape) == 3:
                stage = stage.rearrange("p (c n) -> p c n", c=out.shape[1])
            ld(stage, in_, [f"stage{i}"])
            cp("pool" if i else "dve", out, stage, [f"stage{i}"], w)

        def mm(groups, r, w):
            def f(e):
                ins = None
                for (o, l, rh, s0, s1) in groups:
                    ins = e.matmul(o, lhsT=l, rhs=rh, start=s0, stop=s1)
                return ins
            trk.op("pe", f, r, w)

        def transposes(items, r, w):
            def f(e):
                ins = None
                for (o, i, idn) in items:
                    ins = e.transpose(out=o, in_=i, identity=idn)
                return ins
            trk.op("pe", f, r, w)

        def rstd_from_ss(ss, rs, n, scale, key):
            act(rs, ss, AF.Ln, [key], [key + "r"], scale=scale, bias=EPS)
            act(rs, rs, AF.Exp, [key + "r"], [key + "r"], scale=-0.5)

        nt_i = [0]

        def norm_T(src, n, gT, keep_x=False):
            i = nt_i[0]
            nt_i[0] += 1
            x_t = xt[i % 2]
            xk = f"xt{i % 2}"
            xn = xnT[i % 2]
            xnk = f"xnT{i % 2}"
            ld(x_t[:n, :], src, [xk])
            ss = sm[:n, 0:1]
            rs = sm[:n, 1:2]
            act(xsb[:n, :], x_t[:n, :], AF.Square, [xk], ["xsb", "nss"], accum_out=ss)
            rstd_from_ss(ss, rs, n, 1.0 / D, "nss")
            act(xsb[:n, :], x_t[:n, :], AF.Copy, [xk, "nssr"], ["xsb"], scale=rs)
            b = rr("tb", [6, 7])
            pb = bankb(b).rearrange("p (c n) -> p c n", c=8)
            transposes([(pb[:, c, :n], xsb[:n, c * 128:(c + 1) * 128], identb[:n, :n]) for c in range(8)],
                       ["xsb"], [f"bank{b}"])
            tt("dve", xn[:, :, :n], pb[:, :, :n], gT.unsqueeze(2).to_broadcast([128, 8, n]), ALU.mult,
               [f"bank{b}"], [xnk])
            return xn, xnk, x_t, xk

        def proj(xn, xnk, n, wblk, wkey, b):
            mm([(bank(b)[:n, :], xn[:, c, :n], wblk[:, c, :], c == 0, c == 7) for c in range(8)],
               [xnk, wkey], [f"bank{b}"])

        def carve_w(idx):
            return big[:, idx * 2048:(idx + 1) * 2048].bitcast(BF16).rearrange("p (c n) -> p c n", c=8)

        def bk(i):
            return f"bank{i}"

        def rope4(ap):
            return ap.rearrange("p (h t f) -> p h t f", h=8, t=2)

        def rope(src, srckeys, out_ap, okey, n, cosb, sinb, tabkeys, A=None, B=None):
            A = ft[0] if A is None else A
            B = ft[1] if B is None else B
            ka, kb_ = "ft%d" % ft.index(A), "ft%d" % ft.index(B)
            tt("dve", rope4(A[:n, :]), rope4(src), cosb, ALU.mult, list(srckeys) + list(tabkeys), [ka])
            tt("dve", rope4(B[:n, :]), rope4(src), sinb, ALU.mult, list(srckeys) + list(tabkeys), [kb_])
            A4 = rope4(A[:n, :]); B4 = rope4(B[:n, :]); O4 = rope4(out_ap)
            tt("pool", O4[:, :, 0, :], A4[:, :, 0, :], B4[:, :, 1, :], ALU.subtract, [ka, kb_], [okey])
            tt("pool", O4[:, :, 1, :], A4[:, :, 1, :], B4[:, :, 0, :], ALU.add, [ka, kb_], [okey])

        ld(cf[:, :], cf_d[:, :], ["cf"])
        wload(cb[:, :], cb_d[:, :], ["cb"])
        trk.barrier()
        l0 = C("l0")
        l1 = C("l1")
        lbr = l0
        omlr = l1
        tt("dve", l0, l0, l1, ALU.subtract, ["cf"], ["cf"])
        act(l0, l0, AF.Sigmoid, ["cf"], ["cf"])
        ts("dve", l1, l0, -1.0, 1.0, ALU.mult, ALU.add, ["cf"], ["cf"])
        trk.op("pool", lambda e: e.memset(Sst[:], 0.0), (), ["S"])
        trk.op("pool", lambda e: e.memset(Dacc[:], 1.0), (), ["Dacc"])
        trk.op("pool", lambda e: e.memset(mvp[:, :, :, 128:129], 1.0), (), ["mvp"])
        trk.barrier()

        wm = [carve_w(0), carve_w(1)]
        w_mem_r = w_mem.rearrange("(c p) n -> p c n", p=128)
        for j in range(2):
            wload(wm[j], w_mem_r[:, :, j * 512:(j + 1) * 512], [f"wm{j}"])
        for mt in range(2):
            xn, xnk, _, _ = norm_T(mem[mt * 128:(mt + 1) * 128, :], 128, C("memgT"))
            for j in range(2):
                b = rri("pj", 4)
                proj(xn, xnk, 128, wm[j], f"wm{j}", b)
                fi = rri("ftA", 2)
                f = ft[fi]; fk = f"ft{fi}"
                cp("act", f[:, :], bank(b), [bk(b)], [fk])
                stdma(mkv[mt * 128:(mt + 1) * 128, j * 512:(j + 1) * 512], f[:, :], [fk])
                if j == 0:
                    cp("dve", bt[0][:, :], f[:, :], [fk], ["bt0"])
                    tb = 6 + rri("tb", 2)
                    pbk = bankb(tb).rearrange("p (c n) -> p c n", c=8)
                    transposes([(pbk[:, h, :], bt[0][:, h * 128:(h + 1) * 128], identb) for h in range(4)],
                               ["bt0"], [bk(tb)])
                    cp("dve", mkT[:, :, mt * 128:(mt + 1) * 128], pbk[:, 0:4, :], [bk(tb)], ["mkT"])
                else:
                    cp("dve", mvp[:, mt, :, 0:128], f[:, :].rearrange("p (h d) -> p h d", h=4), [fk], ["mvp"])
        trk.barrier()

        chk(0)

        def hgrn_tile(xn, xnk, full, wkeys, par=0, do_proj=True):
            wq, wf, wi, wz = wkeys
            Um = C("Um"); M2 = C("M2"); attm = C("attm"); ind = C("ind")
            fo = 4 * par; bo = 2 * par
            b_f, b_i = (0, 1) if par == 0 else (4, 5)
            if do_proj:
                proj(xn, xnk, 128, wf[0], wf[1], b_f)
                proj(xn, xnk, 128, wi[0], wi[1], b_i)
            sig = ft[fo + 0]; logf = ft[fo + 1]; kin = ft[fo + 2]
            k0, k1, k2, k3 = (f"ft{fo + i}" for i in range(4))
            kb0, kb1 = f"bt{bo}", f"bt{bo + 1}"
            eblk = f"ebl{par}"
            act(sig[:, :], bank(b_f), AF.Exp, [bk(b_f)], [k0], scale=-1.0)
            ts("dve", sig[:, :], sig[:, :], 1.0, None, ALU.add, None, [k0], [k0])
            trk.op("dve", lambda e, sig=sig: e.reciprocal(out=sig[:, :], in_=sig[:, :]), [k0], [k0])
            tt("dve", sig[:, :], sig[:, :], omlr, ALU.mult, [k0], [k0])
            tt("pool", sig[:, :], sig[:, :], lbr, ALU.add, [k0], [k0])
            act(logf[:, :], sig[:, :], AF.Ln, [k0], [k1])
            ts("pool", kin[:, :], sig[:, :], -1.0, 1.0, ALU.mult, ALU.add, [k0], [k2])
            vb = bt[bo]
            cp("act", vb[:, :], bank(b_i), [bk(b_i)], [kb0])
            lhi = lh[2 * par]; llo = lh[2 * par + 1]
            khi, klo = f"lh{2 * par}", f"lh{2 * par + 1}"
            cp("dve", lhi[:, :], logf[:, :], [k1], [khi])
            tt("pool", llo[:, :], logf[:, :], lhi[:, :], ALU.subtract, [k1, khi], [klo])
            mm([(bank(2), M2b, lhi[:, :], True, False), (bank(2), M2b, llo[:, :], False, True)], [khi, klo], [bk(2)])
            ek = ft[fo + 3]
            act(ek[:, :], bank(2), AF.Exp, [bk(2)], [k3])
            khat = bt[bo + 1]
            tt("dve", khat[:, :], kin[:, :], ek[:, :], ALU.mult, [k2, k3], [kb1])
            pbl = bank(3)[:, 0:8].rearrange("p (h t) -> p h t", t=2)
            grp_ = []
            for h in range(4):
                grp_.append((pbl[:, h, :], lhi[:, h * 128:(h + 1) * 128], indb, True, False))
                grp_.append((pbl[:, h, :], llo[:, h * 128:(h + 1) * 128], indb, False, True))
            mm(grp_, [khi, klo], [bk(3)])
            ebl = (sm[:, 8:16] if par == 0 else sm[:, 100:108]).rearrange("p (h t) -> p h t", t=2)
            act(ebl, pbl, AF.Exp, [bk(3)], [eblk])
            if full:
                chk(3.52)
                proj(xn, xnk, 128, wq[0], wq[1], 4)
                proj(xn, xnk, 128, wz[0], wz[1], 5)
                mm([(bank(2), Umb, lhi[:, :], True, False), (bank(2), Umb, llo[:, :], False, True)], [khi, klo], [bk(2)])
                eq = ft[3]; ekm = ft[4]
                act(eq[:, :], bank(2), AF.Exp, [bk(2)], ["ft3"])
                act(ekm[:, :], bank(2), AF.Exp, [bk(2)], ["ft4"], scale=-1.0)
                chk(3.55)
                qt = bt[2]; kt = bt[3]
                tt("dve", qt[:, :], bank(4), eq[:, :], ALU.mult, [bk(4), "ft3"], ["bt2"])
                tt("pool", kt[:, :], kin[:, :], ekm[:, :], ALU.mult, ["ft2", "ft4"], ["bt3"])
                G = ft[5]
                act(G[:, :], bank(5), AF.Silu, [bk(5)], ["ft5"])
                tt("pool", G[:, :], G[:, :], C("hg"), ALU.mult, ["ft5"], ["ft5"])
                chk(3.57)
                pq = bankb(6).rearrange("p (c n) -> p c n", c=8)
                pk2 = bankb(7).rearrange("p (c n) -> p c n", c=8)
                transposes([(pq[:, h, :], qt[:, h * 128:(h + 1) * 128], identb) for h in range(4)], ["bt2"], [bk(6)])
                transposes([(pk2[:, h, :], kt[:, h * 128:(h + 1) * 128], identb) for h in range(4)], ["bt3"], [bk(7)])
                qkT = bt[4].rearrange("p (c n) -> p c n", c=4)
                kkT = bt[5].rearrange("p (c n) -> p c n", c=4)
                cp("act", qkT, pq[:, 0:4, :], [bk(6)], ["bt4"])
                cp("dve", kkT, pk2[:, 0:4, :], [bk(7)], ["bt5"])
                chk(3.6)
                tt("dve", Sbf[:], Sst[:], ebl[:, :, 0:1].to_broadcast([128, 4, 128]), ALU.mult, ["S", eblk], ["Sbf"])
                pa = bank(2)
                mm([(pa[:, h * 128:(h + 1) * 128], kkT[:, h, :], qkT[:, h, :], True, True) for h in range(4)],
                   ["bt4", "bt5"], [bk(2)])
                attb = bt[6]
                tt("dve", attb[:, :].rearrange("p (h t) -> p h t", h=4), pa.rearrange("p (h t) -> p h t", h=4),
                   attm.unsqueeze(1).to_broadcast([128, 4, 128]), ALU.mult, [bk(2)], ["bt6"])
                chk(3.7)
                po = bank(4)
                grp = []
                for h in range(4):
                    grp.append((po[:, h * 128:(h + 1) * 128], qkT[:, h, :], Sbf[:, h, :], True, False))
                    grp.append((po[:, h * 128:(h + 1) * 128], attb[:, h * 128:(h + 1) * 128], vb[:, h * 128:(h + 1) * 128], False, True))
                mm(grp, ["bt4", "Sbf", "bt6", "bt0"], [bk(4)])
                chk(3.8)
            pP = bank(b_f)
            mm([(pP[:, h * 128:(h + 1) * 128], khat[:, h * 128:(h + 1) * 128], vb[:, h * 128:(h + 1) * 128], True, True)
                for h in range(4)], [kb1, kb0], [bk(b_f)])
            for h in range(4):
                stt("dve", Sst[:, h, :], Sst[:, h, :], ebl[:, h, 1:2], pP[:, h * 128:(h + 1) * 128], ALU.mult, ALU.add,
                    ["S", eblk, bk(b_f)], ["S"])
            if not full:
                return None
            return head_norm_gate(bank(4), bk(4), G, "ft5", 128)

        def head_norm_gate(po, pok, G, gk, n):
            ss4 = sm[:n, 16:20]
            rs4 = sm[:n, 20:24]
            for h in range(4):
                act(ft[6][:n, h * 128:(h + 1) * 128], po[:n, h * 128:(h + 1) * 128], AF.Square, [pok], ["ft6", "ss4"],
                    accum_out=ss4[:, h:h + 1])
            rstd_from_ss(ss4, rs4, n, 1.0 / 128, "ss4")
            yb = bt[2]
            for h in range(4):
                stt("dve", yb[:n, h * 128:(h + 1) * 128], po[:n, h * 128:(h + 1) * 128], rs4[:, h:h + 1],
                    G[:n, h * 128:(h + 1) * 128], ALU.mult, ALU.mult, [pok, "ss4r", gk], ["bt2"])
            pq = bankb(6).rearrange("p (c n) -> p c n", c=8)
            transposes([(pq[:, h, :n], yb[:n, h * 128:(h + 1) * 128], identb[:n, :n]) for h in range(4)], ["bt2"], [bk(6)])
            ybT = bt[7].rearrange("p (c n) -> p c n", c=4)
            cp("act", ybT[:, :, :n], pq[:, 0:4, :n], [bk(6)], ["bt7"])
            return ybT

        wf_b = carve_w(0); wi_b = carve_w(1)
        wload(wf_b, w_in_r[:, :, C_BF:C_BF + 512], ["w0"])
        wload(wi_b, w_in_r[:, :, C_BI:C_BI + 512], ["w1"])
        def p1_stage(par, stage):
            fo = 4 * par; bo = 2 * par
            b_f, b_i = (0, 1) if par == 0 else (4, 5)
            sig = ft[fo + 0]; logf = ft[fo + 1]; kin = ft[fo + 2]; ek = ft[fo + 3]
            k0, k1, k2, k3 = (f"ft{fo + i}" for i in range(4))
            vb = bt[bo]; khat = bt[bo + 1]
            kb0, kb1 = f"bt{bo}", f"bt{bo + 1}"
            lhi = lh[2 * par]; llo = lh[2 * par + 1]
            khi, klo = f"lh{2 * par}", f"lh{2 * par + 1}"
            eblk = f"ebl{par}"
            pbl = bank(3)[:, 0:8].rearrange("p (h t) -> p h t", t=2)
            ebl = (sm[:, 8:16] if par == 0 else sm[:, 100:108]).rearrange("p (h t) -> p h t", t=2)
            if stage == "g1":
                act(sig[:, :], bank(b_f), AF.Exp, [bk(b_f)], [k0], scale=-1.0)
                cp("act", vb[:, :], bank(b_i), [bk(b_i)], [kb0])
                ts("dve", sig[:, :], sig[:, :], 1.0, None, ALU.add, None, [k0], [k0])
                trk.op("dve", lambda e, sig=sig: e.reciprocal(out=sig[:, :], in_=sig[:, :]), [k0], [k0])
                tt("dve", sig[:, :], sig[:, :], omlr, ALU.mult, [k0], [k0])
                tt("pool", sig[:, :], sig[:, :], lbr, ALU.add, [k0], [k0])
            elif stage == "g2":
                act(logf[:, :], sig[:, :], AF.Ln, [k0], [k1])
                ts("pool", kin[:, :], sig[:, :], -1.0, 1.0, ALU.mult, ALU.add, [k0], [k2])
                cp("dve", lhi[:, :], logf[:, :], [k1], [khi])
                tt("pool", llo[:, :], logf[:, :], lhi[:, :], ALU.subtract, [k1, khi], [klo])
            elif stage == "mid":
                mm([(bank(2), M2b, lhi[:, :], True, False), (bank(2), M2b, llo[:, :], False, True)], [khi, klo], [bk(2)])
                grp_ = []
                for h in range(4):
                    grp_.append((pbl[:, h, :], lhi[:, h * 128:(h + 1) * 128], indb, True, False))
                    grp_.append((pbl[:, h, :], llo[:, h * 128:(h + 1) * 128], indb, False, True))
                mm(grp_, [khi, klo], [bk(3)])
            else:
                act(ek[:, :], bank(2), AF.Exp, [bk(2)], [k3])
                tt("dve", khat[:, :], kin[:, :], ek[:, :], ALU.mult, [k2, k3], [kb1])
                act(ebl, pbl, AF.Exp, [bk(3)], [eblk])
                pP = bank(2)
                mm([(pP[:, h * 128:(h + 1) * 128], khat[:, h * 128:(h + 1) * 128], vb[:, h * 128:(h + 1) * 128], True, True)
                    for h in range(4)], [kb1, kb0], [bk(2)])
                for h in range(4):
                    stt("dve", Sst[:, h, :], Sst[:, h, :], ebl[:, h, 1:2], pP[:, h * 128:(h + 1) * 128], ALU.mult, ALU.add,
                        ["S", eblk, bk(2)], ["S"])

        def p1_proj(xq, par):
            b_f, b_i = (0, 1) if par == 0 else (4, 5)
            proj(xq[0], xq[1], 128, wf_b, "w0", b_f)
            proj(xq[0], xq[1], 128, wi_b, "w1", b_i)

        xq = [norm_T(xh[0:128, :], 128, C("pregT")), norm_T(xh[128:256, :], 128, C("pregT"))]
        p1_proj(xq[0], 0)
        for t in range(NPREV):
            par = t % 2
            if t + 1 < NPREV:
                p1_proj(xq[(t + 1) % 2], (t + 1) % 2)
            p1_stage(par, "g1")
            if t + 2 < NPREV:
                xq[par] = norm_T(xh[(t + 2) * 128:(t + 3) * 128, :], 128, C("pregT"))
            p1_stage(par, "g2")
            p1_stage(par, "mid")
            p1_stage(par, "back")
        trk.barrier()
        chk(1)

        acc = big[:, 0:16384].rearrange("p (h t) -> p h t", h=8)
        trk.op("pool", lambda e: e.memset(big[0:65, 0:16384], 0.0), (), ["acc"])
        for i in range(3):
            trk.op("pool", lambda e, i=i: e.memset(vss[i][:, :, 64:65], 1.0), (), [f"vss{i}"])
        WB = 8
        for g, d in enumerate(DILS):
            wq = carve_w(WB + 0); wk = carve_w(WB + 1); wv = carve_w(WB + 2)
            wload(wq, w_in_r[:, :, C_AQ + g * 512:C_AQ + (g + 1) * 512], ["wA0"])
            wload(wk, w_in_r[:, :, C_AK + g * 512:C_AK + (g + 1) * 512], ["wA1"])
            wload(wv, w_in_r[:, :, C_AV + g * 512:C_AV + (g + 1) * 512], ["wA2"])
            nb = NT // d
            wb_tokens = 128 * d
            kv_i = 0
            blocks = [(r_, n_) for r_ in range(d) for n_ in range(-1, nb)]

            def pa_norm(bi_):
                r_, n_ = blocks[bi_]
                st_ = TPC + d * 128 * n_ + r_
                src_ = xe[st_:st_ + 127 * d + 1:d, :] if d > 1 else xe[st_:st_ + 128, :]
                return norm_T(src_, 128, C("pregT"))
            nx_pa = pa_norm(0)
            for bidx, (r, n) in enumerate(blocks):
                if True:
                    if n == -1:
                        prev = None
                    own = n >= 0
                    xn, xnk = nx_pa[0], nx_pa[1]
                    pre_done = False
                    ri = rri("ropet", 2); rt = ropet[ri]; rk = f"ropet{ri}"
                    ld(rt[:, :], rope_d[PA_INDEX[(g, r, n)], :, :], [rk])
                    cosb = rt[:, 0:32].unsqueeze(1).unsqueeze(1).to_broadcast([128, 8, 2, 32])
                    sinb = rt[:, 32:64].unsqueeze(1).unsqueeze(1).to_broadcast([128, 8, 2, 32])
                    ks = kv_i % 3
                    kv_i += 1
                    proj(xn, xnk, 128, wk, "wA1", 1)
                    rope(bank(1), [bk(1)], ft[2][:, :], "ft2", 128, cosb, sinb, [rk])
                    need_out = own and (d * (128 * n + 127) + r >= TPC - wb_tokens)
                    if need_out:
                        t0 = d * 128 * n + r - (TPC - wb_tokens)
                        dst = nw[g][t0:t0 + 127 * d + 1:d, 0, :] if d > 1 else nw[g][t0:t0 + 128, 0, :]
                        stdma(dst, ft[2][:, :], ["ft2"])
                    cp("pool", bt[0][:, :], ft[2][:, :], ["ft2"], ["bt0"])
                    pk = bankb(6).rearrange("p (c n) -> p c n", c=8)
                    transposes([(pk[:, c, :], bt[0][:, c * 128:(c + 1) * 128], identb) for c in range(4)], ["bt0"], [bk(6)])
                    cp("act", kTs[ks], pk[:, 0:4, :], [bk(6)], [f"kTs{ks}"])
                    proj(xn, xnk, 128, wv, "wA2", 2)
                    if need_out:
                        cp("act", ft[3][:, :], bank(2), [bk(2)], ["ft3"])
                        dst = nw[g][t0:t0 + 127 * d + 1:d, 1, :] if d > 1 else nw[g][t0:t0 + 128, 1, :]
                        stdma(dst, ft[3][:, :], ["ft3"])
                    cp("act", vss[ks][:, :, 0:64], bank(2).rearrange("p (h d) -> p h d", h=8), [bk(2)], [f"vss{ks}"])
                    if own:
                        proj(xn, xnk, 128, wq, "wA0", 0)
                        rope(bank(0), [bk(0)], ft[6][:, :], "ft6", 128, cosb, sinb, [rk], A=ft[4], B=ft[5])
                        cp("pool", bt[1][:, :], ft[6][:, :], ["ft6"], ["bt1"])
                        pq = bankb(7).rearrange("p (c n) -> p c n", c=8)
                        transposes([(pq[:, c, :], bt[1][:, c * 128:(c + 1) * 128], identb) for c in range(4)], ["bt1"], [bk(7)])
                        qi = rri("qTs", 2)
                        cp("dve", qTs[qi], pq[:, 0:4, :], [bk(7)], [f"qTs{qi}"])
                        mp = mfirst if n == 0 else mprev
                        kp, kc = prev, ks
                        tok0 = d * 128 * n + r
                        for h in range(8):
                            pr, off = h // 2, (h % 2) * 64
                            sbk = 3 + rri("sbk", 2)
                            ps = bank(sbk)
                            mm([(ps[:, 0:128], kTs[kp][off:off + 64, pr, :], qTs[qi][off:off + 64, pr, :], True, False),
                                (ps[:, 0:128], identb, mp, False, True),
                                (ps[:, 128:256], kTs[kc][off:off + 64, pr, :], qTs[qi][off:off + 64, pr, :], True, False),
                                (ps[:, 128:256], identb, mcur, False, True)],
                               [f"kTs{kp}", f"kTs{kc}", f"qTs{qi}"], [bk(sbk)])
                            pi = rri("ptb", 2)
                            act(ptb[pi][:, :], ps[:, 0:256], AF.Exp, [bk(sbk)], [f"ptb{pi}"], scale=0.125)
                            pob_ = (5, 0)[h % 2]
                            po = bank(pob_)[0:65, 0:128]
                            pok = bk(pob_)
                            mm([(po, vss[kp][:, h, :], ptb[pi][:, 0:128], True, False),
                                (po, vss[kc][:, h, :], ptb[pi][:, 128:256], False, True)],
                               [f"vss{kp}", f"vss{kc}", f"ptb{pi}"], [pok])
                            a_sl = acc[0:65, h, tok0:tok0 + 127 * d + 1:d] if d > 1 else acc[0:65, h, tok0:tok0 + 128]
                            tt("dve", a_sl, po, a_sl, ALU.add, [pok, "acc"], ["acc"])
                            if h == 3 and bidx + 1 < len(blocks):
                                nx_pa = pa_norm(bidx + 1)
                                pre_done = True
                    if not pre_done and bidx + 1 < len(blocks):
                        nx_pa = pa_norm(bidx + 1)
                    prev = ks
        trk.barrier()

        chk(2)

        waz = carve_w(WB + 0); wga0 = carve_w(WB + 1); wga1 = carve_w(WB + 2)
        wload(waz, w_in_r[:, :, C_AZ:C_AZ + 512], ["wA0"])
        wload(wga0, w_in_r[:, :, C_GA:C_GA + 512], ["wA1"])
        wload(wga1, w_in_r[:, :, C_GA + 512:C_GA + 1024], ["wA2"])
        wpa = big[:, 22528:26624].bitcast(BF16).rearrange("p (h n) -> p h n", h=8)
        wload(wpa[0:64, :, :], w_pa.rearrange("(h p) n -> p h n", p=64), ["wpa"])
        ones64 = big[64:65, 26624:26688]
        trk.op("pool", lambda e: e.memset(ones64, 1.0), (), ["ones64"])
        rl = big[64:65, 26688:26688 + 1024]
        yaT_tiles = [bt[0], bt[1]]
        nx_af = norm_T(xe[TPC:TPC + 128, :], 128, C("pregT"))
        for t in range(NT):
            xn, xnk = nx_af[0], nx_af[1]
            trk.op("dve", lambda e, t=t: e.reciprocal(out=rl.rearrange("p (h n) -> p h n", h=8),
                                                      in_=acc[64:65, :, t * 128:(t + 1) * 128]), ["acc"], ["rl"])
            for h in range(8):
                zi = rri("azb", 2)
                pz = bank(zi)
                mm([(pz[0:64, 0:128], waz[:, c, h * 64:(h + 1) * 64], xn[:, c, :], c == 0, c == 7) for c in range(8)],
                   [xnk, "wA0"], [bk(zi)])
                gi = rri("gz", 2)
                gz = ft[gi]; gzk = f"ft{gi}"
                act(gz[0:64, 0:128], pz[0:64, 0:128], AF.Silu, [bk(zi)], [gzk])
                bi_ = 2 + (h % 2)
                pbc = bank(bi_)
                mm([(pbc[0:64, 0:128], ones64, rl[:, h * 128:(h + 1) * 128], True, True)], ["ones64", "rl"], [bk(bi_)])
                tt("pool", gz[0:64, 0:128], gz[0:64, 0:128], acc[0:64, h, t * 128:(t + 1) * 128], ALU.mult, [gzk, "acc"], [gzk])
                ydst = yaT_tiles[h // 4][0:64, (h % 4) * 128:(h % 4) * 128 + 128]
                tt("dve", ydst, gz[0:64, 0:128], pbc[0:64, 0:128], ALU.mult, [gzk, bk(bi_)], [f"bt{h // 4}"])
                if h == 3 and t + 1 < NT:
                    nx_af = norm_T(xe[TPC + (t + 1) * 128:TPC + (t + 2) * 128, :], 128, C("pregT"))
            grp = []
            for half in range(2):
                for h in range(8):
                    grp.append((bank(4 + half), yaT_tiles[h // 4][0:64, (h % 4) * 128:(h % 4) * 128 + 128],
                                wpa[0:64, h, half * 512:(half + 1) * 512], h == 0, h == 7))
            mm(grp, ["bt0", "bt1", "wpa"], [bk(4), bk(5)])
            for half, (wg, wgk) in enumerate(((wga0, "wA1"), (wga1, "wA2"))):
                proj(xn, xnk, 128, wg, wgk, 2 + half)
                sg = ft[2 + half]
                act(sg[:, :], bank(2 + half), AF.Sigmoid, [bk(2 + half)], [f"ft{2 + half}"])
                tt("dve", sg[:, :], bank(4 + half), sg[:, :], ALU.mult, [bk(4 + half), f"ft{2 + half}"], [f"ft{2 + half}"])
                stdma(ga_d[t * 128:(t + 1) * 128, half * 512:(half + 1) * 512], sg[:, :], [f"ft{2 + half}"])
        trk.barrier()

        chk(3)

        names = [("bq", C_BQ), ("bf", C_BF), ("bi", C_BI), ("bz", C_BZ), ("cq", C_CQ), ("cz", C_CZ),
                 ("gb0", C_GB), ("gb1", C_GB + 512), ("gc0", C_GC), ("gc1", C_GC + 512)]
        W3 = {}
        for i, (nm, col) in enumerate(names):
            W3[nm] = (carve_w(i), "w3" + nm)
            wload(W3[nm][0], w_in_r[:, :, col:col + 512], ["w3" + nm])
        gat = big[:, 20480:21504]
        wpb = big[:, 22528:24576].bitcast(BF16).rearrange("p (c n) -> p c n", c=4)
        wpc = big[:, 24576:26624].bitcast(BF16).rearrange("p (c n) -> p c n", c=4)
        wo = big[:, 26624:30720].bitcast(BF16).rearrange("p (c n) -> p c n", c=8)
        wload(wpb, w_pb.rearrange("(c p) n -> p c n", p=128), ["wpb"])
        wload(wpc, w_pc.rearrange("(c p) n -> p c n", p=128), ["wpc"])
        wload(wo, w_out.rearrange("(c p) n -> p c n", p=128), ["wo"])

        chk(3.5)

        def epilogue(n, xn, xnk, x_t, xk, ybT, ycT, ga_src, y_dst, gate_src=None):
            for (yT, w_, wk_, b0, key) in ((ybT, wpb, "wpb", 0, "bt7"), (ycT, wpc, "wpc", 2, "bt6")):
                grp = []
                for half in range(2):
                    for c in range(4):
                        grp.append((bank(b0 + half)[:n, :], yT[:, c, :n], w_[:, c, half * 512:(half + 1) * 512], c == 0, c == 3))
                mm(grp, [key, wk_], [bk(b0), bk(b0 + 1)])
            mg = [ft[8], ft[9]]
            gsrc, gkey = ga_src
            for half in range(2):
                sgb = ft[0]; sgc = ft[1]
                if gate_src is None:
                    proj(xn, xnk, n, W3[f"gb{half}"][0], W3[f"gb{half}"][1], 4)
                    act(sgb[:n, :], bank(4)[:n, :], AF.Sigmoid, [bk(4)], ["ft0"])
                    proj(xn, xnk, n, W3[f"gc{half}"][0], W3[f"gc{half}"][1], 5)
                    act(sgc[:n, :], bank(5)[:n, :], AF.Sigmoid, [bk(5)], ["ft1"])
                else:
                    act(sgb[:n, :], gate_src[:, C_GB + half * 512:C_GB + (half + 1) * 512], AF.Sigmoid, ["U"], ["ft0"])
                    act(sgc[:n, :], gate_src[:, C_GC + half * 512:C_GC + (half + 1) * 512], AF.Sigmoid, ["U"], ["ft1"])
                tt("dve", sgb[:n, :], bank(0 + half)[:n, :], sgb[:n, :], ALU.mult, [bk(half), "ft0"], ["ft0"])
                tt("dve", sgc[:n, :], bank(2 + half)[:n, :], sgc[:n, :], ALU.mult, [bk(2 + half), "ft1"], ["ft1"])
                tt("pool", sgb[:n, :], sgb[:n, :], sgc[:n, :], ALU.add, ["ft0", "ft1"], ["ft0"])
                tt("pool", bt[half][:n, :], sgb[:n, :], gsrc[:n, half * 512:(half + 1) * 512], ALU.add, ["ft0", gkey], [f"bt{half}"])
            pm = bankb(6).rearrange("p (c n) -> p c n", c=8)
            pm2 = bankb(7).rearrange("p (c n) -> p c n", c=8)
            transposes([(pm[:, c, :n], bt[0][:n, c * 128:(c + 1) * 128], identb[:n, :n]) for c in range(4)], ["bt0"], [bk(6)])
            transposes([(pm2[:, c, :n], bt[1][:n, c * 128:(c + 1) * 128], identb[:n, :n]) for c in range(4)], ["bt1"], [bk(7)])
            mT = bt[2].rearrange("p (c n) -> p c n", c=4)
            mT2 = bt[3].rearrange("p (c n) -> p c n", c=4)
            cp("act", mT[:, :, :n], pm[:, 0:4, :n], [bk(6)], ["bt2"])
            cp("dve", mT2[:, :, :n], pm2[:, 0:4, :n], [bk(7)], ["bt3"])
            grp = []
            for half in range(2):
                for c in range(8):
                    src_ = (mT if c < 4 else mT2)[:, c % 4, :n]
                    grp.append((bank(half)[:n, :], src_, wo[:, c, half * 512:(half + 1) * 512], c == 0, c == 7))
            mm(grp, ["bt2", "bt3", "wo"], [bk(0), bk(1)])
            ssz = sm[:n, 28:30]
            for half in range(2):
                act(ft[2][:n, :], bank(half)[:n, :], AF.Square, [bk(half)], ["ft2", "ssz"], accum_out=ssz[:, half:half + 1])
            tt("dve", ssz[:, 0:1], ssz[:, 0:1], ssz[:, 1:2], ALU.add, ["ssz"], ["ssz"])
            rsz = sm[:n, 30:31]
            rstd_from_ss(ssz[:, 0:1], rsz, n, 1.0 / D, "ssz")
            postg = C("postg")
            for half in range(2):
                stt("dve", mg[half][:n, :], bank(half)[:n, :], rsz, postg[:n, half * 512:(half + 1) * 512], ALU.mult, ALU.mult,
                    [bk(half), "sszr"], [f"ft{8 + half}"])
                tt("pool", mg[half][:n, :], mg[half][:n, :], x_t[:n, half * 512:(half + 1) * 512], ALU.add, [f"ft{8 + half}", xk], [f"ft{8 + half}"])
                stdma(y_dst[:, half * 512:(half + 1) * 512], mg[half][:n, :], [f"ft{8 + half}"])

        nx_p3 = norm_T(xe[TPC:TPC + 128, :], 128, C("pregT"))
        for t in range(NT):
            xn, xnk, x_t, xk = nx_p3
            ybT = hgrn_tile(xn, xnk, True, (W3["bq"], W3["bf"], W3["bi"], W3["bz"]))
            chk(4)
            proj(xn, xnk, 128, W3["cq"][0], W3["cq"][1], 0)
            proj(xn, xnk, 128, W3["cz"][0], W3["cz"][1], 1)
            cqb = bt[0]
            cp("act", cqb[:, :], bank(0), [bk(0)], ["bt0"])
            Gc = ft[0]
            act(Gc[:, :], bank(1), AF.Silu, [bk(1)], ["ft0"])
            pq = bankb(7).rearrange("p (c n) -> p c n", c=8)
            transposes([(pq[:, h, :], cqb[:, h * 128:(h + 1) * 128], identb) for h in range(4)], ["bt0"], [bk(7)])
            cqT = bt[1].rearrange("p (c n) -> p c n", c=4)
            cp("dve", cqT, pq[:, 0:4, :], [bk(7)], ["bt1"])
            if t + 1 < NT:
                nx_p3 = norm_T(xe[TPC + (t + 1) * 128:TPC + (t + 2) * 128, :], 128, C("pregT"))
            yc = bt[3]
            for h in range(4):
                sbk = 2 + rri("sbk3", 2)
                ps = bank(sbk)
                mm([(ps[:, mb * 128:(mb + 1) * 128], mkT[:, h, mb * 128:(mb + 1) * 128], cqT[:, h, :], True, True) for mb in range(2)],
                   ["mkT", "bt1"], [bk(sbk)])
                pi = rri("ptb", 2)
                act(ptb[pi][:, :], ps[:, 0:256], AF.Exp, [bk(sbk)], [f"ptb{pi}"], scale=float(128 ** -0.5))
                po = bank(5)[:, 0:129] if h % 2 == 0 else bank(5)[:, 256:385]
                mm([(po, ptb[pi][:, mb * 128:(mb + 1) * 128], mvp[:, mb, h, :], mb == 0, mb == 1) for mb in range(2)],
                   [f"ptb{pi}", "mvp"], [bk(5)])
                rlc = sm[:, 32 + h:33 + h]
                trk.op("dve", lambda e, rlc=rlc, po=po: e.reciprocal(out=rlc, in_=po[:, 128:129]), [bk(5)], [f"rlc{h}"])
                stt("dve", yc[:, h * 128:(h + 1) * 128], po[:, 0:128], rlc, Gc[:, h * 128:(h + 1) * 128], ALU.mult, ALU.mult,
                    [bk(5), f"rlc{h}", "ft0"], ["bt3"])
            pq = bankb(7).rearrange("p (c n) -> p c n", c=8)
            transposes([(pq[:, h, :], yc[:, h * 128:(h + 1) * 128], identb) for h in range(4)], ["bt3"], [bk(7)])
            ycT = bt[6].rearrange("p (c n) -> p c n", c=4)
            cp("act", ycT, pq[:, 0:4, :], [bk(7)], ["bt6"])
            chk(5)
            ld(gat, ga_d[t * 128:(t + 1) * 128, :], ["gat"])
            epilogue(128, xn, xnk, x_t, xk, ybT, ycT, (gat, "gat"), y_p[t * 128:(t + 1) * 128, :])
            chk(6)
        stdma(hg_p.rearrange("h k v -> k h v"), Sst[:], ["S"])
        trk.barrier()

        if do_sample:
            n = 16
            xn, xnk, x_t, xk = norm_T(xsamp[:, :], n, C("pregT"))
            U = big[0:16, 6144:6144 + NIN]
            for blk in range(NIN // 512):
                si = blk % 3
                wslot = carve_w(si)
                wkey = f"wS{si}"
                wload(wslot, w_in_r[:, :, blk * 512:(blk + 1) * 512], [wkey])
                b = rri("pjs", 4)
                proj(xn, xnk, n, wslot, wkey, b)
                cp("act" if blk % 2 else "dve", U[:, blk * 512:(blk + 1) * 512], bank(b)[:n, :], [bk(b)], ["U"])
            trk.barrier()
            Ssm = big[:, 0:4096].rearrange("p (b h v) -> p b h v", b=8, h=4)
            kvs = [big[:, 4096:5120], big[:, 5120:6144]]
            selb = big[0:16, 17408:19456].rearrange("p (b m) -> p b m", b=16)
            wpas = big[:, 19456:21504].bitcast(BF16).rearrange("p (c n) -> p c n", c=4)
            gas = big[0:16, 21504:22528]
            wload(wpas, w_pa.rearrange("(c p) n -> p c n", p=128), ["wpas"])
            ld(selb, selb_d[:, :, :], ["selb"])
            eye16v = C("eye16").rearrange("p (b c) -> p b c", b=16)
            rs_ = C("ropes")
            cosb = rs_[:n, 0:32].unsqueeze(1).unsqueeze(1).to_broadcast([n, 8, 2, 32])
            sinb = rs_[:n, 32:64].unsqueeze(1).unsqueeze(1).to_broadcast([n, 8, 2, 32])
            for g in range(3):
                rope(U[:, C_AQ + g * 512:C_AQ + (g + 1) * 512], ["U"], ft[2 + g][:n, :], f"ft{2 + g}", n, cosb, sinb, [])
                rope(U[:, C_AK + g * 512:C_AK + (g + 1) * 512], ["U"], ft[5 + g][:n, :], f"ft{5 + g}", n, cosb, sinb, [])
                stdma(nws[g][:, 0, :], ft[5 + g][:n, :], [f"ft{5 + g}"])
                stdma(nws[g][:, 1, :], U[:, C_AV + g * 512:C_AV + (g + 1) * 512], ["U"])

            def samp_attn(items, nh, dh, scale, pso, psl, pkeys):
                tot = len(items)

                def qbc(j):
                    _, q_j, qkey_j, b_j = items[j]
                    qb_ = rri("qbc", 4)
                    mm([(bank(qb_), selb[:, b_j, :], q_j, True, True)], [qkey_j, "selb"], [bk(qb_)])
                    return qb_
                qb_next = qbc(0)
                for it, (src, q_ap, qkey, b) in enumerate(items):
                    si = rri("kvs", 2)
                    kv = kvs[si]; kvk = f"kvs{si}"
                    ld(kv, src, [kvk])
                    qb = qb_next
                    if it + 1 < tot:
                        qb_next = qbc(it + 1)
                    pi_ = 8 + rri("prod", 2)
                    prod = ft[pi_]; pk_ = f"ft{pi_}"
                    tt("dve", prod[:, :], kv[:, 0:512], bank(qb), ALU.mult, [kvk, bk(qb)], [pk_])
                    sc_i = rri("s8", 2)
                    s8 = sm[:, 40 + 8 * sc_i:40 + 8 * sc_i + nh]; s8k = f"s8{sc_i}"
                    trk.op("dve", lambda e, s8=s8, prod=prod: e.reduce_sum(
                        out=s8, in_=prod[:, :].rearrange("p (h d) -> p h d", h=nh), axis=AX.X), [pk_], [s8k])
                    act(s8, s8, AF.Exp, [s8k], [s8k], scale=scale)
                    vi = rri("pv", 2)
                    pv = (bt[1], bt[5])[vi]; pvk = ("bt1", "bt5")[vi]
                    tt("pool", pv[:, :].rearrange("p (h d) -> p h d", h=nh), kv[:, 512:1024].rearrange("p (h d) -> p h d", h=nh),
                       s8.unsqueeze(2).to_broadcast([128, nh, dh]), ALU.mult, [kvk, s8k], [pvk])
                    mm([(pso, eye16b[:, b, :], pv[:, :], it == 0, it == tot - 1),
                        (psl, eye16v[:, b, :], s8, it == 0, it == tot - 1)], [pvk, s8k], pkeys)

            items = []
            for g, d in enumerate(DILS):
                for b in range(16):
                    src = cw[g][b, 0:127 * d + 1:d, :] if d > 1 else cw[g][b, :, :]
                    items.append((src, ft[2 + g][:n, :], f"ft{2 + g}", b))
            pso = bank(4)[:n, :]
            psl = bank(5)[:n, 0:8]
            samp_attn(items, 8, 64, 0.125, pso, psl, [bk(4), bk(5)])
            oS = gas
            oS = big[0:16, 28672 - 1024:28672 - 512]
            oS = ft[8]
            lS = sm[:n, 64:72]
            cp("dve", oS[:n, :], pso, [bk(4)], ["ft8"])
            cp("dve", lS, psl, [bk(5)], ["lS"])
            for g in range(3):
                tt("dve", ft[0][:n, :], ft[2 + g][:n, :], ft[5 + g][:n, :], ALU.mult, [f"ft{2 + g}", f"ft{5 + g}"], ["ft0"])
                pn = sm[:n, 72:80]
                trk.op("dve", lambda e, pn=pn: e.reduce_sum(out=pn, in_=ft[0][:n, :].rearrange("p (h d) -> p h d", h=8), axis=AX.X),
                       ["ft0"], ["pn"])
                act(pn, pn, AF.Exp, ["pn"], ["pn"], scale=0.125)
                tt("dve", ft[1][:n, :].rearrange("p (h d) -> p h d", h=8),
                   U[:, C_AV + g * 512:C_AV + (g + 1) * 512].rearrange("p (h d) -> p h d", h=8),
                   pn.unsqueeze(2).to_broadcast([n, 8, 64]), ALU.mult, ["U", "pn"], ["ft1"])
                tt("dve", oS[:n, :], oS[:n, :], ft[1][:n, :], ALU.add, ["ft8", "ft1"], ["ft8"])
                tt("dve", lS, lS, pn, ALU.add, ["lS", "pn"], ["lS"])
            trk.op("dve", lambda e: e.reciprocal(out=lS, in_=lS), ["lS"], ["lS"])
            Gz = ft[9]
            act(Gz[:n, :], U[:, C_AZ:C_AZ + 512], AF.Silu, ["U"], ["ft9"])
            tt("dve", oS[:n, :].rearrange("p (h d) -> p h d", h=8), oS[:n, :].rearrange("p (h d) -> p h d", h=8),
               lS.unsqueeze(2).to_broadcast([n, 8, 64]), ALU.mult, ["ft8", "lS"], ["ft8"])
            tt("dve", bt[0][:n, :], oS[:n, :], Gz[:n, :], ALU.mult, ["ft8", "ft9"], ["bt0"])
            pq = bankb(6).rearrange("p (c n) -> p c n", c=8)
            transposes([(pq[:, c, :n], bt[0][:n, c * 128:(c + 1) * 128], identb[:n, :n]) for c in range(4)], ["bt0"], [bk(6)])
            yaT = bt[4].rearrange("p (c n) -> p c n", c=4)
            cp("act", yaT[:, :, :n], pq[:, 0:4, :n], [bk(6)], ["bt4"])
            grp = []
            for half in range(2):
                for c in range(4):
                    grp.append((bank(half)[:n, :], yaT[:, c, :n], wpas[:, c, half * 512:(half + 1) * 512], c == 0, c == 3))
            mm(grp, ["bt4", "wpas"], [bk(0), bk(1)])
            for half in range(2):
                act(ft[0][:n, :], U[:, C_GA + half * 512:C_GA + (half + 1) * 512], AF.Sigmoid, ["U"], ["ft0"])
                tt("dve", gas[:, half * 512:(half + 1) * 512], bank(half)[:n, :], ft[0][:n, :], ALU.mult, [bk(half), "ft0"], ["gas"])

            fS = ft[0]; kinS = ft[1]
            act(fS[:n, :], U[:, C_BF:C_BF + 512], AF.Sigmoid, ["U"], ["ft0"])
            tt("dve", fS[:n, :], fS[:n, :], omlr[:n, :], ALU.mult, ["ft0"], ["ft0"])
            tt("dve", fS[:n, :], fS[:n, :], lbr[:n, :], ALU.add, ["ft0"], ["ft0"])
            ts("dve", kinS[:n, :], fS[:n, :], -1.0, 1.0, ALU.mult, ALU.add, ["ft0"], ["ft1"])
            pT = bank(6)[:, 0:128]
            transposes([(pT[:, h * 16:(h + 1) * 16], fS[:n, h * 128:(h + 1) * 128], identf[:n, :n]) for h in range(4)] +
                       [(pT[:, 64 + h * 16:64 + (h + 1) * 16], U[:, C_BQ + h * 128:C_BQ + (h + 1) * 128], identf[:n, :n]) for h in range(4)],
                       ["ft0", "U"], [bk(6)])
            fqT = ft[2]
            cp("dve", fqT[:, 0:128], pT, [bk(6)], ["ft2"])
            fT = fqT[:, 0:64].rearrange("p (h b) -> p h b", h=4)
            qT = fqT[:, 64:128].rearrange("p (h b) -> p h b", h=4)
            qTm = [ft[3], ft[4]]
            for hp in range(2):
                tt("dve", qTm[hp][:, :].rearrange("p (h b c) -> p h b c", h=2, b=16),
                   qT[:, 2 * hp:2 * hp + 2, :].unsqueeze(3).to_broadcast([128, 2, 16, 16]),
                   eye16v.unsqueeze(1).to_broadcast([128, 2, 16, 16]), ALU.mult, ["ft2"], [f"ft{3 + hp}"])
            pob = bank(7)[:n, :]
            zt = ft[8]
            trk.op("pool", lambda e: e.memset(zt[:, :], 0.0), (), ["ft8"])
            mm([(pob, eye16v[:, 0, :], zt[:, :], True, False)], ["ft8"], [bk(7)])
            vexp = ft[5]
            for half in range(2):
                for bl in range(8):
                    b = half * 8 + bl
                    ld(Ssm[:, bl, :, :], sh[b].rearrange("h k v -> k h v"), [f"Ssm{bl}"])
                for q4 in range(2):
                    for h in range(4):
                        b0 = half * 8 + q4 * 4
                        tt("dve", vexp[:n, :].rearrange("p (b v) -> p b v", b=4),
                           U[:, C_BI + h * 128:C_BI + (h + 1) * 128].unsqueeze(1).to_broadcast([n, 4, 128]),
                           identf[:n, b0:b0 + 4].unsqueeze(2).to_broadcast([n, 4, 128]), ALU.mult, ["U"], ["ft5"])
                        kb_i = rri("kvb", 4)
                        mm([(bank(kb_i), kinS[:n, h * 128:(h + 1) * 128], vexp[:n, :], True, True)], ["ft1", "ft5"], [bk(kb_i)])
                        for bi_ in range(4):
                            bl = q4 * 4 + bi_
                            b = half * 8 + bl
                            stt("dve", Ssm[:, bl, h, :], Ssm[:, bl, h, :], fT[:, h, b:b + 1], bank(kb_i)[:, bi_ * 128:(bi_ + 1) * 128],
                                ALU.mult, ALU.add, [f"Ssm{bl}", "ft2", bk(kb_i)], [f"Ssm{bl}"])
                for bl in range(8):
                    b = half * 8 + bl
                    mm([(pob[:, h * 128:(h + 1) * 128], qTm[h // 2][:, :].rearrange("p (h b c) -> p h b c", h=2, b=16)[:, h % 2, b, :],
                         Ssm[:, bl, h, :], False, b == 15) for h in range(4)],
                       [f"Ssm{bl}", "ft3", "ft4"], [bk(7)])
                    stdma(hg_s[b].rearrange("h k v -> k h v"), Ssm[:, bl, :, :], [f"Ssm{bl}"])
            Gb = ft[6]
            act(Gb[:n, :], U[:, C_BZ:C_BZ + 512], AF.Silu, ["U"], ["ft6"])
            tt("dve", ft[7][:n, :], Gb[:n, :], C("hg")[:n, :], ALU.mult, ["ft6"], ["ft7"])
            ybT = head_norm_gate(bank(7), bk(7), ft[7], "ft7", n)

            items = []
            for b in range(16):
                for mb in range(2):
                    items.append((cm[b, mb * 128:(mb + 1) * 128, :], U[:, C_CQ:C_CQ + 512], "U", b))
            pso2 = bank(4)[:n, :]
            psl2 = bank(5)[:n, 0:4]
            samp_attn(items, 4, 128, float(128 ** -0.5), pso2, psl2, [bk(4), bk(5)])
            lC = sm[:n, 80:84]
            trk.op("dve", lambda e: e.reciprocal(out=lC, in_=psl2), [bk(5)], ["lC"])
            Gc = ft[9]
            act(Gc[:n, :], U[:, C_CZ:C_CZ + 512], AF.Silu, ["U"], ["ft9"])
            tt("dve", ft[8][:n, :].rearrange("p (h d) -> p h d", h=4), pso2.rearrange("p (h d) -> p h d", h=4),
               lC.unsqueeze(2).to_broadcast([n, 4, 128]), ALU.mult, [bk(4), "lC"], ["ft8"])
            tt("dve", bt[3][:n, :], ft[8][:n, :], Gc[:n, :], ALU.mult, ["ft8", "ft9"], ["bt3"])
            pq = bankb(6).rearrange("p (c n) -> p c n", c=8)
            transposes([(pq[:, c, :n], bt[3][:n, c * 128:(c + 1) * 128], identb[:n, :n]) for c in range(4)], ["bt3"], [bk(6)])
            ycT = bt[6].rearrange("p (c n) -> p c n", c=4)
            cp("act", ycT[:, :, :n], pq[:, 0:4, :n], [bk(6)], ["bt6"])
            epilogue(n, xn, xnk, x_t, xk, ybT, ycT, (gas, "gas"), y_s[:, :], gate_src=U)

    return nc, trk, body


def _consts(core, inputs):
    cf = np.zeros((128, NCF), np.float32)

    def put(name, arr):
        a, b = _CF[name]
        cf[:arr.shape[0], a:b] = arr
    s = np.arange(128)
    U = (s[:, None] <= s[None, :]).astype(np.float32)
    put("identf", np.eye(128, dtype=np.float32))
    put("Um", U - U[:, 63:64])
    put("M2", (s[:, None] > s[None, :]).astype(np.float32))
    put("attm", U)
    ind = np.ones((128, 2), np.float32)
    ind[64:, 0] = 0.0
    put("ind", ind)
    put("pregT", inputs["norm_pre"][0].reshape(8, 128).T)
    put("memgT", inputs["mem_norm"][0].reshape(8, 128).T)
    put("postg", np.broadcast_to(inputs["norm_post"][0][None, :], (128, 1024)))
    put("hg", np.broadcast_to(inputs["hgrn_out_norm"][0][None, :], (128, 512)))
    put("l0", np.broadcast_to(inputs["hgrn_lb_logits"][0][None, :], (128, 512)))
    put("l1", np.broadcast_to(inputs["hgrn_lb_logits"][1][None, :], (128, 512)))
    cm = (np.arange(8) < core).astype(np.float32)
    put("cmask", np.broadcast_to(cm[None, :], (128, 8)))
    put("eye16", np.broadcast_to(np.eye(16, dtype=np.float32).reshape(1, 256), (128, 256)))
    inv = (10000.0 ** (-np.arange(0, 64, 2, dtype=np.float32) / np.float32(64))).astype(np.float32)
    ang = (np.float32(PAST) * inv).astype(np.float32)
    put("ropes", np.broadcast_to(np.concatenate([np.cos(ang), np.sin(ang)])[None, :].astype(np.float32), (128, 64)))
    cb = np.zeros((128, NCB), np.float32)
    cb[:, 0:128] = np.eye(128)
    j = s[:, None]
    i = s[None, :]
    cb[:, 128:256] = np.where(j >= i, 0.0, NEG)
    cb[:, 256:384] = np.where(j <= i, 0.0, NEG)
    cb[:, 384:512] = NEG if core == 0 else cb[:, 128:256]
    cb[:, 512:640] = (s[:, None] > s[None, :])
    cb[:, 640:768] = U - U[:, 63:64]
    cb[:, 768:770] = ind
    cb[:, 772:1028] = np.eye(16, dtype=np.float32).reshape(1, 256)
    rope = np.zeros((len(PA_BLOCKS), 128, 64), np.float32)
    for bi, (g, r, n) in enumerate(PA_BLOCKS):
        d = DILS[g]
        pos = core * TPC + d * (128 * n + np.arange(128)) + r
        angp = pos.astype(np.float32)[:, None] * inv[None, :]
        rope[bi, :, 0:32] = np.cos(angp)
        rope[bi, :, 32:64] = np.sin(angp)
    return cf, cb, rope


_PROG = {}


def kernel(**inputs):
    inp = {k: np.asarray(v) for k, v in inputs.items()}
    if "nc" not in _PROG:
        _PROG["nc"] = build_program()
    nc = _PROG["nc"]
    xp = inp["x_prompt"][0]
    selb = np.zeros((16, 16, 128), np.float32)
    for b in range(16):
        selb[b, b, :] = 1.0
    in_maps = []
    for c in range(NCORE):
        xe = np.zeros((2 * TPC, D), np.float32)
        if c > 0:
            xe[:TPC] = xp[(c - 1) * TPC:c * TPC]
        xe[TPC:] = xp[c * TPC:(c + 1) * TPC]
        cf, cb, rope = _consts(c, inp)
        sl = slice(16 * c, 16 * c + 16)
        xh = np.zeros((NPREV * 128, D), np.float32)
        if c > 0:
            xh[NPREV * 128 - c * TPC:] = xp[:c * TPC]
        in_maps.append({
            "xe": xe, "xh": xh,
            "xs": np.ascontiguousarray(inp["x_sample"][sl, 0, :]),
            "mem": np.ascontiguousarray(inp["mem_prompt"][0]),
            "cw0": np.ascontiguousarray(inp["cache_win128_kv"][0, sl].reshape(16, 128, 1024)),
            "cw1": np.ascontiguousarray(inp["cache_win512_kv"][0, sl].reshape(16, 512, 1024)),
            "cw2": np.ascontiguousarray(inp["cache_win2048_kv"][0, sl].reshape(16, 2048, 1024)),
            "sh": np.ascontiguousarray(inp["state_hgrn"][0, sl]),
            "cm": np.ascontiguousarray(inp["cache_mem_kv"][0, sl].reshape(16, 256, 1024)),
            "w_in": np.ascontiguousarray(inp["w_in"][0]),
            "w_mem": np.ascontiguousarray(inp["w_mem_kv"][0]),
            "w_pa": np.ascontiguousarray(inp["w_branch_a"][0]),
            "w_pb": np.ascontiguousarray(inp["w_branch_b"][0]),
            "w_pc": np.ascontiguousarray(inp["w_branch_c"][0]),
            "w_out": np.ascontiguousarray(inp["w_out"][0]),
            "cf": cf, "cb": cb, "rope": rope, "selb": selb,
        })
    if os.environ.get("K_NOSAMPLE"):
        for m in in_maps:
            for k in ("cw0", "cw1", "cw2", "sh", "cm"):
                m[k] = np.ascontiguousarray(m[k][0:1])
    res = run_bass_kernel_spmd(nc, in_maps, core_ids=list(range(NCORE)))
    R = res.results
    y_prompt = np.concatenate([R[c]["y_p"] for c in range(NCORE)], axis=0).reshape(1, T, D)
    y_sample = np.concatenate([R[c]["y_s"] for c in range(NCORE)], axis=0).reshape(128, 1, D)
    last = R[NCORE - 1]
    nw0 = last["nw0"].reshape(1, 1, 128, 2, 8, 64)
    nw1 = last["nw1"].reshape(1, 1, 512, 2, 8, 64)
    nw2 = last["nw2"].reshape(1, 1, 2048, 2, 8, 64)
    hgp = last["hg_p"].reshape(1, 1, 4, 128, 128)
    mkv = R[0]["mkv"].reshape(1, 1, 256, 2, 4, 128)
    nws = [np.concatenate([R[c][f"nws{g}"] for c in range(NCORE)], axis=0).reshape(1, 128, 1, 2, 8, 64) for g in range(3)]
    hgs = np.concatenate([R[c]["hg_s"] for c in range(NCORE)], axis=0).reshape(1, 128, 4, 128, 128)
    return (y_prompt, y_sample, nw0, nw1, nw2, hgp, mkv, nws[0], nws[1], nws[2], hgs)
```

```python
import numpy as np
from contextlib import ExitStack
import concourse.bass as bass
import concourse.mybir as mybir
from concourse.bass_utils import run_bass_kernel_spmd

F32 = mybir.dt.float32
BF16 = mybir.dt.bfloat16
AF = mybir.ActivationFunctionType
ALU = mybir.AluOpType
AX = mybir.AxisListType

NCORE = 8
T = 16384
TPC = 2048
NT = 16
D = 1024
NIN = 11264
EPS = 1e-6
PAST = 8192
DILS = (1, 4, 16)
NPREV = (NCORE - 1) * NT
NEG = -30000.0
C_AQ, C_AK, C_AV, C_AZ = 0, 1536, 3072, 4608
C_BQ, C_BF, C_BI, C_BZ = 5120, 5632, 6144, 6656
C_CQ, C_CZ = 7168, 7680
C_GA, C_GB, C_GC = 8192, 9216, 10240

_CF = {}
_off = 0
for _n, _w in (("identf", 128), ("Um", 128), ("M2", 128), ("attm", 128), ("ind", 2), ("pregT", 8), ("memgT", 8),
               ("postg", 1024), ("hg", 512), ("l0", 512), ("l1", 512), ("cmask", 8), ("eye16", 256),
               ("ropes", 64)):
    _CF[_n] = (_off, _off + _w)
    _off += _w
NCF = _off
NCB = 1028

def _pa_blocks():
    out = []
    for g, d in enumerate(DILS):
        nb = NT // d
        for r in range(d):
            for n in range(-1, nb):
                out.append((g, r, n))
    return out
PA_BLOCKS = _pa_blocks()
PA_INDEX = {b: i for i, b in enumerate(PA_BLOCKS)}


class Trk:
    ENGS = ("pe", "act", "dve", "pool", "sp")

    def __init__(self):
        self.streams = {e: [] for e in self.ENGS}
        self.cnt = {e: 0 for e in self.ENGS}
        self.known = {e: {} for e in self.ENGS}
        self.lastw = {}
        self.readers = {}
        self.lanegroups = {}
        self.lanecnt = {}
        self.lane_rr = {}

    def add_lanes(self, group, n):
        names = [f"{group}{i}" for i in range(n)]
        self.lanegroups[group] = names
        self.lane_rr[group] = 0
        for x in names:
            self.lanecnt[x] = 0

    def _deps(self, eng, r, w):
        deps = {}

        def add(k, c):
            if deps.get(k, 0) < c:
                deps[k] = c
        for key in r:
            lw = self.lastw.get(key)
            if lw is not None:
                add(lw[0], lw[1])
        for key in w:
            lw = self.lastw.get(key)
            if lw is not None:
                add(lw[0], lw[1])
            for k, c in self.readers.get(key, {}).items():
                add(k, c)
        waits = []
        for k, c in deps.items():
            if k == ("E", "pe") and eng == "pe":
                continue
            if self.known[eng].get(k, 0) >= c:
                continue
            self.known[eng][k] = c
            waits.append((k, c))
        return waits

    def _commit(self, k, c, r, w):
        for key in w:
            self.lastw[key] = (k, c)
            self.readers[key] = {}
        for key in r:
            d = self.readers.setdefault(key, {})
            if d.get(k, 0) < c:
                d[k] = c

    def op(self, eng, fn, r=(), w=()):
        waits = self._deps(eng, r, w)
        self.cnt[eng] += 1
        k = ("E", eng)
        self.streams[eng].append((waits, fn, k))
        self._commit(k, self.cnt[eng], r, w)

    def dma(self, q, group, fn, r=(), w=()):
        lanes = self.lanegroups[group]
        i = self.lane_rr[group]
        self.lane_rr[group] = (i + 1) % len(lanes)
        lane = lanes[i]
        waits = self._deps(q, r, w)
        k = ("L", lane)
        c = self.lanecnt[lane]
        if c > 0 and self.known[q].get(k, 0) < c:
            self.known[q][k] = c
            waits.append((k, c))
        self.lanecnt[lane] = c + 1
        self.streams[q].append((waits, fn, k))
        self._commit(k, c + 1, r, w)

    def barrier(self):
        for e in self.ENGS:
            waits = []
            for o in self.ENGS:
                if o == "sp" or self.cnt[o] == 0:
                    continue
                k = ("E", o)
                if o == e and e == "pe":
                    continue
                if self.known[e].get(k, 0) < self.cnt[o]:
                    self.known[e][k] = self.cnt[o]
                    waits.append((k, self.cnt[o]))
            for lane, c in self.lanecnt.items():
                k = ("L", lane)
                if c > 0 and self.known[e].get(k, 0) < c:
                    self.known[e][k] = c
                    waits.append((k, c))
            if waits:
                self.streams[e].append((waits, None, None))

    def emit(self, nc):
        with ExitStack() as st:
            sems = {}
            for e in self.ENGS:
                if e != "sp":
                    sems[("E", e)] = st.enter_context(nc.semaphore("e_" + e))
            for lane in self.lanecnt:
                sems[("L", lane)] = st.enter_context(nc.semaphore("l_" + lane))
            block = st.enter_context(nc.Block())

            def mult(k):
                return 16 if (k[0] == "L" and not k[1].startswith("cc")) else 1

            def run(eng, h):
                for waits, fn, mine in self.streams[eng]:
                    for k, c in waits:
                        h.wait_ge(sems[k], c * mult(k))
                    if fn is None:
                        continue
                    ins = fn(h)
                    if mine[0] == "L" and mine[1].startswith("cc"):
                        ins.then_inc(sems[mine])
                    else:
                        ins.then_inc(sems[mine], mult(mine))

            block.tensor(lambda h: run("pe", h))
            block.scalar(lambda h: run("act", h))
            block.vector(lambda h: run("dve", h))
            block.gpsimd(lambda h: run("pool", h))
            block.sync(lambda h: run("sp", h))


import os
class _Stop(Exception):
    pass


def build_program(do_sample=True):
    nc, trk, body = _build_program(do_sample)
    try:
        body()
    except _Stop:
        pass
    trk.barrier()
    trk.emit(nc)
    return nc


def _build_program(do_sample=True):
    do_sample = do_sample and not os.environ.get("K_NOSAMPLE")
    no_cc = bool(os.environ.get("K_NOCC"))
    stop = float(os.environ.get("K_STOP", "99"))
    nc = bass.Bass("TRN2", target_bir_lowering=False)
    trk = Trk()
    trk.add_lanes("ld", 6)
    trk.add_lanes("st", 8)
    trk.add_lanes("wl", 4)
    trk.add_lanes("cc", 1)

    def din(name, shape):
        return nc.dram_tensor(name, list(shape), F32, kind="ExternalInput").ap()

    def dout(name, shape):
        return nc.dram_tensor(name, list(shape), F32, kind="ExternalOutput").ap()

    xe = din("xe", (2 * TPC, D))
    SM = 16 if do_sample else 1
    xh = din("xh", (NPREV * 128, D))
    xsamp = din("xs", (16, D))
    mem = din("mem", (256, D))
    cw = [din("cw0", (SM, 128, 1024)), din("cw1", (SM, 512, 1024)), din("cw2", (SM, 2048, 1024))]
    sh = din("sh", (SM, 4, 128, 128))
    cm = din("cm", (SM, 256, 1024))
    w_in = din("w_in", (D, NIN))
    w_mem = din("w_mem", (D, 1024))
    w_pa = din("w_pa", (512, D))
    w_pb = din("w_pb", (512, D))
    w_pc = din("w_pc", (512, D))
    w_out = din("w_out", (D, D))
    cf_d = din("cf", (128, NCF))
    cb_d = din("cb", (128, NCB))
    rope_d = din("rope", (len(PA_BLOCKS), 128, 64))
    selb_d = din("selb", (16, 16, 128))

    y_p = dout("y_p", (TPC, D))
    y_s = dout("y_s", (16, D))
    nw = [dout("nw0", (128, 2, 512)), dout("nw1", (512, 2, 512)), dout("nw2", (2048, 2, 512))]
    hg_p = dout("hg_p", (4, 128, 128))
    mkv = dout("mkv", (256, 1024))
    nws = [dout("nws0", (16, 2, 512)), dout("nws1", (16, 2, 512)), dout("nws2", (16, 2, 512))]
    hg_s = dout("hg_s", (16, 4, 128, 128))

    ga_d = nc.dram_tensor("ga_scr", [TPC, D], F32).ap()

    w_in_r = w_in.rearrange("(c p) n -> p c n", p=128)

    st = ExitStack()

    def chk(v):
        if stop <= v:
            raise _Stop()

    def body():
        def sb(name, shape, dt=F32):
            return st.enter_context(nc.sbuf_tensor(name, list(shape), dt))

        cf = sb("cf_sb", (128, NCF))
        cb = sb("cb_sb", (128, NCB), BF16)
        big = sb("big", (128, 30720))
        xt = [sb("xt0", (128, D)), sb("xt1", (128, D))]
        xsb = sb("xsb", (128, D), BF16)
        xnT = [sb("xnT0", (128, 8, 128), BF16), sb("xnT1", (128, 8, 128), BF16)]
        ft = [sb(f"ft{i}", (128, 512)) for i in range(10)]
        bt = [sb(f"bt{i}", (128, 512), BF16) for i in range(8)]
        sm = sb("sm", (128, 128))
        Sst = sb("Sst", (128, 4, 128))
        Sbf = sb("Sbf", (128, 4, 128), BF16)
        Dacc = sb("Dacc", (128, 4))
        ropet = [sb("ropet0", (128, 64)), sb("ropet1", (128, 64))]
        ptb = [sb("ptb0", (128, 256), BF16), sb("ptb1", (128, 256), BF16)]
        _kb = 28000
        kTs = [big[:, _kb + i * 256:_kb + (i + 1) * 256].bitcast(BF16).rearrange("p (c n) -> p c n", c=4) for i in range(3)]
        _vb = _kb + 768
        vss = [big[:, _vb + i * 260:_vb + (i + 1) * 260].bitcast(BF16).rearrange("p (h d) -> p h d", h=8) for i in range(3)]
        _qb = _vb + 780
        qTs = [big[:, _qb + i * 256:_qb + (i + 1) * 256].bitcast(BF16).rearrange("p (c n) -> p c n", c=4) for i in range(2)]
        mkT = sb("mkT", (128, 4, 256), BF16)
        mvp = sb("mvp", (128, 2, 4, 129), BF16)
        gall = big[:, 8192:8192 + 8 * 516].rearrange("p (c n) -> p c n", c=8)
        psum = [st.enter_context(nc.psum_tensor(f"ps{i}", [128, 1024], F32)) for i in range(4)]

        def C(name):
            a, b = _CF[name]
            return cf[:, a:b]

        identf = C("identf")
        identb = cb[:, 0:128]
        mprev = cb[:, 128:256]
        mcur = cb[:, 256:384]
        mfirst = cb[:, 384:512]
        M2b = cb[:, 512:640]
        Umb = cb[:, 640:768]
        indb = cb[:, 768:770]
        eye16b = cb[:, 772:1028].rearrange("p (b c) -> p b c", b=16)
        lh = [sb(f"lh{i}", (128, 512), BF16) for i in range(4)]

        def bank(i):
            return psum[i // 2][:, (i % 2) * 512:(i % 2) * 512 + 512]

        def bankb(i):
            return bank(i).bitcast(BF16)

        _rr = {}

        def rr(name, items):
            i = _rr.get(name, 0)
            _rr[name] = i + 1
            return items[i % len(items)]

        def rri(name, n):
            i = _rr.get(name, 0)
            _rr[name] = i + 1
            return i % n

        def act(out, in_, func, r, w, **kw):
            trk.op("act", lambda e: e.activation(out=out, in_=in_, func=func, **kw), r, w)

        def tt(eng, out, in0, in1, op, r, w):
            trk.op(eng, lambda e: e.tensor_tensor(out=out, in0=in0, in1=in1, op=op), r, w)

        def ts(eng, out, in0, s1, s2, op0, op1, r, w):
            if s2 is None:
                trk.op(eng, lambda e: e.tensor_scalar(out=out, in0=in0, scalar1=s1, scalar2=None, op0=op0), r, w)
            else:
                trk.op(eng, lambda e: e.tensor_scalar(out=out, in0=in0, scalar1=s1, scalar2=s2, op0=op0, op1=op1), r, w)

        def stt(eng, out, in0, scalar, in1, op0, op1, r, w):
            trk.op(eng, lambda e: e.scalar_tensor_tensor(out=out, in0=in0, scalar=scalar, in1=in1, op0=op0, op1=op1), r, w)

        def cp(eng, out, in_, r, w):
            if eng == "act":
                trk.op("act", lambda e: e.copy(out=out, in_=in_), r, w)
            else:
                trk.op(eng, lambda e: e.tensor_copy(out=out, in_=in_), r, w)

        def ld(out, in_, w, q="sp", r=()):
            trk.dma(q, "ld", lambda e: e.dma_start(out=out, in_=in_), r, w)

        def stdma(out, in_, r, q="sp"):
            trk.dma(q, "st", lambda e: e.dma_start(out=out, in_=in_), r, ())

        def wload(out, in_, w):
            trk.dma("pool", "wl", lambda e: e.dma_start(out=out, in_=in_), (), w)

        _stg = [0]

        def wload_nocast(out, in_, w):
            i = _stg[0] % 2
            _stg[0] += 1
            n = 1
            for d_ in out.shape[1:]:
                n *= d_
            stage = big[:, 12288 + i * 4096:12288 + i * 4096 + n]
            if len(out.shape) == 3:
                stage = stage.rearrange("p (c n) -> p c n", c=out.shape[1])
            ld(stage, in_, [f"stage{i}"])
            cp("pool" if i else "dve", out, stage, [f"stage{i}"], w)

        def mm(groups, r, w):
            def f(e):
                ins = None
                for (o, l, rh, s0, s1) in groups:
                    ins = e.matmul(o, lhsT=l, rhs=rh, start=s0, stop=s1)
                return ins
            trk.op("pe", f, r, w)

        def transposes(items, r, w):
            def f(e):
                ins = None
                for (o, i, idn) in items:
                    ins = e.transpose(out=o, in_=i, identity=idn)
                return ins
            trk.op("pe", f, r, w)

        def rstd_from_ss(ss, rs, n, scale, key):
            act(rs, ss, AF.Ln, [key], [key + "r"], scale=scale, bias=EPS)
            act(rs, rs, AF.Exp, [key + "r"], [key + "r"], scale=-0.5)

        nt_i = [0]

        def norm_T(src, n, gT, keep_x=False):
            i = nt_i[0]
            nt_i[0] += 1
            x_t = xt[i % 2]
            xk = f"xt{i % 2}"
            xn = xnT[i % 2]
            xnk = f"xnT{i % 2}"
            ld(x_t[:n, :], src, [xk])
            ss = sm[:n, 0:1]
            rs = sm[:n, 1:2]
            act(xsb[:n, :], x_t[:n, :], AF.Square, [xk], ["xsb", "nss"], accum_out=ss)
            rstd_from_ss(ss, rs, n, 1.0 / D, "nss")
            act(xsb[:n, :], x_t[:n, :], AF.Copy, [xk, "nssr"], ["xsb"], scale=rs)
            b = rr("tb", [6, 7])
            pb = bankb(b).rearrange("p (c n) -> p c n", c=8)
            transposes([(pb[:, c, :n], xsb[:n, c * 128:(c + 1) * 128], identb[:n, :n]) for c in range(8)],
                       ["xsb"], [f"bank{b}"])
            tt("dve", xn[:, :, :n], pb[:, :, :n], gT.unsqueeze(2).to_broadcast([128, 8, n]), ALU.mult,
               [f"bank{b}"], [xnk])
            return xn, xnk, x_t, xk

        def proj(xn, xnk, n, wblk, wkey, b):
            mm([(bank(b)[:n, :], xn[:, c, :n], wblk[:, c, :], c == 0, c == 7) for c in range(8)],
               [xnk, wkey], [f"bank{b}"])

        def carve_w(idx):
            return big[:, idx * 2048:(idx + 1) * 2048].bitcast(BF16).rearrange("p (c n) -> p c n", c=8)

        def bk(i):
            return f"bank{i}"

        def rope4(ap):
            return ap.rearrange("p (h t f) -> p h t f", h=8, t=2)

        def rope(src, srckeys, out_ap, okey, n, cosb, sinb, tabkeys, A=None, B=None):
            A = ft[0] if A is None else A
            B = ft[1] if B is None else B
            ka, kb_ = "ft%d" % ft.index(A), "ft%d" % ft.index(B)
            tt("dve", rope4(A[:n, :]), rope4(src), cosb, ALU.mult, list(srckeys) + list(tabkeys), [ka])
            tt("dve", rope4(B[:n, :]), rope4(src), sinb, ALU.mult, list(srckeys) + list(tabkeys), [kb_])
            A4 = rope4(A[:n, :]); B4 = rope4(B[:n, :]); O4 = rope4(out_ap)
            tt("pool", O4[:, :, 0, :], A4[:, :, 0, :], B4[:, :, 1, :], ALU.subtract, [ka, kb_], [okey])
            tt("pool", O4[:, :, 1, :], A4[:, :, 1, :], B4[:, :, 0, :], ALU.add, [ka, kb_], [okey])

        ld(cf[:, :], cf_d[:, :], ["cf"])
        wload(cb[:, :], cb_d[:, :], ["cb"])
        trk.barrier()
        l0 = C("l0")
        l1 = C("l1")
        lbr = l0
        omlr = l1
        tt("dve", l0, l0, l1, ALU.subtract, ["cf"], ["cf"])
        act(l0, l0, AF.Sigmoid, ["cf"], ["cf"])
        ts("dve", l1, l0, -1.0, 1.0, ALU.mult, ALU.add, ["cf"], ["cf"])
        trk.op("pool", lambda e: e.memset(Sst[:], 0.0), (), ["S"])
        trk.op("pool", lambda e: e.memset(Dacc[:], 1.0), (), ["Dacc"])
        trk.op("pool", lambda e: e.memset(mvp[:, :, :, 128:129], 1.0), (), ["mvp"])
        trk.barrier()

        wm = [carve_w(0), carve_w(1)]
        w_mem_r = w_mem.rearrange("(c p) n -> p c n", p=128)
        for j in range(2):
            wload(wm[j], w_mem_r[:, :, j * 512:(j + 1) * 512], [f"wm{j}"])
        for mt in range(2):
            xn, xnk, _, _ = norm_T(mem[mt * 128:(mt + 1) * 128, :], 128, C("memgT"))
            for j in range(2):
                b = rri("pj", 4)
                proj(xn, xnk, 128, wm[j], f"wm{j}", b)
                fi = rri("ftA", 2)
                f = ft[fi]; fk = f"ft{fi}"
                cp("act", f[:, :], bank(b), [bk(b)], [fk])
                stdma(mkv[mt * 128:(mt + 1) * 128, j * 512:(j + 1) * 512], f[:, :], [fk])
                if j == 0:
                    cp("dve", bt[0][:, :], f[:, :], [fk], ["bt0"])
                    tb = 6 + rri("tb", 2)
                    pbk = bankb(tb).rearrange("p (c n) -> p c n", c=8)
                    transposes([(pbk[:, h, :], bt[0][:, h * 128:(h + 1) * 128], identb) for h in range(4)],
                               ["bt0"], [bk(tb)])
                    cp("dve", mkT[:, :, mt * 128:(mt + 1) * 128], pbk[:, 0:4, :], [bk(tb)], ["mkT"])
                else:
                    cp("dve", mvp[:, mt, :, 0:128], f[:, :].rearrange("p (h d) -> p h d", h=4), [fk], ["mvp"])
        trk.barrier()

        chk(0)

        def hgrn_tile(xn, xnk, full, wkeys, par=0, do_proj=True):
            wq, wf, wi, wz = wkeys
            Um = C("Um"); M2 = C("M2"); attm = C("attm"); ind = C("ind")
            fo = 4 * par; bo = 2 * par
            b_f, b_i = (0, 1) if par == 0 else (4, 5)
            if do_proj:
                proj(xn, xnk, 128, wf[0], wf[1], b_f)
                proj(xn, xnk, 128, wi[0], wi[1], b_i)
            sig = ft[fo + 0]; logf = ft[fo + 1]; kin = ft[fo + 2]
            k0, k1, k2, k3 = (f"ft{fo + i}" for i in range(4))
            kb0, kb1 = f"bt{bo}", f"bt{bo + 1}"
            eblk = f"ebl{par}"
            act(sig[:, :], bank(b_f), AF.Exp, [bk(b_f)], [k0], scale=-1.0)
            ts("dve", sig[:, :], sig[:, :], 1.0, None, ALU.add, None, [k0], [k0])
            trk.op("dve", lambda e, sig=sig: e.reciprocal(out=sig[:, :], in_=sig[:, :]), [k0], [k0])
            tt("dve", sig[:, :], sig[:, :], omlr, ALU.mult, [k0], [k0])
            tt("pool", sig[:, :], sig[:, :], lbr, ALU.add, [k0], [k0])
            act(logf[:, :], sig[:, :], AF.Ln, [k0], [k1])
            ts("pool", kin[:, :], sig[:, :], -1.0, 1.0, ALU.mult, ALU.add, [k0], [k2])
            vb = bt[bo]
            cp("act", vb[:, :], bank(b_i), [bk(b_i)], [kb0])
            lhi = lh[2 * par]; llo = lh[2 * par + 1]
            khi, klo = f"lh{2 * par}", f"lh{2 * par + 1}"
            cp("dve", lhi[:, :], logf[:, :], [k1], [khi])
            tt("pool", llo[:, :], logf[:, :], lhi[:, :], ALU.subtract, [k1, khi], [klo])
            mm([(bank(2), M2b, lhi[:, :], True, False), (bank(2), M2b, llo[:, :], False, True)], [khi, klo], [bk(2)])
            ek = ft[fo + 3]
            act(ek[:, :], bank(2), AF.Exp, [bk(2)], [k3])
            khat = bt[bo + 1]
            tt("dve", khat[:, :], kin[:, :], ek[:, :], ALU.mult, [k2, k3], [kb1])
            pbl = bank(3)[:, 0:8].rearrange("p (h t) -> p h t", t=2)
            grp_ = []
            for h in range(4):
                grp_.append((pbl[:, h, :], lhi[:, h * 128:(h + 1) * 128], indb, True, False))
                grp_.append((pbl[:, h, :], llo[:, h * 128:(h + 1) * 128], indb, False, True))
            mm(grp_, [khi, klo], [bk(3)])
            ebl = (sm[:, 8:16] if par == 0 else sm[:, 100:108]).rearrange("p (h t) -> p h t", t=2)
            act(ebl, pbl, AF.Exp, [bk(3)], [eblk])
            if full:
                chk(3.52)
                proj(xn, xnk, 128, wq[0], wq[1], 4)
                proj(xn, xnk, 128, wz[0], wz[1], 5)
                mm([(bank(2), Umb, lhi[:, :], True, False), (bank(2), Umb, llo[:, :], False, True)], [khi, klo], [bk(2)])
                eq = ft[3]; ekm = ft[4]
                act(eq[:, :], bank(2), AF.Exp, [bk(2)], ["ft3"])
                act(ekm[:, :], bank(2), AF.Exp, [bk(2)], ["ft4"], scale=-1.0)
                chk(3.55)
                qt = bt[2]; kt = bt[3]
                tt("dve", qt[:, :], bank(4), eq[:, :], ALU.mult, [bk(4), "ft3"], ["bt2"])
                tt("pool", kt[:, :], kin[:, :], ekm[:, :], ALU.mult, ["ft2", "ft4"], ["bt3"])
                G = ft[5]
                act(G[:, :], bank(5), AF.Silu, [bk(5)], ["ft5"])
                tt("pool", G[:, :], G[:, :], C("hg"), ALU.mult, ["ft5"], ["ft5"])
                chk(3.57)
                pq = bankb(6).rearrange("p (c n) -> p c n", c=8)
                pk2 = bankb(7).rearrange("p (c n) -> p c n", c=8)
                transposes([(pq[:, h, :], qt[:, h * 128:(h + 1) * 128], identb) for h in range(4)], ["bt2"], [bk(6)])
                transposes([(pk2[:, h, :], kt[:, h * 128:(h + 1) * 128], identb) for h in range(4)], ["bt3"], [bk(7)])
                qkT = bt[4].rearrange("p (c n) -> p c n", c=4)
                kkT = bt[5].rearrange("p (c n) -> p c n", c=4)
                cp("act", qkT, pq[:, 0:4, :], [bk(6)], ["bt4"])
                cp("dve", kkT, pk2[:, 0:4, :], [bk(7)], ["bt5"])
                chk(3.6)
                tt("dve", Sbf[:], Sst[:], ebl[:, :, 0:1].to_broadcast([128, 4, 128]), ALU.mult, ["S", eblk], ["Sbf"])
                pa = bank(2)
                mm([(pa[:, h * 128:(h + 1) * 128], kkT[:, h, :], qkT[:, h, :], True, True) for h in range(4)],
                   ["bt4", "bt5"], [bk(2)])
                attb = bt[6]
                tt("dve", attb[:, :].rearrange("p (h t) -> p h t", h=4), pa.rearrange("p (h t) -> p h t", h=4),
                   attm.unsqueeze(1).to_broadcast([128, 4, 128]), ALU.mult, [bk(2)], ["bt6"])
                chk(3.7)
                po = bank(4)
                grp = []
                for h in range(4):
                    grp.append((po[:, h * 128:(h + 1) * 128], qkT[:, h, :], Sbf[:, h, :], True, False))
                    grp.append((po[:, h * 128:(h + 1) * 128], attb[:, h * 128:(h + 1) * 128], vb[:, h * 128:(h + 1) * 128], False, True))
                mm(grp, ["bt4", "Sbf", "bt6", "bt0"], [bk(4)])
                chk(3.8)
            pP = bank(b_f)
            mm([(pP[:, h * 128:(h + 1) * 128], khat[:, h * 128:(h + 1) * 128], vb[:, h * 128:(h + 1) * 128], True, True)
                for h in range(4)], [kb1, kb0], [bk(b_f)])
            for h in range(4):
                stt("dve", Sst[:, h, :], Sst[:, h, :], ebl[:, h, 1:2], pP[:, h * 128:(h + 1) * 128], ALU.mult, ALU.add,
                    ["S", eblk, bk(b_f)], ["S"])
            if not full:
                return None
            return head_norm_gate(bank(4), bk(4), G, "ft5", 128)

        def head_norm_gate(po, pok, G, gk, n):
            ss4 = sm[:n, 16:20]
            rs4 = sm[:n, 20:24]
            for h in range(4):
                act(ft[6][:n, h * 128:(h + 1) * 128], po[:n, h * 128:(h + 1) * 128], AF.Square, [pok], ["ft6", "ss4"],
                    accum_out=ss4[:, h:h + 1])
            rstd_from_ss(ss4, rs4, n, 1.0 / 128, "ss4")
            yb = bt[2]
            for h in range(4):
                stt("dve", yb[:n, h * 128:(h + 1) * 128], po[:n, h * 128:(h + 1) * 128], rs4[:, h:h + 1],
                    G[:n, h * 128:(h + 1) * 128], ALU.mult, ALU.mult, [pok, "ss4r", gk], ["bt2"])
            pq = bankb(6).rearrange("p (c n) -> p c n", c=8)
            transposes([(pq[:, h, :n], yb[:n, h * 128:(h + 1) * 128], identb[:n, :n]) for h in range(4)], ["bt2"], [bk(6)])
            ybT = bt[7].rearrange("p (c n) -> p c n", c=4)
            cp("act", ybT[:, :, :n], pq[:, 0:4, :n], [bk(6)], ["bt7"])
            return ybT

        wf_b = carve_w(0); wi_b = carve_w(1)
        wload(wf_b, w_in_r[:, :, C_BF:C_BF + 512], ["w0"])
        wload(wi_b, w_in_r[:, :, C_BI:C_BI + 512], ["w1"])
        def p1_stage(par, stage):
            fo = 4 * par; bo = 2 * par
            b_f, b_i = (0, 1) if par == 0 else (4, 5)
            sig = ft[fo + 0]; logf = ft[fo + 1]; kin = ft[fo + 2]; ek = ft[fo + 3]
            k0, k1, k2, k3 = (f"ft{fo + i}" for i in range(4))
            vb = bt[bo]; khat = bt[bo + 1]
            kb0, kb1 = f"bt{bo}", f"bt{bo + 1}"
            lhi = lh[2 * par]; llo = lh[2 * par + 1]
            khi, klo = f"lh{2 * par}", f"lh{2 * par + 1}"
            eblk = f"ebl{par}"
            pbl = bank(3)[:, 0:8].rearrange("p (h t) -> p h t", t=2)
            ebl = (sm[:, 8:16] if par == 0 else sm[:, 100:108]).rearrange("p (h t) -> p h t", t=2)
            if stage == "g1":
                act(sig[:, :], bank(b_f), AF.Exp, [bk(b_f)], [k0], scale=-1.0)
                cp("act", vb[:, :], bank(b_i), [bk(b_i)], [kb0])
                ts("dve", sig[:, :], sig[:, :], 1.0, None, ALU.add, None, [k0], [k0])
                trk.op("dve", lambda e, sig=sig: e.reciprocal(out=sig[:, :], in_=sig[:, :]), [k0], [k0])
                tt("dve", sig[:, :], sig[:, :], omlr, ALU.mult, [k0], [k0])
                tt("pool", sig[:, :], sig[:, :], lbr, ALU.add, [k0], [k0])
            elif stage == "g2":
                act(logf[:, :], sig[:, :], AF.Ln, [k0], [k1])
                ts("pool", kin[:, :], sig[:, :], -1.0, 1.0, ALU.mult, ALU.add, [k0], [k2])
                cp("dve", lhi[:, :], logf[:, :], [k1], [khi])
                tt("pool", llo[:, :], logf[:, :], lhi[:, :], ALU.subtract, [k1, khi], [klo])
            elif stage == "mid":
                mm([(bank(2), M2b, lhi[:, :], True, False), (bank(2), M2b, llo[:, :], False, True)], [khi, klo], [bk(2)])
                grp_ = []
                for h in range(4):
                    grp_.append((pbl[:, h, :], lhi[:, h * 128:(h + 1) * 128], indb, True, False))
                    grp_.append((pbl[:, h, :], llo[:, h * 128:(h + 1) * 128], indb, False, True))
                mm(grp_, [khi, klo], [bk(3)])
            else:
                act(ek[:, :], bank(2), AF.Exp, [bk(2)], [k3])
                tt("dve", khat[:, :], kin[:, :], ek[:, :], ALU.mult, [k2, k3], [kb1])
                act(ebl, pbl, AF.Exp, [bk(3)], [eblk])
                pP = bank(2)
                mm([(pP[:, h * 128:(h + 1) * 128], khat[:, h * 128:(h + 1) * 128], vb[:, h * 128:(h + 1) * 128], True, True)
                    for h in range(4)], [kb1, kb0], [bk(2)])
                for h in range(4):
                    stt("dve", Sst[:, h, :], Sst[:, h, :], ebl[:, h, 1:2], pP[:, h * 128:(h + 1) * 128], ALU.mult, ALU.add,
                        ["S", eblk, bk(2)], ["S"])

        def p1_proj(xq, par):
            b_f, b_i = (0, 1) if par == 0 else (4, 5)
            proj(xq[0], xq[1], 128, wf_b, "w0", b_f)
            proj(xq[0], xq[1], 128, wi_b, "w1", b_i)

        xq = [norm_T(xh[0:128, :], 128, C("pregT")), norm_T(xh[128:256, :], 128, C("pregT"))]
        p1_proj(xq[0], 0)
        for t in range(NPREV):
            par = t % 2
            if t + 1 < NPREV:
                p1_proj(xq[(t + 1) % 2], (t + 1) % 2)
            p1_stage(par, "g1")
            if t + 2 < NPREV:
                xq[par] = norm_T(xh[(t + 2) * 128:(t + 3) * 128, :], 128, C("pregT"))
            p1_stage(par, "g2")
            p1_stage(par, "mid")
            p1_stage(par, "back")
        trk.barrier()
        chk(1)

        acc = big[:, 0:16384].rearrange("p (h t) -> p h t", h=8)
        trk.op("pool", lambda e: e.memset(big[0:65, 0:16384], 0.0), (), ["acc"])
        for i in range(3):
            trk.op("pool", lambda e, i=i: e.memset(vss[i][:, :, 64:65], 1.0), (), [f"vss{i}"])
        WB = 8
        for g, d in enumerate(DILS):
            wq = carve_w(WB + 0); wk = carve_w(WB + 1); wv = carve_w(WB + 2)
            wload(wq, w_in_r[:, :, C_AQ + g * 512:C_AQ + (g + 1) * 512], ["wA0"])
            wload(wk, w_in_r[:, :, C_AK + g * 512:C_AK + (g + 1) * 512], ["wA1"])
            wload(wv, w_in_r[:, :, C_AV + g * 512:C_AV + (g + 1) * 512], ["wA2"])
            nb = NT // d
            wb_tokens = 128 * d
            kv_i = 0
            blocks = [(r_, n_) for r_ in range(d) for n_ in range(-1, nb)]

            def pa_norm(bi_):
                r_, n_ = blocks[bi_]
                st_ = TPC + d * 128 * n_ + r_
                src_ = xe[st_:st_ + 127 * d + 1:d, :] if d > 1 else xe[st_:st_ + 128, :]
                return norm_T(src_, 128, C("pregT"))
            nx_pa = pa_norm(0)
            for bidx, (r, n) in enumerate(blocks):
                if True:
                    if n == -1:
                        prev = None
                    own = n >= 0
                    xn, xnk = nx_pa[0], nx_pa[1]
                    pre_done = False
                    ri = rri("ropet", 2); rt = ropet[ri]; rk = f"ropet{ri}"
                    ld(rt[:, :], rope_d[PA_INDEX[(g, r, n)], :, :], [rk])
                    cosb = rt[:, 0:32].unsqueeze(1).unsqueeze(1).to_broadcast([128, 8, 2, 32])
                    sinb = rt[:, 32:64].unsqueeze(1).unsqueeze(1).to_broadcast([128, 8, 2, 32])
                    ks = kv_i % 3
                    kv_i += 1
                    proj(xn, xnk, 128, wk, "wA1", 1)
                    proj(xn, xnk, 128, wv, "wA2", 2)
                    if own:
                        proj(xn, xnk, 128, wq, "wA0", 0)
                    rope(bank(1), [bk(1)], ft[2][:, :], "ft2", 128, cosb, sinb, [rk])
                    need_out = own and (d * (128 * n + 127) + r >= TPC - wb_tokens)
                    if need_out:
                        t0 = d * 128 * n + r - (TPC - wb_tokens)
                        dst = nw[g][t0:t0 + 127 * d + 1:d, 0, :] if d > 1 else nw[g][t0:t0 + 128, 0, :]
                        stdma(dst, ft[2][:, :], ["ft2"])
                    cp("pool", bt[0][:, :], ft[2][:, :], ["ft2"], ["bt0"])
                    pk = bankb(6).rearrange("p (c n) -> p c n", c=8)
                    transposes([(pk[:, c, :], bt[0][:, c * 128:(c + 1) * 128], identb) for c in range(4)], ["bt0"], [bk(6)])
                    cp("act", kTs[ks], pk[:, 0:4, :], [bk(6)], [f"kTs{ks}"])
                    if need_out:
                        cp("act", ft[3][:, :], bank(2), [bk(2)], ["ft3"])
                        dst = nw[g][t0:t0 + 127 * d + 1:d, 1, :] if d > 1 else nw[g][t0:t0 + 128, 1, :]
                        stdma(dst, ft[3][:, :], ["ft3"])
                    cp("act", vss[ks][:, :, 0:64], bank(2).rearrange("p (h d) -> p h d", h=8), [bk(2)], [f"vss{ks}"])
                    if own:
                        rope(bank(0), [bk(0)], ft[6][:, :], "ft6", 128, cosb, sinb, [rk], A=ft[4], B=ft[5])
                        cp("pool", bt[1][:, :], ft[6][:, :], ["ft6"], ["bt1"])
                        pq = bankb(7).rearrange("p (c n) -> p c n", c=8)
                        transposes([(pq[:, c, :], bt[1][:, c * 128:(c + 1) * 128], identb) for c in range(4)], ["bt1"], [bk(7)])
                        qi = rri("qTs", 2)
                        cp("dve", qTs[qi], pq[:, 0:4, :], [bk(7)], [f"qTs{qi}"])
                        mp = mfirst if n == 0 else mprev
                        kp, kc = prev, ks
                        tok0 = d * 128 * n + r
                        for h in range(8):
                            pr, off = h // 2, (h % 2) * 64
                            sbk = 3 + rri("sbk", 2)
                            ps = bank(sbk)
                            mm([(ps[:, 0:128], kTs[kp][off:off + 64, pr, :], qTs[qi][off:off + 64, pr, :], True, False),
                                (ps[:, 0:128], identb, mp, False, True),
                                (ps[:, 128:256], kTs[kc][off:off + 64, pr, :], qTs[qi][off:off + 64, pr, :], True, False),
                                (ps[:, 128:256], identb, mcur, False, True)],
                               [f"kTs{kp}", f"kTs{kc}", f"qTs{qi}"], [bk(sbk)])
                            pi = rri("ptb", 2)
                            act(ptb[pi][:, :], ps[:, 0:256], AF.Exp, [bk(sbk)], [f"ptb{pi}"], scale=0.125)
                            pob_ = (5, 0)[h % 2]
                            po = bank(pob_)[0:65, 0:128]
                            pok = bk(pob_)
                            mm([(po, vss[kp][:, h, :], ptb[pi][:, 0:128], True, False),
                                (po, vss[kc][:, h, :], ptb[pi][:, 128:256], False, True)],
                               [f"vss{kp}", f"vss{kc}", f"ptb{pi}"], [pok])
                            a_sl = acc[0:65, h, tok0:tok0 + 127 * d + 1:d] if d > 1 else acc[0:65, h, tok0:tok0 + 128]
                            tt("dve", a_sl, po, a_sl, ALU.add, [pok, "acc"], ["acc"])
                            if h == 3 and bidx + 1 < len(blocks):
                                nx_pa = pa_norm(bidx + 1)
                                pre_done = True
                    if not pre_done and bidx + 1 < len(blocks):
                        nx_pa = pa_norm(bidx + 1)
                    prev = ks
        trk.barrier()

        chk(2)

        waz = carve_w(WB + 0); wga0 = carve_w(WB + 1); wga1 = carve_w(WB + 2)
        wload(waz, w_in_r[:, :, C_AZ:C_AZ + 512], ["wA0"])
        wload(wga0, w_in_r[:, :, C_GA:C_GA + 512], ["wA1"])
        wload(wga1, w_in_r[:, :, C_GA + 512:C_GA + 1024], ["wA2"])
        wpa = big[:, 22528:26624].bitcast(BF16).rearrange("p (h n) -> p h n", h=8)
        wload(wpa[0:64, :, :], w_pa.rearrange("(h p) n -> p h n", p=64), ["wpa"])
        ones64 = big[64:65, 26624:26688]
        trk.op("pool", lambda e: e.memset(ones64, 1.0), (), ["ones64"])
        rl = big[64:65, 26688:26688 + 1024]
        yaT_tiles = [bt[0], bt[1]]
        nx_af = norm_T(xe[TPC:TPC + 128, :], 128, C("pregT"))
        for t in range(NT):
            xn, xnk = nx_af[0], nx_af[1]
            trk.op("dve", lambda e, t=t: e.reciprocal(out=rl.rearrange("p (h n) -> p h n", h=8),
                                                      in_=acc[64:65, :, t * 128:(t + 1) * 128]), ["acc"], ["rl"])
            for h in range(8):
                zi = rri("azb", 2)
                pz = bank(zi)
                mm([(pz[0:64, 0:128], waz[:, c, h * 64:(h + 1) * 64], xn[:, c, :], c == 0, c == 7) for c in range(8)],
                   [xnk, "wA0"], [bk(zi)])
                gi = rri("gz", 2)
                gz = ft[gi]; gzk = f"ft{gi}"
                act(gz[0:64, 0:128], pz[0:64, 0:128], AF.Silu, [bk(zi)], [gzk])
                bi_ = 2 + (h % 2)
                pbc = bank(bi_)
                mm([(pbc[0:64, 0:128], ones64, rl[:, h * 128:(h + 1) * 128], True, True)], ["ones64", "rl"], [bk(bi_)])
                tt("pool", gz[0:64, 0:128], gz[0:64, 0:128], acc[0:64, h, t * 128:(t + 1) * 128], ALU.mult, [gzk, "acc"], [gzk])
                ydst = yaT_tiles[h // 4][0:64, (h % 4) * 128:(h % 4) * 128 + 128]
                tt("dve", ydst, gz[0:64, 0:128], pbc[0:64, 0:128], ALU.mult, [gzk, bk(bi_)], [f"bt{h // 4}"])
                if h == 3 and t + 1 < NT:
                    nx_af = norm_T(xe[TPC + (t + 1) * 128:TPC + (t + 2) * 128, :], 128, C("pregT"))
            grp = []
            for half in range(2):
                for h in range(8):
                    grp.append((bank(4 + half), yaT_tiles[h // 4][0:64, (h % 4) * 128:(h % 4) * 128 + 128],
                                wpa[0:64, h, half * 512:(half + 1) * 512], h == 0, h == 7))
            mm(grp, ["bt0", "bt1", "wpa"], [bk(4), bk(5)])
            for half, (wg, wgk) in enumerate(((wga0, "wA1"), (wga1, "wA2"))):
                proj(xn, xnk, 128, wg, wgk, 2 + half)
                sg = ft[2 + half]
                act(sg[:, :], bank(2 + half), AF.Sigmoid, [bk(2 + half)], [f"ft{2 + half}"])
                tt("dve", sg[:, :], bank(4 + half), sg[:, :], ALU.mult, [bk(4 + half), f"ft{2 + half}"], [f"ft{2 + half}"])
                stdma(ga_d[t * 128:(t + 1) * 128, half * 512:(half + 1) * 512], sg[:, :], [f"ft{2 + half}"])
        trk.barrier()

        chk(3)

        names = [("bq", C_BQ), ("bf", C_BF), ("bi", C_BI), ("bz", C_BZ), ("cq", C_CQ), ("cz", C_CZ),
                 ("gb0", C_GB), ("gb1", C_GB + 512), ("gc0", C_GC), ("gc1", C_GC + 512)]
        W3 = {}
        for i, (nm, col) in enumerate(names):
            W3[nm] = (carve_w(i), "w3" + nm)
            wload(W3[nm][0], w_in_r[:, :, col:col + 512], ["w3" + nm])
        gat = big[:, 20480:21504]
        wpb = big[:, 22528:24576].bitcast(BF16).rearrange("p (c n) -> p c n", c=4)
        wpc = big[:, 24576:26624].bitcast(BF16).rearrange("p (c n) -> p c n", c=4)
        wo = big[:, 26624:30720].bitcast(BF16).rearrange("p (c n) -> p c n", c=8)
        wload(wpb, w_pb.rearrange("(c p) n -> p c n", p=128), ["wpb"])
        wload(wpc, w_pc.rearrange("(c p) n -> p c n", p=128), ["wpc"])
        wload(wo, w_out.rearrange("(c p) n -> p c n", p=128), ["wo"])

        chk(3.5)

        def epilogue(n, xn, xnk, x_t, xk, ybT, ycT, ga_src, y_dst, gate_src=None):
            for (yT, w_, wk_, b0, key) in ((ybT, wpb, "wpb", 0, "bt7"), (ycT, wpc, "wpc", 2, "bt6")):
                grp = []
                for half in range(2):
                    for c in range(4):
                        grp.append((bank(b0 + half)[:n, :], yT[:, c, :n], w_[:, c, half * 512:(half + 1) * 512], c == 0, c == 3))
                mm(grp, [key, wk_], [bk(b0), bk(b0 + 1)])
            mg = [ft[8], ft[9]]
            gsrc, gkey = ga_src
            for half in range(2):
                sgb = ft[0]; sgc = ft[1]
                if gate_src is None:
                    proj(xn, xnk, n, W3[f"gb{half}"][0], W3[f"gb{half}"][1], 4)
                    act(sgb[:n, :], bank(4)[:n, :], AF.Sigmoid, [bk(4)], ["ft0"])
                    proj(xn, xnk, n, W3[f"gc{half}"][0], W3[f"gc{half}"][1], 5)
                    act(sgc[:n, :], bank(5)[:n, :], AF.Sigmoid, [bk(5)], ["ft1"])
                else:
                    act(sgb[:n, :], gate_src[:, C_GB + half * 512:C_GB + (half + 1) * 512], AF.Sigmoid, ["U"], ["ft0"])
                    act(sgc[:n, :], gate_src[:, C_GC + half * 512:C_GC + (half + 1) * 512], AF.Sigmoid, ["U"], ["ft1"])
                tt("dve", sgb[:n, :], bank(0 + half)[:n, :], sgb[:n, :], ALU.mult, [bk(half), "ft0"], ["ft0"])
                tt("dve", sgc[:n, :], bank(2 + half)[:n, :], sgc[:n, :], ALU.mult, [bk(2 + half), "ft1"], ["ft1"])
                tt("pool", sgb[:n, :], sgb[:n, :], sgc[:n, :], ALU.add, ["ft0", "ft1"], ["ft0"])
                tt("pool", bt[half][:n, :], sgb[:n, :], gsrc[:n, half * 512:(half + 1) * 512], ALU.add, ["ft0", gkey], [f"bt{half}"])
            pm = bankb(6).rearrange("p (c n) -> p c n", c=8)
            pm2 = bankb(7).rearrange("p (c n) -> p c n", c=8)
            transposes([(pm[:, c, :n], bt[0][:n, c * 128:(c + 1) * 128], identb[:n, :n]) for c in range(4)], ["bt0"], [bk(6)])
            transposes([(pm2[:, c, :n], bt[1][:n, c * 128:(c + 1) * 128], identb[:n, :n]) for c in range(4)], ["bt1"], [bk(7)])
            mT = bt[2].rearrange("p (c n) -> p c n", c=4)
            mT2 = bt[3].rearrange("p (c n) -> p c n", c=4)
            cp("act", mT[:, :, :n], pm[:, 0:4, :n], [bk(6)], ["bt2"])
            cp("dve", mT2[:, :, :n], pm2[:, 0:4, :n], [bk(7)], ["bt3"])
            grp = []
            for half in range(2):
                for c in range(8):
                    src_ = (mT if c < 4 else mT2)[:, c % 4, :n]
                    grp.append((bank(half)[:n, :], src_, wo[:, c, half * 512:(half + 1) * 512], c == 0, c == 7))
            mm(grp, ["bt2", "bt3", "wo"], [bk(0), bk(1)])
            ssz = sm[:n, 28:30]
            for half in range(2):
                act(ft[2][:n, :], bank(half)[:n, :], AF.Square, [bk(half)], ["ft2", "ssz"], accum_out=ssz[:, half:half + 1])
            tt("dve", ssz[:, 0:1], ssz[:, 0:1], ssz[:, 1:2], ALU.add, ["ssz"], ["ssz"])
            rsz = sm[:n, 30:31]
            rstd_from_ss(ssz[:, 0:1], rsz, n, 1.0 / D, "ssz")
            postg = C("postg")
            for half in range(2):
                stt("dve", mg[half][:n, :], bank(half)[:n, :], rsz, postg[:n, half * 512:(half + 1) * 512], ALU.mult, ALU.mult,
                    [bk(half), "sszr"], [f"ft{8 + half}"])
                tt("pool", mg[half][:n, :], mg[half][:n, :], x_t[:n, half * 512:(half + 1) * 512], ALU.add, [f"ft{8 + half}", xk], [f"ft{8 + half}"])
                stdma(y_dst[:, half * 512:(half + 1) * 512], mg[half][:n, :], [f"ft{8 + half}"])

        nx_p3 = norm_T(xe[TPC:TPC + 128, :], 128, C("pregT"))
        for t in range(NT):
            xn, xnk, x_t, xk = nx_p3
            ybT = hgrn_tile(xn, xnk, True, (W3["bq"], W3["bf"], W3["bi"], W3["bz"]))
            chk(4)
            proj(xn, xnk, 128, W3["cq"][0], W3["cq"][1], 0)
            proj(xn, xnk, 128, W3["cz"][0], W3["cz"][1], 1)
            cqb = bt[0]
            cp("act", cqb[:, :], bank(0), [bk(0)], ["bt0"])
            Gc = ft[0]
            act(Gc[:, :], bank(1), AF.Silu, [bk(1)], ["ft0"])
            pq = bankb(7).rearrange("p (c n) -> p c n", c=8)
            transposes([(pq[:, h, :], cqb[:, h * 128:(h + 1) * 128], identb) for h in range(4)], ["bt0"], [bk(7)])
            cqT = bt[1].rearrange("p (c n) -> p c n", c=4)
            cp("dve", cqT, pq[:, 0:4, :], [bk(7)], ["bt1"])
            if t + 1 < NT:
                nx_p3 = norm_T(xe[TPC + (t + 1) * 128:TPC + (t + 2) * 128, :], 128, C("pregT"))
            yc = bt[3]
            for h in range(4):
                sbk = 2 + rri("sbk3", 2)
                ps = bank(sbk)
                mm([(ps[:, mb * 128:(mb + 1) * 128], mkT[:, h, mb * 128:(mb + 1) * 128], cqT[:, h, :], True, True) for mb in range(2)],
                   ["mkT", "bt1"], [bk(sbk)])
                pi = rri("ptb", 2)
                act(ptb[pi][:, :], ps[:, 0:256], AF.Exp, [bk(sbk)], [f"ptb{pi}"], scale=float(128 ** -0.5))
                po = bank(5)[:, 0:129] if h % 2 == 0 else bank(5)[:, 256:385]
                mm([(po, ptb[pi][:, mb * 128:(mb + 1) * 128], mvp[:, mb, h, :], mb == 0, mb == 1) for mb in range(2)],
                   [f"ptb{pi}", "mvp"], [bk(5)])
                rlc = sm[:, 32 + h:33 + h]
                trk.op("dve", lambda e, rlc=rlc, po=po: e.reciprocal(out=rlc, in_=po[:, 128:129]), [bk(5)], [f"rlc{h}"])
                stt("dve", yc[:, h * 128:(h + 1) * 128], po[:, 0:128], rlc, Gc[:, h * 128:(h + 1) * 128], ALU.mult, ALU.mult,
                    [bk(5), f"rlc{h}", "ft0"], ["bt3"])
            pq = bankb(7).rearrange("p (c n) -> p c n", c=8)
            transposes([(pq[:, h, :], yc[:, h * 128:(h + 1) * 128], identb) for h in range(4)], ["bt3"], [bk(7)])
            ycT = bt[6].rearrange("p (c n) -> p c n", c=4)
            cp("act", ycT, pq[:, 0:4, :], [bk(7)], ["bt6"])
            chk(5)
            ld(gat, ga_d[t * 128:(t + 1) * 128, :], ["gat"])
            epilogue(128, xn, xnk, x_t, xk, ybT, ycT, (gat, "gat"), y_p[t * 128:(t + 1) * 128, :])
            chk(6)
        stdma(hg_p.rearrange("h k v -> k h v"), Sst[:], ["S"])
        trk.barrier()

        if do_sample:
            n = 16
            xn, xnk, x_t, xk = norm_T(xsamp[:, :], n, C("pregT"))
            U = big[0:16, 6144:6144 + NIN]
            for blk in range(NIN // 512):
                si = blk % 3
                wslot = carve_w(si)
                wkey = f"wS{si}"
                wload(wslot, w_in_r[:, :, blk * 512:(blk + 1) * 512], [wkey])
                b = rri("pjs", 4)
                proj(xn, xnk, n, wslot, wkey, b)
                cp("act" if blk % 2 else "dve", U[:, blk * 512:(blk + 1) * 512], bank(b)[:n, :], [bk(b)], ["U"])
            trk.barrier()
            Ssm = big[:, 0:4096].rearrange("p (b h v) -> p b h v", b=8, h=4)
            kvs = [big[:, 4096:5120], big[:, 5120:6144]]
            selb = big[0:16, 17408:19456].rearrange("p (b m) -> p b m", b=16)
            wpas = big[:, 19456:21504].bitcast(BF16).rearrange("p (c n) -> p c n", c=4)
            gas = big[0:16, 21504:22528]
            wload(wpas, w_pa.rearrange("(c p) n -> p c n", p=128), ["wpas"])
            ld(selb, selb_d[:, :, :], ["selb"])
            eye16v = C("eye16").rearrange("p (b c) -> p b c", b=16)
            rs_ = C("ropes")
            cosb = rs_[:n, 0:32].unsqueeze(1).unsqueeze(1).to_broadcast([n, 8, 2, 32])
            sinb = rs_[:n, 32:64].unsqueeze(1).unsqueeze(1).to_broadcast([n, 8, 2, 32])
            for g in range(3):
                rope(U[:, C_AQ + g * 512:C_AQ + (g + 1) * 512], ["U"], ft[2 + g][:n, :], f"ft{2 + g}", n, cosb, sinb, [])
                rope(U[:, C_AK + g * 512:C_AK + (g + 1) * 512], ["U"], ft[5 + g][:n, :], f"ft{5 + g}", n, cosb, sinb, [])
                stdma(nws[g][:, 0, :], ft[5 + g][:n, :], [f"ft{5 + g}"])
                stdma(nws[g][:, 1, :], U[:, C_AV + g * 512:C_AV + (g + 1) * 512], ["U"])

            def samp_attn(items, nh, dh, scale, pso, psl, pkeys):
                tot = len(items)

                def qbc(j):
                    _, q_j, qkey_j, b_j = items[j]
                    qb_ = rri("qbc", 4)
                    mm([(bank(qb_), selb[:, b_j, :], q_j, True, True)], [qkey_j, "selb"], [bk(qb_)])
                    return qb_
                qb_next = qbc(0)
                for it, (src, q_ap, qkey, b) in enumerate(items):
                    si = rri("kvs", 2)
                    kv = kvs[si]; kvk = f"kvs{si}"
                    ld(kv, src, [kvk])
                    qb = qb_next
                    if it + 1 < tot:
                        qb_next = qbc(it + 1)
                    pi_ = 8 + rri("prod", 2)
                    prod = ft[pi_]; pk_ = f"ft{pi_}"
                    tt("dve", prod[:, :], kv[:, 0:512], bank(qb), ALU.mult, [kvk, bk(qb)], [pk_])
                    sc_i = rri("s8", 2)
                    s8 = sm[:, 40 + 8 * sc_i:40 + 8 * sc_i + nh]; s8k = f"s8{sc_i}"
                    trk.op("dve", lambda e, s8=s8, prod=prod: e.reduce_sum(
                        out=s8, in_=prod[:, :].rearrange("p (h d) -> p h d", h=nh), axis=AX.X), [pk_], [s8k])
                    act(s8, s8, AF.Exp, [s8k], [s8k], scale=scale)
                    vi = rri("pv", 2)
                    pv = (bt[1], bt[5])[vi]; pvk = ("bt1", "bt5")[vi]
                    tt("pool", pv[:, :].rearrange("p (h d) -> p h d", h=nh), kv[:, 512:1024].rearrange("p (h d) -> p h d", h=nh),
                       s8.unsqueeze(2).to_broadcast([128, nh, dh]), ALU.mult, [kvk, s8k], [pvk])
                    mm([(pso, eye16b[:, b, :], pv[:, :], it == 0, it == tot - 1),
                        (psl, eye16v[:, b, :], s8, it == 0, it == tot - 1)], [pvk, s8k], pkeys)

            items = []
            for g, d in enumerate(DILS):
                for b in range(16):
                    src = cw[g][b, 0:127 * d + 1:d, :] if d > 1 else cw[g][b, :, :]
                    items.append((src, ft[2 + g][:n, :], f"ft{2 + g}", b))
            pso = bank(4)[:n, :]
            psl = bank(5)[:n, 0:8]
            samp_attn(items, 8, 64, 0.125, pso, psl, [bk(4), bk(5)])
            oS = gas
            oS = big[0:16, 28672 - 1024:28672 - 512]
            oS = ft[8]
            lS = sm[:n, 64:72]
            cp("dve", oS[:n, :], pso, [bk(4)], ["ft8"])
            cp("dve", lS, psl, [bk(5)], ["lS"])
            for g in range(3):
                tt("dve", ft[0][:n, :], ft[2 + g][:n, :], ft[5 + g][:n, :], ALU.mult, [f"ft{2 + g}", f"ft{5 + g}"], ["ft0"])
                pn = sm[:n, 72:80]
                trk.op("dve", lambda e, pn=pn: e.reduce_sum(out=pn, in_=ft[0][:n, :].rearrange("p (h d) -> p h d", h=8), axis=AX.X),
                       ["ft0"], ["pn"])
                act(pn, pn, AF.Exp, ["pn"], ["pn"], scale=0.125)
                tt("dve", ft[1][:n, :].rearrange("p (h d) -> p h d", h=8),
                   U[:, C_AV + g * 512:C_AV + (g + 1) * 512].rearrange("p (h d) -> p h d", h=8),
                   pn.unsqueeze(2).to_broadcast([n, 8, 64]), ALU.mult, ["U", "pn"], ["ft1"])
                tt("dve", oS[:n, :], oS[:n, :], ft[1][:n, :], ALU.add, ["ft8", "ft1"], ["ft8"])
                tt("dve", lS, lS, pn, ALU.add, ["lS", "pn"], ["lS"])
            trk.op("dve", lambda e: e.reciprocal(out=lS, in_=lS), ["lS"], ["lS"])
            Gz = ft[9]
            act(Gz[:n, :], U[:, C_AZ:C_AZ + 512], AF.Silu, ["U"], ["ft9"])
            tt("dve", oS[:n, :].rearrange("p (h d) -> p h d", h=8), oS[:n, :].rearrange("p (h d) -> p h d", h=8),
               lS.unsqueeze(2).to_broadcast([n, 8, 64]), ALU.mult, ["ft8", "lS"], ["ft8"])
            tt("dve", bt[0][:n, :], oS[:n, :], Gz[:n, :], ALU.mult, ["ft8", "ft9"], ["bt0"])
            pq = bankb(6).rearrange("p (c n) -> p c n", c=8)
            transposes([(pq[:, c, :n], bt[0][:n, c * 128:(c + 1) * 128], identb[:n, :n]) for c in range(4)], ["bt0"], [bk(6)])
            yaT = bt[4].rearrange("p (c n) -> p c n", c=4)
            cp("act", yaT[:, :, :n], pq[:, 0:4, :n], [bk(6)], ["bt4"])
            grp = []
            for half in range(2):
                for c in range(4):
                    grp.append((bank(half)[:n, :], yaT[:, c, :n], wpas[:, c, half * 512:(half + 1) * 512], c == 0, c == 3))
            mm(grp, ["bt4", "wpas"], [bk(0), bk(1)])
            for half in range(2):
                act(ft[0][:n, :], U[:, C_GA + half * 512:C_GA + (half + 1) * 512], AF.Sigmoid, ["U"], ["ft0"])
                tt("dve", gas[:, half * 512:(half + 1) * 512], bank(half)[:n, :], ft[0][:n, :], ALU.mult, [bk(half), "ft0"], ["gas"])

            fS = ft[0]; kinS = ft[1]
            act(fS[:n, :], U[:, C_BF:C_BF + 512], AF.Sigmoid, ["U"], ["ft0"])
            tt("dve", fS[:n, :], fS[:n, :], omlr[:n, :], ALU.mult, ["ft0"], ["ft0"])
            tt("dve", fS[:n, :], fS[:n, :], lbr[:n, :], ALU.add, ["ft0"], ["ft0"])
            ts("dve", kinS[:n, :], fS[:n, :], -1.0, 1.0, ALU.mult, ALU.add, ["ft0"], ["ft1"])
            pT = bank(6)[:, 0:128]
            transposes([(pT[:, h * 16:(h + 1) * 16], fS[:n, h * 128:(h + 1) * 128], identf[:n, :n]) for h in range(4)] +
                       [(pT[:, 64 + h * 16:64 + (h + 1) * 16], U[:, C_BQ + h * 128:C_BQ + (h + 1) * 128], identf[:n, :n]) for h in range(4)],
                       ["ft0", "U"], [bk(6)])
            fqT = ft[2]
            cp("dve", fqT[:, 0:128], pT, [bk(6)], ["ft2"])
            fT = fqT[:, 0:64].rearrange("p (h b) -> p h b", h=4)
            qT = fqT[:, 64:128].rearrange("p (h b) -> p h b", h=4)
            qTm = [ft[3], ft[4]]
            for hp in range(2):
                tt("dve", qTm[hp][:, :].rearrange("p (h b c) -> p h b c", h=2, b=16),
                   qT[:, 2 * hp:2 * hp + 2, :].unsqueeze(3).to_broadcast([128, 2, 16, 16]),
                   eye16v.unsqueeze(1).to_broadcast([128, 2, 16, 16]), ALU.mult, ["ft2"], [f"ft{3 + hp}"])
            pob = bank(7)[:n, :]
            zt = ft[8]
            trk.op("pool", lambda e: e.memset(zt[:, :], 0.0), (), ["ft8"])
            mm([(pob, eye16v[:, 0, :], zt[:, :], True, False)], ["ft8"], [bk(7)])
            vexp = ft[5]
            for half in range(2):
                for bl in range(8):
                    b = half * 8 + bl
                    ld(Ssm[:, bl, :, :], sh[b].rearrange("h k v -> k h v"), [f"Ssm{bl}"])
                for q4 in range(2):
                    for h in range(4):
                        b0 = half * 8 + q4 * 4
                        tt("dve", vexp[:n, :].rearrange("p (b v) -> p b v", b=4),
                           U[:, C_BI + h * 128:C_BI + (h + 1) * 128].unsqueeze(1).to_broadcast([n, 4, 128]),
                           identf[:n, b0:b0 + 4].unsqueeze(2).to_broadcast([n, 4, 128]), ALU.mult, ["U"], ["ft5"])
                        kb_i = rri("kvb", 4)
                        mm([(bank(kb_i), kinS[:n, h * 128:(h + 1) * 128], vexp[:n, :], True, True)], ["ft1", "ft5"], [bk(kb_i)])
                        for bi_ in range(4):
                            bl = q4 * 4 + bi_
                            b = half * 8 + bl
                            stt("dve", Ssm[:, bl, h, :], Ssm[:, bl, h, :], fT[:, h, b:b + 1], bank(kb_i)[:, bi_ * 128:(bi_ + 1) * 128],
                                ALU.mult, ALU.add, [f"Ssm{bl}", "ft2", bk(kb_i)], [f"Ssm{bl}"])
                for bl in range(8):
                    b = half * 8 + bl
                    mm([(pob[:, h * 128:(h + 1) * 128], qTm[h // 2][:, :].rearrange("p (h b c) -> p h b c", h=2, b=16)[:, h % 2, b, :],
                         Ssm[:, bl, h, :], False, b == 15) for h in range(4)],
                       [f"Ssm{bl}", "ft3", "ft4"], [bk(7)])
                    stdma(hg_s[b].rearrange("h k v -> k h v"), Ssm[:, bl, :, :], [f"Ssm{bl}"])
            Gb = ft[6]
            act(Gb[:n, :], U[:, C_BZ:C_BZ + 512], AF.Silu, ["U"], ["ft6"])
            tt("dve", ft[7][:n, :], Gb[:n, :], C("hg")[:n, :], ALU.mult, ["ft6"], ["ft7"])
            ybT = head_norm_gate(bank(7), bk(7), ft[7], "ft7", n)

            items = []
            for b in range(16):
                for mb in range(2):
                    items.append((cm[b, mb * 128:(mb + 1) * 128, :], U[:, C_CQ:C_CQ + 512], "U", b))
            pso2 = bank(4)[:n, :]
            psl2 = bank(5)[:n, 0:4]
            samp_attn(items, 4, 128, float(128 ** -0.5), pso2, psl2, [bk(4), bk(5)])
            lC = sm[:n, 80:84]
            trk.op("dve", lambda e: e.reciprocal(out=lC, in_=psl2), [bk(5)], ["lC"])
            Gc = ft[9]
            act(Gc[:n, :], U[:, C_CZ:C_CZ + 512], AF.Silu, ["U"], ["ft9"])
            tt("dve", ft[8][:n, :].rearrange("p (h d) -> p h d", h=4), pso2.rearrange("p (h d) -> p h d", h=4),
               lC.unsqueeze(2).to_broadcast([n, 4, 128]), ALU.mult, [bk(4), "lC"], ["ft8"])
            tt("dve", bt[3][:n, :], ft[8][:n, :], Gc[:n, :], ALU.mult, ["ft8", "ft9"], ["bt3"])
            pq = bankb(6).rearrange("p (c n) -> p c n", c=8)
            transposes([(pq[:, c, :n], bt[3][:n, c * 128:(c + 1) * 128], identb[:n, :n]) for c in range(4)], ["bt3"], [bk(6)])
            ycT = bt[6].rearrange("p (c n) -> p c n", c=4)
            cp("act", ycT[:, :, :n], pq[:, 0:4, :n], [bk(6)], ["bt6"])
            epilogue(n, xn, xnk, x_t, xk, ybT, ycT, (gas, "gas"), y_s[:, :], gate_src=U)

    return nc, trk, body


def _consts(core, inputs):
    cf = np.zeros((128, NCF), np.float32)

    def put(name, arr):
        a, b = _CF[name]
        cf[:arr.shape[0], a:b] = arr
    s = np.arange(128)
    U = (s[:, None] <= s[None, :]).astype(np.float32)
    put("identf", np.eye(128, dtype=np.float32))
    put("Um", U - U[:, 63:64])
    put("M2", (s[:, None] > s[None, :]).astype(np.float32))
    put("attm", U)
    ind = np.ones((128, 2), np.float32)
    ind[64:, 0] = 0.0
    put("ind", ind)
    put("pregT", inputs["norm_pre"][0].reshape(8, 128).T)
    put("memgT", inputs["mem_norm"][0].reshape(8, 128).T)
    put("postg", np.broadcast_to(inputs["norm_post"][0][None, :], (128, 1024)))
    put("hg", np.broadcast_to(inputs["hgrn_out_norm"][0][None, :], (128, 512)))
    put("l0", np.broadcast_to(inputs["hgrn_lb_logits"][0][None, :], (128, 512)))
    put("l1", np.broadcast_to(inputs["hgrn_lb_logits"][1][None, :], (128, 512)))
    cm = (np.arange(8) < core).astype(np.float32)
    put("cmask", np.broadcast_to(cm[None, :], (128, 8)))
    put("eye16", np.broadcast_to(np.eye(16, dtype=np.float32).reshape(1, 256), (128, 256)))
    inv = (10000.0 ** (-np.arange(0, 64, 2, dtype=np.float32) / np.float32(64))).astype(np.float32)
    ang = (np.float32(PAST) * inv).astype(np.float32)
    put("ropes", np.broadcast_to(np.concatenate([np.cos(ang), np.sin(ang)])[None, :].astype(np.float32), (128, 64)))
    cb = np.zeros((128, NCB), np.float32)
    cb[:, 0:128] = np.eye(128)
    j = s[:, None]
    i = s[None, :]
    cb[:, 128:256] = np.where(j >= i, 0.0, NEG)
    cb[:, 256:384] = np.where(j <= i, 0.0, NEG)
    cb[:, 384:512] = NEG if core == 0 else cb[:, 128:256]
    cb[:, 512:640] = (s[:, None] > s[None, :])
    cb[:, 640:768] = U - U[:, 63:64]
    cb[:, 768:770] = ind
    cb[:, 772:1028] = np.eye(16, dtype=np.float32).reshape(1, 256)
    rope = np.zeros((len(PA_BLOCKS), 128, 64), np.float32)
    for bi, (g, r, n) in enumerate(PA_BLOCKS):
        d = DILS[g]
        pos = core * TPC + d * (128 * n + np.arange(128)) + r
        angp = pos.astype(np.float32)[:, None] * inv[None, :]
        rope[bi, :, 0:32] = np.cos(angp)
        rope[bi, :, 32:64] = np.sin(angp)
    return cf, cb, rope


_PROG = {}


def kernel(**inputs):
    inp = {k: np.asarray(v) for k, v in inputs.items()}
    if "nc" not in _PROG:
        _PROG["nc"] = build_program()
    nc = _PROG["nc"]
    xp = inp["x_prompt"][0]
    selb = np.zeros((16, 16, 128), np.float32)
    for b in range(16):
        selb[b, b, :] = 1.0
    in_maps = []
    for c in range(NCORE):
        xe = np.zeros((2 * TPC, D), np.float32)
        if c > 0:
            xe[:TPC] = xp[(c - 1) * TPC:c * TPC]
        xe[TPC:] = xp[c * TPC:(c + 1) * TPC]
        cf, cb, rope = _consts(c, inp)
        sl = slice(16 * c, 16 * c + 16)
        xh = np.zeros((NPREV * 128, D), np.float32)
        if c > 0:
            xh[NPREV * 128 - c * TPC:] = xp[:c * TPC]
        in_maps.append({
            "xe": xe, "xh": xh,
            "xs": np.ascontiguousarray(inp["x_sample"][sl, 0, :]),
            "mem": np.ascontiguousarray(inp["mem_prompt"][0]),
            "cw0": np.ascontiguousarray(inp["cache_win128_kv"][0, sl].reshape(16, 128, 1024)),
            "cw1": np.ascontiguousarray(inp["cache_win512_kv"][0, sl].reshape(16, 512, 1024)),
            "cw2": np.ascontiguousarray(inp["cache_win2048_kv"][0, sl].reshape(16, 2048, 1024)),
            "sh": np.ascontiguousarray(inp["state_hgrn"][0, sl]),
            "cm": np.ascontiguousarray(inp["cache_mem_kv"][0, sl].reshape(16, 256, 1024)),
            "w_in": np.ascontiguousarray(inp["w_in"][0]),
            "w_mem": np.ascontiguousarray(inp["w_mem_kv"][0]),
            "w_pa": np.ascontiguousarray(inp["w_branch_a"][0]),
            "w_pb": np.ascontiguousarray(inp["w_branch_b"][0]),
            "w_pc": np.ascontiguousarray(inp["w_branch_c"][0]),
            "w_out": np.ascontiguousarray(inp["w_out"][0]),
            "cf": cf, "cb": cb, "rope": rope, "selb": selb,
        })
    if os.environ.get("K_NOSAMPLE"):
        for m in in_maps:
            for k in ("cw0", "cw1", "cw2", "sh", "cm"):
                m[k] = np.ascontiguousarray(m[k][0:1])
    res = run_bass_kernel_spmd(nc, in_maps, core_ids=list(range(NCORE)))
    R = res.results
    y_prompt = np.concatenate([R[c]["y_p"] for c in range(NCORE)], axis=0).reshape(1, T, D)
    y_sample = np.concatenate([R[c]["y_s"] for c in range(NCORE)], axis=0).reshape(128, 1, D)
    last = R[NCORE - 1]
    nw0 = last["nw0"].reshape(1, 1, 128, 2, 8, 64)
    nw1 = last["nw1"].reshape(1, 1, 512, 2, 8, 64)
    nw2 = last["nw2"].reshape(1, 1, 2048, 2, 8, 64)
    hgp = last["hg_p"].reshape(1, 1, 4, 128, 128)
    mkv = R[0]["mkv"].reshape(1, 1, 256, 2, 4, 128)
    nws = [np.concatenate([R[c][f"nws{g}"] for c in range(NCORE)], axis=0).reshape(1, 128, 1, 2, 8, 64) for g in range(3)]
    hgs = np.concatenate([R[c]["hg_s"] for c in range(NCORE)], axis=0).reshape(1, 128, 4, 128, 128)
    return (y_prompt, y_sample, nw0, nw1, nw2, hgp, mkv, nws[0], nws[1], nws[2], hgs)
```
